# Optimizing a Trainium2 kernel written in Bass

```python
import math
import jax, jax.numpy as jnp
from jax import lax
import numpy as np

D_MODEL = 1024
BATCH = 4
SEQ = 4096
DEPTH = 4

N_MIXERS = 3
EPS = 1e-6
NEG_INF = -1e30

NUM_BUCKETS = 32
MAX_DISTANCE = 1024
N_BIAS_HEADS = 16

A_HEADS = 16
A_KV_HEADS = 4
A_GROUP = A_HEADS // A_KV_HEADS
A_HEAD_DIM = D_MODEL // A_HEADS
A_WINDOW = 128
A_QKV_WIDTH = (A_HEADS + 2 * A_KV_HEADS) * A_HEAD_DIM

B_HEADS = 8
B_HEAD_DIM = D_MODEL // (2 * B_HEADS)
B_QBLOCK = 128

C_BRANCHES = ((128, 1), (512, 4), (2048, 16))
C_HEADS_PER_GROUP = 4
C_KV_HEADS = 4
C_HEAD_DIM = 128
C_Q_WIDTH = len(C_BRANCHES) * C_HEADS_PER_GROUP * C_HEAD_DIM
C_QKV_WIDTH = C_Q_WIDTH + 2 * C_KV_HEADS * C_HEAD_DIM

D_FF = 2816
CONV_WIDTH = 3

kernel_name = 'hybrid_interleaved_encoder_block'


def rmsnorm(x, g):
    xf = x.astype(jnp.float32)
    y = xf * lax.rsqrt(jnp.mean(xf * xf, axis=-1, keepdims=True) + EPS)
    return (y * g.astype(jnp.float32)).astype(x.dtype)


def rel_bucket(rel):
    nb = NUM_BUCKETS // 2
    max_exact = nb // 2
    n = jnp.abs(rel)
    nf = jnp.maximum(n, 1).astype(jnp.float32)
    large = max_exact + (jnp.log(nf / max_exact) / math.log(MAX_DISTANCE / max_exact) * (nb - max_exact)).astype(jnp.int32)
    large = jnp.minimum(large, nb - 1)
    return jnp.where(rel > 0, nb, 0) + jnp.where(n < max_exact, n, large)


def banded_attention(q, k, v, kvalid, table_cols, half, dil):
    N, L, Hk, G, dh = q.shape
    blk = half
    nb = L // blk

    def windows(t):
        tp = jnp.pad(t, [(0, 0), (blk, blk)] + [(0, 0)] * (t.ndim - 2))
        tp = tp.reshape((N, nb + 2, blk) + t.shape[2:])
        return jnp.concatenate([tp[:, :-2], tp[:, 1:-1], tp[:, 2:]], axis=2)

    kw, vw, mw = windows(k), windows(v), windows(kvalid)
    qb = q.reshape(N, nb, blk, Hk, G, dh)
    s = jnp.einsum('nbqhgd,nbkhd->nbhgqk', qb, kw, preferred_element_type=jnp.float32)
    offs = jnp.arange(3 * blk)[None, :] - blk - jnp.arange(blk)[:, None]
    bias = table_cols[rel_bucket(offs * dil)].astype(jnp.float32)
    bias = bias.transpose(2, 0, 1).reshape(Hk, G, blk, 3 * blk)
    mask = (jnp.abs(offs) <= half) & mw[:, :, None, None, None, :]
    logits = jnp.where(mask, s * (dh ** -0.5) + bias, NEG_INF)
    lse = jax.nn.logsumexp(logits, axis=-1)
    p = jnp.exp(logits - lse[..., None])
    o = jnp.einsum('nbhgqk,nbkhd->nbqhgd', p.astype(v.dtype), vw)
    return o.reshape(N, L, Hk, G, -1), lse.transpose(0, 1, 4, 2, 3).reshape(N, L, Hk, G)


def windowed_gqa_sink(h, w_qkv, q_gain, k_gain, sink, w_o, rel_table):
    B, S, _ = h.shape
    q, k, v = jnp.split(h @ w_qkv, [A_HEADS * A_HEAD_DIM, (A_HEADS + A_KV_HEADS) * A_HEAD_DIM], axis=-1)
    q = rmsnorm(q.reshape(B, S, A_KV_HEADS, A_GROUP, A_HEAD_DIM), q_gain)
    k = rmsnorm(k.reshape(B, S, A_KV_HEADS, A_HEAD_DIM), k_gain)
    v = v.reshape(B, S, A_KV_HEADS, A_HEAD_DIM)
    kvalid = jnp.ones((B, S), dtype=bool)
    o, lse = banded_attention(q, k, v, kvalid, rel_table, A_WINDOW, 1)
    keep = jax.nn.sigmoid(lse - sink.reshape(A_KV_HEADS, A_GROUP).astype(jnp.float32))
    o = o * keep[..., None].astype(o.dtype)
    return o.reshape(B, S, -1) @ w_o


def diff_attention(h, w_qkv, q_gain, k_gain, lam_q1, lam_k1, lam_q2, lam_k2, sub_gain, w_o, rel_table, lambda_init):
    B, S, _ = h.shape
    H, d = B_HEADS, B_HEAD_DIM
    q, k, v = jnp.split(h @ w_qkv, 3, axis=-1)
    q = rmsnorm(q.reshape(B, S, H, 2, d), q_gain)
    k = rmsnorm(k.reshape(B, S, H, 2, d), k_gain)
    v = v.reshape(B, S, H, 2 * d)
    lam = (jnp.exp(jnp.sum(lam_q1.astype(jnp.float32) * lam_k1.astype(jnp.float32)))
           - jnp.exp(jnp.sum(lam_q2.astype(jnp.float32) * lam_k2.astype(jnp.float32))) + lambda_init)
    nqb = S // B_QBLOCK
    qb = q.reshape(B, nqb, B_QBLOCK, H, 2, d).transpose(1, 0, 2, 3, 4, 5)
    kpos = jnp.arange(S)
    table = rel_table.reshape(NUM_BUCKETS, H, 2)

    def block(args):
        qblk, b = args
        qpos = b * B_QBLOCK + jnp.arange(B_QBLOCK)
        bias = table[rel_bucket(kpos[None, :] - qpos[:, None])].astype(jnp.float32)
        s = jnp.einsum('bqhjd,bkhjd->bhjqk', qblk, k, preferred_element_type=jnp.float32) * (d ** -0.5)
        a = jax.nn.softmax(s + bias.transpose(2, 3, 0, 1), axis=-1)
        a = a[:, :, 0] - lam * a[:, :, 1]
        return jnp.einsum('bhqk,bkhe->bqhe', a.astype(v.dtype), v)

    o = lax.map(block, (qb, jnp.arange(nqb)))
    o = o.transpose(1, 0, 2, 3, 4).reshape(B, S, H, 2 * d)
    o = rmsnorm(o, sub_gain) * (1.0 - lambda_init)
    return o.reshape(B, S, -1) @ w_o


def dilated_branch(q, k, v, dil, half, table_cols):
    B, S = q.shape[:2]
    span = dil * half
    L = -(-S // span) * span

    def pad(t):
        return jnp.pad(t, [(0, 0), (0, L - S)] + [(0, 0)] * (t.ndim - 2))

    def to_sub(t):
        rest = t.shape[2:]
        t = jnp.moveaxis(t.reshape((B, L // dil, dil) + rest), 2, 1)
        return t.reshape((B * dil, L // dil) + rest)

    def from_sub(t):
        rest = t.shape[2:]
        t = jnp.moveaxis(t.reshape((B, dil, L // dil) + rest), 1, 2)
        return t.reshape((B, L) + rest)[:, :S]

    kvalid = jnp.broadcast_to(jnp.arange(L) < S, (B, L))
    o, lse = banded_attention(to_sub(pad(q)), to_sub(pad(k)), to_sub(pad(v)), to_sub(kvalid), table_cols, half, dil)
    return from_sub(o), from_sub(lse)


def dilated_attention(h, w_qkv, q_gain, k_gain, w_o, rel_table):
    B, S, _ = h.shape
    Hg, Hk, d = C_HEADS_PER_GROUP, C_KV_HEADS, C_HEAD_DIM
    q, k, v = jnp.split(h @ w_qkv, [C_Q_WIDTH, C_Q_WIDTH + Hk * d], axis=-1)
    q = rmsnorm(q.reshape(B, S, len(C_BRANCHES), Hk, 1, d), q_gain)
    k = rmsnorm(k.reshape(B, S, Hk, d), k_gain)
    v = v.reshape(B, S, Hk, d)
    outs, lses = [], []
    for g, (window, dil) in enumerate(C_BRANCHES):
        half = window // (2 * dil)
        o, lse = dilated_branch(q[:, :, g], k, v, dil, half, rel_table[:, g * Hg:(g + 1) * Hg])
        outs.append(o)
        lses.append(lse)
    wts = jax.nn.softmax(jnp.stack(lses), axis=0)
    o = jnp.sum(wts[..., None].astype(v.dtype) * jnp.stack(outs), axis=0)
    return o.reshape(B, S, -1) @ w_o


def conv_ffn(h, w_up, conv_w, conv_b, w_down):
    u = h @ w_up
    u = lax.conv_general_dilated(u, conv_w, window_strides=(1,),
                                 padding=((CONV_WIDTH // 2, CONV_WIDTH // 2),),
                                 dimension_numbers=('NWC', 'WIO', 'NWC'),
                                 feature_group_count=u.shape[-1]) + conv_b
    gate, val = jnp.split(u, 2, axis=-1)
    return (jax.nn.silu(gate) * val) @ w_down


def lambda_init_fn(layer):
    return 0.8 - 0.6 * math.exp(-0.3 * layer)


def setup_inputs(seed: int = 0) -> dict:
    key = jax.random.key(seed)
    keys = iter(jax.random.split(key, 128))

    def normal(shape, scale):
        return jax.random.normal(next(keys), shape, jnp.float32) * scale

    def gain(n):
        return 1.0 + normal((n,), 0.02)

    inp = {'x': normal((BATCH, SEQ, D_MODEL), 1.0),
           'rel_bias': normal((NUM_BUCKETS, N_BIAS_HEADS), 0.5)}
    for i in range(DEPTH):
        p = 'l%d_' % i
        kind = i % N_MIXERS
        inp[p + 'attn_norm'] = gain(D_MODEL)
        if kind == 0:
            inp[p + 'w_qkv'] = normal((D_MODEL, A_QKV_WIDTH), D_MODEL ** -0.5)
            inp[p + 'q_gain'] = gain(A_HEAD_DIM)
            inp[p + 'k_gain'] = gain(A_HEAD_DIM)
            inp[p + 'sink'] = normal((A_HEADS,), 0.5)
            inp[p + 'w_o'] = normal((A_HEADS * A_HEAD_DIM, D_MODEL), (A_HEADS * A_HEAD_DIM) ** -0.5)
        elif kind == 1:
            inp[p + 'w_qkv'] = normal((D_MODEL, 3 * D_MODEL), D_MODEL ** -0.5)
            inp[p + 'q_gain'] = gain(B_HEAD_DIM)
            inp[p + 'k_gain'] = gain(B_HEAD_DIM)
            inp[p + 'lambda_q1'] = normal((B_HEAD_DIM,), 0.1)
            inp[p + 'lambda_k1'] = normal((B_HEAD_DIM,), 0.1)
            inp[p + 'lambda_q2'] = normal((B_HEAD_DIM,), 0.1)
            inp[p + 'lambda_k2'] = normal((B_HEAD_DIM,), 0.1)
            inp[p + 'sub_gain'] = gain(2 * B_HEAD_DIM)
            inp[p + 'w_o'] = normal((D_MODEL, D_MODEL), D_MODEL ** -0.5)
        else:
            inp[p + 'w_qkv'] = normal((D_MODEL, C_QKV_WIDTH), D_MODEL ** -0.5)
            inp[p + 'q_gain'] = gain(C_HEAD_DIM)
            inp[p + 'k_gain'] = gain(C_HEAD_DIM)
            inp[p + 'w_o'] = normal((C_KV_HEADS * C_HEAD_DIM, D_MODEL), (C_KV_HEADS * C_HEAD_DIM) ** -0.5)
        inp[p + 'ffn_norm'] = gain(D_MODEL)
        inp[p + 'w_up'] = normal((D_MODEL, 2 * D_FF), D_MODEL ** -0.5)
        inp[p + 'conv_w'] = normal((CONV_WIDTH, 1, 2 * D_FF), CONV_WIDTH ** -0.5)
        inp[p + 'conv_b'] = normal((2 * D_FF,), 0.01)
        inp[p + 'w_down'] = normal((D_FF, D_MODEL), D_FF ** -0.5)
    return inp


def reference(x, rel_bias,
              l0_attn_norm, l0_w_qkv, l0_q_gain, l0_k_gain, l0_sink, l0_w_o,
              l0_ffn_norm, l0_w_up, l0_conv_w, l0_conv_b, l0_w_down,
              l1_attn_norm, l1_w_qkv, l1_q_gain, l1_k_gain, l1_lambda_q1, l1_lambda_k1,
              l1_lambda_q2, l1_lambda_k2, l1_sub_gain, l1_w_o,
              l1_ffn_norm, l1_w_up, l1_conv_w, l1_conv_b, l1_w_down,
              l2_attn_norm, l2_w_qkv, l2_q_gain, l2_k_gain, l2_w_o,
              l2_ffn_norm, l2_w_up, l2_conv_w, l2_conv_b, l2_w_down,
              l3_attn_norm, l3_w_qkv, l3_q_gain, l3_k_gain, l3_sink, l3_w_o,
              l3_ffn_norm, l3_w_up, l3_conv_w, l3_conv_b, l3_w_down):
    attn_norms = [l0_attn_norm, l1_attn_norm, l2_attn_norm, l3_attn_norm]
    mixer_params = [
        (l0_w_qkv, l0_q_gain, l0_k_gain, l0_sink, l0_w_o),
        (l1_w_qkv, l1_q_gain, l1_k_gain, l1_lambda_q1, l1_lambda_k1, l1_lambda_q2, l1_lambda_k2, l1_sub_gain, l1_w_o),
        (l2_w_qkv, l2_q_gain, l2_k_gain, l2_w_o),
        (l3_w_qkv, l3_q_gain, l3_k_gain, l3_sink, l3_w_o),
    ]
    ffn_norms = [l0_ffn_norm, l1_ffn_norm, l2_ffn_norm, l3_ffn_norm]
    ffn_params = [
        (l0_w_up, l0_conv_w, l0_conv_b, l0_w_down),
        (l1_w_up, l1_conv_w, l1_conv_b, l1_w_down),
        (l2_w_up, l2_conv_w, l2_conv_b, l2_w_down),
        (l3_w_up, l3_conv_w, l3_conv_b, l3_w_down),
    ]
    for i in range(DEPTH):
        kind = i % N_MIXERS
        h = rmsnorm(x, attn_norms[i])
        if kind == 0:
            y = windowed_gqa_sink(h, *mixer_params[i], rel_bias)
        elif kind == 1:
            y = diff_attention(h, *mixer_params[i], rel_bias, lambda_init_fn(i))
        else:
            y = dilated_attention(h, *mixer_params[i], rel_bias)
        x = x + y
        x = x + conv_ffn(rmsnorm(x, ffn_norms[i]), *ffn_params[i])
    return x
```

```python
import math
from contextlib import ExitStack

import numpy as np
import ml_dtypes

import concourse.bass as bass
import concourse.mybir as mybir
from concourse.ap import AP
from concourse.bass_utils import run_bass_kernel_spmd

F32 = mybir.dt.float32
BF16 = mybir.dt.bfloat16
AF = mybir.ActivationFunctionType
ALU = mybir.AluOpType
AX = mybir.AxisListType
NPBF = ml_dtypes.bfloat16

NCORES = 8
T = 2048
NT = 16
D = 1024
DFF = 2816
EPS = 1e-6
NEG = -30000.0

LAYERS = [
    dict(kind="A", F=1536, nqb=8, nkb=2, FV=256, nfc=8),
    dict(kind="B", F=3072, nqb=8, nkb=8, FV=1024, nfc=8),
    dict(kind="C", F=2560, nqb=12, nkb=4, FV=512, nfc=4),
    dict(kind="A", F=1536, nqb=8, nkb=2, FV=256, nfc=8),
]
NI = {"A": 3, "B": 44, "C": 25}
NUNIT_BT = {"A": 16, "B": 16, "C": 4}


def lambda_init_fn(layer):
    return 0.8 - 0.6 * math.exp(-0.3 * layer)


class Op:
    __slots__ = ("eng", "fn", "deps", "needs_inc", "val", "semkey", "is_dma")

    def __init__(self, eng, fn, deps, semkey, is_dma):
        self.eng = eng
        self.fn = fn
        self.deps = deps
        self.needs_inc = False
        self.val = None
        self.semkey = semkey
        self.is_dma = is_dma


class Prog:
    ENGS = ("sync", "scalar", "vector", "gpsimd", "tensor")

    def __init__(self, nc):
        self.nc = nc
        self.ops = {e: [] for e in self.ENGS}
        self.lastw = {}
        self.readers = {}
        self.es = ExitStack()
        self.outs = []
        self.fence = []
        self.last_dma = {}

    def barrier(self):
        fence = []
        for e in self.ENGS:
            for o in reversed(self.ops[e]):
                if not o.is_dma:
                    fence.append(o)
                    break
        fence.extend(self.last_dma.values())
        self.fence = fence

    def op(self, eng, fn, reads=(), writes=(), slot=None, out=False):
        deps = list(self.fence)
        for b in reads:
            w = self.lastw.get(b)
            if w is not None:
                deps.append(w)
        for b in writes:
            w = self.lastw.get(b)
            if w is not None:
                deps.append(w)
            deps.extend(self.readers.get(b, ()))
        is_dma = slot is not None
        semkey = ("dma", slot) if is_dma else ("eng", eng)
        o = Op(eng, fn, deps, semkey, is_dma)
        self.ops[eng].append(o)
        if is_dma:
            self.last_dma[slot] = o
        for b in writes:
            self.lastw[b] = o
            self.readers[b] = []
        for b in reads:
            self.readers.setdefault(b, []).append(o)
        if out:
            self.outs.append(o)
        return o

    @staticmethod
    def _skip(d, o):
        return d is o or (d.eng == "tensor" and o.eng == "tensor" and not d.is_dma and not o.is_dma)

    def finalize(self):
        nc = self.nc
        final_waits = self.outs
        for e in self.ENGS:
            for o in self.ops[e]:
                for d in o.deps:
                    if not self._skip(d, o):
                        d.needs_inc = True
        for d in final_waits:
            d.needs_inc = True
        for e in self.ENGS:
            for o in self.ops[e]:
                if o.is_dma:
                    o.needs_inc = True
        counters = {}
        for e in self.ENGS:
            for o in self.ops[e]:
                if o.needs_inc:
                    c = counters.get(o.semkey, 0) + (16 if o.is_dma else 1)
                    counters[o.semkey] = c
                    o.val = c
        sems = {}
        for i, k in enumerate(counters):
            sems[k] = self.es.enter_context(nc.semaphore("s%d" % i))
        self.nsem = len(sems)
        block = self.es.enter_context(nc.Block())

        def run(e, engine):
            known = {}
            for o in self.ops[e]:
                need = {}
                for d in o.deps:
                    if self._skip(d, o) or d.val is None:
                        continue
                    if need.get(d.semkey, 0) < d.val:
                        need[d.semkey] = d.val
                for k, v in need.items():
                    if known.get(k, 0) >= v:
                        continue
                    engine.wait_ge(sems[k], v)
                    known[k] = v
                ins = o.fn(engine)
                if o.needs_inc:
                    ins.then_inc(sems[o.semkey], 16 if o.is_dma else 1)
            if e == "sync":
                need = {}
                for d in final_waits:
                    if need.get(d.semkey, 0) < d.val:
                        need[d.semkey] = d.val
                for k, v in need.items():
                    engine.wait_ge(sems[k], v)

        @block.sync
        def _(eng):
            run("sync", eng)

        @block.scalar
        def _(eng):
            run("scalar", eng)

        @block.vector
        def _(eng):
            run("vector", eng)

        @block.gpsimd
        def _(eng):
            run("gpsimd", eng)

        @block.tensor
        def _(eng):
            run("tensor", eng)

        self.es.close()


SB_BASE = 16512
PERM_END = SB_BASE + 136256
SCR_END = 229376


class Builder:
    def __init__(self):
        self.nc = bass.Bass("TRN2", target_bir_lowering=False)
        self.P = Prog(self.nc)
        self.din_names = []
        self.dout_names = []
        self.d = {}
        self.uid = 0
        self.perm_off = SB_BASE
        self.scr_off = PERM_END
        nc = self.nc
        self.x = self.perm("x", [128, NT, D], F32)
        self.actT = self.perm("actT", [128, 8, T + 2], BF16)
        self.ws = [self.perm("ws%d" % i, [128, 4, 512], F32) for i in range(2)]
        self.wb = [self.perm("wb%d" % i, [128, 8, 512], BF16) for i in range(2)]
        self.ident = self.perm("ident", [128, 128], BF16)
        self.identf = self.perm("identf", [128, 128], F32)
        self.ss = self.perm("ss", [128, NT], F32)
        self.rstd = self.perm("rstd", [128, NT], F32)
        self.ss2 = self.perm("ss2", [128, NT], F32)
        self.epsb = self.perm("epsb", [128, 1], F32)
        self.flags = self.perm("flags", [128, 2], F32)
        self.gcol = self.perm("gcol", [128, 8], F32)
        self.hbs = [self.perm("hb%d" % i, [128, D], BF16) for i in range(2)]
        self.nscr = 0
        assert self.perm_off <= PERM_END, self.perm_off
        self.banks = [nc.alloc_psum_tensor("bank%d" % i, [128, 512], F32) for i in range(8)]
        self.wcount = 0
        self.init_consts()

    def perm(self, name, shape, dt):
        n = int(np.prod(shape[1:])) * (4 if dt == F32 else 2)
        n = (n + 31) // 32 * 32
        t = self.nc.alloc_sbuf_tensor_at(name, shape, dt, offset=self.perm_off)
        self.perm_off += n
        return t

    def scr_reset(self):
        if self.nscr > 0:
            self.P.barrier()
        self.nscr += 1
        self.scr_off = PERM_END

    def scr(self, name, shape, dt):
        n = int(np.prod(shape[1:])) * (4 if dt == F32 else 2)
        n = (n + 31) // 32 * 32
        self.uid += 1
        t = self.nc.alloc_sbuf_tensor_at("%s_%d" % (name, self.uid), shape, dt, offset=self.scr_off)
        self.scr_off += n
        assert self.scr_off <= SCR_END, (name, self.scr_off)
        return t

    def din(self, name, shape, dt):
        if name not in self.d:
            self.d[name] = self.nc.dram_tensor(name, list(shape), dt, kind="ExternalInput")
            self.din_names.append(name)
        return self.d[name]

    def dout(self, name, shape, dt):
        if name not in self.d:
            self.d[name] = self.nc.dram_tensor(name, list(shape), dt, kind="ExternalOutput")
            self.dout_names.append(name)
        return self.d[name]

    def bank_bf(self, i):
        return self.banks[i][:].bitcast(BF16).rearrange("p (c t) -> p c t", t=128)

    def init_consts(self):
        P = self.P
        idd = self.din("ident", [128, 128], F32).ap()
        fl = self.din("flags", [128, 2], F32).ap()
        P.op("sync", lambda e: e.dma_start(out=self.identf[:], in_=idd), writes=["identf"], slot="c_id")
        P.op("sync", lambda e: e.dma_start(out=self.flags[:], in_=fl), writes=["flags"], slot="c_fl")
        P.op("vector", lambda e: e.tensor_copy(out=self.ident[:], in_=self.identf[:]), reads=["identf"], writes=["ident"])
        P.op("vector", lambda e: e.memset(self.epsb[:], EPS), writes=["eps"])

    def barrier_keys(self):
        pass

    def load_x(self):
        P = self.P
        xin = self.din("x_in", [T, D], F32).ap().rearrange("(t p) d -> p t d", p=128)
        for t0 in range(0, NT, 4):
            P.op("sync", lambda e, t0=t0: e.dma_start(out=self.x[:, t0:t0 + 4, :], in_=xin[:, t0:t0 + 4, :]),
                 writes=[("x", t) for t in range(t0, t0 + 4)], slot="xld%d" % t0)

    def store_x(self):
        P = self.P
        xo = self.dout("x_out", [T, D], F32).ap().rearrange("(t p) d -> p t d", p=128)
        for t0 in range(0, NT, 4):
            P.op("sync", lambda e, t0=t0: e.dma_start(out=xo[:, t0:t0 + 4, :], in_=self.x[:, t0:t0 + 4, :]),
                 reads=[("x", t) for t in range(t0, t0 + 4)], slot="xst%d" % t0, out=True)

    def load_actT(self):
        P = self.P
        a = self.din("actT_in", [128, 8, T + 2], BF16).ap()
        P.op("sync", lambda e: e.dma_start(out=self.actT[:], in_=a),
             writes=[("actT", c, t) for c in range(8) for t in range(NT)] + ["halo"], slot="aTld")

    def store_actT(self):
        P = self.P
        a = self.dout("actT_out", [128, 8, T + 2], BF16).ap()
        P.op("sync", lambda e: e.dma_start(out=a, in_=self.actT[:]),
             reads=[("actT", c, t) for c in range(8) for t in range(NT)] + ["halo"], slot="aTst", out=True)

    def load_unit(self, W, kchunks, segs, scale=None, scale_reads=()):
        P = self.P
        i = self.wcount
        self.wcount += 1
        slot = i % 2
        wb = self.wb[slot]
        key = ("wb", slot)
        Wa = W.ap()
        ncol = sum(s[1] for s in segs)
        halves = [kchunks[0:4], kchunks[4:8]]
        allkeys = []
        for hi, kc in enumerate(halves):
            if not kc:
                continue
            ws = self.ws[hi]
            co = 0
            for si, (c0, cn) in enumerate(segs):
                k0 = kc[0]
                src = Wa[k0 * 128:(k0 + len(kc)) * 128, c0:c0 + cn].rearrange("(c p) f -> p c f", p=128)
                P.op("sync", lambda e, ws=ws, src=src, co=co, cn=cn, n=len(kc): e.dma_start(out=ws[:, 0:n, co:co + cn], in_=src),
                     writes=[("ws", hi, si)], slot="ws%d_%d" % (hi, si))
                co += cn
            n = len(kc)
            if scale is None:
                P.op("gpsimd", lambda e, ws=ws, wb=wb, hi=hi, n=n, ncol=ncol: e.tensor_copy(out=wb[:, hi * 4:hi * 4 + n, 0:ncol], in_=ws[:, 0:n, 0:ncol]),
                     reads=[("ws", hi, si) for si in range(len(segs))], writes=[(key, hi, 0)])
                allkeys.append((key, hi, 0))
            else:
                for j, k in enumerate(kc):
                    sc = scale(k)
                    P.op("gpsimd", lambda e, ws=ws, wb=wb, hi=hi, j=j, ncol=ncol, sc=sc: e.tensor_scalar(out=wb[:, hi * 4 + j, 0:ncol], in0=ws[:, j, 0:ncol], scalar1=sc, scalar2=None, op0=ALU.mult),
                         reads=[("ws", hi, si) for si in range(len(segs))] + list(scale_reads), writes=[(key, hi, j)])
                    allkeys.append((key, hi, j))
        return slot, allkeys

    def phase_norm(self):
        P = self.P
        junk = self.banks[7]
        hbs = self.hbs
        P.op("vector", lambda e: e.memset(self.ss[:], 0.0), writes=[("ss", t) for t in range(NT)] + ["rstd"])
        P.op("vector", lambda e: e.memset(self.ss2[:], 0.0), writes=[("ss2", t) for t in range(NT)])
        for t in range(NT):
            P.op("scalar", lambda e, t=t: e.activation(out=junk[:, 0:512], in_=self.x[:, t, 0:512], func=AF.Square, accum_out=self.ss[:, t:t + 1]),
                 reads=[("x", t)], writes=[("ss", t), ("bank", 7)])
            P.op("scalar", lambda e, t=t: e.activation(out=junk[:, 0:512], in_=self.x[:, t, 512:1024], func=AF.Square, accum_out=self.ss2[:, t:t + 1]),
                 reads=[("x", t)], writes=[("ss2", t), ("bank", 7)])
        P.op("vector", lambda e: e.tensor_tensor(out=self.ss[:], in0=self.ss[:], in1=self.ss2[:], op=ALU.add),
             reads=[("ss", t) for t in range(NT)] + [("ss2", t) for t in range(NT)], writes=[("ss", t) for t in range(NT)])
        P.op("scalar", lambda e: e.activation(out=self.rstd[:], in_=self.ss[:], func=AF.Ln, scale=1.0 / D, bias=self.epsb[:, 0:1]),
             reads=[("ss", t) for t in range(NT)] + ["eps"], writes=["rstd"])
        P.op("scalar", lambda e: e.activation(out=self.rstd[:], in_=self.rstd[:], func=AF.Exp, scale=-0.5), reads=["rstd"], writes=["rstd"])
        for t in range(NT):
            hb = hbs[t % 2]
            hk = ("hb", t % 2)
            pk = ("bank", t % 2)
            pT = self.bank_bf(t % 2)
            P.op("vector", lambda e, t=t, hb=hb: e.tensor_scalar(out=hb[:], in0=self.x[:, t, :], scalar1=self.rstd[:, t:t + 1], scalar2=None, op0=ALU.mult),
                 reads=[("x", t), "rstd"], writes=[hk])
            for c in range(8):
                P.op("tensor", lambda e, c=c, hb=hb, pT=pT: e.transpose(out=pT[:, c, :], in_=hb[:, c * 128:(c + 1) * 128], identity=self.ident[:]),
                     reads=[hk, "ident"], writes=[pk])
            P.op("scalar", lambda e, t=t, pT=pT: e.activation(out=self.actT[:, :, 1 + t * 128:1 + (t + 1) * 128], in_=pT, func=AF.Copy),
                 reads=[pk], writes=[("actT", c, t) for c in range(8)])

    def load_gcol(self, name):
        P = self.P
        g = self.din(name, [128, 8], F32).ap()
        P.op("sync", lambda e: e.dma_start(out=self.gcol[:], in_=g), writes=["gcol"], slot="gcol")

    def phase_qkv(self, li):
        P = self.P
        L = LAYERS[li]
        kind = L["kind"]
        pfx = "l%d_" % li
        self.load_gcol(pfx + "gcol_attn")
        self.phase_norm()
        self.scr_reset()
        W = self.din(pfx + "w_qkv", [D, L["F"]], F32)
        dh = 128 if kind == "C" else 64
        geff_d = self.din(pfx + "qkg", [2, dh], F32).ap()
        qT_d = self.dout("qT", [L["nqb"], 128, T], BF16).ap()
        kT_d = self.dout("kT_own", [L["nkb"], 128, T], BF16).ap()
        v_d = self.dout("v_own", [T, L["FV"]], BF16).ap()
        gq = self.scr("gq", [128, dh], F32)
        gk = self.scr("gk", [128, dh], F32)
        P.op("sync", lambda e: e.dma_start(out=gq[:], in_=geff_d[0].partition_broadcast(128)), writes=["gq"], slot="gq")
        P.op("sync", lambda e: e.dma_start(out=gk[:], in_=geff_d[1].partition_broadcast(128)), writes=["gk"], slot="gk")
        P.op("vector", lambda e: e.scalar_tensor_tensor(out=gk[:], in0=gq[:], scalar=float(dh) ** -0.5, in1=gk[:], op0=ALU.mult, op1=ALU.mult),
             reads=["gq", "gk"], writes=["gk"])
        stages = [self.scr("stage", [128, 4, T], BF16) for _ in range(2)]
        sqs = [self.scr("sq", [128, 512], F32) for _ in range(2)]
        kfs = [self.scr("kf", [128, 512], F32) for _ in range(2)]
        qns = [self.scr("qn", [128, 512], BF16) for _ in range(2)]
        vsts = [self.scr("vst", [128, 512], BF16) for _ in range(2)]
        ssq = [self.scr("ssq", [128, 8], F32) for _ in range(2)]
        if kind == "A":
            chunks = [[("q", 0, 512, 0)], [("q", 0, 512, 4)], [("k", 0, 256, 0), ("v", 256, 256, 0)]]
        elif kind == "B":
            chunks = [[("q", 0, 512, 0)], [("q", 0, 512, 4)], [("k", 0, 512, 0)], [("k", 0, 512, 4)], [("v", 0, 512, 0)], [("v", 0, 512, 512)]]
        else:
            chunks = [[("q", 0, 512, 0)], [("q", 0, 512, 4)], [("q", 0, 512, 8)], [("k", 0, 512, 0)], [("v", 0, 512, 0)]]
        gsc = lambda k: self.gcol[:, k:k + 1]
        units = [None] * len(chunks)
        units[0] = self.load_unit(W, list(range(8)), [(0, 512)], scale=gsc, scale_reads=["gcol"])
        it = 0
        for ci, segs in enumerate(chunks):
            if ci + 1 < len(chunks):
                units[ci + 1] = self.load_unit(W, list(range(8)), [((ci + 1) * 512, 512)], scale=gsc, scale_reads=["gcol"])
            slot, wkeys = units[ci]
            wb = self.wb[slot]
            stage = stages[ci % 2]
            stk = ("stage", ci % 2)
            for t in range(NT):
                bi = 2 + (it % 2)
                py = self.banks[bi]
                pyk = ("bank", bi)
                for c in range(8):
                    P.op("tensor", lambda e, c=c, t=t, py=py, wb=wb: e.matmul(py[:, 0:512], lhsT=self.actT[:, c, 1 + t * 128:1 + (t + 1) * 128], rhs=wb[:, c, 0:512], start=(c == 0), stop=(c == 7)),
                         reads=[("actT", c, t)] + wkeys, writes=[pyk])
                for (ty, off, w, dst) in segs:
                    if ty == "v":
                        vst = vsts[it % 2]
                        vk = ("vst", it % 2)
                        P.op("scalar", lambda e, py=py, vst=vst, off=off, w=w: e.activation(out=vst[:, 0:w], in_=py[:, off:off + w], func=AF.Copy),
                             reads=[pyk], writes=[vk])
                        P.op("gpsimd", lambda e, vst=vst, t=t, dst=dst, w=w: e.dma_start(out=v_d[t * 128:(t + 1) * 128, dst:dst + w], in_=vst[:, 0:w]),
                             reads=[vk], slot="vst%d" % (it % 2), out=True)
                        continue
                    nh = w // dh
                    sq = sqs[it % 2]
                    sqk = ("sq", it % 2)
                    s_ = ssq[it % 2]
                    sk = ("ssq", it % 2)
                    qn = qns[it % 2]
                    qk = ("qn", it % 2)
                    P.op("scalar", lambda e, py=py, sq=sq, off=off, w=w: e.activation(out=sq[:, 0:w], in_=py[:, off:off + w], func=AF.Square),
                         reads=[pyk], writes=[sqk])
                    P.op("vector", lambda e, sq=sq, s_=s_, w=w, nh=nh: e.tensor_reduce(out=s_[:, 0:nh], in_=sq[:, 0:w].rearrange("p (h d) -> p h d", d=dh), axis=AX.X, op=ALU.add),
                         reads=[sqk], writes=[sk])
                    P.op("scalar", lambda e, s_=s_, nh=nh: e.activation(out=s_[:, 0:nh], in_=s_[:, 0:nh], func=AF.Ln, scale=1.0 / dh, bias=self.epsb[:, 0:1]),
                         reads=[sk, "eps"], writes=[sk])
                    P.op("scalar", lambda e, s_=s_, nh=nh: e.activation(out=s_[:, 0:nh], in_=s_[:, 0:nh], func=AF.Exp, scale=-0.5), reads=[sk], writes=[sk])
                    rb = AP(s_.tensor if hasattr(s_, "tensor") else s_, 0, [[8, 128], [1, nh], [0, dh]])
                    if ty == "q":
                        P.op("vector", lambda e, py=py, qn=qn, off=off, w=w, rb=rb: e.tensor_tensor(out=qn[:, 0:w].rearrange("p (h d) -> p h d", d=dh), in0=py[:, off:off + w].rearrange("p (h d) -> p h d", d=dh), in1=rb, op=ALU.mult),
                             reads=[pyk, sk], writes=[qk])
                    else:
                        kf = kfs[it % 2]
                        kfk = ("kf", it % 2)
                        gb = AP(gk.tensor if hasattr(gk, "tensor") else gk, 0, [[dh, 128], [0, nh], [1, dh]])
                        P.op("vector", lambda e, py=py, kf=kf, off=off, w=w, rb=rb: e.tensor_tensor(out=kf[:, 0:w].rearrange("p (h d) -> p h d", d=dh), in0=py[:, off:off + w].rearrange("p (h d) -> p h d", d=dh), in1=rb, op=ALU.mult),
                             reads=[pyk, sk], writes=[kfk])
                        P.op("gpsimd", lambda e, kf=kf, qn=qn, w=w, gb=gb: e.tensor_tensor(out=qn[:, 0:w].rearrange("p (h d) -> p h d", d=dh), in0=kf[:, 0:w].rearrange("p (h d) -> p h d", d=dh), in1=gb, op=ALU.mult),
                             reads=[kfk, "gk"], writes=[qk])
                    nb = w // 128
                    tb = it % 2
                    pT = self.bank_bf(tb)
                    pk = ("bank", tb)
                    for j in range(nb):
                        P.op("tensor", lambda e, j=j, qn=qn, pT=pT: e.transpose(out=pT[:, j, :], in_=qn[:, j * 128:(j + 1) * 128], identity=self.ident[:]),
                             reads=[qk, "ident"], writes=[pk])
                    P.op("scalar", lambda e, pT=pT, stage=stage, nb=nb, t=t: e.activation(out=stage[:, 0:nb, t * 128:(t + 1) * 128], in_=pT[:, 0:nb, :], func=AF.Copy),
                         reads=[pk], writes=[(stk, t)])
                it += 1
            for (ty, off, w, dst) in segs:
                if ty == "v":
                    continue
                dd = qT_d if ty == "q" else kT_d
                for j in range(w // 128):
                    P.op("gpsimd", lambda e, dd=dd, j=j, dst=dst, stage=stage: e.dma_start(out=dd[dst + j], in_=stage[:, j, :]),
                         reads=[(stk, t) for t in range(NT)], slot="stg%d_%d" % (ci % 2, j), out=True)

    def attn_item(self, it, mms, tb_ap, ncols, pvs, tbkey, extra_reads):
        P = self.P
        bi = it % 2
        st = self.banks[bi]
        stk = ("bank", bi)
        sc = self.a_sc[it % 2]
        sck = ("sc", it % 2)
        pt = self.a_pt[it % 2]
        ptk = ("pt", it % 2)
        for (lh, rh, c0, n) in mms:
            P.op("tensor", lambda e, lh=lh, rh=rh, c0=c0, n=n, st=st: e.matmul(st[:, c0:c0 + n], lhsT=lh, rhs=rh, start=True, stop=True),
                 reads=extra_reads, writes=[stk])
        P.op("vector", lambda e, st=st, sc=sc, tb_ap=tb_ap, ncols=ncols: e.tensor_tensor(out=sc[:, 0:ncols], in0=st[:, 0:ncols], in1=tb_ap, op=ALU.add),
             reads=[stk, tbkey], writes=[sck])
        P.op("scalar", lambda e, sc=sc, pt=pt, ncols=ncols: e.activation(out=pt[:, 0:ncols], in_=sc[:, 0:ncols], func=AF.Exp),
             reads=[sck], writes=[ptk])
        for (c0, v_ap, ab, wdt, s0, s1) in pvs:
            acc = self.banks[ab]
            P.op("tensor", lambda e, c0=c0, v_ap=v_ap, acc=acc, wdt=wdt, s0=s0, s1=s1, pt=pt: e.matmul(acc[:, 0:wdt], lhsT=pt[:, c0:c0 + 128], rhs=v_ap, start=s0, stop=s1),
                 reads=[ptk] + extra_reads, writes=[("bank", ab)])

    def out_transpose(self, on, onk, chunk, qt, trk):
        P = self.P
        bi = 6 + (trk % 2)
        pT = self.bank_bf(bi)
        P.op("tensor", lambda e, on=on, pT=pT: e.transpose(out=pT[:, 0, :], in_=on, identity=self.ident[:]),
             reads=[onk, "ident"], writes=[("bank", bi)])
        P.op("scalar", lambda e, pT=pT, chunk=chunk, qt=qt: e.activation(out=self.actT[:, chunk, 1 + qt * 128:1 + (qt + 1) * 128], in_=pT[:, 0, :], func=AF.Copy),
             reads=[("bank", bi)], writes=[("actT", chunk, qt)])

    def phase_attn(self, li):
        P = self.P
        L = LAYERS[li]
        kind = L["kind"]
        pfx = "l%d_" % li
        self.scr_reset()
        nkb, nqb, FV = L["nkb"], L["nqb"], L["FV"]
        qT_d = self.din("qT", [nqb, 128, T], BF16).ap()
        kTo_d = self.din("kT_own", [nkb, 128, T], BF16).ap()
        vo_d = self.din("v_own", [T, FV], BF16).ap()
        kTG_d = self.din("kT_G", [2, nkb, 128, T], BF16).ap()
        vG_d = self.din("v_G", [2 * T, FV], BF16).ap()
        ni = NI[kind]
        bt_d = self.din("bt_" + kind, [NUNIT_BT[kind], 128, ni * 128], F32).ap()
        self.a_sc = [self.scr("sc", [128, 512], F32) for _ in range(2)]
        self.a_pt = [self.scr("pt", [128, 512], BF16) for _ in range(2)]
        rr = [self.scr("rr", [128, 4], F32) for _ in range(4)]
        ons = [self.scr("on", [128, 128], BF16) for _ in range(2)]
        f1 = self.flags[:, 0:1]
        f0 = self.flags[:, 1:2]
        vG_r = vG_d.rearrange("(kb p) f -> p kb f", p=128)
        vo_r = vo_d.rearrange("(kb p) f -> p kb f", p=128)
        it = 0
        fin = 0
        if kind == "B":
            dv = 128
            KTs = [self.scr("KT", [128, 2 * T], BF16) for _ in range(2)]
            Vs = [self.scr("V", [128, 32, dv + 1], BF16) for _ in range(2)]
            QT = self.scr("QT", [128, T], BF16)
            TB = self.scr("TB", [128, ni * 128], F32)
            o1 = self.scr("o1", [128, NT, 128], F32)
            ods = [self.scr("od", [128, 128], F32) for _ in range(2)]
            lam = self.hbs[0][:].bitcast(F32)[:, 0:256].rearrange("p (a d) -> p a d", d=64)
            lamv = self.scr("lamv", [128, 4], F32)
            lam_d = self.din(pfx + "lam", [4, 64], F32).ap()
            P.op("sync", lambda e: e.dma_start(out=lam, in_=AP(lam_d.tensor, 0, [[0, 128], [4 * 64, 1], [1, 256]]) if False else AP(lam_d.tensor, 0, [[0, 128], [64, 4], [1, 64]])), writes=["lam"], slot="lam")
            P.op("vector", lambda e: e.tensor_tensor(out=lam[:, 0, :], in0=lam[:, 0, :], in1=lam[:, 1, :], op=ALU.mult), reads=["lam"], writes=["lam"])
            P.op("vector", lambda e: e.tensor_tensor(out=lam[:, 2, :], in0=lam[:, 2, :], in1=lam[:, 3, :], op=ALU.mult), reads=["lam"], writes=["lam"])
            P.op("vector", lambda e: e.tensor_reduce(out=lamv[:, 0:1], in_=lam[:, 0, :], axis=AX.X, op=ALU.add), reads=["lam"], writes=["lamv"])
            P.op("vector", lambda e: e.tensor_reduce(out=lamv[:, 1:2], in_=lam[:, 2, :], axis=AX.X, op=ALU.add), reads=["lam"], writes=["lamv"])
            P.op("scalar", lambda e: e.activation(out=lamv[:, 0:2], in_=lamv[:, 0:2], func=AF.Exp), reads=["lamv"], writes=["lamv"])
            P.op("vector", lambda e: e.scalar_tensor_tensor(out=lamv[:, 2:3], in0=lamv[:, 1:2], scalar=-lambda_init_fn(li), in1=lamv[:, 0:1], op0=ALU.add, op1=ALU.subtract),
                 reads=["lamv"], writes=["lamv"])
            neglam = lamv[:, 2:3]
            for vb in Vs:
                P.op("gpsimd", lambda e, vb=vb: e.memset(vb[:, :, dv:dv + 1], 1.0), writes=[("Vones", id(vb))])

            def load_head(h):
                s = h % 2
                P.op("sync", lambda e, h=h, s=s: e.dma_start(out=KTs[s][:, 0:T], in_=kTG_d[0, h]), writes=[("KT", s)], slot="kt%da" % s)
                P.op("sync", lambda e, h=h, s=s: e.dma_start(out=KTs[s][:, T:2 * T], in_=kTG_d[1, h]), writes=[("KT", s)], slot="kt%db" % s)
                for half in range(2):
                    P.op("sync", lambda e, h=h, s=s, half=half: e.dma_start(out=Vs[s][:, half * 16:(half + 1) * 16, 0:dv], in_=vG_r[:, half * 16:(half + 1) * 16, h * dv:(h + 1) * dv]),
                         writes=[("V", s)], slot="v%d_%d" % (s, half))

            load_head(0)
            for h in range(8):
                s = h % 2
                if h + 1 < 8:
                    load_head(h + 1)
                P.op("sync", lambda e, h=h: e.dma_start(out=QT[:], in_=qT_d[h]), writes=["QT"], slot="qt")
                KT, V = KTs[s], Vs[s]
                rds = [("KT", s), ("V", s), "QT", ("Vones", id(V))]
                for j in range(2):
                    for half in range(2):
                        hw = ni * 64
                        P.op("sync", lambda e, h=h, j=j, half=half, hw=hw: e.dma_start(out=TB[:, half * hw:(half + 1) * hw], in_=bt_d[h * 2 + j, :, half * hw:(half + 1) * hw]),
                             writes=["TB"], slot="tb%d" % half)
                    for qc in range(4):
                        for kb in range(32):
                            dp = min(max(kb - 4 * qc, -12), 28)
                            i0 = 28 - dp
                            mms = [(KT[64 * j:64 * j + 64, kb * 128:(kb + 1) * 128], QT[64 * j:64 * j + 64, qc * 512:(qc + 1) * 512], 0, 512)]
                            pvs = [(jq * 128, V[:, kb, 0:dv + 1], 2 + jq, dv + 1, kb == 0, kb == 31) for jq in range(4)]
                            self.attn_item(it, mms, TB[:, i0 * 128:i0 * 128 + 512], 512, pvs, "TB", rds)
                            it += 1
                        for jq in range(4):
                            qt = qc * 4 + jq
                            acc = self.banks[2 + jq]
                            ak = ("bank", 2 + jq)
                            r = rr[fin % 4]
                            rk = ("rr", fin % 4)
                            fin += 1
                            P.op("vector", lambda e, acc=acc, r=r: e.reciprocal(out=r[:, 0:1], in_=acc[:, dv:dv + 1]), reads=[ak], writes=[rk])
                            if j == 0:
                                P.op("vector", lambda e, acc=acc, r=r, qt=qt: e.tensor_scalar(out=o1[:, qt, :], in0=acc[:, 0:dv], scalar1=r[:, 0:1], scalar2=None, op0=ALU.mult),
                                     reads=[ak, rk], writes=[("o1", qt)])
                            else:
                                od = ods[fin % 2]
                                odk = ("od", fin % 2)
                                on = ons[fin % 2]
                                onk = ("on", fin % 2)
                                P.op("vector", lambda e, r=r: e.tensor_scalar(out=r[:, 1:2], in0=r[:, 0:1], scalar1=neglam, scalar2=None, op0=ALU.mult),
                                     reads=[rk, "lamv"], writes=[rk])
                                P.op("vector", lambda e, acc=acc, r=r, qt=qt, od=od: e.scalar_tensor_tensor(out=od[:], in0=acc[:, 0:dv], scalar=r[:, 1:2], in1=o1[:, qt, :], op0=ALU.mult, op1=ALU.add),
                                     reads=[ak, rk, ("o1", qt)], writes=[odk])
                                P.op("vector", lambda e, r=r: e.memset(r[:, 2:3], 0.0), reads=[], writes=[rk])
                                P.op("scalar", lambda e, od=od, on=on, r=r: e.activation(out=on[:], in_=od[:], func=AF.Square, accum_out=r[:, 2:3]),
                                     reads=[odk], writes=[rk, onk])
                                P.op("scalar", lambda e, r=r: e.activation(out=r[:, 2:3], in_=r[:, 2:3], func=AF.Ln, scale=1.0 / 128, bias=self.epsb[:, 0:1]),
                                     reads=[rk, "eps"], writes=[rk])
                                P.op("scalar", lambda e, r=r: e.activation(out=r[:, 2:3], in_=r[:, 2:3], func=AF.Exp, scale=-0.5), reads=[rk], writes=[rk])
                                P.op("vector", lambda e, od=od, on=on, r=r: e.tensor_scalar(out=on[:], in0=od[:], scalar1=r[:, 2:3], scalar2=None, op0=ALU.mult),
                                     reads=[odk, rk], writes=[onk])
                                self.out_transpose(on[:], onk, h, qt, fin)
        elif kind == "A":
            dv = 64
            NE = 18
            KTs = [self.scr("KT", [64, NE * 128], BF16) for _ in range(2)]
            Vs = [self.scr("V", [128, NE, dv + 1], BF16) for _ in range(2)]
            QTs = [self.scr("QT", [64, T], BF16) for _ in range(2)]
            TBs = [self.scr("TB", [128, ni * 128], F32) for _ in range(2)]
            pairs = [self.scr("pair", [128, NT, 128], BF16) for _ in range(2)]
            esink = self.scr("esink", [128, 16], F32)
            sink_d = self.din(pfx + "sink", [16], F32).ap()
            P.op("sync", lambda e: e.dma_start(out=esink[:], in_=sink_d.partition_broadcast(128)), writes=["esink"], slot="esink")
            P.op("scalar", lambda e: e.activation(out=esink[:], in_=esink[:], func=AF.Exp), reads=["esink"], writes=["esink"])
            for vb in Vs:
                P.op("gpsimd", lambda e, vb=vb: e.memset(vb[:, :, dv:dv + 1], 1.0), writes=[("Vones", id(vb))])

            def load_kv(kvh):
                s = kvh % 2
                blk, r0 = kvh // 2, (kvh % 2) * 64
                P.op("sync", lambda e: e.dma_start(out=KTs[s][:, 0:128], in_=kTG_d[0, blk, r0:r0 + 64, T - 128:T]), writes=[("KT", s)], slot="kt%da" % s)
                P.op("sync", lambda e: e.dma_start(out=KTs[s][:, 128:128 + T], in_=kTo_d[blk, r0:r0 + 64, :]), writes=[("KT", s)], slot="kt%db" % s)
                P.op("sync", lambda e: e.dma_start(out=KTs[s][:, 128 + T:256 + T], in_=kTG_d[1, blk, r0:r0 + 64, 0:128]), writes=[("KT", s)], slot="kt%dc" % s)
                P.op("sync", lambda e: e.dma_start(out=Vs[s][:, 0:1, 0:dv], in_=vG_r[:, 15:16, kvh * dv:(kvh + 1) * dv]), writes=[("V", s)], slot="v%da" % s)
                P.op("sync", lambda e: e.dma_start(out=Vs[s][:, 1:17, 0:dv], in_=vo_r[:, :, kvh * dv:(kvh + 1) * dv]), writes=[("V", s)], slot="v%db" % s)
                P.op("sync", lambda e: e.dma_start(out=Vs[s][:, 17:18, 0:dv], in_=vG_r[:, 16:17, kvh * dv:(kvh + 1) * dv]), writes=[("V", s)], slot="v%dc" % s)
                P.op("vector", lambda e: e.tensor_scalar(out=Vs[s][:, 0, :], in0=Vs[s][:, 0, :], scalar1=f1, scalar2=None, op0=ALU.mult),
                     reads=[("V", s), "flags", ("Vones", id(Vs[s]))], writes=[("V", s)])
                P.op("vector", lambda e: e.tensor_scalar(out=Vs[s][:, 17, :], in0=Vs[s][:, 17, :], scalar1=f0, scalar2=None, op0=ALU.mult),
                     reads=[("V", s), "flags", ("Vones", id(Vs[s]))], writes=[("V", s)])

            def load_q(h):
                s = h % 2
                P.op("sync", lambda e: e.dma_start(out=QTs[s][:], in_=qT_d[h // 2, (h % 2) * 64:(h % 2) * 64 + 64, :]), writes=[("QT", s)], slot="qt%d" % s)
                P.op("sync", lambda e: e.dma_start(out=TBs[s][:], in_=bt_d[h]), writes=[("TB", s)], slot="tb%d" % s)

            load_kv(0)
            load_q(0)
            for h in range(16):
                kvh = h // 4
                s = kvh % 2
                if h % 4 == 0 and kvh + 1 < 4:
                    load_kv(kvh + 1)
                if h + 1 < 16:
                    load_q(h + 1)
                KT, V, QT, TB = KTs[s], Vs[s], QTs[h % 2], TBs[h % 2]
                pair = pairs[(h // 2) % 2]
                rds = [("KT", s), ("V", s), ("QT", h % 2), ("Vones", id(V))]
                for qt in range(NT):
                    ab = 2 + (it % 4)
                    mms = [(KT[0:64, (qt + 2 - i) * 128:(qt + 3 - i) * 128], QT[0:64, qt * 128:(qt + 1) * 128], i * 128, 128) for i in range(3)]
                    pvs = [(i * 128, V[:, qt + 2 - i, 0:dv + 1], ab, dv + 1, i == 0, i == 2) for i in range(3)]
                    self.attn_item(it, mms, TB[:, 0:384], 384, pvs, ("TB", h % 2), rds)
                    it += 1
                    acc = self.banks[ab]
                    ak = ("bank", ab)
                    r = rr[fin % 4]
                    rk = ("rr", fin % 4)
                    fin += 1
                    P.op("vector", lambda e, acc=acc, r=r, h=h: e.tensor_tensor(out=r[:, 0:1], in0=acc[:, dv:dv + 1], in1=esink[:, h:h + 1], op=ALU.add),
                         reads=[ak, "esink"], writes=[rk])
                    P.op("vector", lambda e, r=r: e.reciprocal(out=r[:, 1:2], in_=r[:, 0:1]), reads=[rk], writes=[rk])
                    P.op("vector", lambda e, acc=acc, r=r, pair=pair, qt=qt, h=h: e.tensor_scalar(out=pair[:, qt, (h % 2) * 64:(h % 2) * 64 + 64], in0=acc[:, 0:dv], scalar1=r[:, 1:2], scalar2=None, op0=ALU.mult),
                         reads=[ak, rk], writes=[("pair", (h // 2) % 2, qt, h % 2)])
                if h % 2 == 1:
                    for qt in range(NT):
                        fin += 1
                        bi = 6 + (fin % 2)
                        pT = self.bank_bf(bi)
                        P.op("tensor", lambda e, pair=pair, qt=qt, pT=pT: e.transpose(out=pT[:, 0, :], in_=pair[:, qt, :], identity=self.ident[:]),
                             reads=[("pair", (h // 2) % 2, qt, 0), ("pair", (h // 2) % 2, qt, 1), "ident"], writes=[("bank", bi)])
                        P.op("scalar", lambda e, pT=pT, h=h, qt=qt: e.activation(out=self.actT[:, h // 2, 1 + qt * 128:1 + (qt + 1) * 128], in_=pT[:, 0, :], func=AF.Copy),
                             reads=[("bank", bi)], writes=[("actT", h // 2, qt)])
        else:
            dv = 128
            NE = 32
            KTs = [self.scr("KT", [128, NE * 128], BF16) for _ in range(2)]
            Vs = [self.scr("V", [128, NE, dv + 1], BF16) for _ in range(2)]
            QTs = [self.scr("QT", [128, 3, T], BF16) for _ in range(1)]
            TBs = [self.scr("TB", [128, ni * 128], F32) for _ in range(1)]
            for vb in Vs:
                P.op("gpsimd", lambda e, vb=vb: e.memset(vb[:, :, dv:dv + 1], 1.0), writes=[("Vones", id(vb))])

            def load_kv(j):
                s = j % 2
                P.op("sync", lambda e: e.dma_start(out=KTs[s][:, 0:1024], in_=kTG_d[0, j, :, T - 1024:T]), writes=[("KT", s)], slot="kt%da" % s)
                P.op("sync", lambda e: e.dma_start(out=KTs[s][:, 1024:1024 + T], in_=kTo_d[j]), writes=[("KT", s)], slot="kt%db" % s)
                P.op("sync", lambda e: e.dma_start(out=KTs[s][:, 1024 + T:2048 + T], in_=kTG_d[1, j, :, 0:1024]), writes=[("KT", s)], slot="kt%dc" % s)
                P.op("sync", lambda e: e.dma_start(out=Vs[s][:, 0:8, 0:dv], in_=vG_r[:, 8:16, j * dv:(j + 1) * dv]), writes=[("V", s)], slot="v%da" % s)
                P.op("sync", lambda e: e.dma_start(out=Vs[s][:, 8:24, 0:dv], in_=vo_r[:, :, j * dv:(j + 1) * dv]), writes=[("V", s)], slot="v%db" % s)
                P.op("sync", lambda e: e.dma_start(out=Vs[s][:, 24:32, 0:dv], in_=vG_r[:, 16:24, j * dv:(j + 1) * dv]), writes=[("V", s)], slot="v%dc" % s)
                P.op("vector", lambda e: e.tensor_scalar(out=Vs[s][:, 0:8, :], in0=Vs[s][:, 0:8, :], scalar1=f1, scalar2=None, op0=ALU.mult),
                     reads=[("V", s), "flags", ("Vones", id(Vs[s]))], writes=[("V", s)])
                P.op("vector", lambda e: e.tensor_scalar(out=Vs[s][:, 24:32, :], in0=Vs[s][:, 24:32, :], scalar1=f0, scalar2=None, op0=ALU.mult),
                     reads=[("V", s), "flags", ("Vones", id(Vs[s]))], writes=[("V", s)])

            ents = [(0, d_) for d_ in (1, 0, -1)] + [(1, d_) for d_ in (2, 1, 0, -1, -2)] + [(2, d_) for d_ in range(8, -9, -1)]
            load_kv(0)
            for j in range(4):
                s = j % 2
                if j + 1 < 4:
                    load_kv(j + 1)
                QT, TB = QTs[0], TBs[0]
                for g in range(3):
                    P.op("sync", lambda e, g=g, j=j: e.dma_start(out=QT[:, g, :], in_=qT_d[g * 4 + j]), writes=["QT"], slot="qt%d" % g)
                P.op("sync", lambda e, j=j: e.dma_start(out=TB[:], in_=bt_d[j]), writes=["TB"], slot="tb")
                KT, V = KTs[s], Vs[s]
                rds = [("KT", s), ("V", s), "QT", ("Vones", id(V))]
                for qt in range(NT):
                    ab = 2 + (qt % 4)
                    for i0 in range(0, 25, 4):
                        grp = list(range(i0, min(i0 + 4, 25)))
                        mms = []
                        pvs = []
                        for n_, idx in enumerate(grp):
                            g, dl = ents[idx]
                            eb = qt + 8 + dl
                            mms.append((KT[:, eb * 128:(eb + 1) * 128], QT[:, g, qt * 128:(qt + 1) * 128], n_ * 128, 128))
                            pvs.append((n_ * 128, V[:, eb, 0:dv + 1], ab, dv + 1, idx == 0, idx == 24))
                        self.attn_item(it, mms, TB[:, i0 * 128:(i0 + len(grp)) * 128], len(grp) * 128, pvs, "TB", rds)
                        it += 1
                    acc = self.banks[ab]
                    ak = ("bank", ab)
                    r = rr[fin % 4]
                    rk = ("rr", fin % 4)
                    on = ons[fin % 2]
                    onk = ("on", fin % 2)
                    fin += 1
                    P.op("vector", lambda e, acc=acc, r=r: e.reciprocal(out=r[:, 0:1], in_=acc[:, dv:dv + 1]), reads=[ak], writes=[rk])
                    P.op("vector", lambda e, acc=acc, r=r, on=on: e.tensor_scalar(out=on[:], in0=acc[:, 0:dv], scalar1=r[:, 0:1], scalar2=None, op0=ALU.mult),
                         reads=[ak, rk], writes=[onk])
                    self.out_transpose(on[:], onk, j, qt, fin)

    def phase_wo(self, li):
        P = self.P
        L = LAYERS[li]
        pfx = "l%d_" % li
        nfc = L["nfc"]
        W = self.din(pfx + "w_o", [nfc * 128, D], F32)
        scale = None
        srd = ()
        if L["kind"] == "B":
            sg = self.din(pfx + "sgcol", [128, 1], F32).ap()
            sgt = self.scr("sgt", [128, 1], F32)
            P.op("sync", lambda e: e.dma_start(out=sgt[:], in_=sg), writes=["sgt"], slot="sgt")
            P.op("vector", lambda e: e.tensor_scalar(out=sgt[:], in0=sgt[:], scalar1=1.0 - lambda_init_fn(li), scalar2=None, op0=ALU.mult), reads=["sgt"], writes=["sgt"])
            scale = lambda k: sgt[:, 0:1]
            srd = ["sgt"]
        kch = list(range(nfc))
        units = [None, None]
        units[0] = self.load_unit(W, kch, [(0, 512)], scale=scale, scale_reads=srd)
        for n in range(2):
            if n == 0:
                units[1] = self.load_unit(W, kch, [(512, 512)], scale=scale, scale_reads=srd)
            slot, wkeys = units[n]
            wb = self.wb[slot]
            for t in range(NT):
                bi = 2 + (t % 2)
                py = self.banks[bi]
                for c in range(nfc):
                    P.op("tensor", lambda e, c=c, t=t, py=py, wb=wb: e.matmul(py[:, 0:512], lhsT=self.actT[:, c, 1 + t * 128:1 + (t + 1) * 128], rhs=wb[:, c, 0:512], start=(c == 0), stop=(c == nfc - 1)),
                         reads=[("actT", c, t)] + wkeys, writes=[("bank", bi)])
                P.op("vector", lambda e, t=t, n=n, py=py: e.tensor_tensor(out=self.x[:, t, n * 512:(n + 1) * 512], in0=py[:, 0:512], in1=self.x[:, t, n * 512:(n + 1) * 512], op=ALU.add),
                     reads=[("bank", bi), ("x", t)], writes=[("x", t)])

    def phase_halo_out(self):
        P = self.P
        hs = self.scr("hs", [128, 8, 2], BF16)
        ho = self.dout("halo_own", [128, 8, 2], BF16).ap()
        P.op("vector", lambda e: e.tensor_copy(out=hs[:, :, 0:1], in_=self.actT[:, :, 1:2]), reads=[("actT", c, 0) for c in range(8)], writes=["hs"])
        P.op("vector", lambda e: e.tensor_copy(out=hs[:, :, 1:2], in_=self.actT[:, :, T:T + 1]), reads=[("actT", c, NT - 1) for c in range(8)], writes=["hs"])
        P.op("sync", lambda e: e.dma_start(out=ho, in_=hs[:]), reads=["hs"], slot="hso", out=True)

    def phase_ffn(self, li):
        P = self.P
        pfx = "l%d_" % li
        self.scr_reset()
        Wu = self.din(pfx + "w_up", [D, 2 * DFF], F32)
        Wd = self.din(pfx + "w_down", [DFF, D], F32)
        cw_d = self.din(pfx + "convp", [128, 4, 44], F32).ap()
        hG = self.din("halo_G", [2, 128, 8, 2], BF16).ap()
        self.load_gcol(pfx + "gcol_ffn")
        cw = self.scr("cw", [128, 4, 44], F32)
        hin = self.scr("hin", [128, 2, 8, 2], BF16)
        P.op("sync", lambda e: e.dma_start(out=cw[:], in_=cw_d), writes=["cw"], slot="cw")
        for r in range(2):
            P.op("sync", lambda e, r=r: e.dma_start(out=hin[:, r], in_=hG[r]), writes=["hin"], slot="hin%d" % r)
        P.op("vector", lambda e: e.tensor_scalar(out=self.actT[:, :, 0:1], in0=hin[:, 0, :, 1:2], scalar1=self.flags[:, 0:1], scalar2=None, op0=ALU.mult),
             reads=["hin", "flags"], writes=["halo"])
        P.op("vector", lambda e: e.tensor_scalar(out=self.actT[:, :, T + 1:T + 2], in0=hin[:, 1, :, 0:1], scalar1=self.flags[:, 1:2], scalar2=None, op0=ALU.mult),
             reads=["hin", "flags"], writes=["halo"])
        TC = 1024
        a = self.scr("a", [128, 22, TC], BF16)
        Us = [[self.scr("U", [128, TC + 2], F32) for _ in range(2)] for _ in range(2)]
        T1 = [self.scr("T1", [128, TC], F32) for _ in range(2)]
        gsc = lambda k: self.gcol[:, k:k + 1]
        up_units = [[(f0 * 128, 256), (DFF + f0 * 128, 256)] for f0 in range(0, 22, 2)]
        dn_units = [(list(range(0, 8)), n) for n in range(2)]
        dn_units = [(kc, n) for n in range(2) for kc in (list(range(0, 8)), list(range(8, 16)), list(range(16, 22)))]
        pcount = 0
        for k in range(2):
            cb = TC * k
            allr = [("actT", c, t) for c in range(8) for t in range(8 * k, 8 * k + 8)] + ["halo"]
            if k == 0:
                allr += [("actT", c, 8) for c in range(8)]
            else:
                allr += [("actT", c, 7) for c in range(8)]
            nxt = self.load_unit(Wu, list(range(8)), up_units[0], scale=gsc, scale_reads=["gcol"])
            for ui in range(11):
                cur = nxt
                if ui + 1 < 11:
                    nxt = self.load_unit(Wu, list(range(8)), up_units[ui + 1], scale=gsc, scale_reads=["gcol"])
                slot, wkeys = cur
                wb = self.wb[slot]
                for pi in range(2):
                    f = ui * 2 + pi
                    par = pcount % 2
                    pcount += 1
                    for which in range(2):
                        off = which * 256 + pi * 128
                        fc = f + 22 * which
                        bA, bB, bE = self.banks[4 * which], self.banks[4 * which + 1], self.banks[4 * which + 2]
                        bks = [("bank", 4 * which + i) for i in range(3)]
                        for c in range(8):
                            P.op("tensor", lambda e, c=c, wb=wb, off=off, bA=bA, cb=cb: e.matmul(bA[:, 0:512], lhsT=wb[:, c, off:off + 128], rhs=self.actT[:, c, cb + 1:cb + 513], start=(c == 0), stop=(c == 7)),
                                 reads=allr + wkeys, writes=[bks[0]])
                        for c in range(8):
                            P.op("tensor", lambda e, c=c, wb=wb, off=off, bB=bB, cb=cb: e.matmul(bB[:, 0:512], lhsT=wb[:, c, off:off + 128], rhs=self.actT[:, c, cb + 513:cb + 1025], start=(c == 0), stop=(c == 7)),
                                 reads=allr + wkeys, writes=[bks[1]])
                        for c in range(8):
                            P.op("tensor", lambda e, c=c, wb=wb, off=off, bE=bE, cb=cb: e.matmul(bE[:, 0:2], lhsT=wb[:, c, off:off + 128], rhs=self.actT[:, c, cb:cb + 1026:1025], start=(c == 0), stop=(c == 7)),
                                 reads=allr + wkeys, writes=[bks[2]])
                        U = Us[which][par]
                        uk = ("U", which, par)
                        t1 = T1[which]
                        tk = ("T1", which)
                        P.op("scalar", lambda e, U=U, bA=bA: e.activation(out=U[:, 1:513], in_=bA[:, 0:512], func=AF.Copy), reads=[bks[0]], writes=[(uk, 0)])
                        P.op("scalar", lambda e, U=U, bB=bB: e.activation(out=U[:, 513:1025], in_=bB[:, 0:512], func=AF.Copy), reads=[bks[1]], writes=[(uk, 1)])
                        P.op("scalar", lambda e, U=U, bE=bE: e.activation(out=U[:, 0:1026:1025], in_=bE[:, 0:2], func=AF.Copy), reads=[bks[2]], writes=[(uk, 2)])
                        P.op("scalar", lambda e, t1=t1, bA=bA, fc=fc: e.activation(out=t1[:, 0:512], in_=bA[:, 0:512], func=AF.Identity, scale=cw[:, 1, fc:fc + 1], bias=cw[:, 3, fc:fc + 1]),
                             reads=[bks[0], "cw"], writes=[(tk, 0)])
                        P.op("scalar", lambda e, t1=t1, bB=bB, fc=fc: e.activation(out=t1[:, 512:1024], in_=bB[:, 0:512], func=AF.Identity, scale=cw[:, 1, fc:fc + 1], bias=cw[:, 3, fc:fc + 1]),
                             reads=[bks[1], "cw"], writes=[(tk, 1)])
                        eng = "vector"
                        P.op(eng, lambda e, t1=t1, U=U, fc=fc: e.scalar_tensor_tensor(out=t1[:], in0=U[:, 0:TC], scalar=cw[:, 0, fc:fc + 1], in1=t1[:], op0=ALU.mult, op1=ALU.add),
                             reads=[(uk, 0), (uk, 1), (uk, 2), (tk, 0), (tk, 1), "cw"], writes=[(tk, 0), (tk, 1)])
                        P.op(eng, lambda e, t1=t1, U=U, fc=fc: e.scalar_tensor_tensor(out=t1[:], in0=U[:, 2:TC + 2], scalar=cw[:, 2, fc:fc + 1], in1=t1[:], op0=ALU.mult, op1=ALU.add),
                             reads=[(uk, 0), (uk, 1), (uk, 2), (tk, 0), (tk, 1), "cw"], writes=[(tk, 0), (tk, 1)])
                    P.op("scalar", lambda e: e.activation(out=T1[0][:], in_=T1[0][:], func=AF.Silu), reads=[(("T1", 0), 0), (("T1", 0), 1)], writes=[(("T1", 0), 0), (("T1", 0), 1)])
                    P.op("vector", lambda e, f=f: e.tensor_tensor(out=a[:, f, :], in0=T1[0][:], in1=T1[1][:], op=ALU.mult),
                         reads=[(("T1", 0), 0), (("T1", 0), 1), (("T1", 1), 0), (("T1", 1), 1)], writes=[("a", f)])
            nxt = self.load_unit(Wd, dn_units[0][0], [(dn_units[0][1] * 512, 512)])
            for di, (kc, n) in enumerate(dn_units):
                cur = nxt
                if di + 1 < len(dn_units):
                    nxt = self.load_unit(Wd, dn_units[di + 1][0], [(dn_units[di + 1][1] * 512, 512)])
                slot, wkeys = cur
                wb = self.wb[slot]
                for t in range(8):
                    for ci, f in enumerate(kc):
                        P.op("tensor", lambda e, t=t, ci=ci, f=f, wb=wb: e.matmul(self.banks[t][:, 0:512], lhsT=a[:, f, t * 128:(t + 1) * 128], rhs=wb[:, ci, 0:512], start=(f == 0), stop=(f == 21)),
                             reads=[("a", f)] + wkeys, writes=[("bank", t)])
                if kc[-1] == 21:
                    for t in range(8):
                        tt = 8 * k + t
                        P.op("vector", lambda e, t=t, tt=tt, n=n: e.tensor_tensor(out=self.x[:, tt, n * 512:(n + 1) * 512], in0=self.banks[t][:, 0:512], in1=self.x[:, tt, n * 512:(n + 1) * 512], op=ALU.add),
                             reads=[("bank", t), ("x", tt)], writes=[("x", tt)])

    def finish(self):
        self.P.finalize()
        return self.nc


def rel_bucket_np(rel):
    nb = 16
    max_exact = 8
    n = np.abs(rel)
    nf = np.maximum(n, 1).astype(np.float32)
    large = max_exact + (np.log(nf / np.float32(max_exact)) / np.float32(math.log(1024 / max_exact)) * np.float32(nb - max_exact)).astype(np.int32)
    large = np.minimum(large, nb - 1)
    return np.where(rel > 0, nb, 0) + np.where(n < max_exact, n, large)


def toeplitz_tile(col, delta, window=None, dil=1):
    kp = np.arange(128)[:, None]
    qp = np.arange(128)[None, :]
    rel = delta * 128 + kp - qp
    return rel


def bias_tables(rel_bias, kind, hf):
    rb = np.asarray(rel_bias, np.float32)
    kp = np.arange(128)[:, None]
    qp = np.arange(128)[None, :]
    if kind == "A":
        out = np.empty((16, 128, 3 * 128), np.float32)
        for i, dl in enumerate((1, 0, -1)):
            rel = dl * 128 + kp - qp
            bk = rel_bucket_np(rel)
            ok = np.abs(rel) <= 128
            for h in range(16):
                out[h, :, i * 128:(i + 1) * 128] = np.where(ok, rb[bk, h], NEG)
        return out
    if kind == "B":
        out = np.empty((16, 128, 44 * 128), np.float32)
        for i in range(44):
            dl = 28 - i - 16 * hf
            rel = dl * 128 + kp - qp
            bk = rel_bucket_np(rel)
            for m in range(16):
                out[m, :, i * 128:(i + 1) * 128] = rb[bk, m]
        return out
    ents = [(0, d_) for d_ in (1, 0, -1)] + [(1, d_) for d_ in (2, 1, 0, -1, -2)] + [(2, d_) for d_ in range(8, -9, -1)]
    dils = (1, 4, 16)
    out = np.empty((4, 128, 25 * 128), np.float32)
    for i, (g, dl) in enumerate(ents):
        rel = dl * 128 + kp - qp
        dil = dils[g]
        ok = (rel % dil == 0) & (np.abs(rel) <= 64 * dil)
        bk = rel_bucket_np(rel)
        for j in range(4):
            out[j, :, i * 128:(i + 1) * 128] = np.where(ok, rb[bk, g * 4 + j], NEG)
    return out


def gcols(g):
    return np.ascontiguousarray(np.asarray(g, np.float32).reshape(8, 128).T)


_PROG_CACHE = {}
DEBUG_STOP = None
DEBUG_OUT = {}


def get_prog(key, fn):
    if key not in _PROG_CACHE:
        b = Builder()
        fn(b)
        nc = b.finish()
        _PROG_CACHE[key] = (nc, list(b.din_names), list(b.dout_names))
    return _PROG_CACHE[key]


def launch(key, fn, per_core_inputs):
    nc, dins, douts = get_prog(key, fn)
    in_maps = []
    for c in range(NCORES):
        m = {}
        for n in dins:
            m[n] = per_core_inputs[c][n]
        in_maps.append(m)
    res = run_bass_kernel_spmd(nc, in_maps, core_ids=list(range(NCORES)))
    return res.results


def kernel(**inp):
    x = np.ascontiguousarray(np.asarray(inp["x"], np.float32))
    rel_bias = np.asarray(inp["rel_bias"], np.float32)
    ident = np.eye(128, dtype=np.float32)
    base = []
    for c in range(NCORES):
        hf = c % 2
        fl = np.empty((128, 2), np.float32)
        fl[:, 0] = 1.0 if hf == 1 else 0.0
        fl[:, 1] = 1.0 if hf == 0 else 0.0
        base.append({"ident": ident, "flags": fl})
    bt_cache = {}

    def bt(kind, hf):
        k = (kind, hf if kind == "B" else 0)
        if k not in bt_cache:
            bt_cache[k] = bias_tables(rel_bias, kind, k[1])
        return bt_cache[k]

    def layer_consts(li):
        L = LAYERS[li]
        p = "l%d_" % li
        d = {}
        d[p + "gcol_attn"] = gcols(inp[p + "attn_norm"])
        d[p + "gcol_ffn"] = gcols(inp[p + "ffn_norm"])
        d[p + "w_qkv"] = np.ascontiguousarray(np.asarray(inp[p + "w_qkv"], np.float32))
        d[p + "w_o"] = np.ascontiguousarray(np.asarray(inp[p + "w_o"], np.float32))
        d[p + "w_up"] = np.ascontiguousarray(np.asarray(inp[p + "w_up"], np.float32))
        d[p + "w_down"] = np.ascontiguousarray(np.asarray(inp[p + "w_down"], np.float32))
        d[p + "qkg"] = np.ascontiguousarray(np.stack([np.asarray(inp[p + "q_gain"], np.float32), np.asarray(inp[p + "k_gain"], np.float32)]))
        cwv = np.asarray(inp[p + "conv_w"], np.float32).reshape(3, 44, 128)
        cbv = np.asarray(inp[p + "conv_b"], np.float32).reshape(1, 44, 128)
        d[p + "convp"] = np.ascontiguousarray(np.concatenate([cwv, cbv], 0).transpose(2, 0, 1))
        if L["kind"] == "A":
            d[p + "sink"] = np.ascontiguousarray(np.asarray(inp[p + "sink"], np.float32))
        if L["kind"] == "B":
            d[p + "lam"] = np.ascontiguousarray(np.stack([np.asarray(inp[p + k], np.float32) for k in ("lambda_q1", "lambda_k1", "lambda_q2", "lambda_k2")]))
            d[p + "sgcol"] = np.ascontiguousarray(np.asarray(inp[p + "sub_gain"], np.float32).reshape(128, 1))
        return d

    consts = [layer_consts(li) for li in range(4)]
    xs = [np.ascontiguousarray(x[c // 2, (c % 2) * T:(c % 2 + 1) * T, :]) for c in range(NCORES)]
    state = [dict(x_in=xs[c]) for c in range(NCORES)]

    def merged(c, *dicts):
        m = dict(base[c])
        for dd in dicts:
            m.update(dd)
        return m

    def fA(li):
        def f(b):
            b.load_x()
            b.phase_qkv(li)
        return f

    def fB(li):
        def f(b):
            b.load_x()
            b.phase_attn(li)
            b.phase_wo(li)
            b.load_gcol("l%d_gcol_ffn" % li)
            b.phase_norm()
            b.phase_halo_out()
            b.store_x()
            b.store_actT()
        return f

    def fCA(li, nxt):
        def f(b):
            b.load_x()
            b.load_actT()
            b.phase_ffn(li)
            if nxt is not None:
                b.phase_qkv(nxt)
            b.store_x()
        return f

    resA = launch(("A", 0), fA(0), [merged(c, consts[0], state[c]) for c in range(NCORES)])
    for li in range(4):
        kind = LAYERS[li]["kind"]
        per = []
        for c in range(NCORES):
            c0 = (c // 2) * 2
            kTG = np.stack([resA[c0]["kT_own"], resA[c0 + 1]["kT_own"]])
            vG = np.concatenate([resA[c0]["v_own"], resA[c0 + 1]["v_own"]], 0)
            d = dict(qT=resA[c]["qT"], kT_own=resA[c]["kT_own"], v_own=resA[c]["v_own"], kT_G=kTG, v_G=vG)
            d["bt_" + kind] = bt(kind, c % 2)
            per.append(merged(c, consts[li], state[c], d))
        resB = launch(("B", li), fB(li), per)
        if DEBUG_STOP is not None:
            DEBUG_OUT[("B", li)] = [r["x_out"] for r in resB]
            DEBUG_OUT[("A", li)] = resA
            if DEBUG_STOP == ("B", li):
                return None
        per = []
        for c in range(NCORES):
            c0 = (c // 2) * 2
            hG = np.stack([resB[c0]["halo_own"], resB[c0 + 1]["halo_own"]])
            d = dict(x_in=resB[c]["x_out"], actT_in=resB[c]["actT_out"], halo_G=hG)
            cs = [consts[li]]
            if li + 1 < 4:
                cs.append(consts[li + 1])
            per.append(merged(c, *cs, d))
        resC = launch(("C", li), fCA(li, li + 1 if li + 1 < 4 else None), per)
        for c in range(NCORES):
            state[c] = dict(x_in=resC[c]["x_out"])
        resA = resC
        if DEBUG_STOP is not None:
            DEBUG_OUT[("C", li)] = [r["x_out"] for r in resC]
            if DEBUG_STOP == ("C", li):
                return None
    out = np.empty((4, 2 * T, D), np.float32)
    for c in range(NCORES):
        out[c // 2, (c % 2) * T:(c % 2 + 1) * T, :] = resC[c]["x_out"]
    return out
```

```python
import math
from contextlib import ExitStack

import numpy as np
import ml_dtypes

import concourse.bass as bass
import concourse.mybir as mybir
from concourse.ap import AP
from concourse.bass_utils import run_bass_kernel_spmd

F32 = mybir.dt.float32
BF16 = mybir.dt.bfloat16
AF = mybir.ActivationFunctionType
ALU = mybir.AluOpType
AX = mybir.AxisListType
NPBF = ml_dtypes.bfloat16

NCORES = 4
T = 4096
NT = 32
D = 1024
DFF = 2816
EPS = 1e-6
NEG = -30000.0

LAYERS = [
    dict(kind="A", F=1536, nqb=8, nkb=2, FV=256, nfc=8),
    dict(kind="B", F=3072, nqb=8, nkb=8, FV=1024, nfc=8),
    dict(kind="C", F=2560, nqb=12, nkb=4, FV=512, nfc=4),
    dict(kind="A", F=1536, nqb=8, nkb=2, FV=256, nfc=8),
]
NI = {"A": 3, "B": 28, "C": 25}
NUNIT_BT = {"A": 16, "B": 16, "C": 4}


def lambda_init_fn(layer):
    return 0.8 - 0.6 * math.exp(-0.3 * layer)


class Op:
    __slots__ = ("eng", "fn", "deps", "needs_inc", "val", "semkey", "is_dma", "idx")

    def __init__(self, eng, fn, deps, semkey, is_dma):
        self.eng = eng
        self.fn = fn
        self.deps = deps
        self.needs_inc = False
        self.val = None
        self.semkey = semkey
        self.is_dma = is_dma


class Prog:
    ENGS = ("sync", "scalar", "vector", "gpsimd", "tensor")

    def __init__(self, nc):
        self.nc = nc
        self.ops = {e: [] for e in self.ENGS}
        self.lastw = {}
        self.readers = {}
        self.es = ExitStack()
        self.outs = []
        self.fence = []
        self.fence_pending = set()
        self.last_dma = {}
        self.epoch = 0

    def barrier(self):
        fence = []
        for e in self.ENGS:
            for o in reversed(self.ops[e]):
                if not o.is_dma:
                    fence.append(o)
                    break
        fence.extend(self.last_dma.values())
        self.fence = fence
        self.fence_pending = set(self.ENGS)
        self.epoch += 1

    def op(self, eng, fn, reads=(), writes=(), slot=None, out=False):
        deps = []
        if eng in self.fence_pending:
            deps.extend(self.fence)
            self.fence_pending.discard(eng)
        for b in reads:
            w = self.lastw.get(b)
            if w is not None:
                deps.append(w)
        for b in writes:
            w = self.lastw.get(b)
            if w is not None:
                deps.append(w)
            deps.extend(self.readers.get(b, ()))
        is_dma = slot is not None
        semkey = ("dma", slot) if is_dma else ("eng", eng, self.epoch % 3)
        o = Op(eng, fn, deps, semkey, is_dma)
        o.idx = len(self.ops[eng])
        self.ops[eng].append(o)
        if is_dma:
            self.last_dma[slot] = o
        for b in writes:
            self.lastw[b] = o
            self.readers[b] = []
        for b in reads:
            self.readers.setdefault(b, []).append(o)
        if out:
            self.outs.append(o)
        return o

    @staticmethod
    def _skip(d, o):
        return d is o or (d.eng == "tensor" and o.eng == "tensor" and not d.is_dma and not o.is_dma)

    def finalize(self):
        nc = self.nc
        final_waits = self.outs
        for e in self.ENGS:
            for o in self.ops[e]:
                best = {}
                for d in o.deps:
                    if self._skip(d, o):
                        continue
                    if d.is_dma:
                        d.needs_inc = True
                        continue
                    b = best.get(d.semkey)
                    if b is None or d.idx > b.idx:
                        best[d.semkey] = d
                for d in best.values():
                    d.needs_inc = True
        for d in final_waits:
            d.needs_inc = True
        for e in self.ENGS:
            for o in self.ops[e]:
                if o.is_dma:
                    o.needs_inc = True
        counters = {}
        for e in self.ENGS:
            for o in self.ops[e]:
                if o.needs_inc:
                    c = counters.get(o.semkey, 0) + (16 if o.is_dma else 1)
                    counters[o.semkey] = c
                    o.val = c
        sems = {}
        for i, k in enumerate(counters):
            sems[k] = self.es.enter_context(nc.semaphore("s%d" % i))
        self.nsem = len(sems)
        block = self.es.enter_context(nc.Block())

        def run(e, engine):
            known = {}
            for o in self.ops[e]:
                need = {}
                for d in o.deps:
                    if self._skip(d, o) or d.val is None:
                        continue
                    if need.get(d.semkey, 0) < d.val:
                        need[d.semkey] = d.val
                for k, v in need.items():
                    if known.get(k, 0) >= v:
                        continue
                    engine.wait_ge(sems[k], v)
                    known[k] = v
                ins = o.fn(engine)
                if o.needs_inc:
                    ins.then_inc(sems[o.semkey], 16 if o.is_dma else 1)
            if e == "sync":
                need = {}
                for d in final_waits:
                    if need.get(d.semkey, 0) < d.val:
                        need[d.semkey] = d.val
                for k, v in need.items():
                    engine.wait_ge(sems[k], v)

        @block.sync
        def _(eng):
            run("sync", eng)

        @block.scalar
        def _(eng):
            run("scalar", eng)

        @block.vector
        def _(eng):
            run("vector", eng)

        @block.gpsimd
        def _(eng):
            run("gpsimd", eng)

        @block.tensor
        def _(eng):
            run("tensor", eng)

        self.es.close()


SB_BASE = 16512
SCR_END = 229376


class Builder:
    def __init__(self):
        self.nc = bass.Bass("TRN2", target_bir_lowering=False)
        self.P = Prog(self.nc)
        self.din_names = []
        self.dout_names = []
        self.d = {}
        self.uid = 0
        self.perm_off = SB_BASE
        nc = self.nc
        self.actT = self.perm("actT", [128, 8, T + 2], BF16)
        self.ws = [self.perm("ws%d" % i, [128, 4, 512], F32) for i in range(2)]
        self.wb = [self.perm("wb%d" % i, [128, 8, 512], BF16) for i in range(2)]
        self.xts = [self.perm("xt%d" % i, [128, D], F32) for i in range(4)]
        self.hbs = [self.perm("hb%d" % i, [128, D], BF16) for i in range(2)]
        self.ident = self.perm("ident", [128, 128], BF16)
        self.identf = self.perm("identf", [128, 128], F32)
        self.sst = [self.perm("sst%d" % i, [128, 4], F32) for i in range(4)]
        self.epsb = self.perm("epsb", [128, 1], F32)
        self.gcol = self.perm("gcol", [128, 8], F32)
        self.PERM_END = (self.perm_off + 63) // 64 * 64
        self.scr_off = self.PERM_END
        self.nscr = 0
        self.xcnt = 0
        self.banks = [nc.alloc_psum_tensor("bank%d" % i, [128, 512], F32) for i in range(8)]
        self.wcount = 0
        self.xd = self.dout("x_out", [T, D], F32).ap()
        self.qT_d = nc.dram_tensor("qT_scr", [12, 128, T], BF16, kind="Internal").ap()
        self.kT_d = nc.dram_tensor("kT_scr", [8, 128, T], BF16, kind="Internal").ap()
        self.v_d = nc.dram_tensor("v_scr", [T, 1024], BF16, kind="Internal").ap()
        self.init_consts()

    def perm(self, name, shape, dt):
        n = int(np.prod(shape[1:])) * (4 if dt == F32 else 2)
        n = (n + 31) // 32 * 32
        t = self.nc.alloc_sbuf_tensor_at(name, shape, dt, offset=self.perm_off)
        self.perm_off += n
        return t

    def scr_reset(self):
        if self.nscr > 0:
            self.P.barrier()
        self.nscr += 1
        self.scr_off = self.PERM_END

    def scr(self, name, shape, dt):
        n = int(np.prod(shape[1:])) * (4 if dt == F32 else 2)
        n = (n + 31) // 32 * 32
        self.uid += 1
        t = self.nc.alloc_sbuf_tensor_at("%s_%d" % (name, self.uid), shape, dt, offset=self.scr_off)
        self.scr_off += n
        assert self.scr_off <= SCR_END, (name, self.scr_off)
        return t

    def din(self, name, shape, dt):
        if name not in self.d:
            self.d[name] = self.nc.dram_tensor(name, list(shape), dt, kind="ExternalInput")
            self.din_names.append(name)
        return self.d[name]

    def dout(self, name, shape, dt):
        if name not in self.d:
            self.d[name] = self.nc.dram_tensor(name, list(shape), dt, kind="ExternalOutput")
            self.dout_names.append(name)
        return self.d[name]

    def bank_bf(self, i):
        return self.banks[i][:].bitcast(BF16).rearrange("p (c t) -> p c t", t=128)

    def init_consts(self):
        P = self.P
        idd = self.din("ident", [128, 128], F32).ap()
        xin = self.din("x", [T, D], F32).ap()
        P.op("sync", lambda e: e.dma_start(out=self.identf[:], in_=idd), writes=["identf"], slot="c_id")
        P.op("vector", lambda e: e.tensor_copy(out=self.ident[:], in_=self.identf[:]), reads=["identf"], writes=["ident"])
        P.op("vector", lambda e: e.memset(self.epsb[:], EPS), writes=["eps"])
        P.op("vector", lambda e: e.memset(self.actT[:, :, 0:1], 0.0), writes=["halo"])
        P.op("vector", lambda e: e.memset(self.actT[:, :, T + 1:T + 2], 0.0), writes=["halo"])
        for t0 in range(0, NT, 8):
            P.op("sync", lambda e, t0=t0: e.dma_start(out=self.xd[t0 * 128:(t0 + 8) * 128, :], in_=xin[t0 * 128:(t0 + 8) * 128, :]),
                 writes=[("xd", t, n) for t in range(t0, t0 + 8) for n in range(2)], slot="xcp%d" % (t0 // 8), out=True)

    def x_load(self, t, n=None):
        P = self.P
        i = self.xcnt % 4
        self.xcnt += 1
        xt = self.xts[i]
        key = ("xt", i)
        if n is None:
            P.op("sync", lambda e, t=t, xt=xt: e.dma_start(out=xt[:], in_=self.xd[t * 128:(t + 1) * 128, :]),
                 reads=[("xd", t, 0), ("xd", t, 1)], writes=[key], slot="xl%d" % i)
        else:
            P.op("sync", lambda e, t=t, xt=xt, n=n: e.dma_start(out=xt[:, 0:512], in_=self.xd[t * 128:(t + 1) * 128, n * 512:(n + 1) * 512]),
                 reads=[("xd", t, n)], writes=[key], slot="xl%d" % i)
        return xt, key, i

    def x_store(self, t, n, xt, key, i):
        P = self.P
        P.op("gpsimd", lambda e, t=t, xt=xt, n=n: e.dma_start(out=self.xd[t * 128:(t + 1) * 128, n * 512:(n + 1) * 512], in_=xt[:, 0:512]),
             reads=[key], writes=[("xd", t, n)], slot="xs%d" % i, out=True)

    def load_unit(self, W, kchunks, segs, scale=None, scale_reads=()):
        P = self.P
        i = self.wcount
        self.wcount += 1
        slot = i % 2
        wb = self.wb[slot]
        key = ("wb", slot)
        Wa = W.ap()
        ncol = sum(s[1] for s in segs)
        halves = [kchunks[0:4], kchunks[4:8]]
        allkeys = []
        for hi, kc in enumerate(halves):
            if not kc:
                continue
            ws = self.ws[hi]
            co = 0
            for si, (c0, cn) in enumerate(segs):
                k0 = kc[0]
                src = Wa[k0 * 128:(k0 + len(kc)) * 128, c0:c0 + cn].rearrange("(c p) f -> p c f", p=128)
                P.op("sync", lambda e, ws=ws, src=src, co=co, cn=cn, n=len(kc): e.dma_start(out=ws[:, 0:n, co:co + cn], in_=src),
                     writes=[("ws", hi, si)], slot="ws%d_%d" % (hi, si))
                co += cn
            n = len(kc)
            if scale is None:
                P.op("gpsimd", lambda e, ws=ws, wb=wb, hi=hi, n=n, ncol=ncol: e.tensor_copy(out=wb[:, hi * 4:hi * 4 + n, 0:ncol], in_=ws[:, 0:n, 0:ncol]),
                     reads=[("ws", hi, si) for si in range(len(segs))], writes=[(key, hi, 0)])
                allkeys.append((key, hi, 0))
            else:
                for j, k in enumerate(kc):
                    sc = scale(k)
                    P.op("gpsimd", lambda e, ws=ws, wb=wb, hi=hi, j=j, ncol=ncol, sc=sc: e.tensor_scalar(out=wb[:, hi * 4 + j, 0:ncol], in0=ws[:, j, 0:ncol], scalar1=sc, scalar2=None, op0=ALU.mult),
                         reads=[("ws", hi, si) for si in range(len(segs))] + list(scale_reads), writes=[(key, hi, j)])
                    allkeys.append((key, hi, j))
        return slot, allkeys

    def phase_norm(self):
        P = self.P
        junk = self.banks[7]
        hbs = self.hbs
        pend = [self.x_load(0), self.x_load(1)]
        for t in range(NT):
            if t + 2 < NT:
                pend.append(self.x_load(t + 2))
            xt, xk, _ = pend[t]
            st = self.sst[t % 4]
            sk = ("sst", t % 4)
            P.op("vector", lambda e, st=st: e.memset(st[:], 0.0), writes=[sk])
            P.op("scalar", lambda e, xt=xt, st=st: e.activation(out=junk[:, 0:512], in_=xt[:, 0:512], func=AF.Square, accum_out=st[:, 0:1]),
                 reads=[xk, sk], writes=[sk, ("bank", 7)])
            P.op("scalar", lambda e, xt=xt, st=st: e.activation(out=junk[:, 0:512], in_=xt[:, 512:1024], func=AF.Square, accum_out=st[:, 1:2]),
                 reads=[xk, sk], writes=[sk, ("bank", 7)])
            P.op("vector", lambda e, st=st: e.tensor_tensor(out=st[:, 2:3], in0=st[:, 0:1], in1=st[:, 1:2], op=ALU.add), reads=[sk], writes=[sk])
            P.op("scalar", lambda e, st=st: e.activation(out=st[:, 2:3], in_=st[:, 2:3], func=AF.Ln, scale=1.0 / D, bias=self.epsb[:, 0:1]),
                 reads=[sk, "eps"], writes=[sk])
            P.op("scalar", lambda e, st=st: e.activation(out=st[:, 3:4], in_=st[:, 2:3], func=AF.Exp, scale=-0.5), reads=[sk], writes=[sk])
            hb = hbs[t % 2]
            hk = ("hb", t % 2)
            pk = ("bank", t % 2)
            pT = self.bank_bf(t % 2)
            P.op("vector", lambda e, xt=xt, hb=hb, st=st: e.tensor_scalar(out=hb[:], in0=xt[:], scalar1=st[:, 3:4], scalar2=None, op0=ALU.mult),
                 reads=[xk, sk], writes=[hk])
            for c in range(8):
                P.op("tensor", lambda e, c=c, hb=hb, pT=pT: e.transpose(out=pT[:, c, :], in_=hb[:, c * 128:(c + 1) * 128], identity=self.ident[:]),
                     reads=[hk, "ident"], writes=[pk])
            P.op("scalar", lambda e, t=t, pT=pT: e.activation(out=self.actT[:, :, 1 + t * 128:1 + (t + 1) * 128], in_=pT, func=AF.Copy),
                 reads=[pk], writes=[("actT", c, t) for c in range(8)])

    def load_gcol(self, name):
        P = self.P
        g = self.din(name, [128, 8], F32).ap()
        P.op("sync", lambda e: e.dma_start(out=self.gcol[:], in_=g), writes=["gcol"], slot="gcol")

    def phase_qkv(self, li):
        P = self.P
        L = LAYERS[li]
        kind = L["kind"]
        pfx = "l%d_" % li
        self.load_gcol(pfx + "gcol_attn")
        self.phase_norm()
        self.scr_reset()
        W = self.din(pfx + "w_qkv", [D, L["F"]], F32)
        dh = 128 if kind == "C" else 64
        geff_d = self.din(pfx + "qkg", [2, dh], F32).ap()
        qT_d, kT_d, v_d = self.qT_d, self.kT_d, self.v_d
        gq = self.scr("gq", [128, dh], F32)
        gk = self.scr("gk", [128, dh], F32)
        P.op("sync", lambda e: e.dma_start(out=gq[:], in_=geff_d[0].partition_broadcast(128)), writes=["gq"], slot="gq")
        P.op("sync", lambda e: e.dma_start(out=gk[:], in_=geff_d[1].partition_broadcast(128)), writes=["gk"], slot="gk")
        P.op("vector", lambda e: e.scalar_tensor_tensor(out=gk[:], in0=gq[:], scalar=float(dh) ** -0.5, in1=gk[:], op0=ALU.mult, op1=ALU.mult),
             reads=["gq", "gk"], writes=["gk"])
        stages = [self.scr("stage", [128, 4, T], BF16) for _ in range(2)]
        sqs = [self.scr("sq", [128, 512], F32) for _ in range(2)]
        kfs = [self.scr("kf", [128, 512], F32) for _ in range(2)]
        qns = [self.scr("qn", [128, 512], BF16) for _ in range(2)]
        vsts = [self.scr("vst", [128, 512], BF16) for _ in range(2)]
        ssq = [self.scr("ssq", [128, 8], F32) for _ in range(2)]
        if kind == "A":
            chunks = [[("q", 0, 512, 0)], [("q", 0, 512, 4)], [("k", 0, 256, 0), ("v", 256, 256, 0)]]
        elif kind == "B":
            chunks = [[("q", 0, 512, 0)], [("q", 0, 512, 4)], [("k", 0, 512, 0)], [("k", 0, 512, 4)], [("v", 0, 512, 0)], [("v", 0, 512, 512)]]
        else:
            chunks = [[("q", 0, 512, 0)], [("q", 0, 512, 4)], [("q", 0, 512, 8)], [("k", 0, 512, 0)], [("v", 0, 512, 0)]]
        gsc = lambda k: self.gcol[:, k:k + 1]
        units = [None] * len(chunks)
        units[0] = self.load_unit(W, list(range(8)), [(0, 512)], scale=gsc, scale_reads=["gcol"])
        it = 0
        for ci, segs in enumerate(chunks):
            if ci + 1 < len(chunks):
                units[ci + 1] = self.load_unit(W, list(range(8)), [((ci + 1) * 512, 512)], scale=gsc, scale_reads=["gcol"])
            slot, wkeys = units[ci]
            wb = self.wb[slot]
            stage = stages[ci % 2]
            stk = ("stage", ci % 2)
            for t in range(NT):
                bi = 2 + (it % 2)
                py = self.banks[bi]
                pyk = ("bank", bi)
                for c in range(8):
                    P.op("tensor", lambda e, c=c, t=t, py=py, wb=wb: e.matmul(py[:, 0:512], lhsT=self.actT[:, c, 1 + t * 128:1 + (t + 1) * 128], rhs=wb[:, c, 0:512], start=(c == 0), stop=(c == 7)),
                         reads=[("actT", c, t)] + wkeys, writes=[pyk])
                for (ty, off, w, dst) in segs:
                    if ty == "v":
                        vst = vsts[it % 2]
                        vk = ("vst", it % 2)
                        P.op("scalar", lambda e, py=py, vst=vst, off=off, w=w: e.activation(out=vst[:, 0:w], in_=py[:, off:off + w], func=AF.Copy),
                             reads=[pyk], writes=[vk])
                        P.op("gpsimd", lambda e, vst=vst, t=t, dst=dst, w=w: e.dma_start(out=v_d[t * 128:(t + 1) * 128, dst:dst + w], in_=vst[:, 0:w]),
                             reads=[vk], writes=[("vd", t, dst)], slot="vst%d" % (it % 2))
                        continue
                    nh = w // dh
                    sq = sqs[it % 2]
                    sqk = ("sq", it % 2)
                    s_ = ssq[it % 2]
                    sk = ("ssq", it % 2)
                    qn = qns[it % 2]
                    qk = ("qn", it % 2)
                    P.op("scalar", lambda e, py=py, sq=sq, off=off, w=w: e.activation(out=sq[:, 0:w], in_=py[:, off:off + w], func=AF.Square),
                         reads=[pyk], writes=[sqk])
                    P.op("vector", lambda e, sq=sq, s_=s_, w=w, nh=nh: e.tensor_reduce(out=s_[:, 0:nh], in_=sq[:, 0:w].rearrange("p (h d) -> p h d", d=dh), axis=AX.X, op=ALU.add),
                         reads=[sqk], writes=[sk])
                    P.op("scalar", lambda e, s_=s_, nh=nh: e.activation(out=s_[:, 0:nh], in_=s_[:, 0:nh], func=AF.Ln, scale=1.0 / dh, bias=self.epsb[:, 0:1]),
                         reads=[sk, "eps"], writes=[sk])
                    P.op("scalar", lambda e, s_=s_, nh=nh: e.activation(out=s_[:, 0:nh], in_=s_[:, 0:nh], func=AF.Exp, scale=-0.5), reads=[sk], writes=[sk])
                    rb = AP(s_, 0, [[8, 128], [1, nh], [0, dh]])
                    if ty == "q":
                        P.op("vector", lambda e, py=py, qn=qn, off=off, w=w, rb=rb: e.tensor_tensor(out=qn[:, 0:w].rearrange("p (h d) -> p h d", d=dh), in0=py[:, off:off + w].rearrange("p (h d) -> p h d", d=dh), in1=rb, op=ALU.mult),
                             reads=[pyk, sk], writes=[qk])
                    else:
                        kf = kfs[it % 2]
                        kfk = ("kf", it % 2)
                        gb = AP(gk, 0, [[dh, 128], [0, nh], [1, dh]])
                        P.op("vector", lambda e, py=py, kf=kf, off=off, w=w, rb=rb: e.tensor_tensor(out=kf[:, 0:w].rearrange("p (h d) -> p h d", d=dh), in0=py[:, off:off + w].rearrange("p (h d) -> p h d", d=dh), in1=rb, op=ALU.mult),
                             reads=[pyk, sk], writes=[kfk])
                        P.op("gpsimd", lambda e, kf=kf, qn=qn, w=w, gb=gb: e.tensor_tensor(out=qn[:, 0:w].rearrange("p (h d) -> p h d", d=dh), in0=kf[:, 0:w].rearrange("p (h d) -> p h d", d=dh), in1=gb, op=ALU.mult),
                             reads=[kfk, "gk"], writes=[qk])
                    nb = w // 128
                    tb = it % 2
                    pT = self.bank_bf(tb)
                    pk = ("bank", tb)
                    for j in range(nb):
                        P.op("tensor", lambda e, j=j, qn=qn, pT=pT: e.transpose(out=pT[:, j, :], in_=qn[:, j * 128:(j + 1) * 128], identity=self.ident[:]),
                             reads=[qk, "ident"], writes=[pk])
                    P.op("scalar", lambda e, pT=pT, stage=stage, nb=nb, t=t: e.activation(out=stage[:, 0:nb, t * 128:(t + 1) * 128], in_=pT[:, 0:nb, :], func=AF.Copy),
                         reads=[pk], writes=[(stk, t)])
                it += 1
            for (ty, off, w, dst) in segs:
                if ty == "v":
                    continue
                dd = qT_d if ty == "q" else kT_d
                dk = "qTd" if ty == "q" else "kTd"
                for j in range(w // 128):
                    P.op("gpsimd", lambda e, dd=dd, j=j, dst=dst, stage=stage: e.dma_start(out=dd[dst + j], in_=stage[:, j, :]),
                         reads=[(stk, t) for t in range(NT)], writes=[(dk, dst + j)], slot="stg%d_%d" % (ci % 2, j))

    def attn_item(self, it, mms, tb_ap, ncols, pvs, tbkey, extra_reads):
        P = self.P
        bi = it % 2
        st = self.banks[bi]
        stk = ("bank", bi)
        sc = self.a_sc[it % 2]
        sck = ("sc", it % 2)
        pt = self.a_pt[it % 2]
        ptk = ("pt", it % 2)
        for (lh, rh, c0, n) in mms:
            P.op("tensor", lambda e, lh=lh, rh=rh, c0=c0, n=n, st=st: e.matmul(st[:, c0:c0 + n], lhsT=lh, rhs=rh, start=True, stop=True),
                 reads=extra_reads, writes=[stk])
        P.op("vector", lambda e, st=st, sc=sc, tb_ap=tb_ap, ncols=ncols: e.tensor_tensor(out=sc[:, 0:ncols], in0=st[:, 0:ncols], in1=tb_ap, op=ALU.add),
             reads=[stk, tbkey], writes=[sck])
        P.op("scalar", lambda e, sc=sc, pt=pt, ncols=ncols: e.activation(out=pt[:, 0:ncols], in_=sc[:, 0:ncols], func=AF.Exp),
             reads=[sck], writes=[ptk])
        for (c0, v_ap, ab, wdt, s0, s1) in pvs:
            acc = self.banks[ab]
            P.op("tensor", lambda e, c0=c0, v_ap=v_ap, acc=acc, wdt=wdt, s0=s0, s1=s1, pt=pt: e.matmul(acc[:, 0:wdt], lhsT=pt[:, c0:c0 + 128], rhs=v_ap, start=s0, stop=s1),
                 reads=[ptk] + extra_reads, writes=[("bank", ab)])

    def out_transpose(self, on, onk, chunk, qt, trk):
        P = self.P
        bi = 6 + (trk % 2)
        pT = self.bank_bf(bi)
        P.op("tensor", lambda e, on=on, pT=pT: e.transpose(out=pT[:, 0, :], in_=on, identity=self.ident[:]),
             reads=[onk, "ident"], writes=[("bank", bi)])
        P.op("scalar", lambda e, pT=pT, chunk=chunk, qt=qt: e.activation(out=self.actT[:, chunk, 1 + qt * 128:1 + (qt + 1) * 128], in_=pT[:, 0, :], func=AF.Copy),
             reads=[("bank", bi)], writes=[("actT", chunk, qt)])

    def phase_attn(self, li):
        P = self.P
        L = LAYERS[li]
        kind = L["kind"]
        pfx = "l%d_" % li
        self.scr_reset()
        nkb, nqb, FV = L["nkb"], L["nqb"], L["FV"]
        qT_d, kT_d, v_d = self.qT_d, self.kT_d, self.v_d
        ni = NI[kind]
        bt_d = self.din("bt_" + kind, [NUNIT_BT[kind], 128, ni * 128], F32).ap()
        self.a_sc = [self.scr("sc", [128, 512], F32) for _ in range(2)]
        self.a_pt = [self.scr("pt", [128, 512], BF16) for _ in range(2)]
        rr = [self.scr("rr", [128, 4], F32) for _ in range(4)]
        ons = [self.scr("on", [128, 128], BF16) for _ in range(2)]
        v_r = v_d.rearrange("(kb p) f -> p kb f", p=128)
        qkeys = [("qTd", b) for b in range(nqb)]
        kkeys = [("kTd", b) for b in range(nkb)]
        vkeys = [("vd", t, c0) for t in range(NT) for c0 in range(0, FV, 512 if FV >= 512 else 256)]
        it = 0
        fin = 0
        if kind == "B":
            dv = 128
            KTs = [self.scr("KT", [128, T], BF16) for _ in range(2)]
            Vs = [self.scr("V", [128, NT, dv + 1], BF16) for _ in range(2)]
            QT = self.scr("QT", [128, T], BF16)
            TB = self.scr("TB", [128, ni * 128], F32)
            o1 = self.scr("o1", [128, NT, 128], F32)
            ods = [self.scr("od", [128, 128], F32) for _ in range(2)]
            lam = self.hbs[0][:].bitcast(F32)[:, 0:256].rearrange("p (a d) -> p a d", d=64)
            lamv = self.scr("lamv", [128, 4], F32)
            lam_d = self.din(pfx + "lam", [4, 64], F32).ap()
            P.op("sync", lambda e: e.dma_start(out=lam, in_=AP(lam_d.tensor, 0, [[0, 128], [64, 4], [1, 64]])), writes=["lam", ("hb", 0)], slot="lam")
            P.op("vector", lambda e: e.tensor_tensor(out=lam[:, 0, :], in0=lam[:, 0, :], in1=lam[:, 1, :], op=ALU.mult), reads=["lam"], writes=["lam"])
            P.op("vector", lambda e: e.tensor_tensor(out=lam[:, 2, :], in0=lam[:, 2, :], in1=lam[:, 3, :], op=ALU.mult), reads=["lam"], writes=["lam"])
            P.op("vector", lambda e: e.tensor_reduce(out=lamv[:, 0:1], in_=lam[:, 0, :], axis=AX.X, op=ALU.add), reads=["lam"], writes=["lamv"])
            P.op("vector", lambda e: e.tensor_reduce(out=lamv[:, 1:2], in_=lam[:, 2, :], axis=AX.X, op=ALU.add), reads=["lam"], writes=["lamv", ("hb", 0)])
            P.op("scalar", lambda e: e.activation(out=lamv[:, 0:2], in_=lamv[:, 0:2], func=AF.Exp), reads=["lamv"], writes=["lamv"])
            P.op("vector", lambda e: e.scalar_tensor_tensor(out=lamv[:, 2:3], in0=lamv[:, 1:2], scalar=-lambda_init_fn(li), in1=lamv[:, 0:1], op0=ALU.add, op1=ALU.subtract),
                 reads=["lamv"], writes=["lamv"])
            neglam = lamv[:, 2:3]
            for vb in Vs:
                P.op("gpsimd", lambda e, vb=vb: e.memset(vb[:, :, dv:dv + 1], 1.0), writes=[("Vones", id(vb))])

            def load_head(h):
                s = h % 2
                P.op("sync", lambda e, h=h, s=s: e.dma_start(out=KTs[s][:], in_=kT_d[h]), reads=kkeys, writes=[("KT", s)], slot="kt%da" % s)
                for half in range(2):
                    P.op("sync", lambda e, h=h, s=s, half=half: e.dma_start(out=Vs[s][:, half * 16:(half + 1) * 16, 0:dv], in_=v_r[:, half * 16:(half + 1) * 16, h * dv:(h + 1) * dv]),
                         reads=vkeys, writes=[("V", s)], slot="v%d_%d" % (s, half))

            load_head(0)
            for h in range(8):
                s = h % 2
                if h + 1 < 8:
                    load_head(h + 1)
                P.op("sync", lambda e, h=h: e.dma_start(out=QT[:], in_=qT_d[h]), reads=qkeys, writes=["QT"], slot="qt")
                KT, V = KTs[s], Vs[s]
                rds = [("KT", s), ("V", s), "QT", ("Vones", id(V))]
                for j in range(2):
                    for half in range(2):
                        hw = ni * 64
                        P.op("sync", lambda e, h=h, j=j, half=half, hw=hw: e.dma_start(out=TB[:, half * hw:(half + 1) * hw], in_=bt_d[h * 2 + j, :, half * hw:(half + 1) * hw]),
                             writes=["TB"], slot="tb%d" % half)
                    for qc in range(NT // 4):
                        for kb in range(NT):
                            dp = min(max(kb - 4 * qc, -12), 12)
                            i0 = 12 - dp
                            mms = [(KT[64 * j:64 * j + 64, kb * 128:(kb + 1) * 128], QT[64 * j:64 * j + 64, qc * 512:(qc + 1) * 512], 0, 512)]
                            pvs = [(jq * 128, V[:, kb, 0:dv + 1], 2 + jq, dv + 1, kb == 0, kb == NT - 1) for jq in range(4)]
                            self.attn_item(it, mms, TB[:, i0 * 128:i0 * 128 + 512], 512, pvs, "TB", rds)
                            it += 1
                        for jq in range(4):
                            qt = qc * 4 + jq
                            acc = self.banks[2 + jq]
                            ak = ("bank", 2 + jq)
                            r = rr[fin % 4]
                            rk = ("rr", fin % 4)
                            fin += 1
                            P.op("vector", lambda e, acc=acc, r=r: e.reciprocal(out=r[:, 0:1], in_=acc[:, dv:dv + 1]), reads=[ak], writes=[rk])
                            if j == 0:
                                P.op("vector", lambda e, acc=acc, r=r, qt=qt: e.tensor_scalar(out=o1[:, qt, :], in0=acc[:, 0:dv], scalar1=r[:, 0:1], scalar2=None, op0=ALU.mult),
                                     reads=[ak, rk], writes=[("o1", qt)])
                            else:
                                od = ods[fin % 2]
                                odk = ("od", fin % 2)
                                on = ons[fin % 2]
                                onk = ("on", fin % 2)
                                P.op("vector", lambda e, r=r: e.tensor_scalar(out=r[:, 1:2], in0=r[:, 0:1], scalar1=neglam, scalar2=None, op0=ALU.mult),
                                     reads=[rk, "lamv"], writes=[rk])
                                P.op("vector", lambda e, acc=acc, r=r, qt=qt, od=od: e.scalar_tensor_tensor(out=od[:], in0=acc[:, 0:dv], scalar=r[:, 1:2], in1=o1[:, qt, :], op0=ALU.mult, op1=ALU.add),
                                     reads=[ak, rk, ("o1", qt)], writes=[odk])
                                P.op("vector", lambda e, r=r: e.memset(r[:, 2:3], 0.0), reads=[], writes=[rk])
                                P.op("scalar", lambda e, od=od, on=on, r=r: e.activation(out=on[:], in_=od[:], func=AF.Square, accum_out=r[:, 2:3]),
                                     reads=[odk, rk], writes=[rk, onk])
                                P.op("scalar", lambda e, r=r: e.activation(out=r[:, 2:3], in_=r[:, 2:3], func=AF.Ln, scale=1.0 / 128, bias=self.epsb[:, 0:1]),
                                     reads=[rk, "eps"], writes=[rk])
                                P.op("scalar", lambda e, r=r: e.activation(out=r[:, 2:3], in_=r[:, 2:3], func=AF.Exp, scale=-0.5), reads=[rk], writes=[rk])
                                P.op("vector", lambda e, od=od, on=on, r=r: e.tensor_scalar(out=on[:], in0=od[:], scalar1=r[:, 2:3], scalar2=None, op0=ALU.mult),
                                     reads=[odk, rk], writes=[onk])
                                self.out_transpose(on[:], onk, h, qt, fin)
        elif kind == "A":
            dv = 64
            NE = NT + 2
            KTs = [self.scr("KT", [64, NE * 128], BF16) for _ in range(2)]
            Vs = [self.scr("V", [128, NE, dv + 1], BF16) for _ in range(2)]
            QTs = [self.scr("QT", [64, T], BF16) for _ in range(2)]
            TBs = [self.scr("TB", [128, ni * 128], F32) for _ in range(2)]
            pairs = [self.scr("pair", [128, NT, 128], BF16) for _ in range(2)]
            esink = self.scr("esink", [128, 16], F32)
            sink_d = self.din(pfx + "sink", [16], F32).ap()
            P.op("sync", lambda e: e.dma_start(out=esink[:], in_=sink_d.partition_broadcast(128)), writes=["esink"], slot="esink")
            P.op("scalar", lambda e: e.activation(out=esink[:], in_=esink[:], func=AF.Exp), reads=["esink"], writes=["esink"])
            for s_i in range(2):
                vb, kb_ = Vs[s_i], KTs[s_i]
                P.op("gpsimd", lambda e, vb=vb: e.memset(vb[:], 0.0), writes=[("Vones", s_i), ("V", s_i)])
                P.op("gpsimd", lambda e, vb=vb: e.memset(vb[:, 1:NE - 1, dv:dv + 1], 1.0), writes=[("Vones", s_i), ("V", s_i)])
                P.op("gpsimd", lambda e, kb_=kb_: e.memset(kb_[:], 0.0), writes=[("KT", s_i)])

            def load_kv(kvh):
                s = kvh % 2
                blk, r0 = kvh // 2, (kvh % 2) * 64
                P.op("sync", lambda e: e.dma_start(out=KTs[s][:, 128:128 + T], in_=kT_d[blk, r0:r0 + 64, :]), reads=kkeys, writes=[("KT", s)], slot="kt%db" % s)
                for half in range(2):
                    P.op("sync", lambda e, half=half: e.dma_start(out=Vs[s][:, 1 + half * 16:1 + (half + 1) * 16, 0:dv], in_=v_r[:, half * 16:(half + 1) * 16, kvh * dv:(kvh + 1) * dv]),
                         reads=vkeys, writes=[("V", s)], slot="v%db%d" % (s, half))

            def load_q(h):
                s = h % 2
                P.op("sync", lambda e: e.dma_start(out=QTs[s][:], in_=qT_d[h // 2, (h % 2) * 64:(h % 2) * 64 + 64, :]), reads=qkeys, writes=[("QT", s)], slot="qt%d" % s)
                P.op("sync", lambda e: e.dma_start(out=TBs[s][:], in_=bt_d[h]), writes=[("TB", s)], slot="tb%d" % s)

            load_kv(0)
            load_q(0)
            for h in range(16):
                kvh = h // 4
                s = kvh % 2
                if h % 4 == 0 and kvh + 1 < 4:
                    load_kv(kvh + 1)
                if h + 1 < 16:
                    load_q(h + 1)
                KT, V, QT, TB = KTs[s], Vs[s], QTs[h % 2], TBs[h % 2]
                pair = pairs[(h // 2) % 2]
                rds = [("KT", s), ("V", s), ("QT", h % 2), ("Vones", s)]
                for qt in range(NT):
                    ab = 2 + (it % 4)
                    mms = [(KT[0:64, (qt + 2 - i) * 128:(qt + 3 - i) * 128], QT[0:64, qt * 128:(qt + 1) * 128], i * 128, 128) for i in range(3)]
                    pvs = [(i * 128, V[:, qt + 2 - i, 0:dv + 1], ab, dv + 1, i == 0, i == 2) for i in range(3)]
                    self.attn_item(it, mms, TB[:, 0:384], 384, pvs, ("TB", h % 2), rds)
                    it += 1
                    acc = self.banks[ab]
                    ak = ("bank", ab)
                    r = rr[fin % 4]
                    rk = ("rr", fin % 4)
                    fin += 1
                    P.op("vector", lambda e, acc=acc, r=r, h=h: e.tensor_tensor(out=r[:, 0:1], in0=acc[:, dv:dv + 1], in1=esink[:, h:h + 1], op=ALU.add),
                         reads=[ak, "esink"], writes=[rk])
                    P.op("vector", lambda e, r=r: e.reciprocal(out=r[:, 1:2], in_=r[:, 0:1]), reads=[rk], writes=[rk])
                    P.op("vector", lambda e, acc=acc, r=r, pair=pair, qt=qt, h=h: e.tensor_scalar(out=pair[:, qt, (h % 2) * 64:(h % 2) * 64 + 64], in0=acc[:, 0:dv], scalar1=r[:, 1:2], scalar2=None, op0=ALU.mult),
                         reads=[ak, rk], writes=[("pair", (h // 2) % 2, qt, h % 2)])
                if h % 2 == 1:
                    for qt in range(NT):
                        fin += 1
                        bi = 6 + (fin % 2)
                        pT = self.bank_bf(bi)
                        P.op("tensor", lambda e, pair=pair, qt=qt, pT=pT: e.transpose(out=pT[:, 0, :], in_=pair[:, qt, :], identity=self.ident[:]),
                             reads=[("pair", (h // 2) % 2, qt, 0), ("pair", (h // 2) % 2, qt, 1), "ident"], writes=[("bank", bi)])
                        P.op("scalar", lambda e, pT=pT, h=h, qt=qt: e.activation(out=self.actT[:, h // 2, 1 + qt * 128:1 + (qt + 1) * 128], in_=pT[:, 0, :], func=AF.Copy),
                             reads=[("bank", bi)], writes=[("actT", h // 2, qt)])
        else:
            dv = 128
            NE = NT + 16
            KT = self.scr("KT", [128, NE * 128], BF16)
            V = self.scr("V", [128, NE, dv + 1], BF16)
            QT = self.scr("QT", [128, 3, T], BF16)
            TB = self.scr("TB", [128, ni * 128], F32)
            P.op("gpsimd", lambda e: e.memset(V[:], 0.0), writes=["Vones", "V"])
            P.op("gpsimd", lambda e: e.memset(V[:, 8:NE - 8, dv:dv + 1], 1.0), writes=["Vones", "V"])
            P.op("gpsimd", lambda e: e.memset(KT[:], 0.0), writes=["KT"])
            ents = [(0, d_) for d_ in (1, 0, -1)] + [(1, d_) for d_ in (2, 1, 0, -1, -2)] + [(2, d_) for d_ in range(8, -9, -1)]
            for j in range(4):
                P.op("sync", lambda e, j=j: e.dma_start(out=KT[:, 1024:1024 + T], in_=kT_d[j]), reads=kkeys, writes=["KT"], slot="ktb")
                for half in range(2):
                    P.op("sync", lambda e, j=j, half=half: e.dma_start(out=V[:, 8 + half * 16:8 + (half + 1) * 16, 0:dv], in_=v_r[:, half * 16:(half + 1) * 16, j * dv:(j + 1) * dv]),
                         reads=vkeys, writes=["V"], slot="vb%d" % half)
                for g in range(3):
                    P.op("sync", lambda e, g=g, j=j: e.dma_start(out=QT[:, g, :], in_=qT_d[g * 4 + j]), reads=qkeys, writes=["QT"], slot="qt%d" % g)
                P.op("sync", lambda e, j=j: e.dma_start(out=TB[:], in_=bt_d[j]), writes=["TB"], slot="tb")
                rds = ["KT", "V", "QT", "Vones"]
                for qt in range(NT):
                    ab = 2 + (qt % 4)
                    for i0 in range(0, 25, 4):
                        grp = list(range(i0, min(i0 + 4, 25)))
                        mms = []
                        pvs = []
                        for n_, idx in enumerate(grp):
                            g, dl = ents[idx]
                            eb = qt + 8 + dl
                            mms.append((KT[:, eb * 128:(eb + 1) * 128], QT[:, g, qt * 128:(qt + 1) * 128], n_ * 128, 128))
                            pvs.append((n_ * 128, V[:, eb, 0:dv + 1], ab, dv + 1, idx == 0, idx == 24))
                        self.attn_item(it, mms, TB[:, i0 * 128:(i0 + len(grp)) * 128], len(grp) * 128, pvs, "TB", rds)
                        it += 1
                    acc = self.banks[ab]
                    ak = ("bank", ab)
                    r = rr[fin % 4]
                    rk = ("rr", fin % 4)
                    on = ons[fin % 2]
                    onk = ("on", fin % 2)
                    fin += 1
                    P.op("vector", lambda e, acc=acc, r=r: e.reciprocal(out=r[:, 0:1], in_=acc[:, dv:dv + 1]), reads=[ak], writes=[rk])
                    P.op("vector", lambda e, acc=acc, r=r, on=on: e.tensor_scalar(out=on[:], in0=acc[:, 0:dv], scalar1=r[:, 0:1], scalar2=None, op0=ALU.mult),
                         reads=[ak, rk], writes=[onk])
                    self.out_transpose(on[:], onk, j, qt, fin)

    def phase_wo(self, li):
        P = self.P
        L = LAYERS[li]
        pfx = "l%d_" % li
        nfc = L["nfc"]
        W = self.din(pfx + "w_o", [nfc * 128, D], F32)
        scale = None
        srd = ()
        if L["kind"] == "B":
            sg = self.din(pfx + "sgcol", [128, 1], F32).ap()
            sgt = self.scr("sgt", [128, 1], F32)
            P.op("sync", lambda e: e.dma_start(out=sgt[:], in_=sg), writes=["sgt"], slot="sgt")
            P.op("vector", lambda e: e.tensor_scalar(out=sgt[:], in0=sgt[:], scalar1=1.0 - lambda_init_fn(li), scalar2=None, op0=ALU.mult), reads=["sgt"], writes=["sgt"])
            scale = lambda k: sgt[:, 0:1]
            srd = ["sgt"]
        kch = list(range(nfc))
        units = [self.load_unit(W, kch, [(n * 512, 512)], scale=scale, scale_reads=srd) for n in range(2)]
        pend = [self.x_load(0, 0), self.x_load(0, 1)]
        for t in range(NT):
            for n in range(2):
                nx = t * 2 + n + 2
                if nx < NT * 2:
                    pend.append(self.x_load(nx // 2, nx % 2))
                slot, wkeys = units[n]
                wb = self.wb[slot]
                bi = 2 + n
                py = self.banks[bi]
                for c in range(nfc):
                    P.op("tensor", lambda e, c=c, t=t, py=py, wb=wb: e.matmul(py[:, 0:512], lhsT=self.actT[:, c, 1 + t * 128:1 + (t + 1) * 128], rhs=wb[:, c, 0:512], start=(c == 0), stop=(c == nfc - 1)),
                         reads=[("actT", c, t)] + wkeys, writes=[("bank", bi)])
                xt, xk, xi = pend[t * 2 + n]
                P.op("vector", lambda e, xt=xt, py=py: e.tensor_tensor(out=xt[:, 0:512], in0=py[:, 0:512], in1=xt[:, 0:512], op=ALU.add),
                     reads=[("bank", bi), xk], writes=[xk])
                self.x_store(t, n, xt, xk, xi)

    def phase_ffn(self, li):
        P = self.P
        pfx = "l%d_" % li
        self.load_gcol(pfx + "gcol_ffn")
        self.phase_norm()
        self.scr_reset()
        Wu = self.din(pfx + "w_up", [D, 2 * DFF], F32)
        Wd = self.din(pfx + "w_down", [DFF, D], F32)
        cw_d = self.din(pfx + "convp", [128, 4, 44], F32).ap()
        cw = self.scr("cw", [128, 4, 44], F32)
        P.op("sync", lambda e: e.dma_start(out=cw[:], in_=cw_d), writes=["cw"], slot="cw")
        TC = 1024
        NK = T // TC
        a = self.scr("a", [128, 22, TC], BF16)
        Us = [[self.scr("U", [128, TC + 2], F32) for _ in range(2)] for _ in range(2)]
        T1 = [self.scr("T1", [128, TC], F32) for _ in range(2)]
        gsc = lambda k: self.gcol[:, k:k + 1]
        up_units = [[(f0 * 128, 256), (DFF + f0 * 128, 256)] for f0 in range(0, 22, 2)]
        dn_units = [(kc, n) for n in range(2) for kc in (list(range(0, 8)), list(range(8, 16)), list(range(16, 22)))]
        pcount = 0
        for k in range(NK):
            cb = TC * k
            allr = [("actT", c, t) for c in range(8) for t in range(max(8 * k - 1, 0), min(8 * k + 9, NT))] + ["halo"]
            nxt = self.load_unit(Wu, list(range(8)), up_units[0], scale=gsc, scale_reads=["gcol"])
            for ui in range(11):
                cur = nxt
                if ui + 1 < 11:
                    nxt = self.load_unit(Wu, list(range(8)), up_units[ui + 1], scale=gsc, scale_reads=["gcol"])
                slot, wkeys = cur
                wb = self.wb[slot]
                for pi in range(2):
                    f = ui * 2 + pi
                    par = pcount % 2
                    pcount += 1
                    for which in range(2):
                        off = which * 256 + pi * 128
                        fc = f + 22 * which
                        bA, bB, bE = self.banks[4 * which], self.banks[4 * which + 1], self.banks[4 * which + 2]
                        bks = [("bank", 4 * which + i) for i in range(3)]
                        for c in range(8):
                            P.op("tensor", lambda e, c=c, wb=wb, off=off, bA=bA, cb=cb: e.matmul(bA[:, 0:512], lhsT=wb[:, c, off:off + 128], rhs=self.actT[:, c, cb + 1:cb + 513], start=(c == 0), stop=(c == 7)),
                                 reads=allr + wkeys, writes=[bks[0]])
                        for c in range(8):
                            P.op("tensor", lambda e, c=c, wb=wb, off=off, bB=bB, cb=cb: e.matmul(bB[:, 0:512], lhsT=wb[:, c, off:off + 128], rhs=self.actT[:, c, cb + 513:cb + 1025], start=(c == 0), stop=(c == 7)),
                                 reads=allr + wkeys, writes=[bks[1]])
                        for c in range(8):
                            P.op("tensor", lambda e, c=c, wb=wb, off=off, bE=bE, cb=cb: e.matmul(bE[:, 0:2], lhsT=wb[:, c, off:off + 128], rhs=self.actT[:, c, cb:cb + 1026:1025], start=(c == 0), stop=(c == 7)),
                                 reads=allr + wkeys, writes=[bks[2]])
                        U = Us[which][par]
                        uk = ("U", which, par)
                        t1 = T1[which]
                        tk = ("T1", which)
                        P.op("scalar", lambda e, U=U, bA=bA: e.activation(out=U[:, 1:513], in_=bA[:, 0:512], func=AF.Copy), reads=[bks[0]], writes=[(uk, 0)])
                        P.op("scalar", lambda e, U=U, bB=bB: e.activation(out=U[:, 513:1025], in_=bB[:, 0:512], func=AF.Copy), reads=[bks[1]], writes=[(uk, 1)])
                        P.op("scalar", lambda e, U=U, bE=bE: e.activation(out=U[:, 0:1026:1025], in_=bE[:, 0:2], func=AF.Copy), reads=[bks[2]], writes=[(uk, 2)])
                        P.op("scalar", lambda e, t1=t1, bA=bA, fc=fc: e.activation(out=t1[:, 0:512], in_=bA[:, 0:512], func=AF.Identity, scale=cw[:, 1, fc:fc + 1], bias=cw[:, 3, fc:fc + 1]),
                             reads=[bks[0], "cw"], writes=[(tk, 0)])
                        P.op("scalar", lambda e, t1=t1, bB=bB, fc=fc: e.activation(out=t1[:, 512:1024], in_=bB[:, 0:512], func=AF.Identity, scale=cw[:, 1, fc:fc + 1], bias=cw[:, 3, fc:fc + 1]),
                             reads=[bks[1], "cw"], writes=[(tk, 1)])
                        P.op("vector", lambda e, t1=t1, U=U, fc=fc: e.scalar_tensor_tensor(out=t1[:], in0=U[:, 0:TC], scalar=cw[:, 0, fc:fc + 1], in1=t1[:], op0=ALU.mult, op1=ALU.add),
                             reads=[(uk, 0), (uk, 1), (uk, 2), (tk, 0), (tk, 1), "cw"], writes=[(tk, 0), (tk, 1)])
                        P.op("vector", lambda e, t1=t1, U=U, fc=fc: e.scalar_tensor_tensor(out=t1[:], in0=U[:, 2:TC + 2], scalar=cw[:, 2, fc:fc + 1], in1=t1[:], op0=ALU.mult, op1=ALU.add),
                             reads=[(uk, 0), (uk, 1), (uk, 2), (tk, 0), (tk, 1), "cw"], writes=[(tk, 0), (tk, 1)])
                    P.op("scalar", lambda e: e.activation(out=T1[0][:], in_=T1[0][:], func=AF.Silu), reads=[(("T1", 0), 0), (("T1", 0), 1)], writes=[(("T1", 0), 0), (("T1", 0), 1)])
                    P.op("vector", lambda e, f=f: e.tensor_tensor(out=a[:, f, :], in0=T1[0][:], in1=T1[1][:], op=ALU.mult),
                         reads=[(("T1", 0), 0), (("T1", 0), 1), (("T1", 1), 0), (("T1", 1), 1)], writes=[("a", f)])
            nxt = self.load_unit(Wd, dn_units[0][0], [(dn_units[0][1] * 512, 512)])
            for di, (kc, n) in enumerate(dn_units):
                cur = nxt
                if di + 1 < len(dn_units):
                    nxt = self.load_unit(Wd, dn_units[di + 1][0], [(dn_units[di + 1][1] * 512, 512)])
                slot, wkeys = cur
                wb = self.wb[slot]
                if kc[0] == 0:
                    pend = [self.x_load(8 * k + t, n) for t in range(2)]
                for t in range(8):
                    for ci, f in enumerate(kc):
                        P.op("tensor", lambda e, t=t, ci=ci, f=f, wb=wb: e.matmul(self.banks[t][:, 0:512], lhsT=a[:, f, t * 128:(t + 1) * 128], rhs=wb[:, ci, 0:512], start=(f == 0), stop=(f == 21)),
                             reads=[("a", f)] + wkeys, writes=[("bank", t)])
                if kc[-1] == 21:
                    for t in range(8):
                        tt = 8 * k + t
                        if t + 2 < 8:
                            pend.append(self.x_load(8 * k + t + 2, n))
                        xt, xk, xi = pend[t]
                        P.op("vector", lambda e, t=t, xt=xt: e.tensor_tensor(out=xt[:, 0:512], in0=self.banks[t][:, 0:512], in1=xt[:, 0:512], op=ALU.add),
                             reads=[("bank", t), xk], writes=[xk])
                        self.x_store(tt, n, xt, xk, xi)

    def finish(self):
        self.P.finalize()
        return self.nc


def rel_bucket_np(rel):
    nb = 16
    max_exact = 8
    n = np.abs(rel)
    nf = np.maximum(n, 1).astype(np.float32)
    large = max_exact + (np.log(nf / np.float32(max_exact)) / np.float32(math.log(1024 / max_exact)) * np.float32(nb - max_exact)).astype(np.int32)
    large = np.minimum(large, nb - 1)
    return np.where(rel > 0, nb, 0) + np.where(n < max_exact, n, large)


def bias_tables(rel_bias, kind):
    rb = np.asarray(rel_bias, np.float32)
    kp = np.arange(128)[:, None]
    qp = np.arange(128)[None, :]
    if kind == "A":
        out = np.empty((16, 128, 3 * 128), np.float32)
        for i, dl in enumerate((1, 0, -1)):
            rel = dl * 128 + kp - qp
            bk = rel_bucket_np(rel)
            ok = np.abs(rel) <= 128
            for h in range(16):
                out[h, :, i * 128:(i + 1) * 128] = np.where(ok, rb[bk, h], NEG)
        return out
    if kind == "B":
        out = np.empty((16, 128, 28 * 128), np.float32)
        for i in range(28):
            dl = 12 - i
            rel = dl * 128 + kp - qp
            bk = rel_bucket_np(rel)
            for m in range(16):
                out[m, :, i * 128:(i + 1) * 128] = rb[bk, m]
        return out
    ents = [(0, d_) for d_ in (1, 0, -1)] + [(1, d_) for d_ in (2, 1, 0, -1, -2)] + [(2, d_) for d_ in range(8, -9, -1)]
    dils = (1, 4, 16)
    out = np.empty((4, 128, 25 * 128), np.float32)
    for i, (g, dl) in enumerate(ents):
        rel = dl * 128 + kp - qp
        dil = dils[g]
        ok = (rel % dil == 0) & (np.abs(rel) <= 64 * dil)
        bk = rel_bucket_np(rel)
        for j in range(4):
            out[j, :, i * 128:(i + 1) * 128] = np.where(ok, rb[bk, g * 4 + j], NEG)
    return out


def gcols(g):
    return np.ascontiguousarray(np.asarray(g, np.float32).reshape(8, 128).T)


_PROG = None
DEBUG_LAYERS = 4


def get_prog():
    global _PROG
    if _PROG is None:
        b = Builder()
        for li in range(DEBUG_LAYERS):
            b.phase_qkv(li)
            b.phase_attn(li)
            b.phase_wo(li)
            b.phase_ffn(li)
        nc = b.finish()
        _PROG = (nc, list(b.din_names), list(b.dout_names))
    return _PROG


def kernel(x, rel_bias,
           l0_attn_norm, l0_w_qkv, l0_q_gain, l0_k_gain, l0_sink, l0_w_o,
           l0_ffn_norm, l0_w_up, l0_conv_w, l0_conv_b, l0_w_down,
           l1_attn_norm, l1_w_qkv, l1_q_gain, l1_k_gain, l1_lambda_q1, l1_lambda_k1,
           l1_lambda_q2, l1_lambda_k2, l1_sub_gain, l1_w_o,
           l1_ffn_norm, l1_w_up, l1_conv_w, l1_conv_b, l1_w_down,
           l2_attn_norm, l2_w_qkv, l2_q_gain, l2_k_gain, l2_w_o,
           l2_ffn_norm, l2_w_up, l2_conv_w, l2_conv_b, l2_w_down,
           l3_attn_norm, l3_w_qkv, l3_q_gain, l3_k_gain, l3_sink, l3_w_o,
           l3_ffn_norm, l3_w_up, l3_conv_w, l3_conv_b, l3_w_down):
    inp = dict(locals())
    x = np.ascontiguousarray(np.asarray(x, np.float32))
    rel_bias = np.asarray(rel_bias, np.float32)
    shared = {"ident": np.eye(128, dtype=np.float32)}
    for kind in ("A", "B", "C"):
        shared["bt_" + kind] = bias_tables(rel_bias, kind)
    f32 = lambda a: np.ascontiguousarray(np.asarray(a, np.float32))
    for li in range(4):
        L = LAYERS[li]
        p = "l%d_" % li
        shared[p + "gcol_attn"] = gcols(inp[p + "attn_norm"])
        shared[p + "gcol_ffn"] = gcols(inp[p + "ffn_norm"])
        for w in ("w_qkv", "w_o", "w_up", "w_down"):
            shared[p + w] = f32(inp[p + w])
        shared[p + "qkg"] = np.ascontiguousarray(np.stack([f32(inp[p + "q_gain"]), f32(inp[p + "k_gain"])]))
        cwv = f32(inp[p + "conv_w"]).reshape(3, 44, 128)
        cbv = f32(inp[p + "conv_b"]).reshape(1, 44, 128)
        shared[p + "convp"] = np.ascontiguousarray(np.concatenate([cwv, cbv], 0).transpose(2, 0, 1))
        if L["kind"] == "A":
            shared[p + "sink"] = f32(inp[p + "sink"])
        if L["kind"] == "B":
            shared[p + "lam"] = np.ascontiguousarray(np.stack([f32(inp[p + k]) for k in ("lambda_q1", "lambda_k1", "lambda_q2", "lambda_k2")]))
            shared[p + "sgcol"] = f32(inp[p + "sub_gain"]).reshape(128, 1)
    import time as _t
    _t1 = _t.time()
    nc, dins, douts = get_prog()
    print("[kernel] host prep + build took %.1fs" % (_t.time() - _t1), flush=True)
    in_maps = []
    for c in range(NCORES):
        m = {}
        for n in dins:
            m[n] = x[c] if n == "x" else shared[n]
        in_maps.append(m)
    import time as _t
    _t0 = _t.time()
    res = run_bass_kernel_spmd(nc, in_maps, core_ids=list(range(NCORES)))
    print("[kernel] run_bass_kernel_spmd took %.1fs" % (_t.time() - _t0), flush=True)
    return np.stack([res.results[c]["x_out"] for c in range(NCORES)]).astype(np.float32)
```

```python
import math
from contextlib import ExitStack

import numpy as np
import ml_dtypes

import concourse.bass as bass
import concourse.mybir as mybir
from concourse.ap import AP
from concourse.bass_utils import run_bass_kernel_spmd

F32 = mybir.dt.float32
BF16 = mybir.dt.bfloat16
AF = mybir.ActivationFunctionType
ALU = mybir.AluOpType
AX = mybir.AxisListType
NPBF = ml_dtypes.bfloat16

NCORES = 4
T = 4096
NT = 32
D = 1024
DFF = 2816
EPS = 1e-6
NEG = -30000.0

LAYERS = [
    dict(kind="A", F=1536, nqb=8, nkb=2, FV=256, nfc=8),
    dict(kind="B", F=3072, nqb=8, nkb=8, FV=1024, nfc=8),
    dict(kind="C", F=2560, nqb=12, nkb=4, FV=512, nfc=4),
    dict(kind="A", F=1536, nqb=8, nkb=2, FV=256, nfc=8),
]
NI = {"A": 3, "B": 28, "C": 25}
NUNIT_BT = {"A": 16, "B": 16, "C": 4}


def lambda_init_fn(layer):
    return 0.8 - 0.6 * math.exp(-0.3 * layer)


class Op:
    __slots__ = ("eng", "fn", "deps", "needs_inc", "val", "semkey", "is_dma", "idx")

    def __init__(self, eng, fn, deps, semkey, is_dma):
        self.eng = eng
        self.fn = fn
        self.deps = deps
        self.needs_inc = False
        self.val = None
        self.semkey = semkey
        self.is_dma = is_dma


class Prog:
    ENGS = ("sync", "scalar", "vector", "gpsimd", "tensor")

    def __init__(self, nc):
        self.nc = nc
        self.ops = {e: [] for e in self.ENGS}
        self.lastw = {}
        self.readers = {}
        self.es = ExitStack()
        self.outs = []
        self.fence = []
        self.fence_pending = set()
        self.last_dma = {}
        self.epoch = 0

    def barrier(self):
        fence = []
        for e in self.ENGS:
            for o in reversed(self.ops[e]):
                if not o.is_dma:
                    fence.append(o)
                    break
        fence.extend(self.last_dma.values())
        self.fence = fence
        self.fence_pending = set(self.ENGS)
        self.epoch += 1

    def op(self, eng, fn, reads=(), writes=(), slot=None, out=False):
        deps = []
        if eng in self.fence_pending:
            deps.extend(self.fence)
            self.fence_pending.discard(eng)
        for b in reads:
            w = self.lastw.get(b)
            if w is not None:
                deps.append(w)
        for b in writes:
            w = self.lastw.get(b)
            if w is not None:
                deps.append(w)
            deps.extend(self.readers.get(b, ()))
        is_dma = slot is not None
        semkey = ("dma", slot) if is_dma else ("eng", eng, self.epoch % 3)
        o = Op(eng, fn, deps, semkey, is_dma)
        o.idx = len(self.ops[eng])
        self.ops[eng].append(o)
        if is_dma:
            self.last_dma[slot] = o
        for b in writes:
            self.lastw[b] = o
            self.readers[b] = []
        for b in reads:
            self.readers.setdefault(b, []).append(o)
        if out:
            self.outs.append(o)
        return o

    @staticmethod
    def _skip(d, o):
        return d is o or (d.eng == "tensor" and o.eng == "tensor" and not d.is_dma and not o.is_dma)

    def finalize(self):
        nc = self.nc
        final_waits = self.outs
        for e in self.ENGS:
            for o in self.ops[e]:
                best = {}
                for d in o.deps:
                    if self._skip(d, o):
                        continue
                    if d.is_dma:
                        d.needs_inc = True
                        continue
                    b = best.get(d.semkey)
                    if b is None or d.idx > b.idx:
                        best[d.semkey] = d
                for d in best.values():
                    d.needs_inc = True
        for d in final_waits:
            d.needs_inc = True
        for e in self.ENGS:
            for o in self.ops[e]:
                if o.is_dma:
                    o.needs_inc = True
        counters = {}
        for e in self.ENGS:
            for o in self.ops[e]:
                if o.needs_inc:
                    c = counters.get(o.semkey, 0) + (16 if o.is_dma else 1)
                    counters[o.semkey] = c
                    o.val = c
        sems = {}
        for i, k in enumerate(counters):
            sems[k] = self.es.enter_context(nc.semaphore("s%d" % i))
        self.nsem = len(sems)
        block = self.es.enter_context(nc.Block())

        def run(e, engine):
            known = {}
            for o in self.ops[e]:
                need = {}
                for d in o.deps:
                    if self._skip(d, o) or d.val is None:
                        continue
                    if need.get(d.semkey, 0) < d.val:
                        need[d.semkey] = d.val
                for k, v in need.items():
                    if known.get(k, 0) >= v:
                        continue
                    engine.wait_ge(sems[k], v)
                    known[k] = v
                ins = o.fn(engine)
                if o.needs_inc:
                    ins.then_inc(sems[o.semkey], 16 if o.is_dma else 1)
            if e == "sync":
                need = {}
                for d in final_waits:
                    if need.get(d.semkey, 0) < d.val:
                        need[d.semkey] = d.val
                for k, v in need.items():
                    engine.wait_ge(sems[k], v)

        @block.sync
        def _(eng):
            run("sync", eng)

        @block.scalar
        def _(eng):
            run("scalar", eng)

        @block.vector
        def _(eng):
            run("vector", eng)

        @block.gpsimd
        def _(eng):
            run("gpsimd", eng)

        @block.tensor
        def _(eng):
            run("tensor", eng)

        self.es.close()


SB_BASE = 16512
SCR_END = 229376


class Builder:
    def __init__(self):
        self.nc = bass.Bass("TRN2", target_bir_lowering=False)
        self.P = Prog(self.nc)
        self.din_names = []
        self.dout_names = []
        self.d = {}
        self.uid = 0
        self.perm_off = SB_BASE
        nc = self.nc
        self.actT = self.perm("actT", [128, 8, T + 2], BF16)
        self.ws = [self.perm("ws%d" % i, [128, 4, 512], F32) for i in range(2)]
        self.wb = [self.perm("wb%d" % i, [128, 8, 512], BF16) for i in range(2)]
        self.xts = [self.perm("xt%d" % i, [128, D], F32) for i in range(4)]
        self.hbs = [self.perm("hb%d" % i, [128, D], BF16) for i in range(2)]
        self.ident = self.perm("ident", [128, 128], BF16)
        self.identf = self.perm("identf", [128, 128], F32)
        self.sst = [self.perm("sst%d" % i, [128, 4], F32) for i in range(4)]
        self.epsb = self.perm("epsb", [128, 1], F32)
        self.gcol = self.perm("gcol", [128, 8], F32)
        self.PERM_END = (self.perm_off + 63) // 64 * 64
        self.scr_off = self.PERM_END
        self.nscr = 0
        self.xcnt = 0
        self.banks = [nc.alloc_psum_tensor("bank%d" % i, [128, 512], F32) for i in range(8)]
        self.wcount = 0
        self.xd = self.dout("x_out", [T, D], F32).ap()
        self.qT_d = nc.dram_tensor("qT_scr", [12, 128, T], BF16, kind="Internal").ap()
        self.kT_d = nc.dram_tensor("kT_scr", [8, 128, T], BF16, kind="Internal").ap()
        self.v_d = nc.dram_tensor("v_scr", [T, 1024], BF16, kind="Internal").ap()
        self.wu_bf = nc.dram_tensor("wu_bf", [D, 2 * DFF], BF16, kind="Internal").ap()
        self.wd_bf = nc.dram_tensor("wd_bf", [DFF, D], BF16, kind="Internal").ap()
        self.init_consts()

    def perm(self, name, shape, dt):
        n = int(np.prod(shape[1:])) * (4 if dt == F32 else 2)
        n = (n + 31) // 32 * 32
        t = self.nc.alloc_sbuf_tensor_at(name, shape, dt, offset=self.perm_off)
        self.perm_off += n
        return t

    def scr_reset(self):
        if self.nscr > 0:
            self.P.barrier()
        self.nscr += 1
        self.scr_off = self.PERM_END

    def scr(self, name, shape, dt):
        n = int(np.prod(shape[1:])) * (4 if dt == F32 else 2)
        n = (n + 31) // 32 * 32
        self.uid += 1
        t = self.nc.alloc_sbuf_tensor_at("%s_%d" % (name, self.uid), shape, dt, offset=self.scr_off)
        self.scr_off += n
        assert self.scr_off <= SCR_END, (name, self.scr_off)
        return t

    def din(self, name, shape, dt):
        if name not in self.d:
            self.d[name] = self.nc.dram_tensor(name, list(shape), dt, kind="ExternalInput")
            self.din_names.append(name)
        return self.d[name]

    def dout(self, name, shape, dt):
        if name not in self.d:
            self.d[name] = self.nc.dram_tensor(name, list(shape), dt, kind="ExternalOutput")
            self.dout_names.append(name)
        return self.d[name]

    def bank_bf(self, i):
        return self.banks[i][:].bitcast(BF16).rearrange("p (c t) -> p c t", t=128)

    def init_consts(self):
        P = self.P
        idd = self.din("ident", [128, 128], F32).ap()
        xin = self.din("x", [T, D], F32).ap()
        P.op("sync", lambda e: e.dma_start(out=self.identf[:], in_=idd), writes=["identf"], slot="c_id")
        P.op("vector", lambda e: e.tensor_copy(out=self.ident[:], in_=self.identf[:]), reads=["identf"], writes=["ident"])
        P.op("vector", lambda e: e.memset(self.epsb[:], EPS), writes=["eps"])
        P.op("vector", lambda e: e.memset(self.actT[:, :, 0:1], 0.0), writes=["halo"])
        P.op("vector", lambda e: e.memset(self.actT[:, :, T + 1:T + 2], 0.0), writes=["halo"])
        for t0 in range(0, NT, 8):
            P.op("sync", lambda e, t0=t0: e.dma_start(out=self.xd[t0 * 128:(t0 + 8) * 128, :], in_=xin[t0 * 128:(t0 + 8) * 128, :]),
                 writes=[("xd", t, n) for t in range(t0, t0 + 8) for n in range(2)], slot="xcp%d" % (t0 // 8), out=True)

    def x_load(self, t, n=None):
        P = self.P
        i = self.xcnt % 4
        self.xcnt += 1
        xt = self.xts[i]
        key = ("xt", i)
        if n is None:
            P.op("sync", lambda e, t=t, xt=xt: e.dma_start(out=xt[:], in_=self.xd[t * 128:(t + 1) * 128, :]),
                 reads=[("xd", t, 0), ("xd", t, 1)], writes=[key], slot="xl%d" % i)
        else:
            P.op("sync", lambda e, t=t, xt=xt, n=n: e.dma_start(out=xt[:, 0:512], in_=self.xd[t * 128:(t + 1) * 128, n * 512:(n + 1) * 512]),
                 reads=[("xd", t, n)], writes=[key], slot="xl%d" % i)
        return xt, key, i

    def x_store(self, t, n, xt, key, i):
        P = self.P
        P.op("gpsimd", lambda e, t=t, xt=xt, n=n: e.dma_start(out=self.xd[t * 128:(t + 1) * 128, n * 512:(n + 1) * 512], in_=xt[:, 0:512]),
             reads=[key], writes=[("xd", t, n)], slot="xs%d" % i, out=True)

    def load_unit(self, W, kchunks, segs, scale=None, scale_reads=()):
        P = self.P
        i = self.wcount
        self.wcount += 1
        slot = i % 2
        wb = self.wb[slot]
        key = ("wb", slot)
        Wa = W.ap()
        ncol = sum(s[1] for s in segs)
        halves = [kchunks[0:4], kchunks[4:8]]
        allkeys = []
        for hi, kc in enumerate(halves):
            if not kc:
                continue
            ws = self.ws[hi]
            co = 0
            for si, (c0, cn) in enumerate(segs):
                k0 = kc[0]
                src = Wa[k0 * 128:(k0 + len(kc)) * 128, c0:c0 + cn].rearrange("(c p) f -> p c f", p=128)
                P.op("sync", lambda e, ws=ws, src=src, co=co, cn=cn, n=len(kc): e.dma_start(out=ws[:, 0:n, co:co + cn], in_=src),
                     writes=[("ws", hi, si)], slot="ws%d_%d" % (hi, si))
                co += cn
            n = len(kc)
            if scale is None:
                P.op("gpsimd", lambda e, ws=ws, wb=wb, hi=hi, n=n, ncol=ncol: e.tensor_copy(out=wb[:, hi * 4:hi * 4 + n, 0:ncol], in_=ws[:, 0:n, 0:ncol]),
                     reads=[("ws", hi, si) for si in range(len(segs))], writes=[(key, hi, 0)])
                allkeys.append((key, hi, 0))
            else:
                for j, k in enumerate(kc):
                    sc = scale(k)
                    P.op("gpsimd", lambda e, ws=ws, wb=wb, hi=hi, j=j, ncol=ncol, sc=sc: e.tensor_scalar(out=wb[:, hi * 4 + j, 0:ncol], in0=ws[:, j, 0:ncol], scalar1=sc, scalar2=None, op0=ALU.mult),
                         reads=[("ws", hi, si) for si in range(len(segs))] + list(scale_reads), writes=[(key, hi, j)])
                    allkeys.append((key, hi, j))
        return slot, allkeys

    def precast(self, W, nchunks, ncols, Wbf, keyname, scaled):
        P = self.P
        Wa = W.ap()
        blocks = [(kg, min(4, nchunks - kg), c0) for kg in range(0, nchunks, 4) for c0 in range(0, ncols, 512)]
        keys = []
        for b, (kg, n, c0) in enumerate(blocks):
            ws = self.ws[b % 2]
            wsk = ("ws", b % 2, 0)
            ss_ = b % 4
            stg = self.wb[ss_ // 2][:, (ss_ % 2) * 4:(ss_ % 2) * 4 + n, :]
            stgk = (("wb", ss_ // 2), ss_ % 2, 0)
            src = Wa[kg * 128:(kg + n) * 128, c0:c0 + 512].rearrange("(c p) f -> p c f", p=128)
            P.op("sync", lambda e, ws=ws, src=src, n=n: e.dma_start(out=ws[:, 0:n, :], in_=src), writes=[wsk], slot="ws%d_0" % (b % 2))
            eng = "scalar" if b % 2 == 0 else "vector"
            if not scaled:
                if eng == "scalar":
                    P.op(eng, lambda e, ws=ws, stg=stg, n=n: e.activation(out=stg, in_=ws[:, 0:n, :], func=AF.Copy), reads=[wsk], writes=[stgk])
                else:
                    P.op(eng, lambda e, ws=ws, stg=stg, n=n: e.tensor_copy(out=stg, in_=ws[:, 0:n, :]), reads=[wsk], writes=[stgk])
            else:
                for j in range(n):
                    sc = self.gcol[:, kg + j:kg + j + 1]
                    if eng == "scalar":
                        P.op(eng, lambda e, ws=ws, stg=stg, j=j, sc=sc: e.activation(out=stg[:, j, :], in_=ws[:, j, :], func=AF.Copy, scale=sc), reads=[wsk, "gcol"], writes=[(stgk, j)] if False else [stgk])
                    else:
                        P.op(eng, lambda e, ws=ws, stg=stg, j=j, sc=sc: e.tensor_scalar(out=stg[:, j, :], in0=ws[:, j, :], scalar1=sc, scalar2=None, op0=ALU.mult), reads=[wsk, "gcol"], writes=[stgk])
            dst = Wbf[kg * 128:(kg + n) * 128, c0:c0 + 512].rearrange("(c p) f -> p c f", p=128)
            P.op("gpsimd", lambda e, stg=stg, dst=dst: e.dma_start(out=dst, in_=stg), reads=[stgk], writes=[(keyname, b)], slot="pc%d" % ss_)
            keys.append((keyname, b))
        return keys

    def load_unit_bf(self, Wbf, kchunks, segs, rkeys):
        P = self.P
        i = self.wcount
        self.wcount += 1
        slot = i % 2
        wb = self.wb[slot]
        keys = [(("wb", slot), 0, 0), (("wb", slot), 1, 0)]
        k0, n, co = kchunks[0], len(kchunks), 0
        for si, (c0, cn) in enumerate(segs):
            src = Wbf[k0 * 128:(k0 + n) * 128, c0:c0 + cn].rearrange("(c p) f -> p c f", p=128)
            P.op("sync", lambda e, wb=wb, src=src, n=n, co=co, cn=cn: e.dma_start(out=wb[:, 0:n, co:co + cn], in_=src),
                 reads=rkeys, writes=keys, slot="wbd%d_%d" % (slot, si))
            co += cn
        return slot, keys

    def phase_norm(self):
        P = self.P
        junk = self.banks[7]
        hbs = self.hbs
        pend = [self.x_load(0), self.x_load(1)]
        for t in range(NT):
            if t + 2 < NT:
                pend.append(self.x_load(t + 2))
            xt, xk, _ = pend[t]
            st = self.sst[t % 4]
            sk = ("sst", t % 4)
            P.op("vector", lambda e, st=st: e.memset(st[:], 0.0), writes=[sk])
            P.op("scalar", lambda e, xt=xt, st=st: e.activation(out=junk[:, 0:512], in_=xt[:, 0:512], func=AF.Square, accum_out=st[:, 0:1]),
                 reads=[xk, sk], writes=[sk, ("bank", 7)])
            P.op("scalar", lambda e, xt=xt, st=st: e.activation(out=junk[:, 0:512], in_=xt[:, 512:1024], func=AF.Square, accum_out=st[:, 1:2]),
                 reads=[xk, sk], writes=[sk, ("bank", 7)])
            P.op("vector", lambda e, st=st: e.tensor_tensor(out=st[:, 2:3], in0=st[:, 0:1], in1=st[:, 1:2], op=ALU.add), reads=[sk], writes=[sk])
            P.op("scalar", lambda e, st=st: e.activation(out=st[:, 2:3], in_=st[:, 2:3], func=AF.Ln, scale=1.0 / D, bias=self.epsb[:, 0:1]),
                 reads=[sk, "eps"], writes=[sk])
            P.op("scalar", lambda e, st=st: e.activation(out=st[:, 3:4], in_=st[:, 2:3], func=AF.Exp, scale=-0.5), reads=[sk], writes=[sk])
            hb = hbs[t % 2]
            hk = ("hb", t % 2)
            pk = ("bank", t % 2)
            pT = self.bank_bf(t % 2)
            P.op("vector", lambda e, xt=xt, hb=hb, st=st: e.tensor_scalar(out=hb[:], in0=xt[:], scalar1=st[:, 3:4], scalar2=None, op0=ALU.mult),
                 reads=[xk, sk], writes=[hk])
            for c in range(8):
                P.op("tensor", lambda e, c=c, hb=hb, pT=pT: e.transpose(out=pT[:, c, :], in_=hb[:, c * 128:(c + 1) * 128], identity=self.ident[:]),
                     reads=[hk, "ident"], writes=[pk])
            P.op("scalar", lambda e, t=t, pT=pT: e.activation(out=self.actT[:, :, 1 + t * 128:1 + (t + 1) * 128], in_=pT, func=AF.Copy),
                 reads=[pk], writes=[("actT", c, t) for c in range(8)])

    def load_gcol(self, name):
        P = self.P
        g = self.din(name, [128, 8], F32).ap()
        P.op("sync", lambda e: e.dma_start(out=self.gcol[:], in_=g), writes=["gcol"], slot="gcol")

    def phase_qkv(self, li):
        P = self.P
        L = LAYERS[li]
        kind = L["kind"]
        pfx = "l%d_" % li
        self.load_gcol(pfx + "gcol_attn")
        self.phase_norm()
        self.scr_reset()
        W = self.din(pfx + "w_qkv", [D, L["F"]], F32)
        dh = 128 if kind == "C" else 64
        geff_d = self.din(pfx + "qkg", [2, dh], F32).ap()
        qT_d, kT_d, v_d = self.qT_d, self.kT_d, self.v_d
        gq = self.scr("gq", [128, dh], F32)
        gk = self.scr("gk", [128, dh], F32)
        P.op("sync", lambda e: e.dma_start(out=gq[:], in_=geff_d[0].partition_broadcast(128)), writes=["gq"], slot="gq")
        P.op("sync", lambda e: e.dma_start(out=gk[:], in_=geff_d[1].partition_broadcast(128)), writes=["gk"], slot="gk")
        P.op("vector", lambda e: e.scalar_tensor_tensor(out=gk[:], in0=gq[:], scalar=float(dh) ** -0.5, in1=gk[:], op0=ALU.mult, op1=ALU.mult),
             reads=["gq", "gk"], writes=["gk"])
        stages = [self.scr("stage", [128, 4, T], BF16) for _ in range(2)]
        sqs = [self.scr("sq", [128, 512], F32) for _ in range(2)]
        kfs = [self.scr("kf", [128, 512], F32) for _ in range(2)]
        qns = [self.scr("qn", [128, 512], BF16) for _ in range(2)]
        vsts = [self.scr("vst", [128, 512], BF16) for _ in range(2)]
        ssq = [self.scr("ssq", [128, 8], F32) for _ in range(2)]
        if kind == "A":
            chunks = [[("q", 0, 512, 0)], [("q", 0, 512, 4)], [("k", 0, 256, 0), ("v", 256, 256, 0)]]
        elif kind == "B":
            chunks = [[("q", 0, 512, 0)], [("q", 0, 512, 4)], [("k", 0, 512, 0)], [("k", 0, 512, 4)], [("v", 0, 512, 0)], [("v", 0, 512, 512)]]
        else:
            chunks = [[("q", 0, 512, 0)], [("q", 0, 512, 4)], [("q", 0, 512, 8)], [("k", 0, 512, 0)], [("v", 0, 512, 0)]]
        gsc = lambda k: self.gcol[:, k:k + 1]
        units = [None] * len(chunks)
        units[0] = self.load_unit(W, list(range(8)), [(0, 512)], scale=gsc, scale_reads=["gcol"])
        it = 0
        pend_rest = [None]
        for ci, segs in enumerate(chunks):
            if ci + 1 < len(chunks):
                units[ci + 1] = self.load_unit(W, list(range(8)), [((ci + 1) * 512, 512)], scale=gsc, scale_reads=["gcol"])
            slot, wkeys = units[ci]
            wb = self.wb[slot]
            stage = stages[ci % 2]
            stk = ("stage", ci % 2)
            for t in range(NT):
                bi = 2 + (it % 2)
                py = self.banks[bi]
                pyk = ("bank", bi)
                for c in range(8):
                    P.op("tensor", lambda e, c=c, t=t, py=py, wb=wb: e.matmul(py[:, 0:512], lhsT=self.actT[:, c, 1 + t * 128:1 + (t + 1) * 128], rhs=wb[:, c, 0:512], start=(c == 0), stop=(c == 7)),
                         reads=[("actT", c, t)] + wkeys, writes=[pyk])
                def rest(segs=segs, it=it, t=t, py=py, pyk=pyk, stage=stage, stk=stk):
                    for (ty, off, w, dst) in segs:
                        if ty == "v":
                            vst = vsts[it % 2]
                            vk = ("vst", it % 2)
                            P.op("scalar", lambda e, py=py, vst=vst, off=off, w=w: e.activation(out=vst[:, 0:w], in_=py[:, off:off + w], func=AF.Copy),
                                 reads=[pyk], writes=[vk])
                            P.op("gpsimd", lambda e, vst=vst, t=t, dst=dst, w=w: e.dma_start(out=v_d[t * 128:(t + 1) * 128, dst:dst + w], in_=vst[:, 0:w]),
                                 reads=[vk], writes=[("vd", t, dst)], slot="vst%d" % (it % 2))
                            continue
                        nh = w // dh
                        sq = sqs[it % 2]
                        sqk = ("sq", it % 2)
                        s_ = ssq[it % 2]
                        sk = ("ssq", it % 2)
                        qn = qns[it % 2]
                        qk = ("qn", it % 2)
                        P.op("scalar", lambda e, py=py, sq=sq, off=off, w=w: e.activation(out=sq[:, 0:w], in_=py[:, off:off + w], func=AF.Square),
                             reads=[pyk], writes=[sqk])
                        P.op("vector", lambda e, sq=sq, s_=s_, w=w, nh=nh: e.tensor_reduce(out=s_[:, 0:nh], in_=sq[:, 0:w].rearrange("p (h d) -> p h d", d=dh), axis=AX.X, op=ALU.add),
                             reads=[sqk], writes=[sk])
                        P.op("scalar", lambda e, s_=s_, nh=nh: e.activation(out=s_[:, 0:nh], in_=s_[:, 0:nh], func=AF.Ln, scale=1.0 / dh, bias=self.epsb[:, 0:1]),
                             reads=[sk, "eps"], writes=[sk])
                        P.op("scalar", lambda e, s_=s_, nh=nh: e.activation(out=s_[:, 0:nh], in_=s_[:, 0:nh], func=AF.Exp, scale=-0.5), reads=[sk], writes=[sk])
                        rb = AP(s_, 0, [[8, 128], [1, nh], [0, dh]])
                        if ty == "q":
                            P.op("vector", lambda e, py=py, qn=qn, off=off, w=w, rb=rb: e.tensor_tensor(out=qn[:, 0:w].rearrange("p (h d) -> p h d", d=dh), in0=py[:, off:off + w].rearrange("p (h d) -> p h d", d=dh), in1=rb, op=ALU.mult),
                                 reads=[pyk, sk], writes=[qk])
                        else:
                            kf = kfs[it % 2]
                            kfk = ("kf", it % 2)
                            gb = AP(gk, 0, [[dh, 128], [0, nh], [1, dh]])
                            P.op("vector", lambda e, py=py, kf=kf, off=off, w=w, rb=rb: e.tensor_tensor(out=kf[:, 0:w].rearrange("p (h d) -> p h d", d=dh), in0=py[:, off:off + w].rearrange("p (h d) -> p h d", d=dh), in1=rb, op=ALU.mult),
                                 reads=[pyk, sk], writes=[kfk])
                            P.op("gpsimd", lambda e, kf=kf, qn=qn, w=w, gb=gb: e.tensor_tensor(out=qn[:, 0:w].rearrange("p (h d) -> p h d", d=dh), in0=kf[:, 0:w].rearrange("p (h d) -> p h d", d=dh), in1=gb, op=ALU.mult),
                                 reads=[kfk, "gk"], writes=[qk])
                        nb = w // 128
                        tb = it % 2
                        pT = self.bank_bf(tb)
                        pk = ("bank", tb)
                        for j in range(nb):
                            P.op("tensor", lambda e, j=j, qn=qn, pT=pT: e.transpose(out=pT[:, j, :], in_=qn[:, j * 128:(j + 1) * 128], identity=self.ident[:]),
                                 reads=[qk, "ident"], writes=[pk])
                        P.op("scalar", lambda e, pT=pT, stage=stage, nb=nb, t=t: e.activation(out=stage[:, 0:nb, t * 128:(t + 1) * 128], in_=pT[:, 0:nb, :], func=AF.Copy),
                             reads=[pk], writes=[(stk, t)])
                if pend_rest[0] is not None:
                    pend_rest[0]()
                pend_rest[0] = rest
                it += 1
            if pend_rest[0] is not None:
                pend_rest[0]()
                pend_rest[0] = None
            for (ty, off, w, dst) in segs:
                if ty == "v":
                    continue
                dd = qT_d if ty == "q" else kT_d
                dk = "qTd" if ty == "q" else "kTd"
                for j in range(w // 128):
                    P.op("gpsimd", lambda e, dd=dd, j=j, dst=dst, stage=stage: e.dma_start(out=dd[dst + j], in_=stage[:, j, :]),
                         reads=[(stk, t) for t in range(NT)], writes=[(dk, dst + j)], slot="stg%d_%d" % (ci % 2, j))

    def attn_item(self, it, mms, tb_ap, ncols, pvs, tbkey, extra_reads, post=None):
        P = self.P
        bi = it % 2
        st = self.banks[bi]
        stk = ("bank", bi)
        sc = self.a_sc[it % 2]
        sck = ("sc", it % 2)
        pt = self.a_pt[it % 2]
        ptk = ("pt", it % 2)
        for (lh, rh, c0, n) in mms:
            P.op("tensor", lambda e, lh=lh, rh=rh, c0=c0, n=n, st=st: e.matmul(st[:, c0:c0 + n], lhsT=lh, rhs=rh, start=True, stop=True),
                 reads=extra_reads, writes=[stk])
        P.op("vector", lambda e, st=st, sc=sc, tb_ap=tb_ap, ncols=ncols: e.tensor_tensor(out=sc[:, 0:ncols], in0=st[:, 0:ncols], in1=tb_ap, op=ALU.add),
             reads=[stk, tbkey], writes=[sck])
        P.op("scalar", lambda e, sc=sc, pt=pt, ncols=ncols: e.activation(out=pt[:, 0:ncols], in_=sc[:, 0:ncols], func=AF.Exp),
             reads=[sck], writes=[ptk])
        self.attn_flush()
        self.a_pending = (pt, ptk, pvs, list(extra_reads), post)

    def attn_flush(self):
        P = self.P
        pend = getattr(self, "a_pending", None)
        if pend is None:
            return
        self.a_pending = None
        pt, ptk, pvs, extra_reads, post = pend
        for (c0, v_ap, ab, wdt, s0, s1) in pvs:
            acc = self.banks[ab]
            P.op("tensor", lambda e, c0=c0, v_ap=v_ap, acc=acc, wdt=wdt, s0=s0, s1=s1, pt=pt: e.matmul(acc[:, 0:wdt], lhsT=pt[:, c0:c0 + 128], rhs=v_ap, start=s0, stop=s1),
                 reads=[ptk] + extra_reads, writes=[("bank", ab)])
        if post is not None:
            post()

    def out_transpose(self, on, onk, chunk, qt, trk):
        P = self.P
        bi = 6 + (trk % 2)
        pT = self.bank_bf(bi)
        P.op("tensor", lambda e, on=on, pT=pT: e.transpose(out=pT[:, 0, :], in_=on, identity=self.ident[:]),
             reads=[onk, "ident"], writes=[("bank", bi)])
        P.op("scalar", lambda e, pT=pT, chunk=chunk, qt=qt: e.activation(out=self.actT[:, chunk, 1 + qt * 128:1 + (qt + 1) * 128], in_=pT[:, 0, :], func=AF.Copy),
             reads=[("bank", bi)], writes=[("actT", chunk, qt)])

    def phase_attn(self, li):
        P = self.P
        L = LAYERS[li]
        kind = L["kind"]
        pfx = "l%d_" % li
        self.scr_reset()
        nkb, nqb, FV = L["nkb"], L["nqb"], L["FV"]
        qT_d, kT_d, v_d = self.qT_d, self.kT_d, self.v_d
        ni = NI[kind]
        bt_d = self.din("bt_" + kind, [NUNIT_BT[kind], 128, ni * 128], F32).ap()
        self.a_sc = [self.scr("sc", [128, 512], F32) for _ in range(2)]
        self.a_pt = [self.scr("pt", [128, 512], BF16) for _ in range(2)]
        rr = [self.scr("rr", [128, 4], F32) for _ in range(4)]
        ons = [self.scr("on", [128, 128], BF16) for _ in range(2)]
        v_r = v_d.rearrange("(kb p) f -> p kb f", p=128)
        qkeys = [("qTd", b) for b in range(nqb)]
        kkeys = [("kTd", b) for b in range(nkb)]
        vkeys = [("vd", t, c0) for t in range(NT) for c0 in range(0, FV, 512 if FV >= 512 else 256)]
        it = 0
        fin = 0
        if kind == "B":
            dv = 128
            KTs = [self.scr("KT", [128, T], BF16) for _ in range(2)]
            Vs = [self.scr("V", [128, NT, dv + 1], BF16) for _ in range(2)]
            QT = self.scr("QT", [128, T], BF16)
            TB = self.scr("TB", [128, ni * 128], F32)
            o1 = self.scr("o1", [128, NT, 128], F32)
            ods = [self.scr("od", [128, 128], F32) for _ in range(2)]
            lam = self.hbs[0][:].bitcast(F32)[:, 0:256].rearrange("p (a d) -> p a d", d=64)
            lamv = self.scr("lamv", [128, 4], F32)
            lam_d = self.din(pfx + "lam", [4, 64], F32).ap()
            P.op("sync", lambda e: e.dma_start(out=lam, in_=AP(lam_d.tensor, 0, [[0, 128], [64, 4], [1, 64]])), writes=["lam", ("hb", 0)], slot="lam")
            P.op("vector", lambda e: e.tensor_tensor(out=lam[:, 0, :], in0=lam[:, 0, :], in1=lam[:, 1, :], op=ALU.mult), reads=["lam"], writes=["lam"])
            P.op("vector", lambda e: e.tensor_tensor(out=lam[:, 2, :], in0=lam[:, 2, :], in1=lam[:, 3, :], op=ALU.mult), reads=["lam"], writes=["lam"])
            P.op("vector", lambda e: e.tensor_reduce(out=lamv[:, 0:1], in_=lam[:, 0, :], axis=AX.X, op=ALU.add), reads=["lam"], writes=["lamv"])
            P.op("vector", lambda e: e.tensor_reduce(out=lamv[:, 1:2], in_=lam[:, 2, :], axis=AX.X, op=ALU.add), reads=["lam"], writes=["lamv", ("hb", 0)])
            P.op("scalar", lambda e: e.activation(out=lamv[:, 0:2], in_=lamv[:, 0:2], func=AF.Exp), reads=["lamv"], writes=["lamv"])
            P.op("vector", lambda e: e.scalar_tensor_tensor(out=lamv[:, 2:3], in0=lamv[:, 1:2], scalar=-lambda_init_fn(li), in1=lamv[:, 0:1], op0=ALU.add, op1=ALU.subtract),
                 reads=["lamv"], writes=["lamv"])
            neglam = lamv[:, 2:3]
            for vb in Vs:
                P.op("gpsimd", lambda e, vb=vb: e.memset(vb[:, :, dv:dv + 1], 1.0), writes=[("Vones", id(vb))])

            def load_head(h):
                s = h % 2
                P.op("sync", lambda e, h=h, s=s: e.dma_start(out=KTs[s][:], in_=kT_d[h]), reads=kkeys, writes=[("KT", s)], slot="kt%da" % s)
                for half in range(2):
                    P.op("sync", lambda e, h=h, s=s, half=half: e.dma_start(out=Vs[s][:, half * 16:(half + 1) * 16, 0:dv], in_=v_r[:, half * 16:(half + 1) * 16, h * dv:(h + 1) * dv]),
                         reads=vkeys, writes=[("V", s)], slot="v%d_%d" % (s, half))

            fin_box = [0]

            def fin_B(qc, j, h):
                for jq in range(4):
                    qt = qc * 4 + jq
                    acc = self.banks[2 + jq]
                    ak = ("bank", 2 + jq)
                    fin_box[0] += 1
                    fin = fin_box[0]
                    r = rr[fin % 4]
                    rk = ("rr", fin % 4)
                    P.op("vector", lambda e, acc=acc, r=r: e.reciprocal(out=r[:, 0:1], in_=acc[:, dv:dv + 1]), reads=[ak], writes=[rk])
                    if j == 0:
                        P.op("vector", lambda e, acc=acc, r=r, qt=qt: e.tensor_scalar(out=o1[:, qt, :], in0=acc[:, 0:dv], scalar1=r[:, 0:1], scalar2=None, op0=ALU.mult),
                             reads=[ak, rk], writes=[("o1", qt)])
                    else:
                        od = ods[fin % 2]
                        odk = ("od", fin % 2)
                        on = ons[fin % 2]
                        onk = ("on", fin % 2)
                        P.op("vector", lambda e, r=r: e.tensor_scalar(out=r[:, 1:2], in0=r[:, 0:1], scalar1=neglam, scalar2=None, op0=ALU.mult),
                             reads=[rk, "lamv"], writes=[rk])
                        P.op("vector", lambda e, acc=acc, r=r, qt=qt, od=od: e.scalar_tensor_tensor(out=od[:], in0=acc[:, 0:dv], scalar=r[:, 1:2], in1=o1[:, qt, :], op0=ALU.mult, op1=ALU.add),
                             reads=[ak, rk, ("o1", qt)], writes=[odk])
                        P.op("vector", lambda e, r=r: e.memset(r[:, 2:3], 0.0), reads=[], writes=[rk])
                        P.op("scalar", lambda e, od=od, on=on, r=r: e.activation(out=on[:], in_=od[:], func=AF.Square, accum_out=r[:, 2:3]),
                             reads=[odk, rk], writes=[rk, onk])
                        P.op("scalar", lambda e, r=r: e.activation(out=r[:, 2:3], in_=r[:, 2:3], func=AF.Ln, scale=1.0 / 128, bias=self.epsb[:, 0:1]),
                             reads=[rk, "eps"], writes=[rk])
                        P.op("scalar", lambda e, r=r: e.activation(out=r[:, 2:3], in_=r[:, 2:3], func=AF.Exp, scale=-0.5), reads=[rk], writes=[rk])
                        P.op("vector", lambda e, od=od, on=on, r=r: e.tensor_scalar(out=on[:], in0=od[:], scalar1=r[:, 2:3], scalar2=None, op0=ALU.mult),
                             reads=[odk, rk], writes=[onk])
                        self.out_transpose(on[:], onk, h, qt, fin)

            load_head(0)
            for h in range(8):
                s = h % 2
                self.attn_flush()
                if h + 1 < 8:
                    load_head(h + 1)
                P.op("sync", lambda e, h=h: e.dma_start(out=QT[:], in_=qT_d[h]), reads=qkeys, writes=["QT"], slot="qt")
                KT, V = KTs[s], Vs[s]
                rds = [("KT", s), ("V", s), "QT", ("Vones", id(V))]
                for j in range(2):
                    for half in range(2):
                        hw = ni * 64
                        P.op("sync", lambda e, h=h, j=j, half=half, hw=hw: e.dma_start(out=TB[:, half * hw:(half + 1) * hw], in_=bt_d[h * 2 + j, :, half * hw:(half + 1) * hw]),
                             writes=["TB"], slot="tb%d" % half)
                    for qc in range(NT // 4):
                        for kb in range(NT):
                            dp = min(max(kb - 4 * qc, -12), 12)
                            i0 = 12 - dp
                            mms = [(KT[64 * j:64 * j + 64, kb * 128:(kb + 1) * 128], QT[64 * j:64 * j + 64, qc * 512:(qc + 1) * 512], 0, 512)]
                            pvs = [(jq * 128, V[:, kb, 0:dv + 1], 2 + jq, dv + 1, kb == 0, kb == NT - 1) for jq in range(4)]
                            post = None
                            if kb == NT - 1:
                                post = (lambda qc=qc, j=j, h=h: fin_B(qc, j, h))
                            self.attn_item(it, mms, TB[:, i0 * 128:i0 * 128 + 512], 512, pvs, "TB", rds, post)
                            it += 1
            self.attn_flush()
        elif kind == "A":
            dv = 64
            NE = NT + 2
            KTs = [self.scr("KT", [64, NE * 128], BF16) for _ in range(2)]
            Vs = [self.scr("V", [128, NE, dv + 1], BF16) for _ in range(2)]
            QTs = [self.scr("QT", [64, T], BF16) for _ in range(2)]
            TBs = [self.scr("TB", [128, ni * 128], F32) for _ in range(2)]
            pairs = [self.scr("pair", [128, NT, 128], BF16) for _ in range(2)]
            esink = self.scr("esink", [128, 16], F32)
            sink_d = self.din(pfx + "sink", [16], F32).ap()
            P.op("sync", lambda e: e.dma_start(out=esink[:], in_=sink_d.partition_broadcast(128)), writes=["esink"], slot="esink")
            P.op("scalar", lambda e: e.activation(out=esink[:], in_=esink[:], func=AF.Exp), reads=["esink"], writes=["esink"])
            for s_i in range(2):
                vb, kb_ = Vs[s_i], KTs[s_i]
                P.op("gpsimd", lambda e, vb=vb: e.memset(vb[:], 0.0), writes=[("Vones", s_i), ("V", s_i)])
                P.op("gpsimd", lambda e, vb=vb: e.memset(vb[:, 1:NE - 1, dv:dv + 1], 1.0), writes=[("Vones", s_i), ("V", s_i)])
                P.op("gpsimd", lambda e, kb_=kb_: e.memset(kb_[:], 0.0), writes=[("KT", s_i)])

            def load_kv(kvh):
                s = kvh % 2
                blk, r0 = kvh // 2, (kvh % 2) * 64
                P.op("sync", lambda e: e.dma_start(out=KTs[s][:, 128:128 + T], in_=kT_d[blk, r0:r0 + 64, :]), reads=kkeys, writes=[("KT", s)], slot="kt%db" % s)
                for half in range(2):
                    P.op("sync", lambda e, half=half: e.dma_start(out=Vs[s][:, 1 + half * 16:1 + (half + 1) * 16, 0:dv], in_=v_r[:, half * 16:(half + 1) * 16, kvh * dv:(kvh + 1) * dv]),
                         reads=vkeys, writes=[("V", s)], slot="v%db%d" % (s, half))

            def load_q(h):
                s = h % 2
                P.op("sync", lambda e: e.dma_start(out=QTs[s][:], in_=qT_d[h // 2, (h % 2) * 64:(h % 2) * 64 + 64, :]), reads=qkeys, writes=[("QT", s)], slot="qt%d" % s)
                P.op("sync", lambda e: e.dma_start(out=TBs[s][:], in_=bt_d[h]), writes=[("TB", s)], slot="tb%d" % s)

            fin_box = [0]
            load_kv(0)
            load_q(0)
            for h in range(16):
                kvh = h // 4
                s = kvh % 2
                self.attn_flush()
                if h % 4 == 0 and kvh + 1 < 4:
                    load_kv(kvh + 1)
                if h + 1 < 16:
                    load_q(h + 1)
                KT, V, QT, TB = KTs[s], Vs[s], QTs[h % 2], TBs[h % 2]
                pair = pairs[(h // 2) % 2]
                rds = [("KT", s), ("V", s), ("QT", h % 2), ("Vones", s)]
                for qt in range(NT):
                    ab = 2 + (it % 4)
                    mms = [(KT[0:64, (qt + 2 - i) * 128:(qt + 3 - i) * 128], QT[0:64, qt * 128:(qt + 1) * 128], i * 128, 128) for i in range(3)]
                    pvs = [(i * 128, V[:, qt + 2 - i, 0:dv + 1], ab, dv + 1, i == 0, i == 2) for i in range(3)]
                    def post_A(acc=self.banks[ab], ak=("bank", ab), qt=qt, h=h, pair=pair):
                        fin_box[0] += 1
                        fin = fin_box[0]
                        r = rr[fin % 4]
                        rk = ("rr", fin % 4)
                        P.op("vector", lambda e, acc=acc, r=r, h=h: e.tensor_tensor(out=r[:, 0:1], in0=acc[:, dv:dv + 1], in1=esink[:, h:h + 1], op=ALU.add),
                             reads=[ak, "esink"], writes=[rk])
                        P.op("vector", lambda e, r=r: e.reciprocal(out=r[:, 1:2], in_=r[:, 0:1]), reads=[rk], writes=[rk])
                        P.op("vector", lambda e, acc=acc, r=r, pair=pair, qt=qt, h=h: e.tensor_scalar(out=pair[:, qt, (h % 2) * 64:(h % 2) * 64 + 64], in0=acc[:, 0:dv], scalar1=r[:, 1:2], scalar2=None, op0=ALU.mult),
                             reads=[ak, rk], writes=[("pair", (h // 2) % 2, qt, h % 2)])
                    self.attn_item(it, mms, TB[:, 0:384], 384, pvs, ("TB", h % 2), rds, post_A)
                    it += 1
                if h % 2 == 1:
                    self.attn_flush()
                    for qt in range(NT):
                        fin += 1
                        bi = 6 + (fin % 2)
                        pT = self.bank_bf(bi)
                        P.op("tensor", lambda e, pair=pair, qt=qt, pT=pT: e.transpose(out=pT[:, 0, :], in_=pair[:, qt, :], identity=self.ident[:]),
                             reads=[("pair", (h // 2) % 2, qt, 0), ("pair", (h // 2) % 2, qt, 1), "ident"], writes=[("bank", bi)])
                        P.op("scalar", lambda e, pT=pT, h=h, qt=qt: e.activation(out=self.actT[:, h // 2, 1 + qt * 128:1 + (qt + 1) * 128], in_=pT[:, 0, :], func=AF.Copy),
                             reads=[("bank", bi)], writes=[("actT", h // 2, qt)])
        else:
            dv = 128
            NE = NT + 16
            KT = self.scr("KT", [128, NE * 128], BF16)
            V = self.scr("V", [128, NE, dv + 1], BF16)
            QT = self.scr("QT", [128, 3, T], BF16)
            TB = self.scr("TB", [128, ni * 128], F32)
            P.op("gpsimd", lambda e: e.memset(V[:], 0.0), writes=["Vones", "V"])
            P.op("gpsimd", lambda e: e.memset(V[:, 8:NE - 8, dv:dv + 1], 1.0), writes=["Vones", "V"])
            P.op("gpsimd", lambda e: e.memset(KT[:], 0.0), writes=["KT"])
            ents = [(0, d_) for d_ in (1, 0, -1)] + [(1, d_) for d_ in (2, 1, 0, -1, -2)] + [(2, d_) for d_ in range(8, -9, -1)]
            fin_box = [0]
            for j in range(4):
                self.attn_flush()
                P.op("sync", lambda e, j=j: e.dma_start(out=KT[:, 1024:1024 + T], in_=kT_d[j]), reads=kkeys, writes=["KT"], slot="ktb")
                for half in range(2):
                    P.op("sync", lambda e, j=j, half=half: e.dma_start(out=V[:, 8 + half * 16:8 + (half + 1) * 16, 0:dv], in_=v_r[:, half * 16:(half + 1) * 16, j * dv:(j + 1) * dv]),
                         reads=vkeys, writes=["V"], slot="vb%d" % half)
                for g in range(3):
                    P.op("sync", lambda e, g=g, j=j: e.dma_start(out=QT[:, g, :], in_=qT_d[g * 4 + j]), reads=qkeys, writes=["QT"], slot="qt%d" % g)
                P.op("sync", lambda e, j=j: e.dma_start(out=TB[:], in_=bt_d[j]), writes=["TB"], slot="tb")
                rds = ["KT", "V", "QT", "Vones"]
                for qt in range(NT):
                    ab = 2 + (qt % 4)
                    for i0 in range(0, 25, 4):
                        grp = list(range(i0, min(i0 + 4, 25)))
                        mms = []
                        pvs = []
                        for n_, idx in enumerate(grp):
                            g, dl = ents[idx]
                            eb = qt + 8 + dl
                            mms.append((KT[:, eb * 128:(eb + 1) * 128], QT[:, g, qt * 128:(qt + 1) * 128], n_ * 128, 128))
                            pvs.append((n_ * 128, V[:, eb, 0:dv + 1], ab, dv + 1, idx == 0, idx == 24))
                        post = None
                        if grp[-1] == 24:
                            def post(acc=self.banks[ab], ak=("bank", ab), qt=qt, j=j):
                                fin_box[0] += 1
                                fin = fin_box[0]
                                r = rr[fin % 4]
                                rk = ("rr", fin % 4)
                                on = ons[fin % 2]
                                onk = ("on", fin % 2)
                                P.op("vector", lambda e, acc=acc, r=r: e.reciprocal(out=r[:, 0:1], in_=acc[:, dv:dv + 1]), reads=[ak], writes=[rk])
                                P.op("vector", lambda e, acc=acc, r=r, on=on: e.tensor_scalar(out=on[:], in0=acc[:, 0:dv], scalar1=r[:, 0:1], scalar2=None, op0=ALU.mult),
                                     reads=[ak, rk], writes=[onk])
                                self.out_transpose(on[:], onk, j, qt, fin)
                        self.attn_item(it, mms, TB[:, i0 * 128:(i0 + len(grp)) * 128], len(grp) * 128, pvs, "TB", rds, post)
                        it += 1
            self.attn_flush()

    def phase_wo(self, li):
        P = self.P
        L = LAYERS[li]
        pfx = "l%d_" % li
        nfc = L["nfc"]
        W = self.din(pfx + "w_o", [nfc * 128, D], F32)
        scale = None
        srd = ()
        if L["kind"] == "B":
            sg = self.din(pfx + "sgcol", [128, 1], F32).ap()
            sgt = self.scr("sgt", [128, 1], F32)
            P.op("sync", lambda e: e.dma_start(out=sgt[:], in_=sg), writes=["sgt"], slot="sgt")
            P.op("vector", lambda e: e.tensor_scalar(out=sgt[:], in0=sgt[:], scalar1=1.0 - lambda_init_fn(li), scalar2=None, op0=ALU.mult), reads=["sgt"], writes=["sgt"])
            scale = lambda k: sgt[:, 0:1]
            srd = ["sgt"]
        kch = list(range(nfc))
        units = [self.load_unit(W, kch, [(n * 512, 512)], scale=scale, scale_reads=srd) for n in range(2)]
        pend = [self.x_load(0, 0), self.x_load(0, 1)]
        for t in range(NT):
            for n in range(2):
                nx = t * 2 + n + 2
                if nx < NT * 2:
                    pend.append(self.x_load(nx // 2, nx % 2))
                slot, wkeys = units[n]
                wb = self.wb[slot]
                bi = 2 + n
                py = self.banks[bi]
                for c in range(nfc):
                    P.op("tensor", lambda e, c=c, t=t, py=py, wb=wb: e.matmul(py[:, 0:512], lhsT=self.actT[:, c, 1 + t * 128:1 + (t + 1) * 128], rhs=wb[:, c, 0:512], start=(c == 0), stop=(c == nfc - 1)),
                         reads=[("actT", c, t)] + wkeys, writes=[("bank", bi)])
                xt, xk, xi = pend[t * 2 + n]
                P.op("vector", lambda e, xt=xt, py=py: e.tensor_tensor(out=xt[:, 0:512], in0=py[:, 0:512], in1=xt[:, 0:512], op=ALU.add),
                     reads=[("bank", bi), xk], writes=[xk])
                self.x_store(t, n, xt, xk, xi)

    def phase_ffn(self, li):
        P = self.P
        pfx = "l%d_" % li
        self.load_gcol(pfx + "gcol_ffn")
        self.phase_norm()
        self.scr_reset()
        Wu = self.din(pfx + "w_up", [D, 2 * DFF], F32)
        Wd = self.din(pfx + "w_down", [DFF, D], F32)
        cw_d = self.din(pfx + "convp", [128, 4, 44], F32).ap()
        cw = self.scr("cw", [128, 4, 44], F32)
        P.op("sync", lambda e: e.dma_start(out=cw[:], in_=cw_d), writes=["cw"], slot="cw")
        wu_keys = self.precast(Wu, 8, 2 * DFF, self.wu_bf, "wubf", True)
        wd_keys = self.precast(Wd, 22, D, self.wd_bf, "wdbf", False)
        TC = 1024
        NK = T // TC
        a = self.scr("a", [128, 22, TC], BF16)
        Us = [[self.scr("U", [128, TC + 2], F32) for _ in range(2)] for _ in range(2)]
        T1 = [self.scr("T1", [128, TC], F32) for _ in range(2)]
        gsc = lambda k: self.gcol[:, k:k + 1]
        up_units = [[(f0 * 128, 256), (DFF + f0 * 128, 256)] for f0 in range(0, 22, 2)]
        dn_units = [(kc, n) for n in range(2) for kc in (list(range(0, 8)), list(range(8, 16)), list(range(16, 22)))]
        pcount = 0
        for k in range(NK):
            cb = TC * k
            allr = [("actT", c, t) for c in range(8) for t in range(max(8 * k - 1, 0), min(8 * k + 9, NT))] + ["halo"]
            nxt = self.load_unit_bf(self.wu_bf, list(range(8)), up_units[0], wu_keys)
            for ui in range(11):
                cur = nxt
                if ui + 1 < 11:
                    nxt = self.load_unit_bf(self.wu_bf, list(range(8)), up_units[ui + 1], wu_keys)
                slot, wkeys = cur
                wb = self.wb[slot]
                for pi in range(2):
                    f = ui * 2 + pi
                    par = pcount % 2
                    pcount += 1
                    for which in range(2):
                        off = which * 256 + pi * 128
                        fc = f + 22 * which
                        bA, bB, bE = self.banks[4 * which], self.banks[4 * which + 1], self.banks[4 * which + 2]
                        bks = [("bank", 4 * which + i) for i in range(3)]
                        for c in range(8):
                            P.op("tensor", lambda e, c=c, wb=wb, off=off, bA=bA, cb=cb: e.matmul(bA[:, 0:512], lhsT=wb[:, c, off:off + 128], rhs=self.actT[:, c, cb + 1:cb + 513], start=(c == 0), stop=(c == 7)),
                                 reads=allr + wkeys, writes=[bks[0]])
                        for c in range(8):
                            P.op("tensor", lambda e, c=c, wb=wb, off=off, bB=bB, cb=cb: e.matmul(bB[:, 0:512], lhsT=wb[:, c, off:off + 128], rhs=self.actT[:, c, cb + 513:cb + 1025], start=(c == 0), stop=(c == 7)),
                                 reads=allr + wkeys, writes=[bks[1]])
                        for c in range(8):
                            P.op("tensor", lambda e, c=c, wb=wb, off=off, bE=bE, cb=cb: e.matmul(bE[:, 0:2], lhsT=wb[:, c, off:off + 128], rhs=self.actT[:, c, cb:cb + 1026:1025], start=(c == 0), stop=(c == 7)),
                                 reads=allr + wkeys, writes=[bks[2]])
                        U = Us[which][par]
                        uk = ("U", which, par)
                        t1 = T1[which]
                        tk = ("T1", which)
                        P.op("scalar", lambda e, U=U, bA=bA: e.activation(out=U[:, 1:513], in_=bA[:, 0:512], func=AF.Copy), reads=[bks[0]], writes=[(uk, 0)])
                        P.op("scalar", lambda e, U=U, bB=bB: e.activation(out=U[:, 513:1025], in_=bB[:, 0:512], func=AF.Copy), reads=[bks[1]], writes=[(uk, 1)])
                        P.op("scalar", lambda e, U=U, bE=bE: e.activation(out=U[:, 0:1026:1025], in_=bE[:, 0:2], func=AF.Copy), reads=[bks[2]], writes=[(uk, 2)])
                        P.op("scalar", lambda e, t1=t1, bA=bA, fc=fc: e.activation(out=t1[:, 0:512], in_=bA[:, 0:512], func=AF.Identity, scale=cw[:, 1, fc:fc + 1], bias=cw[:, 3, fc:fc + 1]),
                             reads=[bks[0], "cw"], writes=[(tk, 0)])
                        P.op("scalar", lambda e, t1=t1, bB=bB, fc=fc: e.activation(out=t1[:, 512:1024], in_=bB[:, 0:512], func=AF.Identity, scale=cw[:, 1, fc:fc + 1], bias=cw[:, 3, fc:fc + 1]),
                             reads=[bks[1], "cw"], writes=[(tk, 1)])
                        P.op("vector", lambda e, t1=t1, U=U, fc=fc: e.scalar_tensor_tensor(out=t1[:], in0=U[:, 0:TC], scalar=cw[:, 0, fc:fc + 1], in1=t1[:], op0=ALU.mult, op1=ALU.add),
                             reads=[(uk, 0), (uk, 1), (uk, 2), (tk, 0), (tk, 1), "cw"], writes=[(tk, 0), (tk, 1)])
                        P.op("vector", lambda e, t1=t1, U=U, fc=fc: e.scalar_tensor_tensor(out=t1[:], in0=U[:, 2:TC + 2], scalar=cw[:, 2, fc:fc + 1], in1=t1[:], op0=ALU.mult, op1=ALU.add),
                             reads=[(uk, 0), (uk, 1), (uk, 2), (tk, 0), (tk, 1), "cw"], writes=[(tk, 0), (tk, 1)])
                    P.op("scalar", lambda e: e.activation(out=T1[0][:], in_=T1[0][:], func=AF.Silu), reads=[(("T1", 0), 0), (("T1", 0), 1)], writes=[(("T1", 0), 0), (("T1", 0), 1)])
                    P.op("vector", lambda e, f=f: e.tensor_tensor(out=a[:, f, :], in0=T1[0][:], in1=T1[1][:], op=ALU.mult),
                         reads=[(("T1", 0), 0), (("T1", 0), 1), (("T1", 1), 0), (("T1", 1), 1)], writes=[("a", f)])
            nxt = self.load_unit_bf(self.wd_bf, dn_units[0][0], [(dn_units[0][1] * 512, 512)], wd_keys)
            for di, (kc, n) in enumerate(dn_units):
                cur = nxt
                if di + 1 < len(dn_units):
                    nxt = self.load_unit_bf(self.wd_bf, dn_units[di + 1][0], [(dn_units[di + 1][1] * 512, 512)], wd_keys)
                slot, wkeys = cur
                wb = self.wb[slot]
                if kc[0] == 0:
                    pend = [self.x_load(8 * k + t, n) for t in range(2)]
                for t in range(8):
                    for ci, f in enumerate(kc):
                        P.op("tensor", lambda e, t=t, ci=ci, f=f, wb=wb: e.matmul(self.banks[t][:, 0:512], lhsT=a[:, f, t * 128:(t + 1) * 128], rhs=wb[:, ci, 0:512], start=(f == 0), stop=(f == 21)),
                             reads=[("a", f)] + wkeys, writes=[("bank", t)])
                if kc[-1] == 21:
                    for t in range(8):
                        tt = 8 * k + t
                        if t + 2 < 8:
                            pend.append(self.x_load(8 * k + t + 2, n))
                        xt, xk, xi = pend[t]
                        P.op("vector", lambda e, t=t, xt=xt: e.tensor_tensor(out=xt[:, 0:512], in0=self.banks[t][:, 0:512], in1=xt[:, 0:512], op=ALU.add),
                             reads=[("bank", t), xk], writes=[xk])
                        self.x_store(tt, n, xt, xk, xi)

    def finish(self):
        self.P.finalize()
        return self.nc


def rel_bucket_np(rel):
    nb = 16
    max_exact = 8
    n = np.abs(rel)
    nf = np.maximum(n, 1).astype(np.float32)
    large = max_exact + (np.log(nf / np.float32(max_exact)) / np.float32(math.log(1024 / max_exact)) * np.float32(nb - max_exact)).astype(np.int32)
    large = np.minimum(large, nb - 1)
    return np.where(rel > 0, nb, 0) + np.where(n < max_exact, n, large)


def bias_tables(rel_bias, kind):
    rb = np.asarray(rel_bias, np.float32)
    kp = np.arange(128)[:, None]
    qp = np.arange(128)[None, :]
    if kind == "A":
        out = np.empty((16, 128, 3 * 128), np.float32)
        for i, dl in enumerate((1, 0, -1)):
            rel = dl * 128 + kp - qp
            bk = rel_bucket_np(rel)
            ok = np.abs(rel) <= 128
            for h in range(16):
                out[h, :, i * 128:(i + 1) * 128] = np.where(ok, rb[bk, h], NEG)
        return out
    if kind == "B":
        out = np.empty((16, 128, 28 * 128), np.float32)
        for i in range(28):
            dl = 12 - i
            rel = dl * 128 + kp - qp
            bk = rel_bucket_np(rel)
            for m in range(16):
                out[m, :, i * 128:(i + 1) * 128] = rb[bk, m]
        return out
    ents = [(0, d_) for d_ in (1, 0, -1)] + [(1, d_) for d_ in (2, 1, 0, -1, -2)] + [(2, d_) for d_ in range(8, -9, -1)]
    dils = (1, 4, 16)
    out = np.empty((4, 128, 25 * 128), np.float32)
    for i, (g, dl) in enumerate(ents):
        rel = dl * 128 + kp - qp
        dil = dils[g]
        ok = (rel % dil == 0) & (np.abs(rel) <= 64 * dil)
        bk = rel_bucket_np(rel)
        for j in range(4):
            out[j, :, i * 128:(i + 1) * 128] = np.where(ok, rb[bk, g * 4 + j], NEG)
    return out


def gcols(g):
    return np.ascontiguousarray(np.asarray(g, np.float32).reshape(8, 128).T)


_PROG = None
DEBUG_LAYERS = 4


def get_prog():
    global _PROG
    if _PROG is None:
        b = Builder()
        for li in range(DEBUG_LAYERS):
            b.phase_qkv(li)
            b.phase_attn(li)
            b.phase_wo(li)
            b.phase_ffn(li)
        nc = b.finish()
        _PROG = (nc, list(b.din_names), list(b.dout_names))
    return _PROG


def kernel(x, rel_bias,
           l0_attn_norm, l0_w_qkv, l0_q_gain, l0_k_gain, l0_sink, l0_w_o,
           l0_ffn_norm, l0_w_up, l0_conv_w, l0_conv_b, l0_w_down,
           l1_attn_norm, l1_w_qkv, l1_q_gain, l1_k_gain, l1_lambda_q1, l1_lambda_k1,
           l1_lambda_q2, l1_lambda_k2, l1_sub_gain, l1_w_o,
           l1_ffn_norm, l1_w_up, l1_conv_w, l1_conv_b, l1_w_down,
           l2_attn_norm, l2_w_qkv, l2_q_gain, l2_k_gain, l2_w_o,
           l2_ffn_norm, l2_w_up, l2_conv_w, l2_conv_b, l2_w_down,
           l3_attn_norm, l3_w_qkv, l3_q_gain, l3_k_gain, l3_sink, l3_w_o,
           l3_ffn_norm, l3_w_up, l3_conv_w, l3_conv_b, l3_w_down):
    inp = {
        "x": x,
        "rel_bias": rel_bias,
        "l0_attn_norm": l0_attn_norm,
        "l0_w_qkv": l0_w_qkv,
        "l0_q_gain": l0_q_gain,
        "l0_k_gain": l0_k_gain,
        "l0_sink": l0_sink,
        "l0_w_o": l0_w_o,
        "l0_ffn_norm": l0_ffn_norm,
        "l0_w_up": l0_w_up,
        "l0_conv_w": l0_conv_w,
        "l0_conv_b": l0_conv_b,
        "l0_w_down": l0_w_down,
        "l1_attn_norm": l1_attn_norm,
        "l1_w_qkv": l1_w_qkv,
        "l1_q_gain": l1_q_gain,
        "l1_k_gain": l1_k_gain,
        "l1_lambda_q1": l1_lambda_q1,
        "l1_lambda_k1": l1_lambda_k1,
        "l1_lambda_q2": l1_lambda_q2,
        "l1_lambda_k2": l1_lambda_k2,
        "l1_sub_gain": l1_sub_gain,
        "l1_w_o": l1_w_o,
        "l1_ffn_norm": l1_ffn_norm,
        "l1_w_up": l1_w_up,
        "l1_conv_w": l1_conv_w,
        "l1_conv_b": l1_conv_b,
        "l1_w_down": l1_w_down,
        "l2_attn_norm": l2_attn_norm,
        "l2_w_qkv": l2_w_qkv,
        "l2_q_gain": l2_q_gain,
        "l2_k_gain": l2_k_gain,
        "l2_w_o": l2_w_o,
        "l2_ffn_norm": l2_ffn_norm,
        "l2_w_up": l2_w_up,
        "l2_conv_w": l2_conv_w,
        "l2_conv_b": l2_conv_b,
        "l2_w_down": l2_w_down,
        "l3_attn_norm": l3_attn_norm,
        "l3_w_qkv": l3_w_qkv,
        "l3_q_gain": l3_q_gain,
        "l3_k_gain": l3_k_gain,
        "l3_sink": l3_sink,
        "l3_w_o": l3_w_o,
        "l3_ffn_norm": l3_ffn_norm,
        "l3_w_up": l3_w_up,
        "l3_conv_w": l3_conv_w,
        "l3_conv_b": l3_conv_b,
        "l3_w_down": l3_w_down,
    }
    x = np.ascontiguousarray(np.asarray(x, np.float32))
    rel_bias = np.asarray(rel_bias, np.float32)
    shared = {"ident": np.eye(128, dtype=np.float32)}
    for kind in ("A", "B", "C"):
        shared["bt_" + kind] = bias_tables(rel_bias, kind)
    f32 = lambda a: np.ascontiguousarray(np.asarray(a, np.float32))
    for li in range(4):
        L = LAYERS[li]
        p = "l%d_" % li
        shared[p + "gcol_attn"] = gcols(inp[p + "attn_norm"])
        shared[p + "gcol_ffn"] = gcols(inp[p + "ffn_norm"])
        for w in ("w_qkv", "w_o", "w_up", "w_down"):
            shared[p + w] = f32(inp[p + w])
        shared[p + "qkg"] = np.ascontiguousarray(np.stack([f32(inp[p + "q_gain"]), f32(inp[p + "k_gain"])]))
        cwv = f32(inp[p + "conv_w"]).reshape(3, 44, 128)
        cbv = f32(inp[p + "conv_b"]).reshape(1, 44, 128)
        shared[p + "convp"] = np.ascontiguousarray(np.concatenate([cwv, cbv], 0).transpose(2, 0, 1))
        if L["kind"] == "A":
            shared[p + "sink"] = f32(inp[p + "sink"])
        if L["kind"] == "B":
            shared[p + "lam"] = np.ascontiguousarray(np.stack([f32(inp[p + k]) for k in ("lambda_q1", "lambda_k1", "lambda_q2", "lambda_k2")]))
            shared[p + "sgcol"] = f32(inp[p + "sub_gain"]).reshape(128, 1)
    import time as _t
    _t1 = _t.time()
    nc, dins, douts = get_prog()
    print("[kernel] host prep + build took %.1fs" % (_t.time() - _t1), flush=True)
    in_maps = []
    for c in range(NCORES):
        m = {}
        for n in dins:
            m[n] = x[c] if n == "x" else shared[n]
        in_maps.append(m)
    import time as _t
    _t0 = _t.time()
    res = run_bass_kernel_spmd(nc, in_maps, core_ids=list(range(NCORES)))
    print("[kernel] run_bass_kernel_spmd took %.1fs" % (_t.time() - _t0), flush=True)
    return np.stack([res.results[c]["x_out"] for c in range(NCORES)]).astype(np.float32)
```

```python
import math
from contextlib import ExitStack

import numpy as np
import ml_dtypes

import concourse.bass as bass
import concourse.mybir as mybir
from concourse.ap import AP
from concourse.bass_utils import run_bass_kernel_spmd

F32 = mybir.dt.float32
BF16 = mybir.dt.bfloat16
AF = mybir.ActivationFunctionType
ALU = mybir.AluOpType
AX = mybir.AxisListType
NPBF = ml_dtypes.bfloat16

NCORES = 4
T = 4096
NT = 32
D = 1024
DFF = 2816
EPS = 1e-6
NEG = -30000.0

LAYERS = [
    dict(kind="A", F=1536, nqb=8, nkb=2, FV=256, nfc=8),
    dict(kind="B", F=3072, nqb=8, nkb=8, FV=1024, nfc=8),
    dict(kind="C", F=2560, nqb=12, nkb=4, FV=512, nfc=4),
    dict(kind="A", F=1536, nqb=8, nkb=2, FV=256, nfc=8),
]
NI = {"A": 3, "B": 28, "C": 25}
NUNIT_BT = {"A": 16, "B": 16, "C": 4}


def lambda_init_fn(layer):
    return 0.8 - 0.6 * math.exp(-0.3 * layer)


class Op:
    __slots__ = ("eng", "fn", "deps", "needs_inc", "val", "semkey", "is_dma", "idx")

    def __init__(self, eng, fn, deps, semkey, is_dma):
        self.eng = eng
        self.fn = fn
        self.deps = deps
        self.needs_inc = False
        self.val = None
        self.semkey = semkey
        self.is_dma = is_dma


class Prog:
    ENGS = ("sync", "scalar", "vector", "gpsimd", "tensor")

    def __init__(self, nc):
        self.nc = nc
        self.ops = {e: [] for e in self.ENGS}
        self.lastw = {}
        self.readers = {}
        self.es = ExitStack()
        self.outs = []
        self.fence = []
        self.fence_pending = set()
        self.last_dma = {}
        self.epoch = 0

    def barrier(self):
        fence = []
        for e in self.ENGS:
            for o in reversed(self.ops[e]):
                if not o.is_dma:
                    fence.append(o)
                    break
        fence.extend(self.last_dma.values())
        self.fence = fence
        self.fence_pending = set(self.ENGS)
        self.epoch += 1

    def op(self, eng, fn, reads=(), writes=(), slot=None, out=False):
        deps = []
        if eng in self.fence_pending:
            deps.extend(self.fence)
            self.fence_pending.discard(eng)
        for b in reads:
            w = self.lastw.get(b)
            if w is not None:
                deps.append(w)
        for b in writes:
            w = self.lastw.get(b)
            if w is not None:
                deps.append(w)
            deps.extend(self.readers.get(b, ()))
        is_dma = slot is not None
        semkey = ("dma", slot) if is_dma else ("eng", eng, self.epoch % 3)
        o = Op(eng, fn, deps, semkey, is_dma)
        o.idx = len(self.ops[eng])
        self.ops[eng].append(o)
        if is_dma:
            self.last_dma[slot] = o
        for b in writes:
            self.lastw[b] = o
            self.readers[b] = []
        for b in reads:
            self.readers.setdefault(b, []).append(o)
        if out:
            self.outs.append(o)
        return o

    @staticmethod
    def _skip(d, o):
        return d is o or (d.eng == "tensor" and o.eng == "tensor" and not d.is_dma and not o.is_dma)

    def finalize(self):
        nc = self.nc
        final_waits = self.outs
        for e in self.ENGS:
            for o in self.ops[e]:
                best = {}
                for d in o.deps:
                    if self._skip(d, o):
                        continue
                    if d.is_dma:
                        d.needs_inc = True
                        continue
                    b = best.get(d.semkey)
                    if b is None or d.idx > b.idx:
                        best[d.semkey] = d
                for d in best.values():
                    d.needs_inc = True
        for d in final_waits:
            d.needs_inc = True
        for e in self.ENGS:
            for o in self.ops[e]:
                if o.is_dma:
                    o.needs_inc = True
        counters = {}
        for e in self.ENGS:
            for o in self.ops[e]:
                if o.needs_inc:
                    c = counters.get(o.semkey, 0) + (16 if o.is_dma else 1)
                    counters[o.semkey] = c
                    o.val = c
        sems = {}
        for i, k in enumerate(counters):
            sems[k] = self.es.enter_context(nc.semaphore("s%d" % i))
        self.nsem = len(sems)
        block = self.es.enter_context(nc.Block())

        def run(e, engine):
            known = {}
            for o in self.ops[e]:
                need = {}
                for d in o.deps:
                    if self._skip(d, o) or d.val is None:
                        continue
                    if need.get(d.semkey, 0) < d.val:
                        need[d.semkey] = d.val
                for k, v in need.items():
                    if known.get(k, 0) >= v:
                        continue
                    engine.wait_ge(sems[k], v)
                    known[k] = v
                ins = o.fn(engine)
                if o.needs_inc:
                    ins.then_inc(sems[o.semkey], 16 if o.is_dma else 1)
            if e == "sync":
                need = {}
                for d in final_waits:
                    if need.get(d.semkey, 0) < d.val:
                        need[d.semkey] = d.val
                for k, v in need.items():
                    engine.wait_ge(sems[k], v)

        @block.sync
        def _(eng):
            run("sync", eng)

        @block.scalar
        def _(eng):
            run("scalar", eng)

        @block.vector
        def _(eng):
            run("vector", eng)

        @block.gpsimd
        def _(eng):
            run("gpsimd", eng)

        @block.tensor
        def _(eng):
            run("tensor", eng)

        self.es.close()


SB_BASE = 16512
SCR_END = 229376


class Builder:
    def __init__(self):
        self.nc = bass.Bass("TRN2", target_bir_lowering=False)
        self.P = Prog(self.nc)
        self.din_names = []
        self.dout_names = []
        self.d = {}
        self.uid = 0
        self.perm_off = SB_BASE
        nc = self.nc
        self.actT = self.perm("actT", [128, 8, T + 2], BF16)
        self.ws = [self.perm("ws%d" % i, [128, 4, 512], F32) for i in range(2)]
        self.wb = [self.perm("wb%d" % i, [128, 8, 512], BF16) for i in range(2)]
        self.xts = [self.perm("xt%d" % i, [128, D], F32) for i in range(4)]
        self.hbs = [self.perm("hb%d" % i, [128, D], BF16) for i in range(2)]
        self.ident = self.perm("ident", [128, 128], BF16)
        self.identf = self.perm("identf", [128, 128], F32)
        self.sst = [self.perm("sst%d" % i, [128, 4], F32) for i in range(4)]
        self.epsb = self.perm("epsb", [128, 1], F32)
        self.gcol = self.perm("gcol", [128, 8], F32)
        self.PERM_END = (self.perm_off + 63) // 64 * 64
        self.scr_off = self.PERM_END
        self.nscr = 0
        self.xcnt = 0
        self.banks = [nc.alloc_psum_tensor("bank%d" % i, [128, 512], F32) for i in range(8)]
        self.wcount = 0
        self.xd = self.dout("x_out", [T, D], F32).ap()
        self.qT_d = nc.dram_tensor("qT_scr", [12, 128, T], BF16, kind="Internal").ap()
        self.kT_d = nc.dram_tensor("kT_scr", [8, 128, T], BF16, kind="Internal").ap()
        self.v_d = nc.dram_tensor("v_scr", [T, 1024], BF16, kind="Internal").ap()
        self.wu_bf = nc.dram_tensor("wu_bf", [D, 2 * DFF], BF16, kind="Internal").ap()
        self.wd_bf = nc.dram_tensor("wd_bf", [DFF, D], BF16, kind="Internal").ap()
        self.init_consts()

    def perm(self, name, shape, dt):
        n = int(np.prod(shape[1:])) * (4 if dt == F32 else 2)
        n = (n + 31) // 32 * 32
        t = self.nc.alloc_sbuf_tensor_at(name, shape, dt, offset=self.perm_off)
        self.perm_off += n
        return t

    def scr_reset(self):
        if self.nscr > 0:
            self.P.barrier()
        self.nscr += 1
        self.scr_off = self.PERM_END

    def scr(self, name, shape, dt):
        n = int(np.prod(shape[1:])) * (4 if dt == F32 else 2)
        n = (n + 31) // 32 * 32
        self.uid += 1
        t = self.nc.alloc_sbuf_tensor_at("%s_%d" % (name, self.uid), shape, dt, offset=self.scr_off)
        self.scr_off += n
        assert self.scr_off <= SCR_END, (name, self.scr_off)
        return t

    def din(self, name, shape, dt):
        if name not in self.d:
            self.d[name] = self.nc.dram_tensor(name, list(shape), dt, kind="ExternalInput")
            self.din_names.append(name)
        return self.d[name]

    def dout(self, name, shape, dt):
        if name not in self.d:
            self.d[name] = self.nc.dram_tensor(name, list(shape), dt, kind="ExternalOutput")
            self.dout_names.append(name)
        return self.d[name]

    def bank_bf(self, i):
        return self.banks[i][:].bitcast(BF16).rearrange("p (c t) -> p c t", t=128)

    def init_consts(self):
        P = self.P
        idd = self.din("ident", [128, 128], F32).ap()
        xin = self.din("x", [T, D], F32).ap()
        P.op("sync", lambda e: e.dma_start(out=self.identf[:], in_=idd), writes=["identf"], slot="c_id")
        P.op("vector", lambda e: e.tensor_copy(out=self.ident[:], in_=self.identf[:]), reads=["identf"], writes=["ident"])
        P.op("vector", lambda e: e.memset(self.epsb[:], EPS), writes=["eps"])
        P.op("vector", lambda e: e.memset(self.actT[:, :, 0:1], 0.0), writes=["halo"])
        P.op("vector", lambda e: e.memset(self.actT[:, :, T + 1:T + 2], 0.0), writes=["halo"])
        for t0 in range(0, NT, 8):
            P.op("sync", lambda e, t0=t0: e.dma_start(out=self.xd[t0 * 128:(t0 + 8) * 128, :], in_=xin[t0 * 128:(t0 + 8) * 128, :]),
                 writes=[("xd", t, n) for t in range(t0, t0 + 8) for n in range(2)], slot="xcp%d" % (t0 // 8), out=True)

    def x_load(self, t, n=None):
        P = self.P
        i = self.xcnt % 4
        self.xcnt += 1
        xt = self.xts[i]
        key = ("xt", i)
        if n is None:
            P.op("sync", lambda e, t=t, xt=xt: e.dma_start(out=xt[:], in_=self.xd[t * 128:(t + 1) * 128, :]),
                 reads=[("xd", t, 0), ("xd", t, 1)], writes=[key], slot="xl%d" % i)
        else:
            P.op("sync", lambda e, t=t, xt=xt, n=n: e.dma_start(out=xt[:, 0:512], in_=self.xd[t * 128:(t + 1) * 128, n * 512:(n + 1) * 512]),
                 reads=[("xd", t, n)], writes=[key], slot="xl%d" % i)
        return xt, key, i

    def x_store(self, t, n, xt, key, i):
        P = self.P
        P.op("gpsimd", lambda e, t=t, xt=xt, n=n: e.dma_start(out=self.xd[t * 128:(t + 1) * 128, n * 512:(n + 1) * 512], in_=xt[:, 0:512]),
             reads=[key], writes=[("xd", t, n)], slot="xs%d" % i, out=True)

    def load_unit(self, W, kchunks, segs, scale=None, scale_reads=()):
        P = self.P
        i = self.wcount
        self.wcount += 1
        slot = i % 2
        wb = self.wb[slot]
        key = ("wb", slot)
        Wa = W.ap()
        ncol = sum(s[1] for s in segs)
        halves = [kchunks[0:4], kchunks[4:8]]
        allkeys = []
        for hi, kc in enumerate(halves):
            if not kc:
                continue
            ws = self.ws[hi]
            co = 0
            for si, (c0, cn) in enumerate(segs):
                k0 = kc[0]
                src = Wa[k0 * 128:(k0 + len(kc)) * 128, c0:c0 + cn].rearrange("(c p) f -> p c f", p=128)
                P.op("sync", lambda e, ws=ws, src=src, co=co, cn=cn, n=len(kc): e.dma_start(out=ws[:, 0:n, co:co + cn], in_=src),
                     writes=[("ws", hi, si)], slot="ws%d_%d" % (hi, si))
                co += cn
            n = len(kc)
            if scale is None:
                P.op("gpsimd", lambda e, ws=ws, wb=wb, hi=hi, n=n, ncol=ncol: e.tensor_copy(out=wb[:, hi * 4:hi * 4 + n, 0:ncol], in_=ws[:, 0:n, 0:ncol]),
                     reads=[("ws", hi, si) for si in range(len(segs))], writes=[(key, hi, 0)])
                allkeys.append((key, hi, 0))
            else:
                for j, k in enumerate(kc):
                    sc = scale(k)
                    P.op("gpsimd", lambda e, ws=ws, wb=wb, hi=hi, j=j, ncol=ncol, sc=sc: e.tensor_scalar(out=wb[:, hi * 4 + j, 0:ncol], in0=ws[:, j, 0:ncol], scalar1=sc, scalar2=None, op0=ALU.mult),
                         reads=[("ws", hi, si) for si in range(len(segs))] + list(scale_reads), writes=[(key, hi, j)])
                    allkeys.append((key, hi, j))
        return slot, allkeys

    def precast(self, W, nchunks, ncols, Wbf, keyname, scaled):
        P = self.P
        Wa = W.ap()
        blocks = [(kg, min(4, nchunks - kg), c0) for kg in range(0, nchunks, 4) for c0 in range(0, ncols, 512)]
        keys = []
        for b, (kg, n, c0) in enumerate(blocks):
            ws = self.ws[b % 2]
            wsk = ("ws", b % 2, 0)
            ss_ = b % 4
            stg = self.wb[ss_ // 2][:, (ss_ % 2) * 4:(ss_ % 2) * 4 + n, :]
            stgk = (("wb", ss_ // 2), ss_ % 2, 0)
            src = Wa[kg * 128:(kg + n) * 128, c0:c0 + 512].rearrange("(c p) f -> p c f", p=128)
            P.op("sync", lambda e, ws=ws, src=src, n=n: e.dma_start(out=ws[:, 0:n, :], in_=src), writes=[wsk], slot="ws%d_0" % (b % 2))
            eng = "scalar" if b % 2 == 0 else "vector"
            if not scaled:
                if eng == "scalar":
                    P.op(eng, lambda e, ws=ws, stg=stg, n=n: e.activation(out=stg, in_=ws[:, 0:n, :], func=AF.Copy), reads=[wsk], writes=[stgk])
                else:
                    P.op(eng, lambda e, ws=ws, stg=stg, n=n: e.tensor_copy(out=stg, in_=ws[:, 0:n, :]), reads=[wsk], writes=[stgk])
            else:
                for j in range(n):
                    sc = self.gcol[:, kg + j:kg + j + 1]
                    if eng == "scalar":
                        P.op(eng, lambda e, ws=ws, stg=stg, j=j, sc=sc: e.activation(out=stg[:, j, :], in_=ws[:, j, :], func=AF.Copy, scale=sc), reads=[wsk, "gcol"], writes=[(stgk, j)] if False else [stgk])
                    else:
                        P.op(eng, lambda e, ws=ws, stg=stg, j=j, sc=sc: e.tensor_scalar(out=stg[:, j, :], in0=ws[:, j, :], scalar1=sc, scalar2=None, op0=ALU.mult), reads=[wsk, "gcol"], writes=[stgk])
            dst = Wbf[kg * 128:(kg + n) * 128, c0:c0 + 512].rearrange("(c p) f -> p c f", p=128)
            P.op("gpsimd", lambda e, stg=stg, dst=dst: e.dma_start(out=dst, in_=stg), reads=[stgk], writes=[(keyname, b)], slot="pc%d" % ss_)
            keys.append((keyname, b))
        return keys

    def load_unit_bf(self, Wbf, kchunks, segs, rkeys):
        P = self.P
        i = self.wcount
        self.wcount += 1
        slot = i % 2
        wb = self.wb[slot]
        keys = [(("wb", slot), 0, 0), (("wb", slot), 1, 0)]
        k0, n, co = kchunks[0], len(kchunks), 0
        for si, (c0, cn) in enumerate(segs):
            src = Wbf[k0 * 128:(k0 + n) * 128, c0:c0 + cn].rearrange("(c p) f -> p c f", p=128)
            P.op("sync", lambda e, wb=wb, src=src, n=n, co=co, cn=cn: e.dma_start(out=wb[:, 0:n, co:co + cn], in_=src),
                 reads=rkeys, writes=keys, slot="wbd%d_%d" % (slot, si))
            co += cn
        return slot, keys

    def phase_norm(self):
        P = self.P
        junk = self.banks[7]
        hbs = self.hbs
        pend = [self.x_load(0), self.x_load(1)]
        for t in range(NT):
            if t + 2 < NT:
                pend.append(self.x_load(t + 2))
            xt, xk, _ = pend[t]
            st = self.sst[t % 4]
            sk = ("sst", t % 4)
            P.op("vector", lambda e, st=st: e.memset(st[:], 0.0), writes=[sk])
            P.op("scalar", lambda e, xt=xt, st=st: e.activation(out=junk[:, 0:512], in_=xt[:, 0:512], func=AF.Square, accum_out=st[:, 0:1]),
                 reads=[xk, sk], writes=[sk, ("bank", 7)])
            P.op("scalar", lambda e, xt=xt, st=st: e.activation(out=junk[:, 0:512], in_=xt[:, 512:1024], func=AF.Square, accum_out=st[:, 1:2]),
                 reads=[xk, sk], writes=[sk, ("bank", 7)])
            P.op("vector", lambda e, st=st: e.tensor_tensor(out=st[:, 2:3], in0=st[:, 0:1], in1=st[:, 1:2], op=ALU.add), reads=[sk], writes=[sk])
            P.op("scalar", lambda e, st=st: e.activation(out=st[:, 2:3], in_=st[:, 2:3], func=AF.Ln, scale=1.0 / D, bias=self.epsb[:, 0:1]),
                 reads=[sk, "eps"], writes=[sk])
            P.op("scalar", lambda e, st=st: e.activation(out=st[:, 3:4], in_=st[:, 2:3], func=AF.Exp, scale=-0.5), reads=[sk], writes=[sk])
            hb = hbs[t % 2]
            hk = ("hb", t % 2)
            pk = ("bank", t % 2)
            pT = self.bank_bf(t % 2)
            P.op("vector", lambda e, xt=xt, hb=hb, st=st: e.tensor_scalar(out=hb[:], in0=xt[:], scalar1=st[:, 3:4], scalar2=None, op0=ALU.mult),
                 reads=[xk, sk], writes=[hk])
            for c in range(8):
                P.op("tensor", lambda e, c=c, hb=hb, pT=pT: e.transpose(out=pT[:, c, :], in_=hb[:, c * 128:(c + 1) * 128], identity=self.ident[:]),
                     reads=[hk, "ident"], writes=[pk])
            P.op("scalar", lambda e, t=t, pT=pT: e.activation(out=self.actT[:, :, 1 + t * 128:1 + (t + 1) * 128], in_=pT, func=AF.Copy),
                 reads=[pk], writes=[("actT", c, t) for c in range(8)])

    def load_gcol(self, name):
        P = self.P
        g = self.din(name, [128, 8], F32).ap()
        P.op("sync", lambda e: e.dma_start(out=self.gcol[:], in_=g), writes=["gcol"], slot="gcol")

    def phase_qkv(self, li):
        P = self.P
        L = LAYERS[li]
        kind = L["kind"]
        pfx = "l%d_" % li
        self.load_gcol(pfx + "gcol_attn")
        self.phase_norm()
        self.scr_reset()
        W = self.din(pfx + "w_qkv", [D, L["F"]], F32)
        dh = 128 if kind == "C" else 64
        geff_d = self.din(pfx + "qkg", [2, dh], F32).ap()
        qT_d, kT_d, v_d = self.qT_d, self.kT_d, self.v_d
        gq = self.scr("gq", [128, dh], F32)
        gk = self.scr("gk", [128, dh], F32)
        P.op("sync", lambda e: e.dma_start(out=gq[:], in_=geff_d[0].partition_broadcast(128)), writes=["gq"], slot="gq")
        P.op("sync", lambda e: e.dma_start(out=gk[:], in_=geff_d[1].partition_broadcast(128)), writes=["gk"], slot="gk")
        P.op("vector", lambda e: e.scalar_tensor_tensor(out=gk[:], in0=gq[:], scalar=float(dh) ** -0.5, in1=gk[:], op0=ALU.mult, op1=ALU.mult),
             reads=["gq", "gk"], writes=["gk"])
        stages = [self.scr("stage", [128, 4, T], BF16) for _ in range(2)]
        sqs = [self.scr("sq", [128, 512], F32) for _ in range(2)]
        kfs = [self.scr("kf", [128, 512], F32) for _ in range(2)]
        qns = [self.scr("qn", [128, 512], BF16) for _ in range(2)]
        vsts = [self.scr("vst", [128, 512], BF16) for _ in range(2)]
        ssq = [self.scr("ssq", [128, 8], F32) for _ in range(2)]
        if kind == "A":
            chunks = [[("q", 0, 512, 0)], [("q", 0, 512, 4)], [("k", 0, 256, 0), ("v", 256, 256, 0)]]
        elif kind == "B":
            chunks = [[("q", 0, 512, 0)], [("q", 0, 512, 4)], [("k", 0, 512, 0)], [("k", 0, 512, 4)], [("v", 0, 512, 0)], [("v", 0, 512, 512)]]
        else:
            chunks = [[("q", 0, 512, 0)], [("q", 0, 512, 4)], [("q", 0, 512, 8)], [("k", 0, 512, 0)], [("v", 0, 512, 0)]]
        gsc = lambda k: self.gcol[:, k:k + 1]
        units = [None] * len(chunks)
        units[0] = self.load_unit(W, list(range(8)), [(0, 512)], scale=gsc, scale_reads=["gcol"])
        it = 0
        pend_rest = [None]
        for ci, segs in enumerate(chunks):
            if ci + 1 < len(chunks):
                units[ci + 1] = self.load_unit(W, list(range(8)), [((ci + 1) * 512, 512)], scale=gsc, scale_reads=["gcol"])
            slot, wkeys = units[ci]
            wb = self.wb[slot]
            stage = stages[ci % 2]
            stk = ("stage", ci % 2)
            for t in range(NT):
                bi = 2 + (it % 2)
                py = self.banks[bi]
                pyk = ("bank", bi)
                for c in range(8):
                    P.op("tensor", lambda e, c=c, t=t, py=py, wb=wb: e.matmul(py[:, 0:512], lhsT=self.actT[:, c, 1 + t * 128:1 + (t + 1) * 128], rhs=wb[:, c, 0:512], start=(c == 0), stop=(c == 7)),
                         reads=[("actT", c, t)] + wkeys, writes=[pyk])
                def rest(segs=segs, it=it, t=t, py=py, pyk=pyk, stage=stage, stk=stk):
                    for (ty, off, w, dst) in segs:
                        if ty == "v":
                            vst = vsts[it % 2]
                            vk = ("vst", it % 2)
                            P.op("scalar", lambda e, py=py, vst=vst, off=off, w=w: e.activation(out=vst[:, 0:w], in_=py[:, off:off + w], func=AF.Copy),
                                 reads=[pyk], writes=[vk])
                            P.op("gpsimd", lambda e, vst=vst, t=t, dst=dst, w=w: e.dma_start(out=v_d[t * 128:(t + 1) * 128, dst:dst + w], in_=vst[:, 0:w]),
                                 reads=[vk], writes=[("vd", t, dst)], slot="vst%d" % (it % 2))
                            continue
                        nh = w // dh
                        sq = sqs[it % 2]
                        sqk = ("sq", it % 2)
                        s_ = ssq[it % 2]
                        sk = ("ssq", it % 2)
                        qn = qns[it % 2]
                        qk = ("qn", it % 2)
                        P.op("scalar", lambda e, py=py, sq=sq, off=off, w=w: e.activation(out=sq[:, 0:w], in_=py[:, off:off + w], func=AF.Square),
                             reads=[pyk], writes=[sqk])
                        P.op("vector", lambda e, sq=sq, s_=s_, w=w, nh=nh: e.tensor_reduce(out=s_[:, 0:nh], in_=sq[:, 0:w].rearrange("p (h d) -> p h d", d=dh), axis=AX.X, op=ALU.add),
                             reads=[sqk], writes=[sk])
                        P.op("scalar", lambda e, s_=s_, nh=nh: e.activation(out=s_[:, 0:nh], in_=s_[:, 0:nh], func=AF.Ln, scale=1.0 / dh, bias=self.epsb[:, 0:1]),
                             reads=[sk, "eps"], writes=[sk])
                        P.op("scalar", lambda e, s_=s_, nh=nh: e.activation(out=s_[:, 0:nh], in_=s_[:, 0:nh], func=AF.Exp, scale=-0.5), reads=[sk], writes=[sk])
                        rb = AP(s_, 0, [[8, 128], [1, nh], [0, dh]])
                        if ty == "q":
                            P.op("vector", lambda e, py=py, qn=qn, off=off, w=w, rb=rb: e.tensor_tensor(out=qn[:, 0:w].rearrange("p (h d) -> p h d", d=dh), in0=py[:, off:off + w].rearrange("p (h d) -> p h d", d=dh), in1=rb, op=ALU.mult),
                                 reads=[pyk, sk], writes=[qk])
                        else:
                            kf = kfs[it % 2]
                            kfk = ("kf", it % 2)
                            gb = AP(gk, 0, [[dh, 128], [0, nh], [1, dh]])
                            P.op("vector", lambda e, py=py, kf=kf, off=off, w=w, rb=rb: e.tensor_tensor(out=kf[:, 0:w].rearrange("p (h d) -> p h d", d=dh), in0=py[:, off:off + w].rearrange("p (h d) -> p h d", d=dh), in1=rb, op=ALU.mult),
                                 reads=[pyk, sk], writes=[kfk])
                            P.op("gpsimd", lambda e, kf=kf, qn=qn, w=w, gb=gb: e.tensor_tensor(out=qn[:, 0:w].rearrange("p (h d) -> p h d", d=dh), in0=kf[:, 0:w].rearrange("p (h d) -> p h d", d=dh), in1=gb, op=ALU.mult),
                                 reads=[kfk, "gk"], writes=[qk])
                        nb = w // 128
                        tb = it % 2
                        pT = self.bank_bf(tb)
                        pk = ("bank", tb)
                        for j in range(nb):
                            P.op("tensor", lambda e, j=j, qn=qn, pT=pT: e.transpose(out=pT[:, j, :], in_=qn[:, j * 128:(j + 1) * 128], identity=self.ident[:]),
                                 reads=[qk, "ident"], writes=[pk])
                        P.op("scalar", lambda e, pT=pT, stage=stage, nb=nb, t=t: e.activation(out=stage[:, 0:nb, t * 128:(t + 1) * 128], in_=pT[:, 0:nb, :], func=AF.Copy),
                             reads=[pk], writes=[(stk, t)])
                if pend_rest[0] is not None:
                    pend_rest[0]()
                pend_rest[0] = rest
                it += 1
            if pend_rest[0] is not None:
                pend_rest[0]()
                pend_rest[0] = None
            for (ty, off, w, dst) in segs:
                if ty == "v":
                    continue
                dd = qT_d if ty == "q" else kT_d
                dk = "qTd" if ty == "q" else "kTd"
                for j in range(w // 128):
                    P.op("gpsimd", lambda e, dd=dd, j=j, dst=dst, stage=stage: e.dma_start(out=dd[dst + j], in_=stage[:, j, :]),
                         reads=[(stk, t) for t in range(NT)], writes=[(dk, dst + j)], slot="stg%d_%d" % (ci % 2, j))

    ST_BANKS = (0, 1, 6)

    def attn_item(self, it, mms, tb_ap, ncols, pvs, tbkey, extra_reads, post=None):
        P = self.P
        bi = self.ST_BANKS[it % 3]
        st = self.banks[bi]
        stk = ("bank", bi)
        sc = self.a_sc[it % 3]
        sck = ("sc", it % 3)
        pt = self.a_pt[it % 3]
        ptk = ("pt", it % 3)
        for (lh, rh, c0, n) in mms:
            P.op("tensor", lambda e, lh=lh, rh=rh, c0=c0, n=n, st=st: e.matmul(st[:, c0:c0 + n], lhsT=lh, rhs=rh, start=True, stop=True),
                 reads=extra_reads, writes=[stk])
        P.op("vector", lambda e, st=st, sc=sc, tb_ap=tb_ap, ncols=ncols: e.tensor_tensor(out=sc[:, 0:ncols], in0=st[:, 0:ncols], in1=tb_ap, op=ALU.add),
             reads=[stk, tbkey], writes=[sck])
        P.op("scalar", lambda e, sc=sc, pt=pt, ncols=ncols: e.activation(out=pt[:, 0:ncols], in_=sc[:, 0:ncols], func=AF.Exp),
             reads=[sck], writes=[ptk])
        q = self.a_queue
        q.append((pt, ptk, pvs, list(extra_reads), post))
        while len(q) > 2:
            self._attn_back(q.pop(0))

    def _attn_back(self, pend):
        P = self.P
        pt, ptk, pvs, extra_reads, post = pend
        for (c0, v_ap, ab, wdt, s0, s1) in pvs:
            acc = self.banks[ab]
            P.op("tensor", lambda e, c0=c0, v_ap=v_ap, acc=acc, wdt=wdt, s0=s0, s1=s1, pt=pt: e.matmul(acc[:, 0:wdt], lhsT=pt[:, c0:c0 + 128], rhs=v_ap, start=s0, stop=s1),
                 reads=[ptk] + extra_reads, writes=[("bank", ab)])
        if post is not None:
            post()

    def attn_flush(self):
        q = self.a_queue
        while q:
            self._attn_back(q.pop(0))

    def out_transpose(self, on, onk, chunk, qt, trk):
        P = self.P
        bi = 7
        pT = self.bank_bf(bi)
        P.op("tensor", lambda e, on=on, pT=pT: e.transpose(out=pT[:, 0, :], in_=on, identity=self.ident[:]),
             reads=[onk, "ident"], writes=[("bank", bi)])
        P.op("scalar", lambda e, pT=pT, chunk=chunk, qt=qt: e.activation(out=self.actT[:, chunk, 1 + qt * 128:1 + (qt + 1) * 128], in_=pT[:, 0, :], func=AF.Copy),
             reads=[("bank", bi)], writes=[("actT", chunk, qt)])

    def phase_attn(self, li):
        P = self.P
        L = LAYERS[li]
        kind = L["kind"]
        pfx = "l%d_" % li
        self.scr_reset()
        nkb, nqb, FV = L["nkb"], L["nqb"], L["FV"]
        qT_d, kT_d, v_d = self.qT_d, self.kT_d, self.v_d
        ni = NI[kind]
        bt_d = self.din("bt_" + kind, [NUNIT_BT[kind], 128, ni * 128], F32).ap()
        self.a_sc = [self.scr("sc", [128, 512], F32) for _ in range(3)]
        self.a_pt = [self.scr("pt", [128, 512], BF16) for _ in range(3)]
        self.a_queue = []
        rr = [self.scr("rr", [128, 4], F32) for _ in range(4)]
        ons = [self.scr("on", [128, 128], BF16) for _ in range(2)]
        v_r = v_d.rearrange("(kb p) f -> p kb f", p=128)
        qkeys = [("qTd", b) for b in range(nqb)]
        kkeys = [("kTd", b) for b in range(nkb)]
        vkeys = [("vd", t, c0) for t in range(NT) for c0 in range(0, FV, 512 if FV >= 512 else 256)]
        it = 0
        fin = 0
        if kind == "B":
            dv = 128
            KTs = [self.scr("KT", [128, T], BF16) for _ in range(2)]
            Vs = [self.scr("V", [128, NT, dv + 1], BF16) for _ in range(2)]
            QTz = [self.scr("QT", [128, T], BF16) for _ in range(2)]
            TB = self.scr("TB", [128, ni * 128], F32)
            o1 = self.scr("o1", [128, NT, 128], F32)
            P.op("gpsimd", lambda e: e.memset(QTz[0][64:128, :], 0.0), writes=[("QTz", 0)])
            P.op("gpsimd", lambda e: e.memset(QTz[1][0:64, :], 0.0), writes=[("QTz", 1)])
            ods = [self.scr("od", [128, 128], F32) for _ in range(2)]
            lam = self.hbs[0][:].bitcast(F32)[:, 0:256].rearrange("p (a d) -> p a d", d=64)
            lamv = self.scr("lamv", [128, 4], F32)
            lam_d = self.din(pfx + "lam", [4, 64], F32).ap()
            P.op("sync", lambda e: e.dma_start(out=lam, in_=AP(lam_d.tensor, 0, [[0, 128], [64, 4], [1, 64]])), writes=["lam", ("hb", 0)], slot="lam")
            P.op("vector", lambda e: e.tensor_tensor(out=lam[:, 0, :], in0=lam[:, 0, :], in1=lam[:, 1, :], op=ALU.mult), reads=["lam"], writes=["lam"])
            P.op("vector", lambda e: e.tensor_tensor(out=lam[:, 2, :], in0=lam[:, 2, :], in1=lam[:, 3, :], op=ALU.mult), reads=["lam"], writes=["lam"])
            P.op("vector", lambda e: e.tensor_reduce(out=lamv[:, 0:1], in_=lam[:, 0, :], axis=AX.X, op=ALU.add), reads=["lam"], writes=["lamv"])
            P.op("vector", lambda e: e.tensor_reduce(out=lamv[:, 1:2], in_=lam[:, 2, :], axis=AX.X, op=ALU.add), reads=["lam"], writes=["lamv", ("hb", 0)])
            P.op("scalar", lambda e: e.activation(out=lamv[:, 0:2], in_=lamv[:, 0:2], func=AF.Exp), reads=["lamv"], writes=["lamv"])
            P.op("vector", lambda e: e.scalar_tensor_tensor(out=lamv[:, 2:3], in0=lamv[:, 1:2], scalar=-lambda_init_fn(li), in1=lamv[:, 0:1], op0=ALU.add, op1=ALU.subtract),
                 reads=["lamv"], writes=["lamv"])
            neglam = lamv[:, 2:3]
            for vb in Vs:
                P.op("gpsimd", lambda e, vb=vb: e.memset(vb[:, :, dv:dv + 1], 1.0), writes=[("Vones", id(vb))])

            def load_head(h):
                s = h % 2
                P.op("sync", lambda e, h=h, s=s: e.dma_start(out=KTs[s][:], in_=kT_d[h]), reads=kkeys, writes=[("KT", s)], slot="kt%da" % s)
                for half in range(2):
                    P.op("sync", lambda e, h=h, s=s, half=half: e.dma_start(out=Vs[s][:, half * 16:(half + 1) * 16, 0:dv], in_=v_r[:, half * 16:(half + 1) * 16, h * dv:(h + 1) * dv]),
                         reads=vkeys, writes=[("V", s)], slot="v%d_%d" % (s, half))

            fin_box = [0]

            def fin_B(qc, j, h):
                for jq in range(4):
                    qt = qc * 4 + jq
                    acc = self.banks[2 + jq]
                    ak = ("bank", 2 + jq)
                    fin_box[0] += 1
                    fin = fin_box[0]
                    r = rr[fin % 4]
                    rk = ("rr", fin % 4)
                    P.op("vector", lambda e, acc=acc, r=r: e.reciprocal(out=r[:, 0:1], in_=acc[:, dv:dv + 1]), reads=[ak], writes=[rk])
                    if j == 0:
                        P.op("vector", lambda e, acc=acc, r=r, qt=qt: e.tensor_scalar(out=o1[:, qt, :], in0=acc[:, 0:dv], scalar1=r[:, 0:1], scalar2=None, op0=ALU.mult),
                             reads=[ak, rk], writes=[("o1", qt)])
                    else:
                        od = ods[fin % 2]
                        odk = ("od", fin % 2)
                        on = ons[fin % 2]
                        onk = ("on", fin % 2)
                        P.op("vector", lambda e, r=r: e.tensor_scalar(out=r[:, 1:2], in0=r[:, 0:1], scalar1=neglam, scalar2=None, op0=ALU.mult),
                             reads=[rk, "lamv"], writes=[rk])
                        P.op("vector", lambda e, acc=acc, r=r, qt=qt, od=od: e.scalar_tensor_tensor(out=od[:], in0=acc[:, 0:dv], scalar=r[:, 1:2], in1=o1[:, qt, :], op0=ALU.mult, op1=ALU.add),
                             reads=[ak, rk, ("o1", qt)], writes=[odk])
                        P.op("vector", lambda e, r=r: e.memset(r[:, 2:3], 0.0), reads=[], writes=[rk])
                        P.op("scalar", lambda e, od=od, on=on, r=r: e.activation(out=on[:], in_=od[:], func=AF.Square, accum_out=r[:, 2:3]),
                             reads=[odk, rk], writes=[rk, onk])
                        P.op("scalar", lambda e, r=r: e.activation(out=r[:, 2:3], in_=r[:, 2:3], func=AF.Ln, scale=1.0 / 128, bias=self.epsb[:, 0:1]),
                             reads=[rk, "eps"], writes=[rk])
                        P.op("scalar", lambda e, r=r: e.activation(out=r[:, 2:3], in_=r[:, 2:3], func=AF.Exp, scale=-0.5), reads=[rk], writes=[rk])
                        P.op("vector", lambda e, od=od, on=on, r=r: e.tensor_scalar(out=on[:], in0=od[:], scalar1=r[:, 2:3], scalar2=None, op0=ALU.mult),
                             reads=[odk, rk], writes=[onk])
                        self.out_transpose(on[:], onk, h, qt, fin)

            load_head(0)
            for h in range(8):
                s = h % 2
                self.attn_flush()
                if h + 1 < 8:
                    load_head(h + 1)
                P.op("sync", lambda e, h=h: e.dma_start(out=QTz[0][0:64, :], in_=qT_d[h, 0:64, :]), reads=qkeys + [("QTz", 0)], writes=["QT"], slot="qt")
                P.op("sync", lambda e, h=h: e.dma_start(out=QTz[1][64:128, :], in_=qT_d[h, 64:128, :]), reads=qkeys + [("QTz", 1)], writes=["QT"], slot="qtb")
                KT, V = KTs[s], Vs[s]
                rds = [("KT", s), ("V", s), "QT", ("Vones", id(V))]
                for j in range(2):
                    for half in range(2):
                        hw = ni * 64
                        P.op("sync", lambda e, h=h, j=j, half=half, hw=hw: e.dma_start(out=TB[:, half * hw:(half + 1) * hw], in_=bt_d[h * 2 + j, :, half * hw:(half + 1) * hw]),
                             writes=["TB"], slot="tb%d" % half)
                    for qc in range(NT // 4):
                        for kb in range(NT):
                            dp = min(max(kb - 4 * qc, -12), 12)
                            i0 = 12 - dp
                            mms = [(KT[:, kb * 128:(kb + 1) * 128], QTz[j][:, qc * 512:(qc + 1) * 512], 0, 512)]
                            pvs = [(jq * 128, V[:, kb, 0:dv + 1], 2 + jq, dv + 1, kb == 0, kb == NT - 1) for jq in range(4)]
                            post = None
                            if kb == NT - 1:
                                post = (lambda qc=qc, j=j, h=h: fin_B(qc, j, h))
                            self.attn_item(it, mms, TB[:, i0 * 128:i0 * 128 + 512], 512, pvs, "TB", rds, post)
                            it += 1
            self.attn_flush()
        elif kind == "A":
            dv = 64
            NE = NT + 2
            KTs = [self.scr("KT", [64, NE * 128], BF16) for _ in range(2)]
            Vs = [self.scr("V", [128, NE, dv + 1], BF16) for _ in range(2)]
            QTs = [self.scr("QT", [64, T], BF16) for _ in range(2)]
            TBs = [self.scr("TB", [128, ni * 128], F32) for _ in range(2)]
            pairs = [self.scr("pair", [128, NT, 128], BF16) for _ in range(2)]
            esink = self.scr("esink", [128, 16], F32)
            sink_d = self.din(pfx + "sink", [16], F32).ap()
            P.op("sync", lambda e: e.dma_start(out=esink[:], in_=sink_d.partition_broadcast(128)), writes=["esink"], slot="esink")
            P.op("scalar", lambda e: e.activation(out=esink[:], in_=esink[:], func=AF.Exp), reads=["esink"], writes=["esink"])
            for s_i in range(2):
                vb, kb_ = Vs[s_i], KTs[s_i]
                P.op("gpsimd", lambda e, vb=vb: e.memset(vb[:], 0.0), writes=[("Vones", s_i), ("V", s_i)])
                P.op("gpsimd", lambda e, vb=vb: e.memset(vb[:, 1:NE - 1, dv:dv + 1], 1.0), writes=[("Vones", s_i), ("V", s_i)])
                P.op("gpsimd", lambda e, kb_=kb_: e.memset(kb_[:], 0.0), writes=[("KT", s_i)])

            def load_kv(kvh):
                s = kvh % 2
                blk, r0 = kvh // 2, (kvh % 2) * 64
                P.op("sync", lambda e: e.dma_start(out=KTs[s][:, 128:128 + T], in_=kT_d[blk, r0:r0 + 64, :]), reads=kkeys, writes=[("KT", s)], slot="kt%db" % s)
                for half in range(2):
                    P.op("sync", lambda e, half=half: e.dma_start(out=Vs[s][:, 1 + half * 16:1 + (half + 1) * 16, 0:dv], in_=v_r[:, half * 16:(half + 1) * 16, kvh * dv:(kvh + 1) * dv]),
                         reads=vkeys, writes=[("V", s)], slot="v%db%d" % (s, half))

            def load_q(h):
                s = h % 2
                P.op("sync", lambda e: e.dma_start(out=QTs[s][:], in_=qT_d[h // 2, (h % 2) * 64:(h % 2) * 64 + 64, :]), reads=qkeys, writes=[("QT", s)], slot="qt%d" % s)
                P.op("sync", lambda e: e.dma_start(out=TBs[s][:], in_=bt_d[h]), writes=[("TB", s)], slot="tb%d" % s)

            fin_box = [0]
            load_kv(0)
            load_q(0)
            for h in range(16):
                kvh = h // 4
                s = kvh % 2
                self.attn_flush()
                if h % 4 == 0 and kvh + 1 < 4:
                    load_kv(kvh + 1)
                if h + 1 < 16:
                    load_q(h + 1)
                KT, V, QT, TB = KTs[s], Vs[s], QTs[h % 2], TBs[h % 2]
                pair = pairs[(h // 2) % 2]
                rds = [("KT", s), ("V", s), ("QT", h % 2), ("Vones", s)]
                for qt in range(NT):
                    ab = 2 + (it % 4)
                    mms = [(KT[0:64, (qt + 2 - i) * 128:(qt + 3 - i) * 128], QT[0:64, qt * 128:(qt + 1) * 128], i * 128, 128) for i in range(3)]
                    pvs = [(i * 128, V[:, qt + 2 - i, 0:dv + 1], ab, dv + 1, i == 0, i == 2) for i in range(3)]
                    def post_A(acc=self.banks[ab], ak=("bank", ab), qt=qt, h=h, pair=pair):
                        fin_box[0] += 1
                        fin = fin_box[0]
                        r = rr[fin % 4]
                        rk = ("rr", fin % 4)
                        P.op("vector", lambda e, acc=acc, r=r, h=h: e.tensor_tensor(out=r[:, 0:1], in0=acc[:, dv:dv + 1], in1=esink[:, h:h + 1], op=ALU.add),
                             reads=[ak, "esink"], writes=[rk])
                        P.op("vector", lambda e, r=r: e.reciprocal(out=r[:, 1:2], in_=r[:, 0:1]), reads=[rk], writes=[rk])
                        P.op("vector", lambda e, acc=acc, r=r, pair=pair, qt=qt, h=h: e.tensor_scalar(out=pair[:, qt, (h % 2) * 64:(h % 2) * 64 + 64], in0=acc[:, 0:dv], scalar1=r[:, 1:2], scalar2=None, op0=ALU.mult),
                             reads=[ak, rk], writes=[("pair", (h // 2) % 2, qt, h % 2)])
                    self.attn_item(it, mms, TB[:, 0:384], 384, pvs, ("TB", h % 2), rds, post_A)
                    it += 1
                if h % 2 == 1:
                    self.attn_flush()
                    for qt in range(NT):
                        fin += 1
                        bi = 7
                        pT = self.bank_bf(bi)
                        P.op("tensor", lambda e, pair=pair, qt=qt, pT=pT: e.transpose(out=pT[:, 0, :], in_=pair[:, qt, :], identity=self.ident[:]),
                             reads=[("pair", (h // 2) % 2, qt, 0), ("pair", (h // 2) % 2, qt, 1), "ident"], writes=[("bank", bi)])
                        P.op("scalar", lambda e, pT=pT, h=h, qt=qt: e.activation(out=self.actT[:, h // 2, 1 + qt * 128:1 + (qt + 1) * 128], in_=pT[:, 0, :], func=AF.Copy),
                             reads=[("bank", bi)], writes=[("actT", h // 2, qt)])
        else:
            dv = 128
            NE = NT + 16
            KT = self.scr("KT", [128, NE * 128], BF16)
            V = self.scr("V", [128, NE, dv + 1], BF16)
            QT = self.scr("QT", [128, 3, T], BF16)
            TB = self.scr("TB", [128, ni * 128], F32)
            P.op("gpsimd", lambda e: e.memset(V[:], 0.0), writes=["Vones", "V"])
            P.op("gpsimd", lambda e: e.memset(V[:, 8:NE - 8, dv:dv + 1], 1.0), writes=["Vones", "V"])
            P.op("gpsimd", lambda e: e.memset(KT[:], 0.0), writes=["KT"])
            ents = [(0, d_) for d_ in (1, 0, -1)] + [(1, d_) for d_ in (2, 1, 0, -1, -2)] + [(2, d_) for d_ in range(8, -9, -1)]
            fin_box = [0]
            for j in range(4):
                self.attn_flush()
                P.op("sync", lambda e, j=j: e.dma_start(out=KT[:, 1024:1024 + T], in_=kT_d[j]), reads=kkeys, writes=["KT"], slot="ktb")
                for half in range(2):
                    P.op("sync", lambda e, j=j, half=half: e.dma_start(out=V[:, 8 + half * 16:8 + (half + 1) * 16, 0:dv], in_=v_r[:, half * 16:(half + 1) * 16, j * dv:(j + 1) * dv]),
                         reads=vkeys, writes=["V"], slot="vb%d" % half)
                for g in range(3):
                    P.op("sync", lambda e, g=g, j=j: e.dma_start(out=QT[:, g, :], in_=qT_d[g * 4 + j]), reads=qkeys, writes=["QT"], slot="qt%d" % g)
                P.op("sync", lambda e, j=j: e.dma_start(out=TB[:], in_=bt_d[j]), writes=["TB"], slot="tb")
                rds = ["KT", "V", "QT", "Vones"]
                for qt in range(NT):
                    ab = 2 + (qt % 4)
                    for i0 in range(0, 25, 4):
                        grp = list(range(i0, min(i0 + 4, 25)))
                        mms = []
                        pvs = []
                        for n_, idx in enumerate(grp):
                            g, dl = ents[idx]
                            eb = qt + 8 + dl
                            mms.append((KT[:, eb * 128:(eb + 1) * 128], QT[:, g, qt * 128:(qt + 1) * 128], n_ * 128, 128))
                            pvs.append((n_ * 128, V[:, eb, 0:dv + 1], ab, dv + 1, idx == 0, idx == 24))
                        post = None
                        if grp[-1] == 24:
                            def post(acc=self.banks[ab], ak=("bank", ab), qt=qt, j=j):
                                fin_box[0] += 1
                                fin = fin_box[0]
                                r = rr[fin % 4]
                                rk = ("rr", fin % 4)
                                on = ons[fin % 2]
                                onk = ("on", fin % 2)
                                P.op("vector", lambda e, acc=acc, r=r: e.reciprocal(out=r[:, 0:1], in_=acc[:, dv:dv + 1]), reads=[ak], writes=[rk])
                                P.op("vector", lambda e, acc=acc, r=r, on=on: e.tensor_scalar(out=on[:], in0=acc[:, 0:dv], scalar1=r[:, 0:1], scalar2=None, op0=ALU.mult),
                                     reads=[ak, rk], writes=[onk])
                                self.out_transpose(on[:], onk, j, qt, fin)
                        self.attn_item(it, mms, TB[:, i0 * 128:(i0 + len(grp)) * 128], len(grp) * 128, pvs, "TB", rds, post)
                        it += 1
            self.attn_flush()

    def phase_wo(self, li):
        P = self.P
        L = LAYERS[li]
        pfx = "l%d_" % li
        nfc = L["nfc"]
        W = self.din(pfx + "w_o", [nfc * 128, D], F32)
        scale = None
        srd = ()
        if L["kind"] == "B":
            sg = self.din(pfx + "sgcol", [128, 1], F32).ap()
            sgt = self.scr("sgt", [128, 1], F32)
            P.op("sync", lambda e: e.dma_start(out=sgt[:], in_=sg), writes=["sgt"], slot="sgt")
            P.op("vector", lambda e: e.tensor_scalar(out=sgt[:], in0=sgt[:], scalar1=1.0 - lambda_init_fn(li), scalar2=None, op0=ALU.mult), reads=["sgt"], writes=["sgt"])
            scale = lambda k: sgt[:, 0:1]
            srd = ["sgt"]
        kch = list(range(nfc))
        units = [self.load_unit(W, kch, [(n * 512, 512)], scale=scale, scale_reads=srd) for n in range(2)]
        pend = [self.x_load(0, 0), self.x_load(0, 1)]
        for t in range(NT):
            for n in range(2):
                nx = t * 2 + n + 2
                if nx < NT * 2:
                    pend.append(self.x_load(nx // 2, nx % 2))
                slot, wkeys = units[n]
                wb = self.wb[slot]
                bi = 2 + n
                py = self.banks[bi]
                for c in range(nfc):
                    P.op("tensor", lambda e, c=c, t=t, py=py, wb=wb: e.matmul(py[:, 0:512], lhsT=self.actT[:, c, 1 + t * 128:1 + (t + 1) * 128], rhs=wb[:, c, 0:512], start=(c == 0), stop=(c == nfc - 1)),
                         reads=[("actT", c, t)] + wkeys, writes=[("bank", bi)])
                xt, xk, xi = pend[t * 2 + n]
                P.op("vector", lambda e, xt=xt, py=py: e.tensor_tensor(out=xt[:, 0:512], in0=py[:, 0:512], in1=xt[:, 0:512], op=ALU.add),
                     reads=[("bank", bi), xk], writes=[xk])
                self.x_store(t, n, xt, xk, xi)

    def phase_ffn(self, li):
        P = self.P
        pfx = "l%d_" % li
        self.load_gcol(pfx + "gcol_ffn")
        self.phase_norm()
        self.scr_reset()
        Wu = self.din(pfx + "w_up", [D, 2 * DFF], F32)
        Wd = self.din(pfx + "w_down", [DFF, D], F32)
        cw_d = self.din(pfx + "convp", [128, 4, 44], F32).ap()
        cw = self.scr("cw", [128, 4, 44], F32)
        P.op("sync", lambda e: e.dma_start(out=cw[:], in_=cw_d), writes=["cw"], slot="cw")
        wu_keys = self.precast(Wu, 8, 2 * DFF, self.wu_bf, "wubf", True)
        wd_keys = self.precast(Wd, 22, D, self.wd_bf, "wdbf", False)
        TC = 1024
        NK = T // TC
        a = self.scr("a", [128, 22, TC], BF16)
        Us = [[self.scr("U", [128, TC + 2], F32) for _ in range(2)] for _ in range(2)]
        T1 = [self.scr("T1", [128, TC], F32) for _ in range(2)]
        gsc = lambda k: self.gcol[:, k:k + 1]
        up_units = [[(f0 * 128, 256), (DFF + f0 * 128, 256)] for f0 in range(0, 22, 2)]
        dn_units = [(kc, n) for n in range(2) for kc in (list(range(0, 8)), list(range(8, 16)), list(range(16, 22)))]
        pcount = 0
        for k in range(NK):
            cb = TC * k
            allr = [("actT", c, t) for c in range(8) for t in range(max(8 * k - 1, 0), min(8 * k + 9, NT))] + ["halo"]
            nxt = self.load_unit_bf(self.wu_bf, list(range(8)), up_units[0], wu_keys)
            for ui in range(11):
                cur = nxt
                if ui + 1 < 11:
                    nxt = self.load_unit_bf(self.wu_bf, list(range(8)), up_units[ui + 1], wu_keys)
                slot, wkeys = cur
                wb = self.wb[slot]
                for pi in range(2):
                    f = ui * 2 + pi
                    par = pcount % 2
                    pcount += 1
                    for which in range(2):
                        off = which * 256 + pi * 128
                        fc = f + 22 * which
                        bA, bB, bE = self.banks[4 * which], self.banks[4 * which + 1], self.banks[4 * which + 2]
                        bks = [("bank", 4 * which + i) for i in range(3)]
                        for c in range(8):
                            P.op("tensor", lambda e, c=c, wb=wb, off=off, bA=bA, cb=cb: e.matmul(bA[:, 0:512], lhsT=wb[:, c, off:off + 128], rhs=self.actT[:, c, cb + 1:cb + 513], start=(c == 0), stop=(c == 7)),
                                 reads=allr + wkeys, writes=[bks[0]])
                        for c in range(8):
                            P.op("tensor", lambda e, c=c, wb=wb, off=off, bB=bB, cb=cb: e.matmul(bB[:, 0:512], lhsT=wb[:, c, off:off + 128], rhs=self.actT[:, c, cb + 513:cb + 1025], start=(c == 0), stop=(c == 7)),
                                 reads=allr + wkeys, writes=[bks[1]])
                        for c in range(8):
                            P.op("tensor", lambda e, c=c, wb=wb, off=off, bE=bE, cb=cb: e.matmul(bE[:, 0:2], lhsT=wb[:, c, off:off + 128], rhs=self.actT[:, c, cb:cb + 1026:1025], start=(c == 0), stop=(c == 7)),
                                 reads=allr + wkeys, writes=[bks[2]])
                        U = Us[which][par]
                        uk = ("U", which, par)
                        t1 = T1[which]
                        tk = ("T1", which)
                        P.op("scalar", lambda e, U=U, bA=bA: e.activation(out=U[:, 1:513], in_=bA[:, 0:512], func=AF.Copy), reads=[bks[0]], writes=[(uk, 0)])
                        P.op("scalar", lambda e, U=U, bB=bB: e.activation(out=U[:, 513:1025], in_=bB[:, 0:512], func=AF.Copy), reads=[bks[1]], writes=[(uk, 1)])
                        P.op("scalar", lambda e, U=U, bE=bE: e.activation(out=U[:, 0:1026:1025], in_=bE[:, 0:2], func=AF.Copy), reads=[bks[2]], writes=[(uk, 2)])
                        P.op("scalar", lambda e, t1=t1, bA=bA, fc=fc: e.activation(out=t1[:, 0:512], in_=bA[:, 0:512], func=AF.Identity, scale=cw[:, 1, fc:fc + 1], bias=cw[:, 3, fc:fc + 1]),
                             reads=[bks[0], "cw"], writes=[(tk, 0)])
                        P.op("scalar", lambda e, t1=t1, bB=bB, fc=fc: e.activation(out=t1[:, 512:1024], in_=bB[:, 0:512], func=AF.Identity, scale=cw[:, 1, fc:fc + 1], bias=cw[:, 3, fc:fc + 1]),
                             reads=[bks[1], "cw"], writes=[(tk, 1)])
                        P.op("vector", lambda e, t1=t1, U=U, fc=fc: e.scalar_tensor_tensor(out=t1[:], in0=U[:, 0:TC], scalar=cw[:, 0, fc:fc + 1], in1=t1[:], op0=ALU.mult, op1=ALU.add),
                             reads=[(uk, 0), (uk, 1), (uk, 2), (tk, 0), (tk, 1), "cw"], writes=[(tk, 0), (tk, 1)])
                        P.op("vector", lambda e, t1=t1, U=U, fc=fc: e.scalar_tensor_tensor(out=t1[:], in0=U[:, 2:TC + 2], scalar=cw[:, 2, fc:fc + 1], in1=t1[:], op0=ALU.mult, op1=ALU.add),
                             reads=[(uk, 0), (uk, 1), (uk, 2), (tk, 0), (tk, 1), "cw"], writes=[(tk, 0), (tk, 1)])
                    P.op("scalar", lambda e: e.activation(out=T1[0][:], in_=T1[0][:], func=AF.Silu), reads=[(("T1", 0), 0), (("T1", 0), 1)], writes=[(("T1", 0), 0), (("T1", 0), 1)])
                    P.op("vector", lambda e, f=f: e.tensor_tensor(out=a[:, f, :], in0=T1[0][:], in1=T1[1][:], op=ALU.mult),
                         reads=[(("T1", 0), 0), (("T1", 0), 1), (("T1", 1), 0), (("T1", 1), 1)], writes=[("a", f)])
            nxt = self.load_unit_bf(self.wd_bf, dn_units[0][0], [(dn_units[0][1] * 512, 512)], wd_keys)
            for di, (kc, n) in enumerate(dn_units):
                cur = nxt
                if di + 1 < len(dn_units):
                    nxt = self.load_unit_bf(self.wd_bf, dn_units[di + 1][0], [(dn_units[di + 1][1] * 512, 512)], wd_keys)
                slot, wkeys = cur
                wb = self.wb[slot]
                if kc[0] == 0:
                    pend = [self.x_load(8 * k + t, n) for t in range(2)]
                for t in range(8):
                    for ci, f in enumerate(kc):
                        P.op("tensor", lambda e, t=t, ci=ci, f=f, wb=wb: e.matmul(self.banks[t][:, 0:512], lhsT=a[:, f, t * 128:(t + 1) * 128], rhs=wb[:, ci, 0:512], start=(f == 0), stop=(f == 21)),
                             reads=[("a", f)] + wkeys, writes=[("bank", t)])
                if kc[-1] == 21:
                    for t in range(8):
                        tt = 8 * k + t
                        if t + 2 < 8:
                            pend.append(self.x_load(8 * k + t + 2, n))
                        xt, xk, xi = pend[t]
                        P.op("vector", lambda e, t=t, xt=xt: e.tensor_tensor(out=xt[:, 0:512], in0=self.banks[t][:, 0:512], in1=xt[:, 0:512], op=ALU.add),
                             reads=[("bank", t), xk], writes=[xk])
                        self.x_store(tt, n, xt, xk, xi)

    def finish(self):
        self.P.finalize()
        return self.nc


def rel_bucket_np(rel):
    nb = 16
    max_exact = 8
    n = np.abs(rel)
    nf = np.maximum(n, 1).astype(np.float32)
    large = max_exact + (np.log(nf / np.float32(max_exact)) / np.float32(math.log(1024 / max_exact)) * np.float32(nb - max_exact)).astype(np.int32)
    large = np.minimum(large, nb - 1)
    return np.where(rel > 0, nb, 0) + np.where(n < max_exact, n, large)


def bias_tables(rel_bias, kind):
    rb = np.asarray(rel_bias, np.float32)
    kp = np.arange(128)[:, None]
    qp = np.arange(128)[None, :]
    if kind == "A":
        out = np.empty((16, 128, 3 * 128), np.float32)
        for i, dl in enumerate((1, 0, -1)):
            rel = dl * 128 + kp - qp
            bk = rel_bucket_np(rel)
            ok = np.abs(rel) <= 128
            for h in range(16):
                out[h, :, i * 128:(i + 1) * 128] = np.where(ok, rb[bk, h], NEG)
        return out
    if kind == "B":
        out = np.empty((16, 128, 28 * 128), np.float32)
        for i in range(28):
            dl = 12 - i
            rel = dl * 128 + kp - qp
            bk = rel_bucket_np(rel)
            for m in range(16):
                out[m, :, i * 128:(i + 1) * 128] = rb[bk, m]
        return out
    ents = [(0, d_) for d_ in (1, 0, -1)] + [(1, d_) for d_ in (2, 1, 0, -1, -2)] + [(2, d_) for d_ in range(8, -9, -1)]
    dils = (1, 4, 16)
    out = np.empty((4, 128, 25 * 128), np.float32)
    for i, (g, dl) in enumerate(ents):
        rel = dl * 128 + kp - qp
        dil = dils[g]
        ok = (rel % dil == 0) & (np.abs(rel) <= 64 * dil)
        bk = rel_bucket_np(rel)
        for j in range(4):
            out[j, :, i * 128:(i + 1) * 128] = np.where(ok, rb[bk, g * 4 + j], NEG)
    return out


def gcols(g):
    return np.ascontiguousarray(np.asarray(g, np.float32).reshape(8, 128).T)


_PROG = None
DEBUG_LAYERS = 4


def get_prog():
    global _PROG
    if _PROG is None:
        b = Builder()
        for li in range(DEBUG_LAYERS):
            b.phase_qkv(li)
            b.phase_attn(li)
            b.phase_wo(li)
            b.phase_ffn(li)
        nc = b.finish()
        _PROG = (nc, list(b.din_names), list(b.dout_names))
    return _PROG


def kernel(x, rel_bias,
           l0_attn_norm, l0_w_qkv, l0_q_gain, l0_k_gain, l0_sink, l0_w_o,
           l0_ffn_norm, l0_w_up, l0_conv_w, l0_conv_b, l0_w_down,
           l1_attn_norm, l1_w_qkv, l1_q_gain, l1_k_gain, l1_lambda_q1, l1_lambda_k1,
           l1_lambda_q2, l1_lambda_k2, l1_sub_gain, l1_w_o,
           l1_ffn_norm, l1_w_up, l1_conv_w, l1_conv_b, l1_w_down,
           l2_attn_norm, l2_w_qkv, l2_q_gain, l2_k_gain, l2_w_o,
           l2_ffn_norm, l2_w_up, l2_conv_w, l2_conv_b, l2_w_down,
           l3_attn_norm, l3_w_qkv, l3_q_gain, l3_k_gain, l3_sink, l3_w_o,
           l3_ffn_norm, l3_w_up, l3_conv_w, l3_conv_b, l3_w_down):
    inp = {
        "x": x,
        "rel_bias": rel_bias,
        "l0_attn_norm": l0_attn_norm,
        "l0_w_qkv": l0_w_qkv,
        "l0_q_gain": l0_q_gain,
        "l0_k_gain": l0_k_gain,
        "l0_sink": l0_sink,
        "l0_w_o": l0_w_o,
        "l0_ffn_norm": l0_ffn_norm,
        "l0_w_up": l0_w_up,
        "l0_conv_w": l0_conv_w,
        "l0_conv_b": l0_conv_b,
        "l0_w_down": l0_w_down,
        "l1_attn_norm": l1_attn_norm,
        "l1_w_qkv": l1_w_qkv,
        "l1_q_gain": l1_q_gain,
        "l1_k_gain": l1_k_gain,
        "l1_lambda_q1": l1_lambda_q1,
        "l1_lambda_k1": l1_lambda_k1,
        "l1_lambda_q2": l1_lambda_q2,
        "l1_lambda_k2": l1_lambda_k2,
        "l1_sub_gain": l1_sub_gain,
        "l1_w_o": l1_w_o,
        "l1_ffn_norm": l1_ffn_norm,
        "l1_w_up": l1_w_up,
        "l1_conv_w": l1_conv_w,
        "l1_conv_b": l1_conv_b,
        "l1_w_down": l1_w_down,
        "l2_attn_norm": l2_attn_norm,
        "l2_w_qkv": l2_w_qkv,
        "l2_q_gain": l2_q_gain,
        "l2_k_gain": l2_k_gain,
        "l2_w_o": l2_w_o,
        "l2_ffn_norm": l2_ffn_norm,
        "l2_w_up": l2_w_up,
        "l2_conv_w": l2_conv_w,
        "l2_conv_b": l2_conv_b,
        "l2_w_down": l2_w_down,
        "l3_attn_norm": l3_attn_norm,
        "l3_w_qkv": l3_w_qkv,
        "l3_q_gain": l3_q_gain,
        "l3_k_gain": l3_k_gain,
        "l3_sink": l3_sink,
        "l3_w_o": l3_w_o,
        "l3_ffn_norm": l3_ffn_norm,
        "l3_w_up": l3_w_up,
        "l3_conv_w": l3_conv_w,
        "l3_conv_b": l3_conv_b,
        "l3_w_down": l3_w_down,
    }
    x = np.ascontiguousarray(np.asarray(x, np.float32))
    rel_bias = np.asarray(rel_bias, np.float32)
    shared = {"ident": np.eye(128, dtype=np.float32)}
    for kind in ("A", "B", "C"):
        shared["bt_" + kind] = bias_tables(rel_bias, kind)
    f32 = lambda a: np.ascontiguousarray(np.asarray(a, np.float32))
    for li in range(4):
        L = LAYERS[li]
        p = "l%d_" % li
        shared[p + "gcol_attn"] = gcols(inp[p + "attn_norm"])
        shared[p + "gcol_ffn"] = gcols(inp[p + "ffn_norm"])
        for w in ("w_qkv", "w_o", "w_up", "w_down"):
            shared[p + w] = f32(inp[p + w])
        shared[p + "qkg"] = np.ascontiguousarray(np.stack([f32(inp[p + "q_gain"]), f32(inp[p + "k_gain"])]))
        cwv = f32(inp[p + "conv_w"]).reshape(3, 44, 128)
        cbv = f32(inp[p + "conv_b"]).reshape(1, 44, 128)
        shared[p + "convp"] = np.ascontiguousarray(np.concatenate([cwv, cbv], 0).transpose(2, 0, 1))
        if L["kind"] == "A":
            shared[p + "sink"] = f32(inp[p + "sink"])
        if L["kind"] == "B":
            shared[p + "lam"] = np.ascontiguousarray(np.stack([f32(inp[p + k]) for k in ("lambda_q1", "lambda_k1", "lambda_q2", "lambda_k2")]))
            shared[p + "sgcol"] = f32(inp[p + "sub_gain"]).reshape(128, 1)
    import time as _t
    _t1 = _t.time()
    nc, dins, douts = get_prog()
    print("[kernel] host prep + build took %.1fs" % (_t.time() - _t1), flush=True)
    in_maps = []
    for c in range(NCORES):
        m = {}
        for n in dins:
            m[n] = x[c] if n == "x" else shared[n]
        in_maps.append(m)
    import time as _t
    _t0 = _t.time()
    res = run_bass_kernel_spmd(nc, in_maps, core_ids=list(range(NCORES)))
    print("[kernel] run_bass_kernel_spmd took %.1fs" % (_t.time() - _t0), flush=True)
    return np.stack([res.results[c]["x_out"] for c in range(NCORES)]).astype(np.float32)
```

```python
import math
from contextlib import ExitStack

import numpy as np
import ml_dtypes

import concourse.bass as bass
import concourse.mybir as mybir
from concourse.ap import AP
from concourse.bass_utils import run_bass_kernel_spmd

F32 = mybir.dt.float32
BF16 = mybir.dt.bfloat16
AF = mybir.ActivationFunctionType
ALU = mybir.AluOpType
AX = mybir.AxisListType
NPBF = ml_dtypes.bfloat16

NCORES = 4
T = 4096
NT = 32
D = 1024
DFF = 2816
EPS = 1e-6
NEG = -30000.0

LAYERS = [
    dict(kind="A", F=1536, nqb=8, nkb=2, FV=256, nfc=8),
    dict(kind="B", F=3072, nqb=8, nkb=8, FV=1024, nfc=8),
    dict(kind="C", F=2560, nqb=12, nkb=4, FV=512, nfc=4),
    dict(kind="A", F=1536, nqb=8, nkb=2, FV=256, nfc=8),
]
NI = {"A": 3, "B": 28, "C": 25}
NUNIT_BT = {"A": 16, "B": 16, "C": 4}


def lambda_init_fn(layer):
    return 0.8 - 0.6 * math.exp(-0.3 * layer)


class Op:
    __slots__ = ("eng", "fn", "deps", "needs_inc", "val", "semkey", "is_dma", "idx")

    def __init__(self, eng, fn, deps, semkey, is_dma):
        self.eng = eng
        self.fn = fn
        self.deps = deps
        self.needs_inc = False
        self.val = None
        self.semkey = semkey
        self.is_dma = is_dma


class Prog:
    ENGS = ("sync", "scalar", "vector", "gpsimd", "tensor")

    def __init__(self, nc):
        self.nc = nc
        self.ops = {e: [] for e in self.ENGS}
        self.lastw = {}
        self.readers = {}
        self.es = ExitStack()
        self.outs = []
        self.fence = []
        self.fence_pending = set()
        self.last_dma = {}
        self.epoch = 0

    def barrier(self):
        fence = []
        for e in self.ENGS:
            for o in reversed(self.ops[e]):
                if not o.is_dma:
                    fence.append(o)
                    break
        fence.extend(self.last_dma.values())
        self.fence = fence
        self.fence_pending = set(self.ENGS)
        self.epoch += 1

    def op(self, eng, fn, reads=(), writes=(), slot=None, out=False):
        deps = []
        if eng in self.fence_pending:
            deps.extend(self.fence)
            self.fence_pending.discard(eng)
        for b in reads:
            w = self.lastw.get(b)
            if w is not None:
                deps.append(w)
        for b in writes:
            w = self.lastw.get(b)
            if w is not None:
                deps.append(w)
            deps.extend(self.readers.get(b, ()))
        is_dma = slot is not None
        semkey = ("dma", slot) if is_dma else ("eng", eng, self.epoch % 3)
        o = Op(eng, fn, deps, semkey, is_dma)
        o.idx = len(self.ops[eng])
        self.ops[eng].append(o)
        if is_dma:
            self.last_dma[slot] = o
        for b in writes:
            self.lastw[b] = o
            self.readers[b] = []
        for b in reads:
            self.readers.setdefault(b, []).append(o)
        if out:
            self.outs.append(o)
        return o

    @staticmethod
    def _skip(d, o):
        return d is o or (d.eng == "tensor" and o.eng == "tensor" and not d.is_dma and not o.is_dma)

    def finalize(self):
        nc = self.nc
        final_waits = self.outs
        for e in self.ENGS:
            for o in self.ops[e]:
                best = {}
                for d in o.deps:
                    if self._skip(d, o):
                        continue
                    if d.is_dma:
                        d.needs_inc = True
                        continue
                    b = best.get(d.semkey)
                    if b is None or d.idx > b.idx:
                        best[d.semkey] = d
                for d in best.values():
                    d.needs_inc = True
        for d in final_waits:
            d.needs_inc = True
        for e in self.ENGS:
            for o in self.ops[e]:
                if o.is_dma:
                    o.needs_inc = True
        counters = {}
        for e in self.ENGS:
            for o in self.ops[e]:
                if o.needs_inc:
                    c = counters.get(o.semkey, 0) + (16 if o.is_dma else 1)
                    counters[o.semkey] = c
                    o.val = c
        sems = {}
        for i, k in enumerate(counters):
            sems[k] = self.es.enter_context(nc.semaphore("s%d" % i))
        self.nsem = len(sems)
        block = self.es.enter_context(nc.Block())

        def run(e, engine):
            known = {}
            for o in self.ops[e]:
                need = {}
                for d in o.deps:
                    if self._skip(d, o) or d.val is None:
                        continue
                    if need.get(d.semkey, 0) < d.val:
                        need[d.semkey] = d.val
                for k, v in need.items():
                    if known.get(k, 0) >= v:
                        continue
                    engine.wait_ge(sems[k], v)
                    known[k] = v
                ins = o.fn(engine)
                if o.needs_inc:
                    ins.then_inc(sems[o.semkey], 16 if o.is_dma else 1)
            if e == "sync":
                need = {}
                for d in final_waits:
                    if need.get(d.semkey, 0) < d.val:
                        need[d.semkey] = d.val
                for k, v in need.items():
                    engine.wait_ge(sems[k], v)

        @block.sync
        def _(eng):
            run("sync", eng)

        @block.scalar
        def _(eng):
            run("scalar", eng)

        @block.vector
        def _(eng):
            run("vector", eng)

        @block.gpsimd
        def _(eng):
            run("gpsimd", eng)

        @block.tensor
        def _(eng):
            run("tensor", eng)

        self.es.close()


SB_BASE = 16512
SCR_END = 229376


class Builder:
    def __init__(self):
        self.nc = bass.Bass("TRN2", target_bir_lowering=False)
        self.P = Prog(self.nc)
        self.din_names = []
        self.dout_names = []
        self.d = {}
        self.uid = 0
        self.perm_off = SB_BASE
        nc = self.nc
        self.actT = self.perm("actT", [128, 8, T + 2], BF16)
        self.ws = [self.perm("ws%d" % i, [128, 4, 512], F32) for i in range(2)]
        self.wb = [self.perm("wb%d" % i, [128, 8, 512], BF16) for i in range(2)]
        self.xts = [self.perm("xt%d" % i, [128, D], F32) for i in range(4)]
        self.hbs = [self.perm("hb%d" % i, [128, D], BF16) for i in range(2)]
        self.ident = self.perm("ident", [128, 128], BF16)
        self.identf = self.perm("identf", [128, 128], F32)
        self.sst = [self.perm("sst%d" % i, [128, 4], F32) for i in range(4)]
        self.epsb = self.perm("epsb", [128, 1], F32)
        self.gcol = self.perm("gcol", [128, 8], F32)
        self.PERM_END = (self.perm_off + 63) // 64 * 64
        self.scr_off = self.PERM_END
        self.nscr = 0
        self.xcnt = 0
        self.banks = [nc.alloc_psum_tensor("bank%d" % i, [128, 512], F32) for i in range(8)]
        self.wcount = 0
        self.xd = self.dout("x_out", [T, D], F32).ap()
        self.qT_d = nc.dram_tensor("qT_scr", [12, 128, T], BF16, kind="Internal").ap()
        self.kT_d = nc.dram_tensor("kT_scr", [8, 128, T], BF16, kind="Internal").ap()
        self.v_d = nc.dram_tensor("v_scr", [T, 1024], BF16, kind="Internal").ap()
        self.wu_bf = nc.dram_tensor("wu_bf", [D, 2 * DFF], BF16, kind="Internal").ap()
        self.wd_bf = nc.dram_tensor("wd_bf", [DFF, D], BF16, kind="Internal").ap()
        self.init_consts()

    def perm(self, name, shape, dt):
        n = int(np.prod(shape[1:])) * (4 if dt == F32 else 2)
        n = (n + 31) // 32 * 32
        t = self.nc.alloc_sbuf_tensor_at(name, shape, dt, offset=self.perm_off)
        self.perm_off += n
        return t

    def scr_reset(self):
        if self.nscr > 0:
            self.P.barrier()
        self.nscr += 1
        self.scr_off = self.PERM_END

    def scr(self, name, shape, dt):
        n = int(np.prod(shape[1:])) * (4 if dt == F32 else 2)
        n = (n + 31) // 32 * 32
        self.uid += 1
        t = self.nc.alloc_sbuf_tensor_at("%s_%d" % (name, self.uid), shape, dt, offset=self.scr_off)
        self.scr_off += n
        assert self.scr_off <= SCR_END, (name, self.scr_off)
        return t

    def din(self, name, shape, dt):
        if name not in self.d:
            self.d[name] = self.nc.dram_tensor(name, list(shape), dt, kind="ExternalInput")
            self.din_names.append(name)
        return self.d[name]

    def dout(self, name, shape, dt):
        if name not in self.d:
            self.d[name] = self.nc.dram_tensor(name, list(shape), dt, kind="ExternalOutput")
            self.dout_names.append(name)
        return self.d[name]

    def bank_bf(self, i):
        return self.banks[i][:].bitcast(BF16).rearrange("p (c t) -> p c t", t=128)

    def init_consts(self):
        P = self.P
        idd = self.din("ident", [128, 128], F32).ap()
        xin = self.din("x", [T, D], F32).ap()
        P.op("sync", lambda e: e.dma_start(out=self.identf[:], in_=idd), writes=["identf"], slot="c_id")
        P.op("vector", lambda e: e.tensor_copy(out=self.ident[:], in_=self.identf[:]), reads=["identf"], writes=["ident"])
        P.op("vector", lambda e: e.memset(self.epsb[:], EPS), writes=["eps"])
        P.op("vector", lambda e: e.memset(self.actT[:, :, 0:1], 0.0), writes=["halo"])
        P.op("vector", lambda e: e.memset(self.actT[:, :, T + 1:T + 2], 0.0), writes=["halo"])
        for t0 in range(0, NT, 8):
            P.op("sync", lambda e, t0=t0: e.dma_start(out=self.xd[t0 * 128:(t0 + 8) * 128, :], in_=xin[t0 * 128:(t0 + 8) * 128, :]),
                 writes=[("xd", t, n) for t in range(t0, t0 + 8) for n in range(2)], slot="xcp%d" % (t0 // 8), out=True)

    def x_load(self, t, n=None):
        P = self.P
        i = self.xcnt % 4
        self.xcnt += 1
        xt = self.xts[i]
        key = ("xt", i)
        if n is None:
            P.op("sync", lambda e, t=t, xt=xt: e.dma_start(out=xt[:], in_=self.xd[t * 128:(t + 1) * 128, :]),
                 reads=[("xd", t, 0), ("xd", t, 1)], writes=[key], slot="xl%d" % i)
        else:
            P.op("sync", lambda e, t=t, xt=xt, n=n: e.dma_start(out=xt[:, 0:512], in_=self.xd[t * 128:(t + 1) * 128, n * 512:(n + 1) * 512]),
                 reads=[("xd", t, n)], writes=[key], slot="xl%d" % i)
        return xt, key, i

    def x_store(self, t, n, xt, key, i):
        P = self.P
        P.op("gpsimd", lambda e, t=t, xt=xt, n=n: e.dma_start(out=self.xd[t * 128:(t + 1) * 128, n * 512:(n + 1) * 512], in_=xt[:, 0:512]),
             reads=[key], writes=[("xd", t, n)], slot="xs%d" % i, out=True)

    def load_unit(self, W, kchunks, segs, scale=None, scale_reads=()):
        P = self.P
        i = self.wcount
        self.wcount += 1
        slot = i % 2
        wb = self.wb[slot]
        key = ("wb", slot)
        Wa = W.ap()
        ncol = sum(s[1] for s in segs)
        halves = [kchunks[0:4], kchunks[4:8]]
        allkeys = []
        for hi, kc in enumerate(halves):
            if not kc:
                continue
            ws = self.ws[hi]
            co = 0
            for si, (c0, cn) in enumerate(segs):
                k0 = kc[0]
                src = Wa[k0 * 128:(k0 + len(kc)) * 128, c0:c0 + cn].rearrange("(c p) f -> p c f", p=128)
                P.op("sync", lambda e, ws=ws, src=src, co=co, cn=cn, n=len(kc): e.dma_start(out=ws[:, 0:n, co:co + cn], in_=src),
                     writes=[("ws", hi, si)], slot="ws%d_%d" % (hi, si))
                co += cn
            n = len(kc)
            if scale is None:
                P.op("gpsimd", lambda e, ws=ws, wb=wb, hi=hi, n=n, ncol=ncol: e.tensor_copy(out=wb[:, hi * 4:hi * 4 + n, 0:ncol], in_=ws[:, 0:n, 0:ncol]),
                     reads=[("ws", hi, si) for si in range(len(segs))], writes=[(key, hi, 0)])
                allkeys.append((key, hi, 0))
            else:
                for j, k in enumerate(kc):
                    sc = scale(k)
                    P.op("gpsimd", lambda e, ws=ws, wb=wb, hi=hi, j=j, ncol=ncol, sc=sc: e.tensor_scalar(out=wb[:, hi * 4 + j, 0:ncol], in0=ws[:, j, 0:ncol], scalar1=sc, scalar2=None, op0=ALU.mult),
                         reads=[("ws", hi, si) for si in range(len(segs))] + list(scale_reads), writes=[(key, hi, j)])
                    allkeys.append((key, hi, j))
        return slot, allkeys

    def precast(self, W, nchunks, ncols, Wbf, keyname, scaled):
        P = self.P
        Wa = W.ap()
        blocks = [(kg, min(4, nchunks - kg), c0) for kg in range(0, nchunks, 4) for c0 in range(0, ncols, 512)]
        keys = []
        for b, (kg, n, c0) in enumerate(blocks):
            ws = self.ws[b % 2]
            wsk = ("ws", b % 2, 0)
            ss_ = b % 4
            stg = self.wb[ss_ // 2][:, (ss_ % 2) * 4:(ss_ % 2) * 4 + n, :]
            stgk = (("wb", ss_ // 2), ss_ % 2, 0)
            src = Wa[kg * 128:(kg + n) * 128, c0:c0 + 512].rearrange("(c p) f -> p c f", p=128)
            P.op("sync", lambda e, ws=ws, src=src, n=n: e.dma_start(out=ws[:, 0:n, :], in_=src), writes=[wsk], slot="ws%d_0" % (b % 2))
            eng = "scalar" if b % 2 == 0 else "vector"
            if not scaled:
                if eng == "scalar":
                    P.op(eng, lambda e, ws=ws, stg=stg, n=n: e.activation(out=stg, in_=ws[:, 0:n, :], func=AF.Copy), reads=[wsk], writes=[stgk])
                else:
                    P.op(eng, lambda e, ws=ws, stg=stg, n=n: e.tensor_copy(out=stg, in_=ws[:, 0:n, :]), reads=[wsk], writes=[stgk])
            else:
                for j in range(n):
                    sc = self.gcol[:, kg + j:kg + j + 1]
                    if eng == "scalar":
                        P.op(eng, lambda e, ws=ws, stg=stg, j=j, sc=sc: e.activation(out=stg[:, j, :], in_=ws[:, j, :], func=AF.Copy, scale=sc), reads=[wsk, "gcol"], writes=[(stgk, j)] if False else [stgk])
                    else:
                        P.op(eng, lambda e, ws=ws, stg=stg, j=j, sc=sc: e.tensor_scalar(out=stg[:, j, :], in0=ws[:, j, :], scalar1=sc, scalar2=None, op0=ALU.mult), reads=[wsk, "gcol"], writes=[stgk])
            dst = Wbf[kg * 128:(kg + n) * 128, c0:c0 + 512].rearrange("(c p) f -> p c f", p=128)
            P.op("gpsimd", lambda e, stg=stg, dst=dst: e.dma_start(out=dst, in_=stg), reads=[stgk], writes=[(keyname, b)], slot="pc%d" % ss_)
            keys.append((keyname, b))
        return keys

    def load_unit_bf(self, Wbf, kchunks, segs, rkeys):
        P = self.P
        i = self.wcount
        self.wcount += 1
        slot = i % 2
        wb = self.wb[slot]
        keys = [(("wb", slot), 0, 0), (("wb", slot), 1, 0)]
        k0, n, co = kchunks[0], len(kchunks), 0
        for si, (c0, cn) in enumerate(segs):
            src = Wbf[k0 * 128:(k0 + n) * 128, c0:c0 + cn].rearrange("(c p) f -> p c f", p=128)
            P.op("sync", lambda e, wb=wb, src=src, n=n, co=co, cn=cn: e.dma_start(out=wb[:, 0:n, co:co + cn], in_=src),
                 reads=rkeys, writes=keys, slot="wbd%d_%d" % (slot, si))
            co += cn
        return slot, keys

    def phase_norm(self):
        P = self.P
        junk = self.banks[7]
        hbs = self.hbs
        pend = [self.x_load(0), self.x_load(1)]
        for t in range(NT):
            if t + 2 < NT:
                pend.append(self.x_load(t + 2))
            xt, xk, _ = pend[t]
            st = self.sst[t % 4]
            sk = ("sst", t % 4)
            P.op("vector", lambda e, st=st: e.memset(st[:], 0.0), writes=[sk])
            P.op("scalar", lambda e, xt=xt, st=st: e.activation(out=junk[:, 0:512], in_=xt[:, 0:512], func=AF.Square, accum_out=st[:, 0:1]),
                 reads=[xk, sk], writes=[sk, ("bank", 7)])
            P.op("scalar", lambda e, xt=xt, st=st: e.activation(out=junk[:, 0:512], in_=xt[:, 512:1024], func=AF.Square, accum_out=st[:, 1:2]),
                 reads=[xk, sk], writes=[sk, ("bank", 7)])
            P.op("vector", lambda e, st=st: e.tensor_tensor(out=st[:, 2:3], in0=st[:, 0:1], in1=st[:, 1:2], op=ALU.add), reads=[sk], writes=[sk])
            P.op("scalar", lambda e, st=st: e.activation(out=st[:, 2:3], in_=st[:, 2:3], func=AF.Ln, scale=1.0 / D, bias=self.epsb[:, 0:1]),
                 reads=[sk, "eps"], writes=[sk])
            P.op("scalar", lambda e, st=st: e.activation(out=st[:, 3:4], in_=st[:, 2:3], func=AF.Exp, scale=-0.5), reads=[sk], writes=[sk])
            hb = hbs[t % 2]
            hk = ("hb", t % 2)
            pk = ("bank", t % 2)
            pT = self.bank_bf(t % 2)
            P.op("vector", lambda e, xt=xt, hb=hb, st=st: e.tensor_scalar(out=hb[:], in0=xt[:], scalar1=st[:, 3:4], scalar2=None, op0=ALU.mult),
                 reads=[xk, sk], writes=[hk])
            for c in range(8):
                P.op("tensor", lambda e, c=c, hb=hb, pT=pT: e.transpose(out=pT[:, c, :], in_=hb[:, c * 128:(c + 1) * 128], identity=self.ident[:]),
                     reads=[hk, "ident"], writes=[pk])
            P.op("scalar", lambda e, t=t, pT=pT: e.activation(out=self.actT[:, :, 1 + t * 128:1 + (t + 1) * 128], in_=pT, func=AF.Copy),
                 reads=[pk], writes=[("actT", c, t) for c in range(8)])

    def load_gcol(self, name):
        P = self.P
        g = self.din(name, [128, 8], F32).ap()
        P.op("sync", lambda e: e.dma_start(out=self.gcol[:], in_=g), writes=["gcol"], slot="gcol")

    def phase_qkv(self, li):
        P = self.P
        L = LAYERS[li]
        kind = L["kind"]
        pfx = "l%d_" % li
        self.load_gcol(pfx + "gcol_attn")
        self.phase_norm()
        self.scr_reset()
        W = self.din(pfx + "w_qkv", [D, L["F"]], F32)
        dh = 128 if kind == "C" else 64
        geff_d = self.din(pfx + "qkg", [2, dh], F32).ap()
        qT_d, kT_d, v_d = self.qT_d, self.kT_d, self.v_d
        gq = self.scr("gq", [128, dh], F32)
        gk = self.scr("gk", [128, dh], F32)
        P.op("sync", lambda e: e.dma_start(out=gq[:], in_=geff_d[0].partition_broadcast(128)), writes=["gq"], slot="gq")
        P.op("sync", lambda e: e.dma_start(out=gk[:], in_=geff_d[1].partition_broadcast(128)), writes=["gk"], slot="gk")
        P.op("vector", lambda e: e.scalar_tensor_tensor(out=gk[:], in0=gq[:], scalar=float(dh) ** -0.5, in1=gk[:], op0=ALU.mult, op1=ALU.mult),
             reads=["gq", "gk"], writes=["gk"])
        stages = [self.scr("stage", [128, 4, T], BF16) for _ in range(2)]
        ND = 4
        sqs = [self.scr("sq", [128, 512], F32) for _ in range(ND)]
        kfs = [self.scr("kf", [128, 512], F32) for _ in range(ND)]
        qns = [self.scr("qn", [128, 512], BF16) for _ in range(ND)]
        vsts = [self.scr("vst", [128, 512], BF16) for _ in range(ND)]
        ssq = [self.scr("ssq", [128, 8], F32) for _ in range(ND)]
        PYB = (2, 3, 4, 5)
        PTB = (0, 1, 6, 7)
        if kind == "A":
            chunks = [[("q", 0, 512, 0)], [("q", 0, 512, 4)], [("k", 0, 256, 0), ("v", 256, 256, 0)]]
        elif kind == "B":
            chunks = [[("q", 0, 512, 0)], [("q", 0, 512, 4)], [("k", 0, 512, 0)], [("k", 0, 512, 4)], [("v", 0, 512, 0)], [("v", 0, 512, 512)]]
        else:
            chunks = [[("q", 0, 512, 0)], [("q", 0, 512, 4)], [("q", 0, 512, 8)], [("k", 0, 512, 0)], [("v", 0, 512, 0)]]
        gsc = lambda k: self.gcol[:, k:k + 1]
        units = [None] * len(chunks)
        units[0] = self.load_unit(W, list(range(8)), [(0, 512)], scale=gsc, scale_reads=["gcol"])
        it = 0
        pend_rest = []
        for ci, segs in enumerate(chunks):
            if ci + 1 < len(chunks):
                units[ci + 1] = self.load_unit(W, list(range(8)), [((ci + 1) * 512, 512)], scale=gsc, scale_reads=["gcol"])
            slot, wkeys = units[ci]
            wb = self.wb[slot]
            stage = stages[ci % 2]
            stk = ("stage", ci % 2)
            for t in range(NT):
                bi = PYB[it % ND]
                py = self.banks[bi]
                pyk = ("bank", bi)
                for c in range(8):
                    P.op("tensor", lambda e, c=c, t=t, py=py, wb=wb: e.matmul(py[:, 0:512], lhsT=self.actT[:, c, 1 + t * 128:1 + (t + 1) * 128], rhs=wb[:, c, 0:512], start=(c == 0), stop=(c == 7)),
                         reads=[("actT", c, t)] + wkeys, writes=[pyk])
                def rest(segs=segs, it=it, t=t, py=py, pyk=pyk, stage=stage, stk=stk):
                    for (ty, off, w, dst) in segs:
                        if ty == "v":
                            vst = vsts[it % ND]
                            vk = ("vst", it % ND)
                            P.op("scalar", lambda e, py=py, vst=vst, off=off, w=w: e.activation(out=vst[:, 0:w], in_=py[:, off:off + w], func=AF.Copy),
                                 reads=[pyk], writes=[vk])
                            P.op("gpsimd", lambda e, vst=vst, t=t, dst=dst, w=w: e.dma_start(out=v_d[t * 128:(t + 1) * 128, dst:dst + w], in_=vst[:, 0:w]),
                                 reads=[vk], writes=[("vd", t, dst)], slot="vst%d" % (it % ND))
                            continue
                        nh = w // dh
                        sq = sqs[it % ND]
                        sqk = ("sq", it % ND)
                        s_ = ssq[it % ND]
                        sk = ("ssq", it % ND)
                        qn = qns[it % ND]
                        qk = ("qn", it % ND)
                        P.op("scalar", lambda e, py=py, sq=sq, off=off, w=w: e.activation(out=sq[:, 0:w], in_=py[:, off:off + w], func=AF.Square),
                             reads=[pyk], writes=[sqk])
                        P.op("vector", lambda e, sq=sq, s_=s_, w=w, nh=nh: e.tensor_reduce(out=s_[:, 0:nh], in_=sq[:, 0:w].rearrange("p (h d) -> p h d", d=dh), axis=AX.X, op=ALU.add),
                             reads=[sqk], writes=[sk])
                        P.op("scalar", lambda e, s_=s_, nh=nh: e.activation(out=s_[:, 0:nh], in_=s_[:, 0:nh], func=AF.Ln, scale=1.0 / dh, bias=self.epsb[:, 0:1]),
                             reads=[sk, "eps"], writes=[sk])
                        P.op("scalar", lambda e, s_=s_, nh=nh: e.activation(out=s_[:, 0:nh], in_=s_[:, 0:nh], func=AF.Exp, scale=-0.5), reads=[sk], writes=[sk])
                        rb = AP(s_, 0, [[8, 128], [1, nh], [0, dh]])
                        if ty == "q":
                            P.op("vector", lambda e, py=py, qn=qn, off=off, w=w, rb=rb: e.tensor_tensor(out=qn[:, 0:w].rearrange("p (h d) -> p h d", d=dh), in0=py[:, off:off + w].rearrange("p (h d) -> p h d", d=dh), in1=rb, op=ALU.mult),
                                 reads=[pyk, sk], writes=[qk])
                        else:
                            kf = kfs[it % ND]
                            kfk = ("kf", it % ND)
                            gb = AP(gk, 0, [[dh, 128], [0, nh], [1, dh]])
                            P.op("vector", lambda e, py=py, kf=kf, off=off, w=w, rb=rb: e.tensor_tensor(out=kf[:, 0:w].rearrange("p (h d) -> p h d", d=dh), in0=py[:, off:off + w].rearrange("p (h d) -> p h d", d=dh), in1=rb, op=ALU.mult),
                                 reads=[pyk, sk], writes=[kfk])
                            P.op("gpsimd", lambda e, kf=kf, qn=qn, w=w, gb=gb: e.tensor_tensor(out=qn[:, 0:w].rearrange("p (h d) -> p h d", d=dh), in0=kf[:, 0:w].rearrange("p (h d) -> p h d", d=dh), in1=gb, op=ALU.mult),
                                 reads=[kfk, "gk"], writes=[qk])
                        nb = w // 128
                        tb = PTB[it % ND]
                        pT = self.bank_bf(tb)
                        pk = ("bank", tb)
                        for j in range(nb):
                            P.op("tensor", lambda e, j=j, qn=qn, pT=pT: e.transpose(out=pT[:, j, :], in_=qn[:, j * 128:(j + 1) * 128], identity=self.ident[:]),
                                 reads=[qk, "ident"], writes=[pk])
                        P.op("scalar", lambda e, pT=pT, stage=stage, nb=nb, t=t: e.activation(out=stage[:, 0:nb, t * 128:(t + 1) * 128], in_=pT[:, 0:nb, :], func=AF.Copy),
                             reads=[pk], writes=[(stk, t)])
                pend_rest.append(rest)
                while len(pend_rest) > 2:
                    pend_rest.pop(0)()
                it += 1
            while pend_rest:
                pend_rest.pop(0)()
            for (ty, off, w, dst) in segs:
                if ty == "v":
                    continue
                dd = qT_d if ty == "q" else kT_d
                dk = "qTd" if ty == "q" else "kTd"
                for j in range(w // 128):
                    P.op("gpsimd", lambda e, dd=dd, j=j, dst=dst, stage=stage: e.dma_start(out=dd[dst + j], in_=stage[:, j, :]),
                         reads=[(stk, t) for t in range(NT)], writes=[(dk, dst + j)], slot="stg%d_%d" % (ci % 2, j))

    ST_BANKS = (0, 1, 6)

    def attn_item(self, it, mms, tb_ap, ncols, pvs, tbkey, extra_reads, post=None):
        P = self.P
        bi = self.ST_BANKS[it % 3]
        st = self.banks[bi]
        stk = ("bank", bi)
        sc = self.a_sc[it % 3]
        sck = ("sc", it % 3)
        pt = self.a_pt[it % 3]
        ptk = ("pt", it % 3)
        for (lh, rh, c0, n) in mms:
            P.op("tensor", lambda e, lh=lh, rh=rh, c0=c0, n=n, st=st: e.matmul(st[:, c0:c0 + n], lhsT=lh, rhs=rh, start=True, stop=True),
                 reads=extra_reads, writes=[stk])
        P.op("vector", lambda e, st=st, sc=sc, tb_ap=tb_ap, ncols=ncols: e.tensor_tensor(out=sc[:, 0:ncols], in0=st[:, 0:ncols], in1=tb_ap, op=ALU.add),
             reads=[stk, tbkey], writes=[sck])
        P.op("scalar", lambda e, sc=sc, pt=pt, ncols=ncols: e.activation(out=pt[:, 0:ncols], in_=sc[:, 0:ncols], func=AF.Exp),
             reads=[sck], writes=[ptk])
        q = self.a_queue
        q.append((pt, ptk, pvs, list(extra_reads), post))
        while len(q) > 2:
            self._attn_back(q.pop(0))

    def _attn_back(self, pend):
        P = self.P
        pt, ptk, pvs, extra_reads, post = pend
        for (c0, v_ap, ab, wdt, s0, s1) in pvs:
            acc = self.banks[ab]
            P.op("tensor", lambda e, c0=c0, v_ap=v_ap, acc=acc, wdt=wdt, s0=s0, s1=s1, pt=pt: e.matmul(acc[:, 0:wdt], lhsT=pt[:, c0:c0 + 128], rhs=v_ap, start=s0, stop=s1),
                 reads=[ptk] + extra_reads, writes=[("bank", ab)])
        if post is not None:
            post()

    def attn_flush(self):
        q = self.a_queue
        while q:
            self._attn_back(q.pop(0))

    def out_transpose(self, on, onk, chunk, qt, trk):
        P = self.P
        bi = 7
        pT = self.bank_bf(bi)
        P.op("tensor", lambda e, on=on, pT=pT: e.transpose(out=pT[:, 0, :], in_=on, identity=self.ident[:]),
             reads=[onk, "ident"], writes=[("bank", bi)])
        P.op("scalar", lambda e, pT=pT, chunk=chunk, qt=qt: e.activation(out=self.actT[:, chunk, 1 + qt * 128:1 + (qt + 1) * 128], in_=pT[:, 0, :], func=AF.Copy),
             reads=[("bank", bi)], writes=[("actT", chunk, qt)])

    def phase_attn(self, li):
        P = self.P
        L = LAYERS[li]
        kind = L["kind"]
        pfx = "l%d_" % li
        self.scr_reset()
        nkb, nqb, FV = L["nkb"], L["nqb"], L["FV"]
        qT_d, kT_d, v_d = self.qT_d, self.kT_d, self.v_d
        ni = NI[kind]
        bt_d = self.din("bt_" + kind, [NUNIT_BT[kind], 128, ni * 128], F32).ap()
        self.a_sc = [self.scr("sc", [128, 512], F32) for _ in range(3)]
        self.a_pt = [self.scr("pt", [128, 512], BF16) for _ in range(3)]
        self.a_queue = []
        rr = [self.scr("rr", [128, 4], F32) for _ in range(4)]
        ons = [self.scr("on", [128, 128], BF16) for _ in range(2)]
        v_r = v_d.rearrange("(kb p) f -> p kb f", p=128)
        qkeys = [("qTd", b) for b in range(nqb)]
        kkeys = [("kTd", b) for b in range(nkb)]
        vkeys = [("vd", t, c0) for t in range(NT) for c0 in range(0, FV, 512 if FV >= 512 else 256)]
        it = 0
        fin = 0
        if kind == "B":
            dv = 128
            KTs = [self.scr("KT", [128, T], BF16) for _ in range(2)]
            Vs = [self.scr("V", [128, NT, dv + 1], BF16) for _ in range(2)]
            QTz = [self.scr("QT", [128, T], BF16) for _ in range(2)]
            TB = self.scr("TB", [128, ni * 128], F32)
            o1 = self.scr("o1", [128, NT, 128], F32)
            P.op("gpsimd", lambda e: e.memset(QTz[0][64:128, :], 0.0), writes=[("QTz", 0)])
            P.op("gpsimd", lambda e: e.memset(QTz[1][0:64, :], 0.0), writes=[("QTz", 1)])
            ods = [self.scr("od", [128, 128], F32) for _ in range(2)]
            lam = self.hbs[0][:].bitcast(F32)[:, 0:256].rearrange("p (a d) -> p a d", d=64)
            lamv = self.scr("lamv", [128, 4], F32)
            lam_d = self.din(pfx + "lam", [4, 64], F32).ap()
            P.op("sync", lambda e: e.dma_start(out=lam, in_=AP(lam_d.tensor, 0, [[0, 128], [64, 4], [1, 64]])), writes=["lam", ("hb", 0)], slot="lam")
            P.op("vector", lambda e: e.tensor_tensor(out=lam[:, 0, :], in0=lam[:, 0, :], in1=lam[:, 1, :], op=ALU.mult), reads=["lam"], writes=["lam"])
            P.op("vector", lambda e: e.tensor_tensor(out=lam[:, 2, :], in0=lam[:, 2, :], in1=lam[:, 3, :], op=ALU.mult), reads=["lam"], writes=["lam"])
            P.op("vector", lambda e: e.tensor_reduce(out=lamv[:, 0:1], in_=lam[:, 0, :], axis=AX.X, op=ALU.add), reads=["lam"], writes=["lamv"])
            P.op("vector", lambda e: e.tensor_reduce(out=lamv[:, 1:2], in_=lam[:, 2, :], axis=AX.X, op=ALU.add), reads=["lam"], writes=["lamv", ("hb", 0)])
            P.op("scalar", lambda e: e.activation(out=lamv[:, 0:2], in_=lamv[:, 0:2], func=AF.Exp), reads=["lamv"], writes=["lamv"])
            P.op("vector", lambda e: e.scalar_tensor_tensor(out=lamv[:, 2:3], in0=lamv[:, 1:2], scalar=-lambda_init_fn(li), in1=lamv[:, 0:1], op0=ALU.add, op1=ALU.subtract),
                 reads=["lamv"], writes=["lamv"])
            neglam = lamv[:, 2:3]
            for vb in Vs:
                P.op("gpsimd", lambda e, vb=vb: e.memset(vb[:, :, dv:dv + 1], 1.0), writes=[("Vones", id(vb))])

            def load_head(h):
                s = h % 2
                P.op("sync", lambda e, h=h, s=s: e.dma_start(out=KTs[s][:], in_=kT_d[h]), reads=kkeys, writes=[("KT", s)], slot="kt%da" % s)
                for half in range(2):
                    P.op("sync", lambda e, h=h, s=s, half=half: e.dma_start(out=Vs[s][:, half * 16:(half + 1) * 16, 0:dv], in_=v_r[:, half * 16:(half + 1) * 16, h * dv:(h + 1) * dv]),
                         reads=vkeys, writes=[("V", s)], slot="v%d_%d" % (s, half))

            fin_box = [0]

            def fin_B(qc, j, h):
                for jq in range(4):
                    qt = qc * 4 + jq
                    acc = self.banks[2 + jq]
                    ak = ("bank", 2 + jq)
                    fin_box[0] += 1
                    fin = fin_box[0]
                    r = rr[fin % 4]
                    rk = ("rr", fin % 4)
                    P.op("vector", lambda e, acc=acc, r=r: e.reciprocal(out=r[:, 0:1], in_=acc[:, dv:dv + 1]), reads=[ak], writes=[rk])
                    if j == 0:
                        P.op("vector", lambda e, acc=acc, r=r, qt=qt: e.tensor_scalar(out=o1[:, qt, :], in0=acc[:, 0:dv], scalar1=r[:, 0:1], scalar2=None, op0=ALU.mult),
                             reads=[ak, rk], writes=[("o1", qt)])
                    else:
                        od = ods[fin % 2]
                        odk = ("od", fin % 2)
                        on = ons[fin % 2]
                        onk = ("on", fin % 2)
                        P.op("vector", lambda e, r=r: e.tensor_scalar(out=r[:, 1:2], in0=r[:, 0:1], scalar1=neglam, scalar2=None, op0=ALU.mult),
                             reads=[rk, "lamv"], writes=[rk])
                        P.op("vector", lambda e, acc=acc, r=r, qt=qt, od=od: e.scalar_tensor_tensor(out=od[:], in0=acc[:, 0:dv], scalar=r[:, 1:2], in1=o1[:, qt, :], op0=ALU.mult, op1=ALU.add),
                             reads=[ak, rk, ("o1", qt)], writes=[odk])
                        P.op("vector", lambda e, r=r: e.memset(r[:, 2:3], 0.0), reads=[], writes=[rk])
                        P.op("scalar", lambda e, od=od, on=on, r=r: e.activation(out=on[:], in_=od[:], func=AF.Square, accum_out=r[:, 2:3]),
                             reads=[odk, rk], writes=[rk, onk])
                        P.op("scalar", lambda e, r=r: e.activation(out=r[:, 2:3], in_=r[:, 2:3], func=AF.Ln, scale=1.0 / 128, bias=self.epsb[:, 0:1]),
                             reads=[rk, "eps"], writes=[rk])
                        P.op("scalar", lambda e, r=r: e.activation(out=r[:, 2:3], in_=r[:, 2:3], func=AF.Exp, scale=-0.5), reads=[rk], writes=[rk])
                        P.op("vector", lambda e, od=od, on=on, r=r: e.tensor_scalar(out=on[:], in0=od[:], scalar1=r[:, 2:3], scalar2=None, op0=ALU.mult),
                             reads=[odk, rk], writes=[onk])
                        self.out_transpose(on[:], onk, h, qt, fin)

            load_head(0)
            for h in range(8):
                s = h % 2
                self.attn_flush()
                if h + 1 < 8:
                    load_head(h + 1)
                P.op("sync", lambda e, h=h: e.dma_start(out=QTz[0][0:64, :], in_=qT_d[h, 0:64, :]), reads=qkeys + [("QTz", 0)], writes=["QT"], slot="qt")
                P.op("sync", lambda e, h=h: e.dma_start(out=QTz[1][64:128, :], in_=qT_d[h, 64:128, :]), reads=qkeys + [("QTz", 1)], writes=["QT"], slot="qtb")
                KT, V = KTs[s], Vs[s]
                rds = [("KT", s), ("V", s), "QT", ("Vones", id(V))]
                for j in range(2):
                    for half in range(2):
                        hw = ni * 64
                        P.op("sync", lambda e, h=h, j=j, half=half, hw=hw: e.dma_start(out=TB[:, half * hw:(half + 1) * hw], in_=bt_d[h * 2 + j, :, half * hw:(half + 1) * hw]),
                             writes=["TB"], slot="tb%d" % half)
                    for qc in range(NT // 4):
                        for kb in range(NT):
                            dp = min(max(kb - 4 * qc, -12), 12)
                            i0 = 12 - dp
                            mms = [(KT[:, kb * 128:(kb + 1) * 128], QTz[j][:, qc * 512:(qc + 1) * 512], 0, 512)]
                            pvs = [(jq * 128, V[:, kb, 0:dv + 1], 2 + jq, dv + 1, kb == 0, kb == NT - 1) for jq in range(4)]
                            post = None
                            if kb == NT - 1:
                                post = (lambda qc=qc, j=j, h=h: fin_B(qc, j, h))
                            self.attn_item(it, mms, TB[:, i0 * 128:i0 * 128 + 512], 512, pvs, "TB", rds, post)
                            it += 1
            self.attn_flush()
        elif kind == "A":
            dv = 64
            NE = NT + 2
            KTs = [self.scr("KT", [64, NE * 128], BF16) for _ in range(2)]
            Vs = [self.scr("V", [128, NE, dv + 1], BF16) for _ in range(2)]
            QTs = [self.scr("QT", [64, T], BF16) for _ in range(2)]
            TBs = [self.scr("TB", [128, ni * 128], F32) for _ in range(2)]
            pairs = [self.scr("pair", [128, NT, 128], BF16) for _ in range(2)]
            esink = self.scr("esink", [128, 16], F32)
            sink_d = self.din(pfx + "sink", [16], F32).ap()
            P.op("sync", lambda e: e.dma_start(out=esink[:], in_=sink_d.partition_broadcast(128)), writes=["esink"], slot="esink")
            P.op("scalar", lambda e: e.activation(out=esink[:], in_=esink[:], func=AF.Exp), reads=["esink"], writes=["esink"])
            for s_i in range(2):
                vb, kb_ = Vs[s_i], KTs[s_i]
                P.op("gpsimd", lambda e, vb=vb: e.memset(vb[:], 0.0), writes=[("Vones", s_i), ("V", s_i)])
                P.op("gpsimd", lambda e, vb=vb: e.memset(vb[:, 1:NE - 1, dv:dv + 1], 1.0), writes=[("Vones", s_i), ("V", s_i)])
                P.op("gpsimd", lambda e, kb_=kb_: e.memset(kb_[:], 0.0), writes=[("KT", s_i)])

            def load_kv(kvh):
                s = kvh % 2
                blk, r0 = kvh // 2, (kvh % 2) * 64
                P.op("sync", lambda e: e.dma_start(out=KTs[s][:, 128:128 + T], in_=kT_d[blk, r0:r0 + 64, :]), reads=kkeys, writes=[("KT", s)], slot="kt%db" % s)
                for half in range(2):
                    P.op("sync", lambda e, half=half: e.dma_start(out=Vs[s][:, 1 + half * 16:1 + (half + 1) * 16, 0:dv], in_=v_r[:, half * 16:(half + 1) * 16, kvh * dv:(kvh + 1) * dv]),
                         reads=vkeys, writes=[("V", s)], slot="v%db%d" % (s, half))

            def load_q(h):
                s = h % 2
                P.op("sync", lambda e: e.dma_start(out=QTs[s][:], in_=qT_d[h // 2, (h % 2) * 64:(h % 2) * 64 + 64, :]), reads=qkeys, writes=[("QT", s)], slot="qt%d" % s)
                P.op("sync", lambda e: e.dma_start(out=TBs[s][:], in_=bt_d[h]), writes=[("TB", s)], slot="tb%d" % s)

            fin_box = [0]
            load_kv(0)
            load_q(0)
            for h in range(16):
                kvh = h // 4
                s = kvh % 2
                self.attn_flush()
                if h % 4 == 0 and kvh + 1 < 4:
                    load_kv(kvh + 1)
                if h + 1 < 16:
                    load_q(h + 1)
                KT, V, QT, TB = KTs[s], Vs[s], QTs[h % 2], TBs[h % 2]
                pair = pairs[(h // 2) % 2]
                rds = [("KT", s), ("V", s), ("QT", h % 2), ("Vones", s)]
                for qt in range(NT):
                    ab = 2 + (it % 4)
                    mms = [(KT[0:64, (qt + 2 - i) * 128:(qt + 3 - i) * 128], QT[0:64, qt * 128:(qt + 1) * 128], i * 128, 128) for i in range(3)]
                    pvs = [(i * 128, V[:, qt + 2 - i, 0:dv + 1], ab, dv + 1, i == 0, i == 2) for i in range(3)]
                    def post_A(acc=self.banks[ab], ak=("bank", ab), qt=qt, h=h, pair=pair):
                        fin_box[0] += 1
                        fin = fin_box[0]
                        r = rr[fin % 4]
                        rk = ("rr", fin % 4)
                        P.op("vector", lambda e, acc=acc, r=r, h=h: e.tensor_tensor(out=r[:, 0:1], in0=acc[:, dv:dv + 1], in1=esink[:, h:h + 1], op=ALU.add),
                             reads=[ak, "esink"], writes=[rk])
                        P.op("vector", lambda e, r=r: e.reciprocal(out=r[:, 1:2], in_=r[:, 0:1]), reads=[rk], writes=[rk])
                        P.op("vector", lambda e, acc=acc, r=r, pair=pair, qt=qt, h=h: e.tensor_scalar(out=pair[:, qt, (h % 2) * 64:(h % 2) * 64 + 64], in0=acc[:, 0:dv], scalar1=r[:, 1:2], scalar2=None, op0=ALU.mult),
                             reads=[ak, rk], writes=[("pair", (h // 2) % 2, qt, h % 2)])
                    self.attn_item(it, mms, TB[:, 0:384], 384, pvs, ("TB", h % 2), rds, post_A)
                    it += 1
                if h % 2 == 1:
                    self.attn_flush()
                    for qt in range(NT):
                        fin += 1
                        bi = 7
                        pT = self.bank_bf(bi)
                        P.op("tensor", lambda e, pair=pair, qt=qt, pT=pT: e.transpose(out=pT[:, 0, :], in_=pair[:, qt, :], identity=self.ident[:]),
                             reads=[("pair", (h // 2) % 2, qt, 0), ("pair", (h // 2) % 2, qt, 1), "ident"], writes=[("bank", bi)])
                        P.op("scalar", lambda e, pT=pT, h=h, qt=qt: e.activation(out=self.actT[:, h // 2, 1 + qt * 128:1 + (qt + 1) * 128], in_=pT[:, 0, :], func=AF.Copy),
                             reads=[("bank", bi)], writes=[("actT", h // 2, qt)])
        else:
            dv = 128
            NE = NT + 16
            KT = self.scr("KT", [128, NE * 128], BF16)
            V = self.scr("V", [128, NE, dv + 1], BF16)
            QT = self.scr("QT", [128, 3, T], BF16)
            TB = self.scr("TB", [128, ni * 128], F32)
            P.op("gpsimd", lambda e: e.memset(V[:], 0.0), writes=["Vones", "V"])
            P.op("gpsimd", lambda e: e.memset(V[:, 8:NE - 8, dv:dv + 1], 1.0), writes=["Vones", "V"])
            P.op("gpsimd", lambda e: e.memset(KT[:], 0.0), writes=["KT"])
            ents = [(0, d_) for d_ in (1, 0, -1)] + [(1, d_) for d_ in (2, 1, 0, -1, -2)] + [(2, d_) for d_ in range(8, -9, -1)]
            fin_box = [0]
            for j in range(4):
                self.attn_flush()
                P.op("sync", lambda e, j=j: e.dma_start(out=KT[:, 1024:1024 + T], in_=kT_d[j]), reads=kkeys, writes=["KT"], slot="ktb")
                for half in range(2):
                    P.op("sync", lambda e, j=j, half=half: e.dma_start(out=V[:, 8 + half * 16:8 + (half + 1) * 16, 0:dv], in_=v_r[:, half * 16:(half + 1) * 16, j * dv:(j + 1) * dv]),
                         reads=vkeys, writes=["V"], slot="vb%d" % half)
                for g in range(3):
                    P.op("sync", lambda e, g=g, j=j: e.dma_start(out=QT[:, g, :], in_=qT_d[g * 4 + j]), reads=qkeys, writes=["QT"], slot="qt%d" % g)
                P.op("sync", lambda e, j=j: e.dma_start(out=TB[:], in_=bt_d[j]), writes=["TB"], slot="tb")
                rds = ["KT", "V", "QT", "Vones"]
                for qt in range(NT):
                    ab = 2 + (qt % 4)
                    for i0 in range(0, 25, 4):
                        grp = list(range(i0, min(i0 + 4, 25)))
                        mms = []
                        pvs = []
                        for n_, idx in enumerate(grp):
                            g, dl = ents[idx]
                            eb = qt + 8 + dl
                            mms.append((KT[:, eb * 128:(eb + 1) * 128], QT[:, g, qt * 128:(qt + 1) * 128], n_ * 128, 128))
                            pvs.append((n_ * 128, V[:, eb, 0:dv + 1], ab, dv + 1, idx == 0, idx == 24))
                        post = None
                        if grp[-1] == 24:
                            def post(acc=self.banks[ab], ak=("bank", ab), qt=qt, j=j):
                                fin_box[0] += 1
                                fin = fin_box[0]
                                r = rr[fin % 4]
                                rk = ("rr", fin % 4)
                                on = ons[fin % 2]
                                onk = ("on", fin % 2)
                                P.op("vector", lambda e, acc=acc, r=r: e.reciprocal(out=r[:, 0:1], in_=acc[:, dv:dv + 1]), reads=[ak], writes=[rk])
                                P.op("vector", lambda e, acc=acc, r=r, on=on: e.tensor_scalar(out=on[:], in0=acc[:, 0:dv], scalar1=r[:, 0:1], scalar2=None, op0=ALU.mult),
                                     reads=[ak, rk], writes=[onk])
                                self.out_transpose(on[:], onk, j, qt, fin)
                        self.attn_item(it, mms, TB[:, i0 * 128:(i0 + len(grp)) * 128], len(grp) * 128, pvs, "TB", rds, post)
                        it += 1
            self.attn_flush()

    def phase_wo(self, li):
        P = self.P
        L = LAYERS[li]
        pfx = "l%d_" % li
        nfc = L["nfc"]
        W = self.din(pfx + "w_o", [nfc * 128, D], F32)
        scale = None
        srd = ()
        if L["kind"] == "B":
            sg = self.din(pfx + "sgcol", [128, 1], F32).ap()
            sgt = self.scr("sgt", [128, 1], F32)
            P.op("sync", lambda e: e.dma_start(out=sgt[:], in_=sg), writes=["sgt"], slot="sgt")
            P.op("vector", lambda e: e.tensor_scalar(out=sgt[:], in0=sgt[:], scalar1=1.0 - lambda_init_fn(li), scalar2=None, op0=ALU.mult), reads=["sgt"], writes=["sgt"])
            scale = lambda k: sgt[:, 0:1]
            srd = ["sgt"]
        kch = list(range(nfc))
        units = [self.load_unit(W, kch, [(n * 512, 512)], scale=scale, scale_reads=srd) for n in range(2)]
        pend = [self.x_load(0, 0), self.x_load(0, 1)]
        for t in range(NT):
            for n in range(2):
                nx = t * 2 + n + 2
                if nx < NT * 2:
                    pend.append(self.x_load(nx // 2, nx % 2))
                slot, wkeys = units[n]
                wb = self.wb[slot]
                bi = 2 + n
                py = self.banks[bi]
                for c in range(nfc):
                    P.op("tensor", lambda e, c=c, t=t, py=py, wb=wb: e.matmul(py[:, 0:512], lhsT=self.actT[:, c, 1 + t * 128:1 + (t + 1) * 128], rhs=wb[:, c, 0:512], start=(c == 0), stop=(c == nfc - 1)),
                         reads=[("actT", c, t)] + wkeys, writes=[("bank", bi)])
                xt, xk, xi = pend[t * 2 + n]
                P.op("vector", lambda e, xt=xt, py=py: e.tensor_tensor(out=xt[:, 0:512], in0=py[:, 0:512], in1=xt[:, 0:512], op=ALU.add),
                     reads=[("bank", bi), xk], writes=[xk])
                self.x_store(t, n, xt, xk, xi)

    def phase_ffn(self, li):
        P = self.P
        pfx = "l%d_" % li
        self.load_gcol(pfx + "gcol_ffn")
        self.phase_norm()
        self.scr_reset()
        Wu = self.din(pfx + "w_up", [D, 2 * DFF], F32)
        Wd = self.din(pfx + "w_down", [DFF, D], F32)
        cw_d = self.din(pfx + "convp", [128, 4, 44], F32).ap()
        cw = self.scr("cw", [128, 4, 44], F32)
        P.op("sync", lambda e: e.dma_start(out=cw[:], in_=cw_d), writes=["cw"], slot="cw")
        wu_keys = self.precast(Wu, 8, 2 * DFF, self.wu_bf, "wubf", True)
        wd_keys = self.precast(Wd, 22, D, self.wd_bf, "wdbf", False)
        TC = 1024
        NK = T // TC
        a = self.scr("a", [128, 22, TC], BF16)
        Us = [[self.scr("U", [128, TC + 2], F32) for _ in range(2)] for _ in range(2)]
        T1s = [[self.scr("T1", [128, TC], F32) for _ in range(2)] for _ in range(2)]
        gsc = lambda k: self.gcol[:, k:k + 1]
        up_units = [[(f0 * 128, 256), (DFF + f0 * 128, 256)] for f0 in range(0, 22, 2)]
        dn_units = [(kc, n) for n in range(2) for kc in (list(range(0, 8)), list(range(8, 16)), list(range(16, 22)))]
        pcount = 0
        for k in range(NK):
            cb = TC * k
            allr = [("actT", c, t) for c in range(8) for t in range(max(8 * k - 1, 0), min(8 * k + 9, NT))] + ["halo"]
            nxt = self.load_unit_bf(self.wu_bf, list(range(8)), up_units[0], wu_keys)
            for ui in range(11):
                cur = nxt
                if ui + 1 < 11:
                    nxt = self.load_unit_bf(self.wu_bf, list(range(8)), up_units[ui + 1], wu_keys)
                slot, wkeys = cur
                wb = self.wb[slot]
                for pi in range(2):
                    f = ui * 2 + pi
                    par = pcount % 2
                    pcount += 1
                    for which in range(2):
                        off = which * 256 + pi * 128
                        fc = f + 22 * which
                        bA, bB, bE = self.banks[4 * which], self.banks[4 * which + 1], self.banks[4 * which + 2]
                        bks = [("bank", 4 * which + i) for i in range(3)]
                        for c in range(8):
                            P.op("tensor", lambda e, c=c, wb=wb, off=off, bA=bA, cb=cb: e.matmul(bA[:, 0:512], lhsT=wb[:, c, off:off + 128], rhs=self.actT[:, c, cb + 1:cb + 513], start=(c == 0), stop=(c == 7)),
                                 reads=allr + wkeys, writes=[bks[0]])
                        for c in range(8):
                            P.op("tensor", lambda e, c=c, wb=wb, off=off, bB=bB, cb=cb: e.matmul(bB[:, 0:512], lhsT=wb[:, c, off:off + 128], rhs=self.actT[:, c, cb + 513:cb + 1025], start=(c == 0), stop=(c == 7)),
                                 reads=allr + wkeys, writes=[bks[1]])
                        for c in range(8):
                            P.op("tensor", lambda e, c=c, wb=wb, off=off, bE=bE, cb=cb: e.matmul(bE[:, 0:2], lhsT=wb[:, c, off:off + 128], rhs=self.actT[:, c, cb:cb + 1026:1025], start=(c == 0), stop=(c == 7)),
                                 reads=allr + wkeys, writes=[bks[2]])
                        U = Us[which][par]
                        uk = ("U", which, par)
                        t1 = T1s[which][par]
                        tk = ("T1", which, par)
                        P.op("scalar", lambda e, U=U, bA=bA: e.activation(out=U[:, 1:513], in_=bA[:, 0:512], func=AF.Copy), reads=[bks[0]], writes=[(uk, 0)])
                        P.op("scalar", lambda e, U=U, bB=bB: e.activation(out=U[:, 513:1025], in_=bB[:, 0:512], func=AF.Copy), reads=[bks[1]], writes=[(uk, 1)])
                        P.op("scalar", lambda e, U=U, bE=bE: e.activation(out=U[:, 0:1026:1025], in_=bE[:, 0:2], func=AF.Copy), reads=[bks[2]], writes=[(uk, 2)])
                        P.op("scalar", lambda e, t1=t1, bA=bA, fc=fc: e.activation(out=t1[:, 0:512], in_=bA[:, 0:512], func=AF.Identity, scale=cw[:, 1, fc:fc + 1], bias=cw[:, 3, fc:fc + 1]),
                             reads=[bks[0], "cw"], writes=[(tk, 0)])
                        P.op("scalar", lambda e, t1=t1, bB=bB, fc=fc: e.activation(out=t1[:, 512:1024], in_=bB[:, 0:512], func=AF.Identity, scale=cw[:, 1, fc:fc + 1], bias=cw[:, 3, fc:fc + 1]),
                             reads=[bks[1], "cw"], writes=[(tk, 1)])
                        P.op("vector", lambda e, t1=t1, U=U, fc=fc: e.scalar_tensor_tensor(out=t1[:], in0=U[:, 0:TC], scalar=cw[:, 0, fc:fc + 1], in1=t1[:], op0=ALU.mult, op1=ALU.add),
                             reads=[(uk, 0), (uk, 1), (uk, 2), (tk, 0), (tk, 1), "cw"], writes=[(tk, 0), (tk, 1)])
                        P.op("vector", lambda e, t1=t1, U=U, fc=fc: e.scalar_tensor_tensor(out=t1[:], in0=U[:, 2:TC + 2], scalar=cw[:, 2, fc:fc + 1], in1=t1[:], op0=ALU.mult, op1=ALU.add),
                             reads=[(uk, 0), (uk, 1), (uk, 2), (tk, 0), (tk, 1), "cw"], writes=[(tk, 0), (tk, 1)])
                    tg, tv = T1s[0][par], T1s[1][par]
                    kg, kv = ("T1", 0, par), ("T1", 1, par)
                    P.op("scalar", lambda e, tg=tg: e.activation(out=tg[:], in_=tg[:], func=AF.Silu), reads=[(kg, 0), (kg, 1)], writes=[(kg, 0), (kg, 1)])
                    P.op("vector", lambda e, f=f, tg=tg, tv=tv: e.tensor_tensor(out=a[:, f, :], in0=tg[:], in1=tv[:], op=ALU.mult),
                         reads=[(kg, 0), (kg, 1), (kv, 0), (kv, 1)], writes=[("a", f)])
            nxt = self.load_unit_bf(self.wd_bf, dn_units[0][0], [(dn_units[0][1] * 512, 512)], wd_keys)
            for di, (kc, n) in enumerate(dn_units):
                cur = nxt
                if di + 1 < len(dn_units):
                    nxt = self.load_unit_bf(self.wd_bf, dn_units[di + 1][0], [(dn_units[di + 1][1] * 512, 512)], wd_keys)
                slot, wkeys = cur
                wb = self.wb[slot]
                if kc[0] == 0:
                    pend = [self.x_load(8 * k + t, n) for t in range(2)]
                for t in range(8):
                    for ci, f in enumerate(kc):
                        P.op("tensor", lambda e, t=t, ci=ci, f=f, wb=wb: e.matmul(self.banks[t][:, 0:512], lhsT=a[:, f, t * 128:(t + 1) * 128], rhs=wb[:, ci, 0:512], start=(f == 0), stop=(f == 21)),
                             reads=[("a", f)] + wkeys, writes=[("bank", t)])
                if kc[-1] == 21:
                    for t in range(8):
                        tt = 8 * k + t
                        if t + 2 < 8:
                            pend.append(self.x_load(8 * k + t + 2, n))
                        xt, xk, xi = pend[t]
                        P.op("vector", lambda e, t=t, xt=xt: e.tensor_tensor(out=xt[:, 0:512], in0=self.banks[t][:, 0:512], in1=xt[:, 0:512], op=ALU.add),
                             reads=[("bank", t), xk], writes=[xk])
                        self.x_store(tt, n, xt, xk, xi)

    def finish(self):
        self.P.finalize()
        return self.nc


def rel_bucket_np(rel):
    nb = 16
    max_exact = 8
    n = np.abs(rel)
    nf = np.maximum(n, 1).astype(np.float32)
    large = max_exact + (np.log(nf / np.float32(max_exact)) / np.float32(math.log(1024 / max_exact)) * np.float32(nb - max_exact)).astype(np.int32)
    large = np.minimum(large, nb - 1)
    return np.where(rel > 0, nb, 0) + np.where(n < max_exact, n, large)


def bias_tables(rel_bias, kind):
    rb = np.asarray(rel_bias, np.float32)
    kp = np.arange(128)[:, None]
    qp = np.arange(128)[None, :]
    if kind == "A":
        out = np.empty((16, 128, 3 * 128), np.float32)
        for i, dl in enumerate((1, 0, -1)):
            rel = dl * 128 + kp - qp
            bk = rel_bucket_np(rel)
            ok = np.abs(rel) <= 128
            for h in range(16):
                out[h, :, i * 128:(i + 1) * 128] = np.where(ok, rb[bk, h], NEG)
        return out
    if kind == "B":
        out = np.empty((16, 128, 28 * 128), np.float32)
        for i in range(28):
            dl = 12 - i
            rel = dl * 128 + kp - qp
            bk = rel_bucket_np(rel)
            for m in range(16):
                out[m, :, i * 128:(i + 1) * 128] = rb[bk, m]
        return out
    ents = [(0, d_) for d_ in (1, 0, -1)] + [(1, d_) for d_ in (2, 1, 0, -1, -2)] + [(2, d_) for d_ in range(8, -9, -1)]
    dils = (1, 4, 16)
    out = np.empty((4, 128, 25 * 128), np.float32)
    for i, (g, dl) in enumerate(ents):
        rel = dl * 128 + kp - qp
        dil = dils[g]
        ok = (rel % dil == 0) & (np.abs(rel) <= 64 * dil)
        bk = rel_bucket_np(rel)
        for j in range(4):
            out[j, :, i * 128:(i + 1) * 128] = np.where(ok, rb[bk, g * 4 + j], NEG)
    return out


def gcols(g):
    return np.ascontiguousarray(np.asarray(g, np.float32).reshape(8, 128).T)


_PROG = None
DEBUG_LAYERS = 4


def get_prog():
    global _PROG
    if _PROG is None:
        b = Builder()
        for li in range(DEBUG_LAYERS):
            b.phase_qkv(li)
            b.phase_attn(li)
            b.phase_wo(li)
            b.phase_ffn(li)
        nc = b.finish()
        _PROG = (nc, list(b.din_names), list(b.dout_names))
    return _PROG


def kernel(x, rel_bias,
           l0_attn_norm, l0_w_qkv, l0_q_gain, l0_k_gain, l0_sink, l0_w_o,
           l0_ffn_norm, l0_w_up, l0_conv_w, l0_conv_b, l0_w_down,
           l1_attn_norm, l1_w_qkv, l1_q_gain, l1_k_gain, l1_lambda_q1, l1_lambda_k1,
           l1_lambda_q2, l1_lambda_k2, l1_sub_gain, l1_w_o,
           l1_ffn_norm, l1_w_up, l1_conv_w, l1_conv_b, l1_w_down,
           l2_attn_norm, l2_w_qkv, l2_q_gain, l2_k_gain, l2_w_o,
           l2_ffn_norm, l2_w_up, l2_conv_w, l2_conv_b, l2_w_down,
           l3_attn_norm, l3_w_qkv, l3_q_gain, l3_k_gain, l3_sink, l3_w_o,
           l3_ffn_norm, l3_w_up, l3_conv_w, l3_conv_b, l3_w_down):
    inp = {
        "x": x,
        "rel_bias": rel_bias,
        "l0_attn_norm": l0_attn_norm,
        "l0_w_qkv": l0_w_qkv,
        "l0_q_gain": l0_q_gain,
        "l0_k_gain": l0_k_gain,
        "l0_sink": l0_sink,
        "l0_w_o": l0_w_o,
        "l0_ffn_norm": l0_ffn_norm,
        "l0_w_up": l0_w_up,
        "l0_conv_w": l0_conv_w,
        "l0_conv_b": l0_conv_b,
        "l0_w_down": l0_w_down,
        "l1_attn_norm": l1_attn_norm,
        "l1_w_qkv": l1_w_qkv,
        "l1_q_gain": l1_q_gain,
        "l1_k_gain": l1_k_gain,
        "l1_lambda_q1": l1_lambda_q1,
        "l1_lambda_k1": l1_lambda_k1,
        "l1_lambda_q2": l1_lambda_q2,
        "l1_lambda_k2": l1_lambda_k2,
        "l1_sub_gain": l1_sub_gain,
        "l1_w_o": l1_w_o,
        "l1_ffn_norm": l1_ffn_norm,
        "l1_w_up": l1_w_up,
        "l1_conv_w": l1_conv_w,
        "l1_conv_b": l1_conv_b,
        "l1_w_down": l1_w_down,
        "l2_attn_norm": l2_attn_norm,
        "l2_w_qkv": l2_w_qkv,
        "l2_q_gain": l2_q_gain,
        "l2_k_gain": l2_k_gain,
        "l2_w_o": l2_w_o,
        "l2_ffn_norm": l2_ffn_norm,
        "l2_w_up": l2_w_up,
        "l2_conv_w": l2_conv_w,
        "l2_conv_b": l2_conv_b,
        "l2_w_down": l2_w_down,
        "l3_attn_norm": l3_attn_norm,
        "l3_w_qkv": l3_w_qkv,
        "l3_q_gain": l3_q_gain,
        "l3_k_gain": l3_k_gain,
        "l3_sink": l3_sink,
        "l3_w_o": l3_w_o,
        "l3_ffn_norm": l3_ffn_norm,
        "l3_w_up": l3_w_up,
        "l3_conv_w": l3_conv_w,
        "l3_conv_b": l3_conv_b,
        "l3_w_down": l3_w_down,
    }
    x = np.ascontiguousarray(np.asarray(x, np.float32))
    rel_bias = np.asarray(rel_bias, np.float32)
    shared = {"ident": np.eye(128, dtype=np.float32)}
    for kind in ("A", "B", "C"):
        shared["bt_" + kind] = bias_tables(rel_bias, kind)
    f32 = lambda a: np.ascontiguousarray(np.asarray(a, np.float32))
    for li in range(4):
        L = LAYERS[li]
        p = "l%d_" % li
        shared[p + "gcol_attn"] = gcols(inp[p + "attn_norm"])
        shared[p + "gcol_ffn"] = gcols(inp[p + "ffn_norm"])
        for w in ("w_qkv", "w_o", "w_up", "w_down"):
            shared[p + w] = f32(inp[p + w])
        shared[p + "qkg"] = np.ascontiguousarray(np.stack([f32(inp[p + "q_gain"]), f32(inp[p + "k_gain"])]))
        cwv = f32(inp[p + "conv_w"]).reshape(3, 44, 128)
        cbv = f32(inp[p + "conv_b"]).reshape(1, 44, 128)
        shared[p + "convp"] = np.ascontiguousarray(np.concatenate([cwv, cbv], 0).transpose(2, 0, 1))
        if L["kind"] == "A":
            shared[p + "sink"] = f32(inp[p + "sink"])
        if L["kind"] == "B":
            shared[p + "lam"] = np.ascontiguousarray(np.stack([f32(inp[p + k]) for k in ("lambda_q1", "lambda_k1", "lambda_q2", "lambda_k2")]))
            shared[p + "sgcol"] = f32(inp[p + "sub_gain"]).reshape(128, 1)
    import time as _t
    _t1 = _t.time()
    nc, dins, douts = get_prog()
    print("[kernel] host prep + build took %.1fs" % (_t.time() - _t1), flush=True)
    in_maps = []
    for c in range(NCORES):
        m = {}
        for n in dins:
            m[n] = x[c] if n == "x" else shared[n]
        in_maps.append(m)
    import time as _t
    _t0 = _t.time()
    res = run_bass_kernel_spmd(nc, in_maps, core_ids=list(range(NCORES)))
    print("[kernel] run_bass_kernel_spmd took %.1fs" % (_t.time() - _t0), flush=True)
    return np.stack([res.results[c]["x_out"] for c in range(NCORES)]).astype(np.float32)
```

```python
import math
from contextlib import ExitStack

import numpy as np
import ml_dtypes

import concourse.bass as bass
import concourse.mybir as mybir
from concourse.ap import AP
from concourse.bass_utils import run_bass_kernel_spmd

F32 = mybir.dt.float32
BF16 = mybir.dt.bfloat16
AF = mybir.ActivationFunctionType
ALU = mybir.AluOpType
AX = mybir.AxisListType
NPBF = ml_dtypes.bfloat16

NCORES = 4
T = 4096
NT = 32
D = 1024
DFF = 2816
EPS = 1e-6
NEG = -30000.0

LAYERS = [
    dict(kind="A", F=1536, nqb=8, nkb=2, FV=256, nfc=8),
    dict(kind="B", F=3072, nqb=8, nkb=8, FV=1024, nfc=8),
    dict(kind="C", F=2560, nqb=12, nkb=4, FV=512, nfc=4),
    dict(kind="A", F=1536, nqb=8, nkb=2, FV=256, nfc=8),
]
NI = {"A": 3, "B": 28, "C": 25}
NUNIT_BT = {"A": 16, "B": 16, "C": 4}


def lambda_init_fn(layer):
    return 0.8 - 0.6 * math.exp(-0.3 * layer)


class Op:
    __slots__ = ("eng", "fn", "deps", "needs_inc", "val", "semkey", "is_dma", "idx")

    def __init__(self, eng, fn, deps, semkey, is_dma):
        self.eng = eng
        self.fn = fn
        self.deps = deps
        self.needs_inc = False
        self.val = None
        self.semkey = semkey
        self.is_dma = is_dma


class Prog:
    ENGS = ("sync", "scalar", "vector", "gpsimd", "tensor")

    def __init__(self, nc):
        self.nc = nc
        self.ops = {e: [] for e in self.ENGS}
        self.lastw = {}
        self.readers = {}
        self.es = ExitStack()
        self.outs = []
        self.fence = []
        self.fence_pending = set()
        self.last_dma = {}
        self.epoch = 0

    def barrier(self):
        fence = []
        for e in self.ENGS:
            for o in reversed(self.ops[e]):
                if not o.is_dma:
                    fence.append(o)
                    break
        fence.extend(self.last_dma.values())
        self.fence = fence
        self.fence_pending = set(self.ENGS)
        self.epoch += 1

    def op(self, eng, fn, reads=(), writes=(), slot=None, out=False):
        deps = []
        if eng in self.fence_pending:
            deps.extend(self.fence)
            self.fence_pending.discard(eng)
        for b in reads:
            w = self.lastw.get(b)
            if w is not None:
                deps.append(w)
        for b in writes:
            w = self.lastw.get(b)
            if w is not None:
                deps.append(w)
            deps.extend(self.readers.get(b, ()))
        is_dma = slot is not None
        semkey = ("dma", slot) if is_dma else ("eng", eng, self.epoch % 3)
        o = Op(eng, fn, deps, semkey, is_dma)
        o.idx = len(self.ops[eng])
        self.ops[eng].append(o)
        if is_dma:
            self.last_dma[slot] = o
        for b in writes:
            self.lastw[b] = o
            self.readers[b] = []
        for b in reads:
            self.readers.setdefault(b, []).append(o)
        if out:
            self.outs.append(o)
        return o

    @staticmethod
    def _skip(d, o):
        return d is o or (d.eng == "tensor" and o.eng == "tensor" and not d.is_dma and not o.is_dma)

    def finalize(self):
        nc = self.nc
        final_waits = self.outs
        for e in self.ENGS:
            for o in self.ops[e]:
                best = {}
                for d in o.deps:
                    if self._skip(d, o):
                        continue
                    if d.is_dma:
                        d.needs_inc = True
                        continue
                    b = best.get(d.semkey)
                    if b is None or d.idx > b.idx:
                        best[d.semkey] = d
                for d in best.values():
                    d.needs_inc = True
        for d in final_waits:
            d.needs_inc = True
        for e in self.ENGS:
            for o in self.ops[e]:
                if o.is_dma:
                    o.needs_inc = True
        counters = {}
        for e in self.ENGS:
            for o in self.ops[e]:
                if o.needs_inc:
                    c = counters.get(o.semkey, 0) + (16 if o.is_dma else 1)
                    counters[o.semkey] = c
                    o.val = c
        sems = {}
        for i, k in enumerate(counters):
            sems[k] = self.es.enter_context(nc.semaphore("s%d" % i))
        self.nsem = len(sems)
        block = self.es.enter_context(nc.Block())

        def run(e, engine):
            known = {}
            for o in self.ops[e]:
                need = {}
                for d in o.deps:
                    if self._skip(d, o) or d.val is None:
                        continue
                    if need.get(d.semkey, 0) < d.val:
                        need[d.semkey] = d.val
                for k, v in need.items():
                    if known.get(k, 0) >= v:
                        continue
                    engine.wait_ge(sems[k], v)
                    known[k] = v
                ins = o.fn(engine)
                if o.needs_inc:
                    ins.then_inc(sems[o.semkey], 16 if o.is_dma else 1)
            if e == "sync":
                need = {}
                for d in final_waits:
                    if need.get(d.semkey, 0) < d.val:
                        need[d.semkey] = d.val
                for k, v in need.items():
                    engine.wait_ge(sems[k], v)

        @block.sync
        def _(eng):
            run("sync", eng)

        @block.scalar
        def _(eng):
            run("scalar", eng)

        @block.vector
        def _(eng):
            run("vector", eng)

        @block.gpsimd
        def _(eng):
            run("gpsimd", eng)

        @block.tensor
        def _(eng):
            run("tensor", eng)

        self.es.close()


SB_BASE = 16512
SCR_END = 229376


class Builder:
    def __init__(self):
        self.nc = bass.Bass("TRN2", target_bir_lowering=False)
        self.P = Prog(self.nc)
        self.din_names = []
        self.dout_names = []
        self.d = {}
        self.uid = 0
        self.perm_off = SB_BASE
        nc = self.nc
        self.actT = self.perm("actT", [128, 8, T + 2], BF16)
        self.ws = [self.perm("ws%d" % i, [128, 4, 512], F32) for i in range(2)]
        self.wb = [self.perm("wb%d" % i, [128, 8, 512], BF16) for i in range(2)]
        self.xts = [self.perm("xt%d" % i, [128, D], F32) for i in range(4)]
        self.hbs = [self.perm("hb%d" % i, [128, D], BF16) for i in range(2)]
        self.ident = self.perm("ident", [128, 128], BF16)
        self.identf = self.perm("identf", [128, 128], F32)
        self.sst = [self.perm("sst%d" % i, [128, 4], F32) for i in range(4)]
        self.epsb = self.perm("epsb", [128, 1], F32)
        self.gcol = self.perm("gcol", [128, 8], F32)
        self.PERM_END = (self.perm_off + 63) // 64 * 64
        self.scr_off = self.PERM_END
        self.nscr = 0
        self.xcnt = 0
        self.banks = [nc.alloc_psum_tensor("bank%d" % i, [128, 512], F32) for i in range(8)]
        self.wcount = 0
        self.xd = self.dout("x_out", [T, D], F32).ap()
        self.qT_d = nc.dram_tensor("qT_scr", [12, 128, T], BF16, kind="Internal").ap()
        self.kT_d = nc.dram_tensor("kT_scr", [8, 128, T], BF16, kind="Internal").ap()
        self.v_d = nc.dram_tensor("v_scr", [T, 1024], BF16, kind="Internal").ap()
        self.wu_bf = nc.dram_tensor("wu_bf", [D, 2 * DFF], BF16, kind="Internal").ap()
        self.wd_bf = nc.dram_tensor("wd_bf", [DFF, D], BF16, kind="Internal").ap()
        self.init_consts()

    def perm(self, name, shape, dt):
        n = int(np.prod(shape[1:])) * (4 if dt == F32 else 2)
        n = (n + 31) // 32 * 32
        t = self.nc.alloc_sbuf_tensor_at(name, shape, dt, offset=self.perm_off)
        self.perm_off += n
        return t

    def scr_reset(self):
        if self.nscr > 0:
            self.P.barrier()
        self.nscr += 1
        self.scr_off = self.PERM_END

    def scr(self, name, shape, dt):
        n = int(np.prod(shape[1:])) * (4 if dt == F32 else 2)
        n = (n + 31) // 32 * 32
        self.uid += 1
        t = self.nc.alloc_sbuf_tensor_at("%s_%d" % (name, self.uid), shape, dt, offset=self.scr_off)
        self.scr_off += n
        assert self.scr_off <= SCR_END, (name, self.scr_off)
        return t

    def din(self, name, shape, dt):
        if name not in self.d:
            self.d[name] = self.nc.dram_tensor(name, list(shape), dt, kind="ExternalInput")
            self.din_names.append(name)
        return self.d[name]

    def dout(self, name, shape, dt):
        if name not in self.d:
            self.d[name] = self.nc.dram_tensor(name, list(shape), dt, kind="ExternalOutput")
            self.dout_names.append(name)
        return self.d[name]

    def bank_bf(self, i):
        return self.banks[i][:].bitcast(BF16).rearrange("p (c t) -> p c t", t=128)

    def init_consts(self):
        P = self.P
        idd = self.din("ident", [128, 128], F32).ap()
        xin = self.din("x", [T, D], F32).ap()
        P.op("sync", lambda e: e.dma_start(out=self.identf[:], in_=idd), writes=["identf"], slot="c_id")
        P.op("vector", lambda e: e.tensor_copy(out=self.ident[:], in_=self.identf[:]), reads=["identf"], writes=["ident"])
        P.op("vector", lambda e: e.memset(self.epsb[:], EPS), writes=["eps"])
        P.op("vector", lambda e: e.memset(self.actT[:, :, 0:1], 0.0), writes=["halo"])
        P.op("vector", lambda e: e.memset(self.actT[:, :, T + 1:T + 2], 0.0), writes=["halo"])
        for t0 in range(0, NT, 8):
            P.op("sync", lambda e, t0=t0: e.dma_start(out=self.xd[t0 * 128:(t0 + 8) * 128, :], in_=xin[t0 * 128:(t0 + 8) * 128, :]),
                 writes=[("xd", t, n) for t in range(t0, t0 + 8) for n in range(2)], slot="xcp%d" % (t0 // 8), out=True)

    def x_load(self, t, n=None):
        P = self.P
        i = self.xcnt % 4
        self.xcnt += 1
        xt = self.xts[i]
        key = ("xt", i)
        if n is None:
            P.op("sync", lambda e, t=t, xt=xt: e.dma_start(out=xt[:], in_=self.xd[t * 128:(t + 1) * 128, :]),
                 reads=[("xd", t, 0), ("xd", t, 1)], writes=[key], slot="xl%d" % i)
        else:
            P.op("sync", lambda e, t=t, xt=xt, n=n: e.dma_start(out=xt[:, 0:512], in_=self.xd[t * 128:(t + 1) * 128, n * 512:(n + 1) * 512]),
                 reads=[("xd", t, n)], writes=[key], slot="xl%d" % i)
        return xt, key, i

    def x_store(self, t, n, xt, key, i):
        P = self.P
        P.op("gpsimd", lambda e, t=t, xt=xt, n=n: e.dma_start(out=self.xd[t * 128:(t + 1) * 128, n * 512:(n + 1) * 512], in_=xt[:, 0:512]),
             reads=[key], writes=[("xd", t, n)], slot="xs%d" % i, out=True)

    def load_unit(self, W, kchunks, segs, scale=None, scale_reads=()):
        P = self.P
        i = self.wcount
        self.wcount += 1
        slot = i % 2
        wb = self.wb[slot]
        key = ("wb", slot)
        Wa = W.ap()
        ncol = sum(s[1] for s in segs)
        halves = [kchunks[0:4], kchunks[4:8]]
        allkeys = []
        for hi, kc in enumerate(halves):
            if not kc:
                continue
            ws = self.ws[hi]
            co = 0
            for si, (c0, cn) in enumerate(segs):
                k0 = kc[0]
                src = Wa[k0 * 128:(k0 + len(kc)) * 128, c0:c0 + cn].rearrange("(c p) f -> p c f", p=128)
                P.op("sync", lambda e, ws=ws, src=src, co=co, cn=cn, n=len(kc): e.dma_start(out=ws[:, 0:n, co:co + cn], in_=src),
                     writes=[("ws", hi, si)], slot="ws%d_%d" % (hi, si))
                co += cn
            n = len(kc)
            if scale is None:
                P.op("gpsimd", lambda e, ws=ws, wb=wb, hi=hi, n=n, ncol=ncol: e.tensor_copy(out=wb[:, hi * 4:hi * 4 + n, 0:ncol], in_=ws[:, 0:n, 0:ncol]),
                     reads=[("ws", hi, si) for si in range(len(segs))], writes=[(key, hi, 0)])
                allkeys.append((key, hi, 0))
            else:
                for j, k in enumerate(kc):
                    sc = scale(k)
                    P.op("gpsimd", lambda e, ws=ws, wb=wb, hi=hi, j=j, ncol=ncol, sc=sc: e.tensor_scalar(out=wb[:, hi * 4 + j, 0:ncol], in0=ws[:, j, 0:ncol], scalar1=sc, scalar2=None, op0=ALU.mult),
                         reads=[("ws", hi, si) for si in range(len(segs))] + list(scale_reads), writes=[(key, hi, j)])
                    allkeys.append((key, hi, j))
        return slot, allkeys

    def precast(self, W, nchunks, ncols, Wbf, keyname, scaled):
        P = self.P
        Wa = W.ap()
        blocks = [(kg, min(4, nchunks - kg), c0) for kg in range(0, nchunks, 4) for c0 in range(0, ncols, 512)]
        keys = []
        for b, (kg, n, c0) in enumerate(blocks):
            ws = self.ws[b % 2]
            wsk = ("ws", b % 2, 0)
            ss_ = b % 4
            stg = self.wb[ss_ // 2][:, (ss_ % 2) * 4:(ss_ % 2) * 4 + n, :]
            stgk = (("wb", ss_ // 2), ss_ % 2, 0)
            src = Wa[kg * 128:(kg + n) * 128, c0:c0 + 512].rearrange("(c p) f -> p c f", p=128)
            P.op("sync", lambda e, ws=ws, src=src, n=n: e.dma_start(out=ws[:, 0:n, :], in_=src), writes=[wsk], slot="ws%d_0" % (b % 2))
            eng = "scalar" if b % 2 == 0 else "vector"
            if not scaled:
                if eng == "scalar":
                    P.op(eng, lambda e, ws=ws, stg=stg, n=n: e.activation(out=stg, in_=ws[:, 0:n, :], func=AF.Copy), reads=[wsk], writes=[stgk])
                else:
                    P.op(eng, lambda e, ws=ws, stg=stg, n=n: e.tensor_copy(out=stg, in_=ws[:, 0:n, :]), reads=[wsk], writes=[stgk])
            else:
                for j in range(n):
                    sc = self.gcol[:, kg + j:kg + j + 1]
                    if eng == "scalar":
                        P.op(eng, lambda e, ws=ws, stg=stg, j=j, sc=sc: e.activation(out=stg[:, j, :], in_=ws[:, j, :], func=AF.Copy, scale=sc), reads=[wsk, "gcol"], writes=[(stgk, j)] if False else [stgk])
                    else:
                        P.op(eng, lambda e, ws=ws, stg=stg, j=j, sc=sc: e.tensor_scalar(out=stg[:, j, :], in0=ws[:, j, :], scalar1=sc, scalar2=None, op0=ALU.mult), reads=[wsk, "gcol"], writes=[stgk])
            dst = Wbf[kg * 128:(kg + n) * 128, c0:c0 + 512].rearrange("(c p) f -> p c f", p=128)
            P.op("gpsimd", lambda e, stg=stg, dst=dst: e.dma_start(out=dst, in_=stg), reads=[stgk], writes=[(keyname, b)], slot="pc%d" % ss_)
            keys.append((keyname, b))
        return keys

    def load_unit_bf(self, Wbf, kchunks, segs, rkeys):
        P = self.P
        i = self.wcount
        self.wcount += 1
        slot = i % 2
        wb = self.wb[slot]
        keys = [(("wb", slot), 0, 0), (("wb", slot), 1, 0)]
        k0, n, co = kchunks[0], len(kchunks), 0
        for si, (c0, cn) in enumerate(segs):
            src = Wbf[k0 * 128:(k0 + n) * 128, c0:c0 + cn].rearrange("(c p) f -> p c f", p=128)
            P.op("sync", lambda e, wb=wb, src=src, n=n, co=co, cn=cn: e.dma_start(out=wb[:, 0:n, co:co + cn], in_=src),
                 reads=rkeys, writes=keys, slot="wbd%d_%d" % (slot, si))
            co += cn
        return slot, keys

    def phase_norm(self):
        P = self.P
        junk = self.banks[7]
        hbs = self.hbs
        pend = {}

        def stA(t):
            xt, xk, _ = pend[t]
            st = self.sst[t % 4]
            sk = ("sst", t % 4)
            P.op("vector", lambda e, st=st: e.memset(st[:], 0.0), writes=[sk])
            P.op("scalar", lambda e, xt=xt, st=st: e.activation(out=junk[:, 0:512], in_=xt[:, 0:512], func=AF.Square, accum_out=st[:, 0:1]),
                 reads=[xk, sk], writes=[sk, ("bank", 7)])
            P.op("scalar", lambda e, xt=xt, st=st: e.activation(out=junk[:, 0:512], in_=xt[:, 512:1024], func=AF.Square, accum_out=st[:, 1:2]),
                 reads=[xk, sk], writes=[sk, ("bank", 7)])
            P.op("vector", lambda e, st=st: e.tensor_tensor(out=st[:, 2:3], in0=st[:, 0:1], in1=st[:, 1:2], op=ALU.add), reads=[sk], writes=[sk])

        def stB(t):
            xt, xk, _ = pend[t]
            st = self.sst[t % 4]
            sk = ("sst", t % 4)
            P.op("scalar", lambda e, st=st: e.activation(out=st[:, 2:3], in_=st[:, 2:3], func=AF.Ln, scale=1.0 / D, bias=self.epsb[:, 0:1]),
                 reads=[sk, "eps"], writes=[sk])
            P.op("scalar", lambda e, st=st: e.activation(out=st[:, 3:4], in_=st[:, 2:3], func=AF.Exp, scale=-0.5), reads=[sk], writes=[sk])
            hb = hbs[t % 2]
            hk = ("hb", t % 2)
            pk = ("bank", t % 2)
            pT = self.bank_bf(t % 2)
            P.op("vector", lambda e, xt=xt, hb=hb, st=st: e.tensor_scalar(out=hb[:], in0=xt[:], scalar1=st[:, 3:4], scalar2=None, op0=ALU.mult),
                 reads=[xk, sk], writes=[hk])
            for c in range(8):
                P.op("tensor", lambda e, c=c, hb=hb, pT=pT: e.transpose(out=pT[:, c, :], in_=hb[:, c * 128:(c + 1) * 128], identity=self.ident[:]),
                     reads=[hk, "ident"], writes=[pk])

        def stC(t):
            pk = ("bank", t % 2)
            pT = self.bank_bf(t % 2)
            P.op("scalar", lambda e, t=t, pT=pT: e.activation(out=self.actT[:, :, 1 + t * 128:1 + (t + 1) * 128], in_=pT, func=AF.Copy),
                 reads=[pk], writes=[("actT", c, t) for c in range(8)])

        pend[0] = self.x_load(0)
        pend[1] = self.x_load(1)
        for i in range(NT + 2):
            if i + 2 < NT:
                pend[i + 2] = self.x_load(i + 2)
            if i - 2 >= 0:
                stC(i - 2)
            if 0 <= i - 1 < NT:
                stB(i - 1)
            if i < NT:
                stA(i)

    def load_gcol(self, name):
        P = self.P
        g = self.din(name, [128, 8], F32).ap()
        P.op("sync", lambda e: e.dma_start(out=self.gcol[:], in_=g), writes=["gcol"], slot="gcol")

    def phase_qkv(self, li):
        P = self.P
        L = LAYERS[li]
        kind = L["kind"]
        pfx = "l%d_" % li
        self.load_gcol(pfx + "gcol_attn")
        self.phase_norm()
        self.scr_reset()
        W = self.din(pfx + "w_qkv", [D, L["F"]], F32)
        dh = 128 if kind == "C" else 64
        geff_d = self.din(pfx + "qkg", [2, dh], F32).ap()
        qT_d, kT_d, v_d = self.qT_d, self.kT_d, self.v_d
        gq = self.scr("gq", [128, dh], F32)
        gk = self.scr("gk", [128, dh], F32)
        P.op("sync", lambda e: e.dma_start(out=gq[:], in_=geff_d[0].partition_broadcast(128)), writes=["gq"], slot="gq")
        P.op("sync", lambda e: e.dma_start(out=gk[:], in_=geff_d[1].partition_broadcast(128)), writes=["gk"], slot="gk")
        P.op("vector", lambda e: e.scalar_tensor_tensor(out=gk[:], in0=gq[:], scalar=float(dh) ** -0.5, in1=gk[:], op0=ALU.mult, op1=ALU.mult),
             reads=["gq", "gk"], writes=["gk"])
        stages = [self.scr("stage", [128, 4, T], BF16) for _ in range(2)]
        ND = 4
        sqs = [self.scr("sq", [128, 512], F32) for _ in range(ND)]
        kfs = [self.scr("kf", [128, 512], F32) for _ in range(ND)]
        qns = [self.scr("qn", [128, 512], BF16) for _ in range(ND)]
        vsts = [self.scr("vst", [128, 512], BF16) for _ in range(ND)]
        ssq = [self.scr("ssq", [128, 8], F32) for _ in range(ND)]
        PYB = (2, 3, 4, 5)
        PTB = (0, 1, 6, 7)
        if kind == "A":
            chunks = [[("q", 0, 512, 0)], [("q", 0, 512, 4)], [("k", 0, 256, 0), ("v", 256, 256, 0)]]
        elif kind == "B":
            chunks = [[("q", 0, 512, 0)], [("q", 0, 512, 4)], [("k", 0, 512, 0)], [("k", 0, 512, 4)], [("v", 0, 512, 0)], [("v", 0, 512, 512)]]
        else:
            chunks = [[("q", 0, 512, 0)], [("q", 0, 512, 4)], [("q", 0, 512, 8)], [("k", 0, 512, 0)], [("v", 0, 512, 0)]]
        gsc = lambda k: self.gcol[:, k:k + 1]
        units = [None] * len(chunks)
        units[0] = self.load_unit(W, list(range(8)), [(0, 512)], scale=gsc, scale_reads=["gcol"])
        it = 0
        pend_rest = []

        def pipe_step(keep):
            n = len(pend_rest)
            for idx in range(n):
                ent = pend_rest[idx]
                lag = n - 1 - idx
                want = 3 if keep == 0 else min(3, lag)
                if keep == 0:
                    want = min(3, ent[1] + 1)
                while ent[1] < want:
                    ent[0](ent[1])
                    ent[1] += 1
            while pend_rest and pend_rest[0][1] >= 3:
                pend_rest.pop(0)

        for ci, segs in enumerate(chunks):
            if ci + 1 < len(chunks):
                units[ci + 1] = self.load_unit(W, list(range(8)), [((ci + 1) * 512, 512)], scale=gsc, scale_reads=["gcol"])
            slot, wkeys = units[ci]
            wb = self.wb[slot]
            stage = stages[ci % 2]
            stk = ("stage", ci % 2)
            for t in range(NT):
                bi = PYB[it % ND]
                py = self.banks[bi]
                pyk = ("bank", bi)
                for c in range(8):
                    P.op("tensor", lambda e, c=c, t=t, py=py, wb=wb: e.matmul(py[:, 0:512], lhsT=self.actT[:, c, 1 + t * 128:1 + (t + 1) * 128], rhs=wb[:, c, 0:512], start=(c == 0), stop=(c == 7)),
                         reads=[("actT", c, t)] + wkeys, writes=[pyk])
                def rest(stg_i, segs=segs, it=it, t=t, py=py, pyk=pyk, stage=stage, stk=stk):
                    for (ty, off, w, dst) in segs:
                        if ty == "v":
                            if stg_i != 0:
                                continue
                            vst = vsts[it % ND]
                            vk = ("vst", it % ND)
                            P.op("scalar", lambda e, py=py, vst=vst, off=off, w=w: e.activation(out=vst[:, 0:w], in_=py[:, off:off + w], func=AF.Copy),
                                 reads=[pyk], writes=[vk])
                            P.op("gpsimd", lambda e, vst=vst, t=t, dst=dst, w=w: e.dma_start(out=v_d[t * 128:(t + 1) * 128, dst:dst + w], in_=vst[:, 0:w]),
                                 reads=[vk], writes=[("vd", t, dst)], slot="vst%d" % (it % ND))
                            continue
                        nh = w // dh
                        sq = sqs[it % ND]
                        sqk = ("sq", it % ND)
                        s_ = ssq[it % ND]
                        sk = ("ssq", it % ND)
                        qn = qns[it % ND]
                        qk = ("qn", it % ND)
                        tb = PTB[it % ND]
                        pT = self.bank_bf(tb)
                        pk = ("bank", tb)
                        nb = w // 128
                        if stg_i == 0:
                            P.op("scalar", lambda e, py=py, sq=sq, off=off, w=w: e.activation(out=sq[:, 0:w], in_=py[:, off:off + w], func=AF.Square),
                                 reads=[pyk], writes=[sqk])
                            P.op("vector", lambda e, sq=sq, s_=s_, w=w, nh=nh: e.tensor_reduce(out=s_[:, 0:nh], in_=sq[:, 0:w].rearrange("p (h d) -> p h d", d=dh), axis=AX.X, op=ALU.add),
                                 reads=[sqk], writes=[sk])
                        elif stg_i == 1:
                            P.op("scalar", lambda e, s_=s_, nh=nh: e.activation(out=s_[:, 0:nh], in_=s_[:, 0:nh], func=AF.Ln, scale=1.0 / dh, bias=self.epsb[:, 0:1]),
                                 reads=[sk, "eps"], writes=[sk])
                            P.op("scalar", lambda e, s_=s_, nh=nh: e.activation(out=s_[:, 0:nh], in_=s_[:, 0:nh], func=AF.Exp, scale=-0.5), reads=[sk], writes=[sk])
                            rb = AP(s_, 0, [[8, 128], [1, nh], [0, dh]])
                            if ty == "q":
                                P.op("vector", lambda e, py=py, qn=qn, off=off, w=w, rb=rb: e.tensor_tensor(out=qn[:, 0:w].rearrange("p (h d) -> p h d", d=dh), in0=py[:, off:off + w].rearrange("p (h d) -> p h d", d=dh), in1=rb, op=ALU.mult),
                                     reads=[pyk, sk], writes=[qk])
                            else:
                                kf = kfs[it % ND]
                                kfk = ("kf", it % ND)
                                gb = AP(gk, 0, [[dh, 128], [0, nh], [1, dh]])
                                P.op("vector", lambda e, py=py, kf=kf, off=off, w=w, rb=rb: e.tensor_tensor(out=kf[:, 0:w].rearrange("p (h d) -> p h d", d=dh), in0=py[:, off:off + w].rearrange("p (h d) -> p h d", d=dh), in1=rb, op=ALU.mult),
                                     reads=[pyk, sk], writes=[kfk])
                                P.op("gpsimd", lambda e, kf=kf, qn=qn, w=w, gb=gb: e.tensor_tensor(out=qn[:, 0:w].rearrange("p (h d) -> p h d", d=dh), in0=kf[:, 0:w].rearrange("p (h d) -> p h d", d=dh), in1=gb, op=ALU.mult),
                                     reads=[kfk, "gk"], writes=[qk])
                            for j in range(nb):
                                P.op("tensor", lambda e, j=j, qn=qn, pT=pT: e.transpose(out=pT[:, j, :], in_=qn[:, j * 128:(j + 1) * 128], identity=self.ident[:]),
                                     reads=[qk, "ident"], writes=[pk])
                        else:
                            P.op("scalar", lambda e, pT=pT, stage=stage, nb=nb, t=t: e.activation(out=stage[:, 0:nb, t * 128:(t + 1) * 128], in_=pT[:, 0:nb, :], func=AF.Copy),
                                 reads=[pk], writes=[(stk, t)])
                pend_rest.append([rest, 0])
                pipe_step(3)
                it += 1
            while pend_rest:
                pipe_step(0)
            for (ty, off, w, dst) in segs:
                if ty == "v":
                    continue
                dd = qT_d if ty == "q" else kT_d
                dk = "qTd" if ty == "q" else "kTd"
                for j in range(w // 128):
                    P.op("gpsimd", lambda e, dd=dd, j=j, dst=dst, stage=stage: e.dma_start(out=dd[dst + j], in_=stage[:, j, :]),
                         reads=[(stk, t) for t in range(NT)], writes=[(dk, dst + j)], slot="stg%d_%d" % (ci % 2, j))

    ST_BANKS = (0, 1, 6)

    def attn_item(self, it, mms, tb_ap, ncols, pvs, tbkey, extra_reads, post=None):
        P = self.P
        bi = self.ST_BANKS[it % 3]
        st = self.banks[bi]
        stk = ("bank", bi)
        sc = self.a_sc[it % 3]
        sck = ("sc", it % 3)
        pt = self.a_pt[it % 3]
        ptk = ("pt", it % 3)
        for (lh, rh, c0, n) in mms:
            P.op("tensor", lambda e, lh=lh, rh=rh, c0=c0, n=n, st=st: e.matmul(st[:, c0:c0 + n], lhsT=lh, rhs=rh, start=True, stop=True),
                 reads=extra_reads, writes=[stk])
        P.op("vector", lambda e, st=st, sc=sc, tb_ap=tb_ap, ncols=ncols: e.tensor_tensor(out=sc[:, 0:ncols], in0=st[:, 0:ncols], in1=tb_ap, op=ALU.add),
             reads=[stk, tbkey], writes=[sck])
        P.op("scalar", lambda e, sc=sc, pt=pt, ncols=ncols: e.activation(out=pt[:, 0:ncols], in_=sc[:, 0:ncols], func=AF.Exp),
             reads=[sck], writes=[ptk])
        q = self.a_queue
        q.append((pt, ptk, pvs, list(extra_reads), post))
        while len(q) > 2:
            self._attn_back(q.pop(0))

    def _attn_back(self, pend):
        P = self.P
        pt, ptk, pvs, extra_reads, post = pend
        for (c0, v_ap, ab, wdt, s0, s1) in pvs:
            acc = self.banks[ab]
            P.op("tensor", lambda e, c0=c0, v_ap=v_ap, acc=acc, wdt=wdt, s0=s0, s1=s1, pt=pt: e.matmul(acc[:, 0:wdt], lhsT=pt[:, c0:c0 + 128], rhs=v_ap, start=s0, stop=s1),
                 reads=[ptk] + extra_reads, writes=[("bank", ab)])
        if post is not None:
            post()

    def attn_flush(self):
        q = self.a_queue
        while q:
            self._attn_back(q.pop(0))

    def out_transpose(self, on, onk, chunk, qt, trk):
        P = self.P
        bi = 7
        pT = self.bank_bf(bi)
        P.op("tensor", lambda e, on=on, pT=pT: e.transpose(out=pT[:, 0, :], in_=on, identity=self.ident[:]),
             reads=[onk, "ident"], writes=[("bank", bi)])
        P.op("scalar", lambda e, pT=pT, chunk=chunk, qt=qt: e.activation(out=self.actT[:, chunk, 1 + qt * 128:1 + (qt + 1) * 128], in_=pT[:, 0, :], func=AF.Copy),
             reads=[("bank", bi)], writes=[("actT", chunk, qt)])

    def phase_attn(self, li):
        P = self.P
        L = LAYERS[li]
        kind = L["kind"]
        pfx = "l%d_" % li
        self.scr_reset()
        nkb, nqb, FV = L["nkb"], L["nqb"], L["FV"]
        qT_d, kT_d, v_d = self.qT_d, self.kT_d, self.v_d
        ni = NI[kind]
        bt_d = self.din("bt_" + kind, [NUNIT_BT[kind], 128, ni * 128], F32).ap()
        self.a_sc = [self.scr("sc", [128, 512], F32) for _ in range(3)]
        self.a_pt = [self.scr("pt", [128, 512], BF16) for _ in range(3)]
        self.a_queue = []
        rr = [self.scr("rr", [128, 4], F32) for _ in range(4)]
        ons = [self.scr("on", [128, 128], BF16) for _ in range(2)]
        v_r = v_d.rearrange("(kb p) f -> p kb f", p=128)
        qkeys = [("qTd", b) for b in range(nqb)]
        kkeys = [("kTd", b) for b in range(nkb)]
        vkeys = [("vd", t, c0) for t in range(NT) for c0 in range(0, FV, 512 if FV >= 512 else 256)]
        it = 0
        fin = 0
        if kind == "B":
            dv = 128
            KTs = [self.scr("KT", [128, T], BF16) for _ in range(2)]
            Vs = [self.scr("V", [128, NT, dv + 1], BF16) for _ in range(2)]
            QTz = [self.scr("QT", [128, T], BF16) for _ in range(2)]
            TB = self.scr("TB", [128, ni * 128], F32)
            o1 = self.scr("o1", [128, NT, 128], F32)
            P.op("gpsimd", lambda e: e.memset(QTz[0][64:128, :], 0.0), writes=[("QTz", 0)])
            P.op("gpsimd", lambda e: e.memset(QTz[1][0:64, :], 0.0), writes=[("QTz", 1)])
            ods = [self.scr("od", [128, 128], F32) for _ in range(2)]
            lam = self.hbs[0][:].bitcast(F32)[:, 0:256].rearrange("p (a d) -> p a d", d=64)
            lamv = self.scr("lamv", [128, 4], F32)
            lam_d = self.din(pfx + "lam", [4, 64], F32).ap()
            P.op("sync", lambda e: e.dma_start(out=lam, in_=AP(lam_d.tensor, 0, [[0, 128], [64, 4], [1, 64]])), writes=["lam", ("hb", 0)], slot="lam")
            P.op("vector", lambda e: e.tensor_tensor(out=lam[:, 0, :], in0=lam[:, 0, :], in1=lam[:, 1, :], op=ALU.mult), reads=["lam"], writes=["lam"])
            P.op("vector", lambda e: e.tensor_tensor(out=lam[:, 2, :], in0=lam[:, 2, :], in1=lam[:, 3, :], op=ALU.mult), reads=["lam"], writes=["lam"])
            P.op("vector", lambda e: e.tensor_reduce(out=lamv[:, 0:1], in_=lam[:, 0, :], axis=AX.X, op=ALU.add), reads=["lam"], writes=["lamv"])
            P.op("vector", lambda e: e.tensor_reduce(out=lamv[:, 1:2], in_=lam[:, 2, :], axis=AX.X, op=ALU.add), reads=["lam"], writes=["lamv", ("hb", 0)])
            P.op("scalar", lambda e: e.activation(out=lamv[:, 0:2], in_=lamv[:, 0:2], func=AF.Exp), reads=["lamv"], writes=["lamv"])
            P.op("vector", lambda e: e.scalar_tensor_tensor(out=lamv[:, 2:3], in0=lamv[:, 1:2], scalar=-lambda_init_fn(li), in1=lamv[:, 0:1], op0=ALU.add, op1=ALU.subtract),
                 reads=["lamv"], writes=["lamv"])
            neglam = lamv[:, 2:3]
            for vb in Vs:
                P.op("gpsimd", lambda e, vb=vb: e.memset(vb[:, :, dv:dv + 1], 1.0), writes=[("Vones", id(vb))])

            def load_head(h):
                s = h % 2
                P.op("sync", lambda e, h=h, s=s: e.dma_start(out=KTs[s][:], in_=kT_d[h]), reads=kkeys, writes=[("KT", s)], slot="kt%da" % s)
                for half in range(2):
                    P.op("sync", lambda e, h=h, s=s, half=half: e.dma_start(out=Vs[s][:, half * 16:(half + 1) * 16, 0:dv], in_=v_r[:, half * 16:(half + 1) * 16, h * dv:(h + 1) * dv]),
                         reads=vkeys, writes=[("V", s)], slot="v%d_%d" % (s, half))

            fin_box = [0]

            def fin_B(qc, j, h):
                for jq in range(4):
                    qt = qc * 4 + jq
                    acc = self.banks[2 + jq]
                    ak = ("bank", 2 + jq)
                    fin_box[0] += 1
                    fin = fin_box[0]
                    r = rr[fin % 4]
                    rk = ("rr", fin % 4)
                    P.op("vector", lambda e, acc=acc, r=r: e.reciprocal(out=r[:, 0:1], in_=acc[:, dv:dv + 1]), reads=[ak], writes=[rk])
                    if j == 0:
                        P.op("vector", lambda e, acc=acc, r=r, qt=qt: e.tensor_scalar(out=o1[:, qt, :], in0=acc[:, 0:dv], scalar1=r[:, 0:1], scalar2=None, op0=ALU.mult),
                             reads=[ak, rk], writes=[("o1", qt)])
                    else:
                        od = ods[fin % 2]
                        odk = ("od", fin % 2)
                        on = ons[fin % 2]
                        onk = ("on", fin % 2)
                        P.op("vector", lambda e, r=r: e.tensor_scalar(out=r[:, 1:2], in0=r[:, 0:1], scalar1=neglam, scalar2=None, op0=ALU.mult),
                             reads=[rk, "lamv"], writes=[rk])
                        P.op("vector", lambda e, acc=acc, r=r, qt=qt, od=od: e.scalar_tensor_tensor(out=od[:], in0=acc[:, 0:dv], scalar=r[:, 1:2], in1=o1[:, qt, :], op0=ALU.mult, op1=ALU.add),
                             reads=[ak, rk, ("o1", qt)], writes=[odk])
                        P.op("vector", lambda e, r=r: e.memset(r[:, 2:3], 0.0), reads=[], writes=[rk])
                        P.op("scalar", lambda e, od=od, on=on, r=r: e.activation(out=on[:], in_=od[:], func=AF.Square, accum_out=r[:, 2:3]),
                             reads=[odk, rk], writes=[rk, onk])
                        P.op("scalar", lambda e, r=r: e.activation(out=r[:, 2:3], in_=r[:, 2:3], func=AF.Ln, scale=1.0 / 128, bias=self.epsb[:, 0:1]),
                             reads=[rk, "eps"], writes=[rk])
                        P.op("scalar", lambda e, r=r: e.activation(out=r[:, 2:3], in_=r[:, 2:3], func=AF.Exp, scale=-0.5), reads=[rk], writes=[rk])
                        P.op("vector", lambda e, od=od, on=on, r=r: e.tensor_scalar(out=on[:], in0=od[:], scalar1=r[:, 2:3], scalar2=None, op0=ALU.mult),
                             reads=[odk, rk], writes=[onk])
                        self.out_transpose(on[:], onk, h, qt, fin)

            load_head(0)
            for h in range(8):
                s = h % 2
                self.attn_flush()
                if h + 1 < 8:
                    load_head(h + 1)
                P.op("sync", lambda e, h=h: e.dma_start(out=QTz[0][0:64, :], in_=qT_d[h, 0:64, :]), reads=qkeys + [("QTz", 0)], writes=["QT"], slot="qt")
                P.op("sync", lambda e, h=h: e.dma_start(out=QTz[1][64:128, :], in_=qT_d[h, 64:128, :]), reads=qkeys + [("QTz", 1)], writes=["QT"], slot="qtb")
                KT, V = KTs[s], Vs[s]
                rds = [("KT", s), ("V", s), "QT", ("Vones", id(V))]
                for j in range(2):
                    for half in range(2):
                        hw = ni * 64
                        P.op("sync", lambda e, h=h, j=j, half=half, hw=hw: e.dma_start(out=TB[:, half * hw:(half + 1) * hw], in_=bt_d[h * 2 + j, :, half * hw:(half + 1) * hw]),
                             writes=["TB"], slot="tb%d" % half)
                    for qc in range(NT // 4):
                        for kb in range(NT):
                            dp = min(max(kb - 4 * qc, -12), 12)
                            i0 = 12 - dp
                            mms = [(KT[:, kb * 128:(kb + 1) * 128], QTz[j][:, qc * 512:(qc + 1) * 512], 0, 512)]
                            pvs = [(jq * 128, V[:, kb, 0:dv + 1], 2 + jq, dv + 1, kb == 0, kb == NT - 1) for jq in range(4)]
                            post = None
                            if kb == NT - 1:
                                post = (lambda qc=qc, j=j, h=h: fin_B(qc, j, h))
                            self.attn_item(it, mms, TB[:, i0 * 128:i0 * 128 + 512], 512, pvs, "TB", rds, post)
                            it += 1
            self.attn_flush()
        elif kind == "A":
            dv = 64
            NE = NT + 2
            KTs = [self.scr("KT", [64, NE * 128], BF16) for _ in range(2)]
            Vs = [self.scr("V", [128, NE, dv + 1], BF16) for _ in range(2)]
            QTs = [self.scr("QT", [64, T], BF16) for _ in range(2)]
            TBs = [self.scr("TB", [128, ni * 128], F32) for _ in range(2)]
            pairs = [self.scr("pair", [128, NT, 128], BF16) for _ in range(2)]
            esink = self.scr("esink", [128, 16], F32)
            sink_d = self.din(pfx + "sink", [16], F32).ap()
            P.op("sync", lambda e: e.dma_start(out=esink[:], in_=sink_d.partition_broadcast(128)), writes=["esink"], slot="esink")
            P.op("scalar", lambda e: e.activation(out=esink[:], in_=esink[:], func=AF.Exp), reads=["esink"], writes=["esink"])
            for s_i in range(2):
                vb, kb_ = Vs[s_i], KTs[s_i]
                P.op("gpsimd", lambda e, vb=vb: e.memset(vb[:], 0.0), writes=[("Vones", s_i), ("V", s_i)])
                P.op("gpsimd", lambda e, vb=vb: e.memset(vb[:, 1:NE - 1, dv:dv + 1], 1.0), writes=[("Vones", s_i), ("V", s_i)])
                P.op("gpsimd", lambda e, kb_=kb_: e.memset(kb_[:], 0.0), writes=[("KT", s_i)])

            def load_kv(kvh):
                s = kvh % 2
                blk, r0 = kvh // 2, (kvh % 2) * 64
                P.op("sync", lambda e: e.dma_start(out=KTs[s][:, 128:128 + T], in_=kT_d[blk, r0:r0 + 64, :]), reads=kkeys, writes=[("KT", s)], slot="kt%db" % s)
                for half in range(2):
                    P.op("sync", lambda e, half=half: e.dma_start(out=Vs[s][:, 1 + half * 16:1 + (half + 1) * 16, 0:dv], in_=v_r[:, half * 16:(half + 1) * 16, kvh * dv:(kvh + 1) * dv]),
                         reads=vkeys, writes=[("V", s)], slot="v%db%d" % (s, half))

            def load_q(h):
                s = h % 2
                P.op("sync", lambda e: e.dma_start(out=QTs[s][:], in_=qT_d[h // 2, (h % 2) * 64:(h % 2) * 64 + 64, :]), reads=qkeys, writes=[("QT", s)], slot="qt%d" % s)
                P.op("sync", lambda e: e.dma_start(out=TBs[s][:], in_=bt_d[h]), writes=[("TB", s)], slot="tb%d" % s)

            fin_box = [0]
            load_kv(0)
            load_q(0)
            for h in range(16):
                kvh = h // 4
                s = kvh % 2
                self.attn_flush()
                if h % 4 == 0 and kvh + 1 < 4:
                    load_kv(kvh + 1)
                if h + 1 < 16:
                    load_q(h + 1)
                KT, V, QT, TB = KTs[s], Vs[s], QTs[h % 2], TBs[h % 2]
                pair = pairs[(h // 2) % 2]
                rds = [("KT", s), ("V", s), ("QT", h % 2), ("Vones", s)]
                for qt in range(NT):
                    ab = 2 + (it % 4)
                    mms = [(KT[0:64, (qt + 2 - i) * 128:(qt + 3 - i) * 128], QT[0:64, qt * 128:(qt + 1) * 128], i * 128, 128) for i in range(3)]
                    pvs = [(i * 128, V[:, qt + 2 - i, 0:dv + 1], ab, dv + 1, i == 0, i == 2) for i in range(3)]
                    def post_A(acc=self.banks[ab], ak=("bank", ab), qt=qt, h=h, pair=pair):
                        fin_box[0] += 1
                        fin = fin_box[0]
                        r = rr[fin % 4]
                        rk = ("rr", fin % 4)
                        P.op("vector", lambda e, acc=acc, r=r, h=h: e.tensor_tensor(out=r[:, 0:1], in0=acc[:, dv:dv + 1], in1=esink[:, h:h + 1], op=ALU.add),
                             reads=[ak, "esink"], writes=[rk])
                        P.op("vector", lambda e, r=r: e.reciprocal(out=r[:, 1:2], in_=r[:, 0:1]), reads=[rk], writes=[rk])
                        P.op("vector", lambda e, acc=acc, r=r, pair=pair, qt=qt, h=h: e.tensor_scalar(out=pair[:, qt, (h % 2) * 64:(h % 2) * 64 + 64], in0=acc[:, 0:dv], scalar1=r[:, 1:2], scalar2=None, op0=ALU.mult),
                             reads=[ak, rk], writes=[("pair", (h // 2) % 2, qt, h % 2)])
                    self.attn_item(it, mms, TB[:, 0:384], 384, pvs, ("TB", h % 2), rds, post_A)
                    it += 1
                if h % 2 == 1:
                    self.attn_flush()
                    for qt in range(NT):
                        fin += 1
                        bi = 7
                        pT = self.bank_bf(bi)
                        P.op("tensor", lambda e, pair=pair, qt=qt, pT=pT: e.transpose(out=pT[:, 0, :], in_=pair[:, qt, :], identity=self.ident[:]),
                             reads=[("pair", (h // 2) % 2, qt, 0), ("pair", (h // 2) % 2, qt, 1), "ident"], writes=[("bank", bi)])
                        P.op("scalar", lambda e, pT=pT, h=h, qt=qt: e.activation(out=self.actT[:, h // 2, 1 + qt * 128:1 + (qt + 1) * 128], in_=pT[:, 0, :], func=AF.Copy),
                             reads=[("bank", bi)], writes=[("actT", h // 2, qt)])
        else:
            dv = 128
            NE = NT + 16
            KT = self.scr("KT", [128, NE * 128], BF16)
            V = self.scr("V", [128, NE, dv + 1], BF16)
            QT = self.scr("QT", [128, 3, T], BF16)
            TB = self.scr("TB", [128, ni * 128], F32)
            P.op("gpsimd", lambda e: e.memset(V[:], 0.0), writes=["Vones", "V"])
            P.op("gpsimd", lambda e: e.memset(V[:, 8:NE - 8, dv:dv + 1], 1.0), writes=["Vones", "V"])
            P.op("gpsimd", lambda e: e.memset(KT[:], 0.0), writes=["KT"])
            ents = [(0, d_) for d_ in (1, 0, -1)] + [(1, d_) for d_ in (2, 1, 0, -1, -2)] + [(2, d_) for d_ in range(8, -9, -1)]
            fin_box = [0]
            for j in range(4):
                self.attn_flush()
                P.op("sync", lambda e, j=j: e.dma_start(out=KT[:, 1024:1024 + T], in_=kT_d[j]), reads=kkeys, writes=["KT"], slot="ktb")
                for half in range(2):
                    P.op("sync", lambda e, j=j, half=half: e.dma_start(out=V[:, 8 + half * 16:8 + (half + 1) * 16, 0:dv], in_=v_r[:, half * 16:(half + 1) * 16, j * dv:(j + 1) * dv]),
                         reads=vkeys, writes=["V"], slot="vb%d" % half)
                for g in range(3):
                    P.op("sync", lambda e, g=g, j=j: e.dma_start(out=QT[:, g, :], in_=qT_d[g * 4 + j]), reads=qkeys, writes=["QT"], slot="qt%d" % g)
                P.op("sync", lambda e, j=j: e.dma_start(out=TB[:], in_=bt_d[j]), writes=["TB"], slot="tb")
                rds = ["KT", "V", "QT", "Vones"]
                for qt in range(NT):
                    ab = 2 + (qt % 4)
                    for i0 in range(0, 25, 4):
                        grp = list(range(i0, min(i0 + 4, 25)))
                        mms = []
                        pvs = []
                        for n_, idx in enumerate(grp):
                            g, dl = ents[idx]
                            eb = qt + 8 + dl
                            mms.append((KT[:, eb * 128:(eb + 1) * 128], QT[:, g, qt * 128:(qt + 1) * 128], n_ * 128, 128))
                            pvs.append((n_ * 128, V[:, eb, 0:dv + 1], ab, dv + 1, idx == 0, idx == 24))
                        post = None
                        if grp[-1] == 24:
                            def post(acc=self.banks[ab], ak=("bank", ab), qt=qt, j=j):
                                fin_box[0] += 1
                                fin = fin_box[0]
                                r = rr[fin % 4]
                                rk = ("rr", fin % 4)
                                on = ons[fin % 2]
                                onk = ("on", fin % 2)
                                P.op("vector", lambda e, acc=acc, r=r: e.reciprocal(out=r[:, 0:1], in_=acc[:, dv:dv + 1]), reads=[ak], writes=[rk])
                                P.op("vector", lambda e, acc=acc, r=r, on=on: e.tensor_scalar(out=on[:], in0=acc[:, 0:dv], scalar1=r[:, 0:1], scalar2=None, op0=ALU.mult),
                                     reads=[ak, rk], writes=[onk])
                                self.out_transpose(on[:], onk, j, qt, fin)
                        self.attn_item(it, mms, TB[:, i0 * 128:(i0 + len(grp)) * 128], len(grp) * 128, pvs, "TB", rds, post)
                        it += 1
            self.attn_flush()

    def phase_wo(self, li):
        P = self.P
        L = LAYERS[li]
        pfx = "l%d_" % li
        nfc = L["nfc"]
        W = self.din(pfx + "w_o", [nfc * 128, D], F32)
        scale = None
        srd = ()
        if L["kind"] == "B":
            sg = self.din(pfx + "sgcol", [128, 1], F32).ap()
            sgt = self.scr("sgt", [128, 1], F32)
            P.op("sync", lambda e: e.dma_start(out=sgt[:], in_=sg), writes=["sgt"], slot="sgt")
            P.op("vector", lambda e: e.tensor_scalar(out=sgt[:], in0=sgt[:], scalar1=1.0 - lambda_init_fn(li), scalar2=None, op0=ALU.mult), reads=["sgt"], writes=["sgt"])
            scale = lambda k: sgt[:, 0:1]
            srd = ["sgt"]
        kch = list(range(nfc))
        units = [self.load_unit(W, kch, [(n * 512, 512)], scale=scale, scale_reads=srd) for n in range(2)]
        pend = [self.x_load(0, 0), self.x_load(0, 1)]
        for t in range(NT):
            for n in range(2):
                nx = t * 2 + n + 2
                if nx < NT * 2:
                    pend.append(self.x_load(nx // 2, nx % 2))
                slot, wkeys = units[n]
                wb = self.wb[slot]
                bi = 2 + n
                py = self.banks[bi]
                for c in range(nfc):
                    P.op("tensor", lambda e, c=c, t=t, py=py, wb=wb: e.matmul(py[:, 0:512], lhsT=self.actT[:, c, 1 + t * 128:1 + (t + 1) * 128], rhs=wb[:, c, 0:512], start=(c == 0), stop=(c == nfc - 1)),
                         reads=[("actT", c, t)] + wkeys, writes=[("bank", bi)])
                xt, xk, xi = pend[t * 2 + n]
                P.op("vector", lambda e, xt=xt, py=py: e.tensor_tensor(out=xt[:, 0:512], in0=py[:, 0:512], in1=xt[:, 0:512], op=ALU.add),
                     reads=[("bank", bi), xk], writes=[xk])
                self.x_store(t, n, xt, xk, xi)

    def phase_ffn(self, li):
        P = self.P
        pfx = "l%d_" % li
        self.load_gcol(pfx + "gcol_ffn")
        self.phase_norm()
        self.scr_reset()
        Wu = self.din(pfx + "w_up", [D, 2 * DFF], F32)
        Wd = self.din(pfx + "w_down", [DFF, D], F32)
        cw_d = self.din(pfx + "convp", [128, 4, 44], F32).ap()
        cw = self.scr("cw", [128, 4, 44], F32)
        P.op("sync", lambda e: e.dma_start(out=cw[:], in_=cw_d), writes=["cw"], slot="cw")
        wu_keys = self.precast(Wu, 8, 2 * DFF, self.wu_bf, "wubf", True)
        wd_keys = self.precast(Wd, 22, D, self.wd_bf, "wdbf", False)
        TC = 1024
        NK = T // TC
        a = self.scr("a", [128, 22, TC], BF16)
        Us = [[self.scr("U", [128, TC + 2], F32) for _ in range(2)] for _ in range(2)]
        T1s = [[self.scr("T1", [128, TC], F32) for _ in range(2)] for _ in range(2)]
        gsc = lambda k: self.gcol[:, k:k + 1]
        up_units = [[(f0 * 128, 256), (DFF + f0 * 128, 256)] for f0 in range(0, 22, 2)]
        dn_units = [(kc, n) for n in range(2) for kc in (list(range(0, 8)), list(range(8, 16)), list(range(16, 22)))]
        pcount = 0
        for k in range(NK):
            cb = TC * k
            allr = [("actT", c, t) for c in range(8) for t in range(max(8 * k - 1, 0), min(8 * k + 9, NT))] + ["halo"]
            nxt = self.load_unit_bf(self.wu_bf, list(range(8)), up_units[0], wu_keys)
            for ui in range(11):
                cur = nxt
                if ui + 1 < 11:
                    nxt = self.load_unit_bf(self.wu_bf, list(range(8)), up_units[ui + 1], wu_keys)
                slot, wkeys = cur
                wb = self.wb[slot]
                for pi in range(2):
                    f = ui * 2 + pi
                    par = pcount % 2
                    pcount += 1
                    for which in range(2):
                        off = which * 256 + pi * 128
                        fc = f + 22 * which
                        bA, bB, bE = self.banks[4 * which], self.banks[4 * which + 1], self.banks[4 * which + 2]
                        bks = [("bank", 4 * which + i) for i in range(3)]
                        for c in range(8):
                            P.op("tensor", lambda e, c=c, wb=wb, off=off, bA=bA, cb=cb: e.matmul(bA[:, 0:512], lhsT=wb[:, c, off:off + 128], rhs=self.actT[:, c, cb + 1:cb + 513], start=(c == 0), stop=(c == 7)),
                                 reads=allr + wkeys, writes=[bks[0]])
                        for c in range(8):
                            P.op("tensor", lambda e, c=c, wb=wb, off=off, bB=bB, cb=cb: e.matmul(bB[:, 0:512], lhsT=wb[:, c, off:off + 128], rhs=self.actT[:, c, cb + 513:cb + 1025], start=(c == 0), stop=(c == 7)),
                                 reads=allr + wkeys, writes=[bks[1]])
                        for c in range(8):
                            P.op("tensor", lambda e, c=c, wb=wb, off=off, bE=bE, cb=cb: e.matmul(bE[:, 0:2], lhsT=wb[:, c, off:off + 128], rhs=self.actT[:, c, cb:cb + 1026:1025], start=(c == 0), stop=(c == 7)),
                                 reads=allr + wkeys, writes=[bks[2]])
                        U = Us[which][par]
                        uk = ("U", which, par)
                        t1 = T1s[which][par]
                        tk = ("T1", which, par)
                        P.op("scalar", lambda e, U=U, bA=bA: e.activation(out=U[:, 1:513], in_=bA[:, 0:512], func=AF.Copy), reads=[bks[0]], writes=[(uk, 0)])
                        P.op("scalar", lambda e, U=U, bB=bB: e.activation(out=U[:, 513:1025], in_=bB[:, 0:512], func=AF.Copy), reads=[bks[1]], writes=[(uk, 1)])
                        P.op("scalar", lambda e, U=U, bE=bE: e.activation(out=U[:, 0:1026:1025], in_=bE[:, 0:2], func=AF.Copy), reads=[bks[2]], writes=[(uk, 2)])
                        P.op("scalar", lambda e, t1=t1, bA=bA, fc=fc: e.activation(out=t1[:, 0:512], in_=bA[:, 0:512], func=AF.Identity, scale=cw[:, 1, fc:fc + 1], bias=cw[:, 3, fc:fc + 1]),
                             reads=[bks[0], "cw"], writes=[(tk, 0)])
                        P.op("scalar", lambda e, t1=t1, bB=bB, fc=fc: e.activation(out=t1[:, 512:1024], in_=bB[:, 0:512], func=AF.Identity, scale=cw[:, 1, fc:fc + 1], bias=cw[:, 3, fc:fc + 1]),
                             reads=[bks[1], "cw"], writes=[(tk, 1)])
                        P.op("vector", lambda e, t1=t1, U=U, fc=fc: e.scalar_tensor_tensor(out=t1[:], in0=U[:, 0:TC], scalar=cw[:, 0, fc:fc + 1], in1=t1[:], op0=ALU.mult, op1=ALU.add),
                             reads=[(uk, 0), (uk, 1), (uk, 2), (tk, 0), (tk, 1), "cw"], writes=[(tk, 0), (tk, 1)])
                        P.op("vector", lambda e, t1=t1, U=U, fc=fc: e.scalar_tensor_tensor(out=t1[:], in0=U[:, 2:TC + 2], scalar=cw[:, 2, fc:fc + 1], in1=t1[:], op0=ALU.mult, op1=ALU.add),
                             reads=[(uk, 0), (uk, 1), (uk, 2), (tk, 0), (tk, 1), "cw"], writes=[(tk, 0), (tk, 1)])
                    tg, tv = T1s[0][par], T1s[1][par]
                    kg, kv = ("T1", 0, par), ("T1", 1, par)
                    P.op("scalar", lambda e, tg=tg: e.activation(out=tg[:], in_=tg[:], func=AF.Silu), reads=[(kg, 0), (kg, 1)], writes=[(kg, 0), (kg, 1)])
                    P.op("vector", lambda e, f=f, tg=tg, tv=tv: e.tensor_tensor(out=a[:, f, :], in0=tg[:], in1=tv[:], op=ALU.mult),
                         reads=[(kg, 0), (kg, 1), (kv, 0), (kv, 1)], writes=[("a", f)])
            nxt = self.load_unit_bf(self.wd_bf, dn_units[0][0], [(dn_units[0][1] * 512, 512)], wd_keys)
            for di, (kc, n) in enumerate(dn_units):
                cur = nxt
                if di + 1 < len(dn_units):
                    nxt = self.load_unit_bf(self.wd_bf, dn_units[di + 1][0], [(dn_units[di + 1][1] * 512, 512)], wd_keys)
                slot, wkeys = cur
                wb = self.wb[slot]
                if kc[0] == 0:
                    pend = [self.x_load(8 * k + t, n) for t in range(2)]
                for t in range(8):
                    for ci, f in enumerate(kc):
                        P.op("tensor", lambda e, t=t, ci=ci, f=f, wb=wb: e.matmul(self.banks[t][:, 0:512], lhsT=a[:, f, t * 128:(t + 1) * 128], rhs=wb[:, ci, 0:512], start=(f == 0), stop=(f == 21)),
                             reads=[("a", f)] + wkeys, writes=[("bank", t)])
                if kc[-1] == 21:
                    for t in range(8):
                        tt = 8 * k + t
                        if t + 2 < 8:
                            pend.append(self.x_load(8 * k + t + 2, n))
                        xt, xk, xi = pend[t]
                        P.op("vector", lambda e, t=t, xt=xt: e.tensor_tensor(out=xt[:, 0:512], in0=self.banks[t][:, 0:512], in1=xt[:, 0:512], op=ALU.add),
                             reads=[("bank", t), xk], writes=[xk])
                        self.x_store(tt, n, xt, xk, xi)

    def finish(self):
        self.P.finalize()
        return self.nc


def rel_bucket_np(rel):
    nb = 16
    max_exact = 8
    n = np.abs(rel)
    nf = np.maximum(n, 1).astype(np.float32)
    large = max_exact + (np.log(nf / np.float32(max_exact)) / np.float32(math.log(1024 / max_exact)) * np.float32(nb - max_exact)).astype(np.int32)
    large = np.minimum(large, nb - 1)
    return np.where(rel > 0, nb, 0) + np.where(n < max_exact, n, large)


def bias_tables(rel_bias, kind):
    rb = np.asarray(rel_bias, np.float32)
    kp = np.arange(128)[:, None]
    qp = np.arange(128)[None, :]
    if kind == "A":
        out = np.empty((16, 128, 3 * 128), np.float32)
        for i, dl in enumerate((1, 0, -1)):
            rel = dl * 128 + kp - qp
            bk = rel_bucket_np(rel)
            ok = np.abs(rel) <= 128
            for h in range(16):
                out[h, :, i * 128:(i + 1) * 128] = np.where(ok, rb[bk, h], NEG)
        return out
    if kind == "B":
        out = np.empty((16, 128, 28 * 128), np.float32)
        for i in range(28):
            dl = 12 - i
            rel = dl * 128 + kp - qp
            bk = rel_bucket_np(rel)
            for m in range(16):
                out[m, :, i * 128:(i + 1) * 128] = rb[bk, m]
        return out
    ents = [(0, d_) for d_ in (1, 0, -1)] + [(1, d_) for d_ in (2, 1, 0, -1, -2)] + [(2, d_) for d_ in range(8, -9, -1)]
    dils = (1, 4, 16)
    out = np.empty((4, 128, 25 * 128), np.float32)
    for i, (g, dl) in enumerate(ents):
        rel = dl * 128 + kp - qp
        dil = dils[g]
        ok = (rel % dil == 0) & (np.abs(rel) <= 64 * dil)
        bk = rel_bucket_np(rel)
        for j in range(4):
            out[j, :, i * 128:(i + 1) * 128] = np.where(ok, rb[bk, g * 4 + j], NEG)
    return out


def gcols(g):
    return np.ascontiguousarray(np.asarray(g, np.float32).reshape(8, 128).T)


_PROG = None
DEBUG_LAYERS = 4


def get_prog():
    global _PROG
    if _PROG is None:
        b = Builder()
        for li in range(DEBUG_LAYERS):
            b.phase_qkv(li)
            b.phase_attn(li)
            b.phase_wo(li)
            b.phase_ffn(li)
        nc = b.finish()
        _PROG = (nc, list(b.din_names), list(b.dout_names))
    return _PROG


def kernel(x, rel_bias,
           l0_attn_norm, l0_w_qkv, l0_q_gain, l0_k_gain, l0_sink, l0_w_o,
           l0_ffn_norm, l0_w_up, l0_conv_w, l0_conv_b, l0_w_down,
           l1_attn_norm, l1_w_qkv, l1_q_gain, l1_k_gain, l1_lambda_q1, l1_lambda_k1,
           l1_lambda_q2, l1_lambda_k2, l1_sub_gain, l1_w_o,
           l1_ffn_norm, l1_w_up, l1_conv_w, l1_conv_b, l1_w_down,
           l2_attn_norm, l2_w_qkv, l2_q_gain, l2_k_gain, l2_w_o,
           l2_ffn_norm, l2_w_up, l2_conv_w, l2_conv_b, l2_w_down,
           l3_attn_norm, l3_w_qkv, l3_q_gain, l3_k_gain, l3_sink, l3_w_o,
           l3_ffn_norm, l3_w_up, l3_conv_w, l3_conv_b, l3_w_down):
    inp = {
        "x": x,
        "rel_bias": rel_bias,
        "l0_attn_norm": l0_attn_norm,
        "l0_w_qkv": l0_w_qkv,
        "l0_q_gain": l0_q_gain,
        "l0_k_gain": l0_k_gain,
        "l0_sink": l0_sink,
        "l0_w_o": l0_w_o,
        "l0_ffn_norm": l0_ffn_norm,
        "l0_w_up": l0_w_up,
        "l0_conv_w": l0_conv_w,
        "l0_conv_b": l0_conv_b,
        "l0_w_down": l0_w_down,
        "l1_attn_norm": l1_attn_norm,
        "l1_w_qkv": l1_w_qkv,
        "l1_q_gain": l1_q_gain,
        "l1_k_gain": l1_k_gain,
        "l1_lambda_q1": l1_lambda_q1,
        "l1_lambda_k1": l1_lambda_k1,
        "l1_lambda_q2": l1_lambda_q2,
        "l1_lambda_k2": l1_lambda_k2,
        "l1_sub_gain": l1_sub_gain,
        "l1_w_o": l1_w_o,
        "l1_ffn_norm": l1_ffn_norm,
        "l1_w_up": l1_w_up,
        "l1_conv_w": l1_conv_w,
        "l1_conv_b": l1_conv_b,
        "l1_w_down": l1_w_down,
        "l2_attn_norm": l2_attn_norm,
        "l2_w_qkv": l2_w_qkv,
        "l2_q_gain": l2_q_gain,
        "l2_k_gain": l2_k_gain,
        "l2_w_o": l2_w_o,
        "l2_ffn_norm": l2_ffn_norm,
        "l2_w_up": l2_w_up,
        "l2_conv_w": l2_conv_w,
        "l2_conv_b": l2_conv_b,
        "l2_w_down": l2_w_down,
        "l3_attn_norm": l3_attn_norm,
        "l3_w_qkv": l3_w_qkv,
        "l3_q_gain": l3_q_gain,
        "l3_k_gain": l3_k_gain,
        "l3_sink": l3_sink,
        "l3_w_o": l3_w_o,
        "l3_ffn_norm": l3_ffn_norm,
        "l3_w_up": l3_w_up,
        "l3_conv_w": l3_conv_w,
        "l3_conv_b": l3_conv_b,
        "l3_w_down": l3_w_down,
    }
    x = np.ascontiguousarray(np.asarray(x, np.float32))
    rel_bias = np.asarray(rel_bias, np.float32)
    shared = {"ident": np.eye(128, dtype=np.float32)}
    for kind in ("A", "B", "C"):
        shared["bt_" + kind] = bias_tables(rel_bias, kind)
    f32 = lambda a: np.ascontiguousarray(np.asarray(a, np.float32))
    for li in range(4):
        L = LAYERS[li]
        p = "l%d_" % li
        shared[p + "gcol_attn"] = gcols(inp[p + "attn_norm"])
        shared[p + "gcol_ffn"] = gcols(inp[p + "ffn_norm"])
        for w in ("w_qkv", "w_o", "w_up", "w_down"):
            shared[p + w] = f32(inp[p + w])
        shared[p + "qkg"] = np.ascontiguousarray(np.stack([f32(inp[p + "q_gain"]), f32(inp[p + "k_gain"])]))
        cwv = f32(inp[p + "conv_w"]).reshape(3, 44, 128)
        cbv = f32(inp[p + "conv_b"]).reshape(1, 44, 128)
        shared[p + "convp"] = np.ascontiguousarray(np.concatenate([cwv, cbv], 0).transpose(2, 0, 1))
        if L["kind"] == "A":
            shared[p + "sink"] = f32(inp[p + "sink"])
        if L["kind"] == "B":
            shared[p + "lam"] = np.ascontiguousarray(np.stack([f32(inp[p + k]) for k in ("lambda_q1", "lambda_k1", "lambda_q2", "lambda_k2")]))
            shared[p + "sgcol"] = f32(inp[p + "sub_gain"]).reshape(128, 1)
    import time as _t
    _t1 = _t.time()
    nc, dins, douts = get_prog()
    print("[kernel] host prep + build took %.1fs" % (_t.time() - _t1), flush=True)
    in_maps = []
    for c in range(NCORES):
        m = {}
        for n in dins:
            m[n] = x[c] if n == "x" else shared[n]
        in_maps.append(m)
    import time as _t
    _t0 = _t.time()
    res = run_bass_kernel_spmd(nc, in_maps, core_ids=list(range(NCORES)))
    print("[kernel] run_bass_kernel_spmd took %.1fs" % (_t.time() - _t0), flush=True)
    return np.stack([res.results[c]["x_out"] for c in range(NCORES)]).astype(np.float32)
```

```python
import math
from contextlib import ExitStack

import numpy as np
import ml_dtypes

import concourse.bass as bass
import concourse.mybir as mybir
from concourse.ap import AP
from concourse.bass_utils import run_bass_kernel_spmd

F32 = mybir.dt.float32
BF16 = mybir.dt.bfloat16
AF = mybir.ActivationFunctionType
ALU = mybir.AluOpType
AX = mybir.AxisListType
NPBF = ml_dtypes.bfloat16

NCORES = 4
T = 4096
NT = 32
D = 1024
DFF = 2816
EPS = 1e-6
NEG = -30000.0

LAYERS = [
    dict(kind="A", F=1536, nqb=8, nkb=2, FV=256, nfc=8),
    dict(kind="B", F=3072, nqb=8, nkb=8, FV=1024, nfc=8),
    dict(kind="C", F=2560, nqb=12, nkb=4, FV=512, nfc=4),
    dict(kind="A", F=1536, nqb=8, nkb=2, FV=256, nfc=8),
]
NI = {"A": 3, "B": 28, "C": 25}
NUNIT_BT = {"A": 16, "B": 16, "C": 4}


def lambda_init_fn(layer):
    return 0.8 - 0.6 * math.exp(-0.3 * layer)


class Op:
    __slots__ = ("eng", "fn", "deps", "needs_inc", "val", "semkey", "is_dma", "idx")

    def __init__(self, eng, fn, deps, semkey, is_dma):
        self.eng = eng
        self.fn = fn
        self.deps = deps
        self.needs_inc = False
        self.val = None
        self.semkey = semkey
        self.is_dma = is_dma


class Prog:
    ENGS = ("sync", "scalar", "vector", "gpsimd", "tensor")

    def __init__(self, nc):
        self.nc = nc
        self.ops = {e: [] for e in self.ENGS}
        self.lastw = {}
        self.readers = {}
        self.es = ExitStack()
        self.outs = []
        self.fence = []
        self.fence_pending = set()
        self.last_dma = {}
        self.epoch = 0

    def barrier(self):
        fence = []
        for e in self.ENGS:
            for o in reversed(self.ops[e]):
                if not o.is_dma:
                    fence.append(o)
                    break
        fence.extend(self.last_dma.values())
        self.fence = fence
        self.fence_pending = set(self.ENGS)
        self.epoch += 1

    def op(self, eng, fn, reads=(), writes=(), slot=None, out=False):
        deps = []
        if eng in self.fence_pending:
            deps.extend(self.fence)
            self.fence_pending.discard(eng)
        for b in reads:
            w = self.lastw.get(b)
            if w is not None:
                deps.append(w)
        for b in writes:
            w = self.lastw.get(b)
            if w is not None:
                deps.append(w)
            deps.extend(self.readers.get(b, ()))
        is_dma = slot is not None
        semkey = ("dma", slot) if is_dma else ("eng", eng, self.epoch % 3)
        o = Op(eng, fn, deps, semkey, is_dma)
        o.idx = len(self.ops[eng])
        self.ops[eng].append(o)
        if is_dma:
            self.last_dma[slot] = o
        for b in writes:
            self.lastw[b] = o
            self.readers[b] = []
        for b in reads:
            self.readers.setdefault(b, []).append(o)
        if out:
            self.outs.append(o)
        return o

    @staticmethod
    def _skip(d, o):
        return d is o or (d.eng == "tensor" and o.eng == "tensor" and not d.is_dma and not o.is_dma)

    def finalize(self):
        nc = self.nc
        final_waits = self.outs
        for e in self.ENGS:
            for o in self.ops[e]:
                best = {}
                for d in o.deps:
                    if self._skip(d, o):
                        continue
                    if d.is_dma:
                        d.needs_inc = True
                        continue
                    b = best.get(d.semkey)
                    if b is None or d.idx > b.idx:
                        best[d.semkey] = d
                for d in best.values():
                    d.needs_inc = True
        for d in final_waits:
            d.needs_inc = True
        for e in self.ENGS:
            for o in self.ops[e]:
                if o.is_dma:
                    o.needs_inc = True
        counters = {}
        for e in self.ENGS:
            for o in self.ops[e]:
                if o.needs_inc:
                    c = counters.get(o.semkey, 0) + (16 if o.is_dma else 1)
                    counters[o.semkey] = c
                    o.val = c
        sems = {}
        for i, k in enumerate(counters):
            sems[k] = self.es.enter_context(nc.semaphore("s%d" % i))
        self.nsem = len(sems)
        block = self.es.enter_context(nc.Block())

        def run(e, engine):
            known = {}
            for o in self.ops[e]:
                need = {}
                for d in o.deps:
                    if self._skip(d, o) or d.val is None:
                        continue
                    if need.get(d.semkey, 0) < d.val:
                        need[d.semkey] = d.val
                for k, v in need.items():
                    if known.get(k, 0) >= v:
                        continue
                    engine.wait_ge(sems[k], v)
                    known[k] = v
                ins = o.fn(engine)
                if o.needs_inc:
                    ins.then_inc(sems[o.semkey], 16 if o.is_dma else 1)
            if e == "sync":
                need = {}
                for d in final_waits:
                    if need.get(d.semkey, 0) < d.val:
                        need[d.semkey] = d.val
                for k, v in need.items():
                    engine.wait_ge(sems[k], v)

        @block.sync
        def _(eng):
            run("sync", eng)

        @block.scalar
        def _(eng):
            run("scalar", eng)

        @block.vector
        def _(eng):
            run("vector", eng)

        @block.gpsimd
        def _(eng):
            run("gpsimd", eng)

        @block.tensor
        def _(eng):
            run("tensor", eng)

        self.es.close()


SB_BASE = 16512
SCR_END = 229376


class Builder:
    def __init__(self):
        self.nc = bass.Bass("TRN2", target_bir_lowering=False)
        self.P = Prog(self.nc)
        self.din_names = []
        self.dout_names = []
        self.d = {}
        self.uid = 0
        self.perm_off = SB_BASE
        nc = self.nc
        self.actT = self.perm("actT", [128, 8, T + 2], BF16)
        self.ws = [self.perm("ws%d" % i, [128, 4, 512], F32) for i in range(2)]
        self.wb = [self.perm("wb%d" % i, [128, 8, 512], BF16) for i in range(2)]
        self.xts = [self.perm("xt%d" % i, [128, D], F32) for i in range(4)]
        self.hbs = [self.perm("hb%d" % i, [128, D], BF16) for i in range(2)]
        self.ident = self.perm("ident", [128, 128], BF16)
        self.identf = self.perm("identf", [128, 128], F32)
        self.sst = [self.perm("sst%d" % i, [128, 4], F32) for i in range(4)]
        self.epsb = self.perm("epsb", [128, 1], F32)
        self.gcol = self.perm("gcol", [128, 8], F32)
        self.PERM_END = (self.perm_off + 63) // 64 * 64
        self.scr_off = self.PERM_END
        self.nscr = 0
        self.xcnt = 0
        self.banks = [nc.alloc_psum_tensor("bank%d" % i, [128, 512], F32) for i in range(8)]
        self.wcount = 0
        self.xd = self.dout("x_out", [T, D], F32).ap()
        self.qT_d = nc.dram_tensor("qT_scr", [12, 128, T], BF16, kind="Internal").ap()
        self.kT_d = nc.dram_tensor("kT_scr", [8, 128, T], BF16, kind="Internal").ap()
        self.v_d = nc.dram_tensor("v_scr", [T, 1024], BF16, kind="Internal").ap()
        self.wu_bf = nc.dram_tensor("wu_bf", [D, 2 * DFF], BF16, kind="Internal").ap()
        self.wd_bf = nc.dram_tensor("wd_bf", [DFF, D], BF16, kind="Internal").ap()
        self.init_consts()

    def perm(self, name, shape, dt):
        n = int(np.prod(shape[1:])) * (4 if dt == F32 else 2)
        n = (n + 31) // 32 * 32
        t = self.nc.alloc_sbuf_tensor_at(name, shape, dt, offset=self.perm_off)
        self.perm_off += n
        return t

    def scr_reset(self):
        if self.nscr > 0:
            self.P.barrier()
        self.nscr += 1
        self.scr_off = self.PERM_END

    def scr(self, name, shape, dt):
        n = int(np.prod(shape[1:])) * (4 if dt == F32 else 2)
        n = (n + 31) // 32 * 32
        self.uid += 1
        t = self.nc.alloc_sbuf_tensor_at("%s_%d" % (name, self.uid), shape, dt, offset=self.scr_off)
        self.scr_off += n
        assert self.scr_off <= SCR_END, (name, self.scr_off)
        return t

    def din(self, name, shape, dt):
        if name not in self.d:
            self.d[name] = self.nc.dram_tensor(name, list(shape), dt, kind="ExternalInput")
            self.din_names.append(name)
        return self.d[name]

    def dout(self, name, shape, dt):
        if name not in self.d:
            self.d[name] = self.nc.dram_tensor(name, list(shape), dt, kind="ExternalOutput")
            self.dout_names.append(name)
        return self.d[name]

    def bank_bf(self, i):
        return self.banks[i][:].bitcast(BF16).rearrange("p (c t) -> p c t", t=128)

    def init_consts(self):
        P = self.P
        idd = self.din("ident", [128, 128], F32).ap()
        xin = self.din("x", [T, D], F32).ap()
        P.op("sync", lambda e: e.dma_start(out=self.identf[:], in_=idd), writes=["identf"], slot="c_id")
        P.op("vector", lambda e: e.tensor_copy(out=self.ident[:], in_=self.identf[:]), reads=["identf"], writes=["ident"])
        P.op("vector", lambda e: e.memset(self.epsb[:], EPS), writes=["eps"])
        P.op("vector", lambda e: e.memset(self.actT[:, :, 0:1], 0.0), writes=["halo"])
        P.op("vector", lambda e: e.memset(self.actT[:, :, T + 1:T + 2], 0.0), writes=["halo"])
        for t0 in range(0, NT, 8):
            P.op("sync", lambda e, t0=t0: e.dma_start(out=self.xd[t0 * 128:(t0 + 8) * 128, :], in_=xin[t0 * 128:(t0 + 8) * 128, :]),
                 writes=[("xd", t, n) for t in range(t0, t0 + 8) for n in range(2)], slot="xcp%d" % (t0 // 8), out=True)

    def x_load(self, t, n=None):
        P = self.P
        i = self.xcnt % 4
        self.xcnt += 1
        xt = self.xts[i]
        key = ("xt", i)
        if n is None:
            P.op("sync", lambda e, t=t, xt=xt: e.dma_start(out=xt[:], in_=self.xd[t * 128:(t + 1) * 128, :]),
                 reads=[("xd", t, 0), ("xd", t, 1)], writes=[key], slot="xl%d" % i)
        else:
            P.op("sync", lambda e, t=t, xt=xt, n=n: e.dma_start(out=xt[:, 0:512], in_=self.xd[t * 128:(t + 1) * 128, n * 512:(n + 1) * 512]),
                 reads=[("xd", t, n)], writes=[key], slot="xl%d" % i)
        return xt, key, i

    def x_store(self, t, n, xt, key, i):
        P = self.P
        P.op("gpsimd", lambda e, t=t, xt=xt, n=n: e.dma_start(out=self.xd[t * 128:(t + 1) * 128, n * 512:(n + 1) * 512], in_=xt[:, 0:512]),
             reads=[key], writes=[("xd", t, n)], slot="xs%d" % i, out=True)

    def load_unit(self, W, kchunks, segs, scale=None, scale_reads=()):
        P = self.P
        i = self.wcount
        self.wcount += 1
        slot = i % 2
        wb = self.wb[slot]
        key = ("wb", slot)
        Wa = W.ap()
        ncol = sum(s[1] for s in segs)
        halves = [kchunks[0:4], kchunks[4:8]]
        allkeys = []
        for hi, kc in enumerate(halves):
            if not kc:
                continue
            ws = self.ws[hi]
            co = 0
            for si, (c0, cn) in enumerate(segs):
                k0 = kc[0]
                src = Wa[k0 * 128:(k0 + len(kc)) * 128, c0:c0 + cn].rearrange("(c p) f -> p c f", p=128)
                P.op("sync", lambda e, ws=ws, src=src, co=co, cn=cn, n=len(kc): e.dma_start(out=ws[:, 0:n, co:co + cn], in_=src),
                     writes=[("ws", hi, si)], slot="ws%d_%d" % (hi, si))
                co += cn
            n = len(kc)
            if scale is None:
                P.op("gpsimd", lambda e, ws=ws, wb=wb, hi=hi, n=n, ncol=ncol: e.tensor_copy(out=wb[:, hi * 4:hi * 4 + n, 0:ncol], in_=ws[:, 0:n, 0:ncol]),
                     reads=[("ws", hi, si) for si in range(len(segs))], writes=[(key, hi, 0)])
                allkeys.append((key, hi, 0))
            else:
                for j, k in enumerate(kc):
                    sc = scale(k)
                    P.op("gpsimd", lambda e, ws=ws, wb=wb, hi=hi, j=j, ncol=ncol, sc=sc: e.tensor_scalar(out=wb[:, hi * 4 + j, 0:ncol], in0=ws[:, j, 0:ncol], scalar1=sc, scalar2=None, op0=ALU.mult),
                         reads=[("ws", hi, si) for si in range(len(segs))] + list(scale_reads), writes=[(key, hi, j)])
                    allkeys.append((key, hi, j))
        return slot, allkeys

    def precast(self, W, nchunks, ncols, Wbf, keyname, scaled):
        P = self.P
        Wa = W.ap()
        blocks = [(kg, min(4, nchunks - kg), c0) for kg in range(0, nchunks, 4) for c0 in range(0, ncols, 512)]
        keys = []
        for b, (kg, n, c0) in enumerate(blocks):
            ws = self.ws[b % 2]
            wsk = ("ws", b % 2, 0)
            ss_ = b % 4
            stg = self.wb[ss_ // 2][:, (ss_ % 2) * 4:(ss_ % 2) * 4 + n, :]
            stgk = (("wb", ss_ // 2), ss_ % 2, 0)
            src = Wa[kg * 128:(kg + n) * 128, c0:c0 + 512].rearrange("(c p) f -> p c f", p=128)
            P.op("sync", lambda e, ws=ws, src=src, n=n: e.dma_start(out=ws[:, 0:n, :], in_=src), writes=[wsk], slot="ws%d_0" % (b % 2))
            eng = "scalar" if b % 2 == 0 else "vector"
            if not scaled:
                if eng == "scalar":
                    P.op(eng, lambda e, ws=ws, stg=stg, n=n: e.activation(out=stg, in_=ws[:, 0:n, :], func=AF.Copy), reads=[wsk], writes=[stgk])
                else:
                    P.op(eng, lambda e, ws=ws, stg=stg, n=n: e.tensor_copy(out=stg, in_=ws[:, 0:n, :]), reads=[wsk], writes=[stgk])
            else:
                for j in range(n):
                    sc = self.gcol[:, kg + j:kg + j + 1]
                    if eng == "scalar":
                        P.op(eng, lambda e, ws=ws, stg=stg, j=j, sc=sc: e.activation(out=stg[:, j, :], in_=ws[:, j, :], func=AF.Copy, scale=sc), reads=[wsk, "gcol"], writes=[(stgk, j)] if False else [stgk])
                    else:
                        P.op(eng, lambda e, ws=ws, stg=stg, j=j, sc=sc: e.tensor_scalar(out=stg[:, j, :], in0=ws[:, j, :], scalar1=sc, scalar2=None, op0=ALU.mult), reads=[wsk, "gcol"], writes=[stgk])
            dst = Wbf[kg * 128:(kg + n) * 128, c0:c0 + 512].rearrange("(c p) f -> p c f", p=128)
            P.op("gpsimd", lambda e, stg=stg, dst=dst: e.dma_start(out=dst, in_=stg), reads=[stgk], writes=[(keyname, b)], slot="pc%d" % ss_)
            keys.append((keyname, b))
        return keys

    def load_unit_bf(self, Wbf, kchunks, segs, rkeys):
        P = self.P
        i = self.wcount
        self.wcount += 1
        slot = i % 2
        wb = self.wb[slot]
        keys = [(("wb", slot), 0, 0), (("wb", slot), 1, 0)]
        k0, n, co = kchunks[0], len(kchunks), 0
        for si, (c0, cn) in enumerate(segs):
            src = Wbf[k0 * 128:(k0 + n) * 128, c0:c0 + cn].rearrange("(c p) f -> p c f", p=128)
            P.op("sync", lambda e, wb=wb, src=src, n=n, co=co, cn=cn: e.dma_start(out=wb[:, 0:n, co:co + cn], in_=src),
                 reads=rkeys, writes=keys, slot="wbd%d_%d" % (slot, si))
            co += cn
        return slot, keys

    def phase_norm(self):
        P = self.P
        junk = self.banks[7]
        hbs = self.hbs
        pend = {}

        def stA(t):
            xt, xk, _ = pend[t]
            st = self.sst[t % 4]
            sk = ("sst", t % 4)
            P.op("vector", lambda e, st=st: e.memset(st[:], 0.0), writes=[sk])
            P.op("scalar", lambda e, xt=xt, st=st: e.activation(out=junk[:, 0:512], in_=xt[:, 0:512], func=AF.Square, accum_out=st[:, 0:1]),
                 reads=[xk, sk], writes=[sk, ("bank", 7)])
            P.op("scalar", lambda e, xt=xt, st=st: e.activation(out=junk[:, 0:512], in_=xt[:, 512:1024], func=AF.Square, accum_out=st[:, 1:2]),
                 reads=[xk, sk], writes=[sk, ("bank", 7)])
            P.op("vector", lambda e, st=st: e.tensor_tensor(out=st[:, 2:3], in0=st[:, 0:1], in1=st[:, 1:2], op=ALU.add), reads=[sk], writes=[sk])

        def stB(t):
            xt, xk, _ = pend[t]
            st = self.sst[t % 4]
            sk = ("sst", t % 4)
            P.op("scalar", lambda e, st=st: e.activation(out=st[:, 2:3], in_=st[:, 2:3], func=AF.Ln, scale=1.0 / D, bias=self.epsb[:, 0:1]),
                 reads=[sk, "eps"], writes=[sk])
            P.op("scalar", lambda e, st=st: e.activation(out=st[:, 3:4], in_=st[:, 2:3], func=AF.Exp, scale=-0.5), reads=[sk], writes=[sk])
            hb = hbs[t % 2]
            hk = ("hb", t % 2)
            pk = ("bank", t % 2)
            pT = self.bank_bf(t % 2)
            P.op("vector", lambda e, xt=xt, hb=hb, st=st: e.tensor_scalar(out=hb[:], in0=xt[:], scalar1=st[:, 3:4], scalar2=None, op0=ALU.mult),
                 reads=[xk, sk], writes=[hk])
            for c in range(8):
                P.op("tensor", lambda e, c=c, hb=hb, pT=pT: e.transpose(out=pT[:, c, :], in_=hb[:, c * 128:(c + 1) * 128], identity=self.ident[:]),
                     reads=[hk, "ident"], writes=[pk])

        def stC(t):
            pk = ("bank", t % 2)
            pT = self.bank_bf(t % 2)
            P.op("scalar", lambda e, t=t, pT=pT: e.activation(out=self.actT[:, :, 1 + t * 128:1 + (t + 1) * 128], in_=pT, func=AF.Copy),
                 reads=[pk], writes=[("actT", c, t) for c in range(8)])

        pend[0] = self.x_load(0)
        pend[1] = self.x_load(1)
        for i in range(NT + 2):
            if i + 2 < NT:
                pend[i + 2] = self.x_load(i + 2)
            if i - 2 >= 0:
                stC(i - 2)
            if 0 <= i - 1 < NT:
                stB(i - 1)
            if i < NT:
                stA(i)

    def load_gcol(self, name):
        P = self.P
        g = self.din(name, [128, 8], F32).ap()
        P.op("sync", lambda e: e.dma_start(out=self.gcol[:], in_=g), writes=["gcol"], slot="gcol")

    def phase_qkv(self, li):
        P = self.P
        L = LAYERS[li]
        kind = L["kind"]
        pfx = "l%d_" % li
        self.load_gcol(pfx + "gcol_attn")
        self.phase_norm()
        self.scr_reset()
        W = self.din(pfx + "w_qkv", [D, L["F"]], F32)
        dh = 128 if kind == "C" else 64
        geff_d = self.din(pfx + "qkg", [2, dh], F32).ap()
        qT_d, kT_d, v_d = self.qT_d, self.kT_d, self.v_d
        gq = self.scr("gq", [128, dh], F32)
        gk = self.scr("gk", [128, dh], F32)
        P.op("sync", lambda e: e.dma_start(out=gq[:], in_=geff_d[0].partition_broadcast(128)), writes=["gq"], slot="gq")
        P.op("sync", lambda e: e.dma_start(out=gk[:], in_=geff_d[1].partition_broadcast(128)), writes=["gk"], slot="gk")
        P.op("vector", lambda e: e.scalar_tensor_tensor(out=gk[:], in0=gq[:], scalar=float(dh) ** -0.5, in1=gk[:], op0=ALU.mult, op1=ALU.mult),
             reads=["gq", "gk"], writes=["gk"])
        stages = [self.scr("stage", [128, 4, T], BF16) for _ in range(2)]
        ND = 4
        sqs = [self.scr("sq", [128, 512], F32) for _ in range(ND)]
        kfs = [self.scr("kf", [128, 512], F32) for _ in range(ND)]
        qns = [self.scr("qn", [128, 512], BF16) for _ in range(ND)]
        vsts = [self.scr("vst", [128, 512], BF16) for _ in range(ND)]
        ssq = [self.scr("ssq", [128, 8], F32) for _ in range(ND)]
        PYB = (2, 3, 4, 5)
        PTB = (0, 1, 6, 7)
        if kind == "A":
            chunks = [[("q", 0, 512, 0)], [("q", 0, 512, 4)], [("k", 0, 256, 0), ("v", 256, 256, 0)]]
        elif kind == "B":
            chunks = [[("q", 0, 512, 0)], [("q", 0, 512, 4)], [("k", 0, 512, 0)], [("k", 0, 512, 4)], [("v", 0, 512, 0)], [("v", 0, 512, 512)]]
        else:
            chunks = [[("q", 0, 512, 0)], [("q", 0, 512, 4)], [("q", 0, 512, 8)], [("k", 0, 512, 0)], [("v", 0, 512, 0)]]
        gsc = lambda k: self.gcol[:, k:k + 1]
        units = [None] * len(chunks)
        units[0] = self.load_unit(W, list(range(8)), [(0, 512)], scale=gsc, scale_reads=["gcol"])
        it = 0
        pend_rest = []

        def pipe_step(keep):
            n = len(pend_rest)
            for idx in range(n):
                ent = pend_rest[idx]
                lag = n - 1 - idx
                want = 3 if keep == 0 else min(3, lag)
                if keep == 0:
                    want = min(3, ent[1] + 1)
                while ent[1] < want:
                    ent[0](ent[1])
                    ent[1] += 1
            while pend_rest and pend_rest[0][1] >= 3:
                pend_rest.pop(0)

        for ci, segs in enumerate(chunks):
            if ci + 1 < len(chunks):
                units[ci + 1] = self.load_unit(W, list(range(8)), [((ci + 1) * 512, 512)], scale=gsc, scale_reads=["gcol"])
            slot, wkeys = units[ci]
            wb = self.wb[slot]
            stage = stages[ci % 2]
            stk = ("stage", ci % 2)
            for t in range(NT):
                bi = PYB[it % ND]
                py = self.banks[bi]
                pyk = ("bank", bi)
                for c in range(8):
                    P.op("tensor", lambda e, c=c, t=t, py=py, wb=wb: e.matmul(py[:, 0:512], lhsT=self.actT[:, c, 1 + t * 128:1 + (t + 1) * 128], rhs=wb[:, c, 0:512], start=(c == 0), stop=(c == 7)),
                         reads=[("actT", c, t)] + wkeys, writes=[pyk])
                def rest(stg_i, segs=segs, it=it, t=t, py=py, pyk=pyk, stage=stage, stk=stk):
                    for (ty, off, w, dst) in segs:
                        if ty == "v":
                            if stg_i != 0:
                                continue
                            vst = vsts[it % ND]
                            vk = ("vst", it % ND)
                            P.op("scalar", lambda e, py=py, vst=vst, off=off, w=w: e.activation(out=vst[:, 0:w], in_=py[:, off:off + w], func=AF.Copy),
                                 reads=[pyk], writes=[vk])
                            P.op("gpsimd", lambda e, vst=vst, t=t, dst=dst, w=w: e.dma_start(out=v_d[t * 128:(t + 1) * 128, dst:dst + w], in_=vst[:, 0:w]),
                                 reads=[vk], writes=[("vd", t, dst)], slot="vst%d" % (it % ND))
                            continue
                        nh = w // dh
                        sq = sqs[it % ND]
                        sqk = ("sq", it % ND)
                        s_ = ssq[it % ND]
                        sk = ("ssq", it % ND)
                        qn = qns[it % ND]
                        qk = ("qn", it % ND)
                        tb = PTB[it % ND]
                        pT = self.bank_bf(tb)
                        pk = ("bank", tb)
                        nb = w // 128
                        if stg_i == 0:
                            P.op("scalar", lambda e, py=py, sq=sq, off=off, w=w: e.activation(out=sq[:, 0:w], in_=py[:, off:off + w], func=AF.Square),
                                 reads=[pyk], writes=[sqk])
                            P.op("vector", lambda e, sq=sq, s_=s_, w=w, nh=nh: e.tensor_reduce(out=s_[:, 0:nh], in_=sq[:, 0:w].rearrange("p (h d) -> p h d", d=dh), axis=AX.X, op=ALU.add),
                                 reads=[sqk], writes=[sk])
                        elif stg_i == 1:
                            P.op("scalar", lambda e, s_=s_, nh=nh: e.activation(out=s_[:, 0:nh], in_=s_[:, 0:nh], func=AF.Ln, scale=1.0 / dh, bias=self.epsb[:, 0:1]),
                                 reads=[sk, "eps"], writes=[sk])
                            P.op("scalar", lambda e, s_=s_, nh=nh: e.activation(out=s_[:, 0:nh], in_=s_[:, 0:nh], func=AF.Exp, scale=-0.5), reads=[sk], writes=[sk])
                            rb = AP(s_, 0, [[8, 128], [1, nh], [0, dh]])
                            if ty == "q":
                                P.op("vector", lambda e, py=py, qn=qn, off=off, w=w, rb=rb: e.tensor_tensor(out=qn[:, 0:w].rearrange("p (h d) -> p h d", d=dh), in0=py[:, off:off + w].rearrange("p (h d) -> p h d", d=dh), in1=rb, op=ALU.mult),
                                     reads=[pyk, sk], writes=[qk])
                            else:
                                kf = kfs[it % ND]
                                kfk = ("kf", it % ND)
                                gb = AP(gk, 0, [[dh, 128], [0, nh], [1, dh]])
                                P.op("vector", lambda e, py=py, kf=kf, off=off, w=w, rb=rb: e.tensor_tensor(out=kf[:, 0:w].rearrange("p (h d) -> p h d", d=dh), in0=py[:, off:off + w].rearrange("p (h d) -> p h d", d=dh), in1=rb, op=ALU.mult),
                                     reads=[pyk, sk], writes=[kfk])
                                P.op("gpsimd", lambda e, kf=kf, qn=qn, w=w, gb=gb: e.tensor_tensor(out=qn[:, 0:w].rearrange("p (h d) -> p h d", d=dh), in0=kf[:, 0:w].rearrange("p (h d) -> p h d", d=dh), in1=gb, op=ALU.mult),
                                     reads=[kfk, "gk"], writes=[qk])
                            for j in range(nb):
                                P.op("tensor", lambda e, j=j, qn=qn, pT=pT: e.transpose(out=pT[:, j, :], in_=qn[:, j * 128:(j + 1) * 128], identity=self.ident[:]),
                                     reads=[qk, "ident"], writes=[pk])
                        else:
                            P.op("scalar", lambda e, pT=pT, stage=stage, nb=nb, t=t: e.activation(out=stage[:, 0:nb, t * 128:(t + 1) * 128], in_=pT[:, 0:nb, :], func=AF.Copy),
                                 reads=[pk], writes=[(stk, t)])
                pend_rest.append([rest, 0])
                pipe_step(3)
                it += 1
            while pend_rest:
                pipe_step(0)
            for (ty, off, w, dst) in segs:
                if ty == "v":
                    continue
                dd = qT_d if ty == "q" else kT_d
                dk = "qTd" if ty == "q" else "kTd"
                for j in range(w // 128):
                    P.op("gpsimd", lambda e, dd=dd, j=j, dst=dst, stage=stage: e.dma_start(out=dd[dst + j], in_=stage[:, j, :]),
                         reads=[(stk, t) for t in range(NT)], writes=[(dk, dst + j)], slot="stg%d_%d" % (ci % 2, j))

    ST_BANKS = (0, 1, 6)

    def attn_item(self, it, mms, tb_ap, ncols, pvs, tbkey, extra_reads, post=None, back_fn=None, const_bias=None):
        P = self.P
        bi = self.ST_BANKS[it % 3]
        st = self.banks[bi]
        stk = ("bank", bi)
        sc = self.a_sc[it % 3]
        sck = ("sc", it % 3)
        pt = self.a_pt[it % 3]
        ptk = ("pt", it % 3)
        for (lh, rh, c0, n) in mms:
            P.op("tensor", lambda e, lh=lh, rh=rh, c0=c0, n=n, st=st: e.matmul(st[:, c0:c0 + n], lhsT=lh, rhs=rh, start=True, stop=True),
                 reads=extra_reads, writes=[stk])
        if const_bias is not None:
            P.op("scalar", lambda e, st=st, pt=pt, ncols=ncols: e.activation(out=pt[:, 0:ncols], in_=st[:, 0:ncols], func=AF.Exp, bias=const_bias),
                 reads=[stk, "satb"], writes=[ptk])
        else:
            P.op("vector", lambda e, st=st, sc=sc, tb_ap=tb_ap, ncols=ncols: e.tensor_tensor(out=sc[:, 0:ncols], in0=st[:, 0:ncols], in1=tb_ap, op=ALU.add),
                 reads=[stk, tbkey], writes=[sck])
            P.op("scalar", lambda e, sc=sc, pt=pt, ncols=ncols: e.activation(out=pt[:, 0:ncols], in_=sc[:, 0:ncols], func=AF.Exp),
                 reads=[sck], writes=[ptk])
        q = self.a_queue
        q.append((pt, ptk, pvs, list(extra_reads), post, back_fn))
        while len(q) > 2:
            self._attn_back(q.pop(0))

    def _attn_back(self, pend):
        P = self.P
        pt, ptk, pvs, extra_reads, post, back_fn = pend
        if back_fn is not None:
            back_fn(pt, ptk)
            pvs = []
        for (c0, v_ap, ab, wdt, s0, s1) in pvs:
            acc = self.banks[ab]
            P.op("tensor", lambda e, c0=c0, v_ap=v_ap, acc=acc, wdt=wdt, s0=s0, s1=s1, pt=pt: e.matmul(acc[:, 0:wdt], lhsT=pt[:, c0:c0 + 128], rhs=v_ap, start=s0, stop=s1),
                 reads=[ptk] + extra_reads, writes=[("bank", ab)])
        if post is not None:
            post()

    def attn_flush(self):
        q = self.a_queue
        while q:
            self._attn_back(q.pop(0))

    def out_transpose(self, on, onk, chunk, qt, trk):
        P = self.P
        bi = 7
        pT = self.bank_bf(bi)
        P.op("tensor", lambda e, on=on, pT=pT: e.transpose(out=pT[:, 0, :], in_=on, identity=self.ident[:]),
             reads=[onk, "ident"], writes=[("bank", bi)])
        P.op("scalar", lambda e, pT=pT, chunk=chunk, qt=qt: e.activation(out=self.actT[:, chunk, 1 + qt * 128:1 + (qt + 1) * 128], in_=pT[:, 0, :], func=AF.Copy),
             reads=[("bank", bi)], writes=[("actT", chunk, qt)])

    def phase_attn(self, li):
        P = self.P
        L = LAYERS[li]
        kind = L["kind"]
        pfx = "l%d_" % li
        self.scr_reset()
        nkb, nqb, FV = L["nkb"], L["nqb"], L["FV"]
        qT_d, kT_d, v_d = self.qT_d, self.kT_d, self.v_d
        ni = NI[kind]
        bt_d = self.din("bt_" + kind, [NUNIT_BT[kind], 128, ni * 128], F32).ap()
        self.a_sc = [self.scr("sc", [128, 512], F32) for _ in range(3)]
        self.a_pt = [self.scr("pt", [128, 512], BF16) for _ in range(3)]
        self.a_queue = []
        rr = [self.scr("rr", [128, 4], F32) for _ in range(4)]
        ons = [self.scr("on", [128, 128], BF16) for _ in range(2)]
        v_r = v_d.rearrange("(kb p) f -> p kb f", p=128)
        qkeys = [("qTd", b) for b in range(nqb)]
        kkeys = [("kTd", b) for b in range(nkb)]
        vkeys = [("vd", t, c0) for t in range(NT) for c0 in range(0, FV, 512 if FV >= 512 else 256)]
        it = 0
        fin = 0
        if kind == "B":
            dv = 128
            KTs = [self.scr("KT", [128, T], BF16) for _ in range(2)]
            Vs = [self.scr("V", [128, NT, dv], BF16) for _ in range(2)]
            QTz = [self.scr("QT", [128, T], BF16) for _ in range(2)]
            TB = self.scr("TB", [128, ni * 128], F32)
            o1 = self.scr("o1", [128, T], F32)
            onesb = self.scr("onesb", [128, 128], BF16)
            onesf = self.scr("onesf", [128, 128], F32)
            halves = [self.xts[i][:, k * 512:(k + 1) * 512] for i in range(4) for k in range(2)]
            Rb, odb, sqb, rsb = halves[0:2], halves[2:4], halves[4], halves[5]
            P.op("gpsimd", lambda e: e.memset(onesb[:], 1.0), writes=["onesb"])
            P.op("gpsimd", lambda e: e.memset(onesf[:], 1.0), writes=["onesf"])
            P.op("gpsimd", lambda e: e.memset(QTz[0][64:128, :], 0.0), writes=[("QTz", 0)])
            P.op("gpsimd", lambda e: e.memset(QTz[1][0:64, :], 0.0), writes=[("QTz", 1)])
            lam = self.hbs[0][:].bitcast(F32)[:, 0:256].rearrange("p (a d) -> p a d", d=64)
            lamv = self.scr("lamv", [128, 4], F32)
            lam_d = self.din(pfx + "lam", [4, 64], F32).ap()
            P.op("sync", lambda e: e.dma_start(out=lam, in_=AP(lam_d.tensor, 0, [[0, 128], [64, 4], [1, 64]])), writes=["lam", ("hb", 0)], slot="lam")
            P.op("vector", lambda e: e.tensor_tensor(out=lam[:, 0, :], in0=lam[:, 0, :], in1=lam[:, 1, :], op=ALU.mult), reads=["lam"], writes=["lam"])
            P.op("vector", lambda e: e.tensor_tensor(out=lam[:, 2, :], in0=lam[:, 2, :], in1=lam[:, 3, :], op=ALU.mult), reads=["lam"], writes=["lam"])
            P.op("vector", lambda e: e.tensor_reduce(out=lamv[:, 0:1], in_=lam[:, 0, :], axis=AX.X, op=ALU.add), reads=["lam"], writes=["lamv"])
            P.op("vector", lambda e: e.tensor_reduce(out=lamv[:, 1:2], in_=lam[:, 2, :], axis=AX.X, op=ALU.add), reads=["lam"], writes=["lamv", ("hb", 0)])
            P.op("scalar", lambda e: e.activation(out=lamv[:, 0:2], in_=lamv[:, 0:2], func=AF.Exp), reads=["lamv"], writes=["lamv"])
            P.op("vector", lambda e: e.scalar_tensor_tensor(out=lamv[:, 2:3], in0=lamv[:, 1:2], scalar=-lambda_init_fn(li), in1=lamv[:, 0:1], op0=ALU.add, op1=ALU.subtract),
                 reads=["lamv"], writes=["lamv"])
            neglam = lamv[:, 2:3]
            satb = self.scr("satb", [128, 2, 16], F32)
            rb_d = self.din("rel_bias", [32, 16], F32).ap()
            P.op("sync", lambda e: e.dma_start(out=satb[:, 0, :], in_=rb_d[15].partition_broadcast(128)), writes=["satb"], slot="satb0")
            P.op("sync", lambda e: e.dma_start(out=satb[:, 1, :], in_=rb_d[31].partition_broadcast(128)), writes=["satb"], slot="satb1")

            def load_head(h):
                s = h % 2
                P.op("sync", lambda e, h=h, s=s: e.dma_start(out=KTs[s][:], in_=kT_d[h]), reads=kkeys, writes=[("KT", s)], slot="kt%da" % s)
                for half in range(2):
                    P.op("sync", lambda e, h=h, s=s, half=half: e.dma_start(out=Vs[s][:, half * 16:(half + 1) * 16, :], in_=v_r[:, half * 16:(half + 1) * 16, h * dv:(h + 1) * dv]),
                         reads=vkeys, writes=[("V", s)], slot="v%d_%d" % (s, half))

            def fin_B(g, qc, j, h):
                otb, dnb = (2, 3)[g % 2], (4, 5)[g % 2]
                OT, DN = self.banks[otb], self.banks[dnb]
                R, od = Rb[g % 2], odb[g % 2]
                Rk, odk = ("Rb", g % 2), ("odb", g % 2)
                cs = slice(qc * 512, (qc + 1) * 512)
                P.op("vector", lambda e, R=R, DN=DN: e.reciprocal(out=R, in_=DN[:, 0:512]), reads=[("bank", dnb)], writes=[Rk])
                if j == 0:
                    P.op("vector", lambda e, R=R, OT=OT, cs=cs: e.tensor_tensor(out=o1[:, cs], in0=OT[:, 0:512], in1=R, op=ALU.mult),
                         reads=[("bank", otb), Rk], writes=[("o1", qc)])
                    return
                P.op("vector", lambda e, R=R: e.tensor_scalar(out=R, in0=R, scalar1=neglam, scalar2=None, op0=ALU.mult), reads=[Rk, "lamv"], writes=[Rk])
                P.op("vector", lambda e, R=R, OT=OT, od=od: e.tensor_tensor(out=od, in0=OT[:, 0:512], in1=R, op=ALU.mult),
                     reads=[("bank", otb), Rk], writes=[odk])
                P.op("vector", lambda e, od=od, cs=cs: e.tensor_tensor(out=od, in0=od, in1=o1[:, cs], op=ALU.add), reads=[odk, ("o1", qc)], writes=[odk])
                P.op("scalar", lambda e, od=od: e.activation(out=sqb, in_=od, func=AF.Square), reads=[odk], writes=["sqb"])
                P.op("tensor", lambda e: e.matmul(self.banks[7][:, 0:512], lhsT=onesf[:], rhs=sqb, start=True, stop=True),
                     reads=["sqb", "onesf"], writes=[("bank", 7)])
                P.op("scalar", lambda e: e.activation(out=rsb, in_=self.banks[7][:, 0:512], func=AF.Ln, scale=1.0 / 128, bias=self.epsb[:, 0:1]),
                     reads=[("bank", 7), "eps"], writes=["rsb"])
                P.op("scalar", lambda e: e.activation(out=rsb, in_=rsb, func=AF.Exp, scale=-0.5), reads=["rsb"], writes=["rsb"])
                P.op("vector", lambda e, od=od, h=h, qc=qc: e.tensor_tensor(out=self.actT[:, h, 1 + qc * 512:1 + (qc + 1) * 512], in0=od, in1=rsb, op=ALU.mult),
                     reads=[odk, "rsb"], writes=[("actT", h, t) for t in range(qc * 4, qc * 4 + 4)])

            load_head(0)
            g = 0
            for h in range(8):
                s = h % 2
                self.attn_flush()
                if h + 1 < 8:
                    load_head(h + 1)
                P.op("sync", lambda e, h=h: e.dma_start(out=QTz[0][0:64, :], in_=qT_d[h, 0:64, :]), reads=qkeys + [("QTz", 0)], writes=["QT"], slot="qt")
                P.op("sync", lambda e, h=h: e.dma_start(out=QTz[1][64:128, :], in_=qT_d[h, 64:128, :]), reads=qkeys + [("QTz", 1)], writes=["QT"], slot="qtb")
                KT, V = KTs[s], Vs[s]
                rds = [("KT", s), ("V", s), "QT"]
                for j in range(2):
                    for half in range(2):
                        hw = ni * 64
                        P.op("sync", lambda e, h=h, j=j, half=half, hw=hw: e.dma_start(out=TB[:, half * hw:(half + 1) * hw], in_=bt_d[h * 2 + j, :, half * hw:(half + 1) * hw]),
                             writes=["TB"], slot="tb%d" % half)
                    for qc in range(NT // 4):
                        otb, dnb = (2, 3)[g % 2], (4, 5)[g % 2]
                        for kb in range(NT):
                            dp = min(max(kb - 4 * qc, -12), 12)
                            i0 = 12 - dp
                            mms = [(KT[:, kb * 128:(kb + 1) * 128], QTz[j][:, qc * 512:(qc + 1) * 512], 0, 512)]

                            def back(pt, ptk, kb=kb, V=V, s=s, otb=otb, dnb=dnb):
                                P.op("tensor", lambda e, pt=pt: e.matmul(self.banks[otb][:, 0:512], lhsT=V[:, kb, :], rhs=pt[:, 0:512], start=(kb == 0), stop=(kb == NT - 1)),
                                     reads=[ptk, ("V", s)], writes=[("bank", otb)])
                                P.op("tensor", lambda e, pt=pt: e.matmul(self.banks[dnb][:, 0:512], lhsT=onesb[:], rhs=pt[:, 0:512], start=(kb == 0), stop=(kb == NT - 1)),
                                     reads=[ptk, "onesb"], writes=[("bank", dnb)])
                            post = None
                            if kb == NT - 1:
                                post = (lambda g=g, qc=qc, j=j, h=h: fin_B(g, qc, j, h))
                            cbias = None
                            dfull = kb - 4 * qc
                            if dfull >= 12:
                                cbias = satb[:, 1, 2 * h + j:2 * h + j + 1]
                            elif dfull <= -9:
                                cbias = satb[:, 0, 2 * h + j:2 * h + j + 1]
                            self.attn_item(it, mms, TB[:, i0 * 128:i0 * 128 + 512], 512, [], "TB", rds, post, back, cbias)
                            it += 1
                        g += 1
            self.attn_flush()
            P.barrier()
        elif kind == "A":
            dv = 64
            NE = NT + 2
            KTs = [self.scr("KT", [64, NE * 128], BF16) for _ in range(2)]
            Vs = [self.scr("V", [128, NE, dv + 1], BF16) for _ in range(2)]
            QTs = [self.scr("QT", [64, T], BF16) for _ in range(2)]
            TBs = [self.scr("TB", [128, ni * 128], F32) for _ in range(2)]
            pairs = [self.scr("pair", [128, NT, 128], BF16) for _ in range(2)]
            esink = self.scr("esink", [128, 16], F32)
            sink_d = self.din(pfx + "sink", [16], F32).ap()
            P.op("sync", lambda e: e.dma_start(out=esink[:], in_=sink_d.partition_broadcast(128)), writes=["esink"], slot="esink")
            P.op("scalar", lambda e: e.activation(out=esink[:], in_=esink[:], func=AF.Exp), reads=["esink"], writes=["esink"])
            for s_i in range(2):
                vb, kb_ = Vs[s_i], KTs[s_i]
                P.op("gpsimd", lambda e, vb=vb: e.memset(vb[:], 0.0), writes=[("Vones", s_i), ("V", s_i)])
                P.op("gpsimd", lambda e, vb=vb: e.memset(vb[:, 1:NE - 1, dv:dv + 1], 1.0), writes=[("Vones", s_i), ("V", s_i)])
                P.op("gpsimd", lambda e, kb_=kb_: e.memset(kb_[:], 0.0), writes=[("KT", s_i)])

            def load_kv(kvh):
                s = kvh % 2
                blk, r0 = kvh // 2, (kvh % 2) * 64
                P.op("sync", lambda e: e.dma_start(out=KTs[s][:, 128:128 + T], in_=kT_d[blk, r0:r0 + 64, :]), reads=kkeys, writes=[("KT", s)], slot="kt%db" % s)
                for half in range(2):
                    P.op("sync", lambda e, half=half: e.dma_start(out=Vs[s][:, 1 + half * 16:1 + (half + 1) * 16, 0:dv], in_=v_r[:, half * 16:(half + 1) * 16, kvh * dv:(kvh + 1) * dv]),
                         reads=vkeys, writes=[("V", s)], slot="v%db%d" % (s, half))

            def load_q(h):
                s = h % 2
                P.op("sync", lambda e: e.dma_start(out=QTs[s][:], in_=qT_d[h // 2, (h % 2) * 64:(h % 2) * 64 + 64, :]), reads=qkeys, writes=[("QT", s)], slot="qt%d" % s)
                P.op("sync", lambda e: e.dma_start(out=TBs[s][:], in_=bt_d[h]), writes=[("TB", s)], slot="tb%d" % s)

            fin_box = [0]
            load_kv(0)
            load_q(0)
            for h in range(16):
                kvh = h // 4
                s = kvh % 2
                self.attn_flush()
                if h % 4 == 0 and kvh + 1 < 4:
                    load_kv(kvh + 1)
                if h + 1 < 16:
                    load_q(h + 1)
                KT, V, QT, TB = KTs[s], Vs[s], QTs[h % 2], TBs[h % 2]
                pair = pairs[(h // 2) % 2]
                rds = [("KT", s), ("V", s), ("QT", h % 2), ("Vones", s)]
                for qt in range(NT):
                    ab = 2 + (it % 4)
                    mms = [(KT[0:64, (qt + 2 - i) * 128:(qt + 3 - i) * 128], QT[0:64, qt * 128:(qt + 1) * 128], i * 128, 128) for i in range(3)]
                    pvs = [(i * 128, V[:, qt + 2 - i, 0:dv + 1], ab, dv + 1, i == 0, i == 2) for i in range(3)]
                    def post_A(acc=self.banks[ab], ak=("bank", ab), qt=qt, h=h, pair=pair):
                        fin_box[0] += 1
                        fin = fin_box[0]
                        r = rr[fin % 4]
                        rk = ("rr", fin % 4)
                        P.op("vector", lambda e, acc=acc, r=r, h=h: e.tensor_tensor(out=r[:, 0:1], in0=acc[:, dv:dv + 1], in1=esink[:, h:h + 1], op=ALU.add),
                             reads=[ak, "esink"], writes=[rk])
                        P.op("vector", lambda e, r=r: e.reciprocal(out=r[:, 1:2], in_=r[:, 0:1]), reads=[rk], writes=[rk])
                        P.op("vector", lambda e, acc=acc, r=r, pair=pair, qt=qt, h=h: e.tensor_scalar(out=pair[:, qt, (h % 2) * 64:(h % 2) * 64 + 64], in0=acc[:, 0:dv], scalar1=r[:, 1:2], scalar2=None, op0=ALU.mult),
                             reads=[ak, rk], writes=[("pair", (h // 2) % 2, qt, h % 2)])
                    self.attn_item(it, mms, TB[:, 0:384], 384, pvs, ("TB", h % 2), rds, post_A)
                    it += 1
                if h % 2 == 1:
                    self.attn_flush()
                    for qt in range(NT):
                        fin += 1
                        bi = 7
                        pT = self.bank_bf(bi)
                        P.op("tensor", lambda e, pair=pair, qt=qt, pT=pT: e.transpose(out=pT[:, 0, :], in_=pair[:, qt, :], identity=self.ident[:]),
                             reads=[("pair", (h // 2) % 2, qt, 0), ("pair", (h // 2) % 2, qt, 1), "ident"], writes=[("bank", bi)])
                        P.op("scalar", lambda e, pT=pT, h=h, qt=qt: e.activation(out=self.actT[:, h // 2, 1 + qt * 128:1 + (qt + 1) * 128], in_=pT[:, 0, :], func=AF.Copy),
                             reads=[("bank", bi)], writes=[("actT", h // 2, qt)])
        else:
            dv = 128
            NE = NT + 16
            KT = self.scr("KT", [128, NE * 128], BF16)
            V = self.scr("V", [128, NE, dv + 1], BF16)
            QT = self.scr("QT", [128, 3, T], BF16)
            TB = self.scr("TB", [128, ni * 128], F32)
            P.op("gpsimd", lambda e: e.memset(V[:], 0.0), writes=["Vones", "V"])
            P.op("gpsimd", lambda e: e.memset(V[:, 8:NE - 8, dv:dv + 1], 1.0), writes=["Vones", "V"])
            P.op("gpsimd", lambda e: e.memset(KT[:], 0.0), writes=["KT"])
            ents = [(0, d_) for d_ in (1, 0, -1)] + [(1, d_) for d_ in (2, 1, 0, -1, -2)] + [(2, d_) for d_ in range(8, -9, -1)]
            fin_box = [0]
            for j in range(4):
                self.attn_flush()
                P.op("sync", lambda e, j=j: e.dma_start(out=KT[:, 1024:1024 + T], in_=kT_d[j]), reads=kkeys, writes=["KT"], slot="ktb")
                for half in range(2):
                    P.op("sync", lambda e, j=j, half=half: e.dma_start(out=V[:, 8 + half * 16:8 + (half + 1) * 16, 0:dv], in_=v_r[:, half * 16:(half + 1) * 16, j * dv:(j + 1) * dv]),
                         reads=vkeys, writes=["V"], slot="vb%d" % half)
                for g in range(3):
                    P.op("sync", lambda e, g=g, j=j: e.dma_start(out=QT[:, g, :], in_=qT_d[g * 4 + j]), reads=qkeys, writes=["QT"], slot="qt%d" % g)
                P.op("sync", lambda e, j=j: e.dma_start(out=TB[:], in_=bt_d[j]), writes=["TB"], slot="tb")
                rds = ["KT", "V", "QT", "Vones"]
                for qt in range(NT):
                    ab = 2 + (qt % 4)
                    for i0 in range(0, 25, 4):
                        grp = list(range(i0, min(i0 + 4, 25)))
                        mms = []
                        pvs = []
                        for n_, idx in enumerate(grp):
                            g, dl = ents[idx]
                            eb = qt + 8 + dl
                            mms.append((KT[:, eb * 128:(eb + 1) * 128], QT[:, g, qt * 128:(qt + 1) * 128], n_ * 128, 128))
                            pvs.append((n_ * 128, V[:, eb, 0:dv + 1], ab, dv + 1, idx == 0, idx == 24))
                        post = None
                        if grp[-1] == 24:
                            def post(acc=self.banks[ab], ak=("bank", ab), qt=qt, j=j):
                                fin_box[0] += 1
                                fin = fin_box[0]
                                r = rr[fin % 4]
                                rk = ("rr", fin % 4)
                                on = ons[fin % 2]
                                onk = ("on", fin % 2)
                                P.op("vector", lambda e, acc=acc, r=r: e.reciprocal(out=r[:, 0:1], in_=acc[:, dv:dv + 1]), reads=[ak], writes=[rk])
                                P.op("vector", lambda e, acc=acc, r=r, on=on: e.tensor_scalar(out=on[:], in0=acc[:, 0:dv], scalar1=r[:, 0:1], scalar2=None, op0=ALU.mult),
                                     reads=[ak, rk], writes=[onk])
                                self.out_transpose(on[:], onk, j, qt, fin)
                        self.attn_item(it, mms, TB[:, i0 * 128:(i0 + len(grp)) * 128], len(grp) * 128, pvs, "TB", rds, post)
                        it += 1
            self.attn_flush()

    def phase_wo(self, li):
        P = self.P
        L = LAYERS[li]
        pfx = "l%d_" % li
        nfc = L["nfc"]
        W = self.din(pfx + "w_o", [nfc * 128, D], F32)
        scale = None
        srd = ()
        if L["kind"] == "B":
            sg = self.din(pfx + "sgcol", [128, 1], F32).ap()
            sgt = self.scr("sgt", [128, 1], F32)
            P.op("sync", lambda e: e.dma_start(out=sgt[:], in_=sg), writes=["sgt"], slot="sgt")
            P.op("vector", lambda e: e.tensor_scalar(out=sgt[:], in0=sgt[:], scalar1=1.0 - lambda_init_fn(li), scalar2=None, op0=ALU.mult), reads=["sgt"], writes=["sgt"])
            scale = lambda k: sgt[:, 0:1]
            srd = ["sgt"]
        kch = list(range(nfc))
        units = [self.load_unit(W, kch, [(n * 512, 512)], scale=scale, scale_reads=srd) for n in range(2)]
        pend = [self.x_load(0, 0), self.x_load(0, 1)]
        for t in range(NT):
            for n in range(2):
                nx = t * 2 + n + 2
                if nx < NT * 2:
                    pend.append(self.x_load(nx // 2, nx % 2))
                slot, wkeys = units[n]
                wb = self.wb[slot]
                bi = 2 + n
                py = self.banks[bi]
                for c in range(nfc):
                    P.op("tensor", lambda e, c=c, t=t, py=py, wb=wb: e.matmul(py[:, 0:512], lhsT=self.actT[:, c, 1 + t * 128:1 + (t + 1) * 128], rhs=wb[:, c, 0:512], start=(c == 0), stop=(c == nfc - 1)),
                         reads=[("actT", c, t)] + wkeys, writes=[("bank", bi)])
                xt, xk, xi = pend[t * 2 + n]
                P.op("vector", lambda e, xt=xt, py=py: e.tensor_tensor(out=xt[:, 0:512], in0=py[:, 0:512], in1=xt[:, 0:512], op=ALU.add),
                     reads=[("bank", bi), xk], writes=[xk])
                self.x_store(t, n, xt, xk, xi)

    def phase_ffn(self, li):
        P = self.P
        pfx = "l%d_" % li
        self.load_gcol(pfx + "gcol_ffn")
        self.phase_norm()
        self.scr_reset()
        Wu = self.din(pfx + "w_up", [D, 2 * DFF], F32)
        Wd = self.din(pfx + "w_down", [DFF, D], F32)
        cw_d = self.din(pfx + "convp", [128, 4, 44], F32).ap()
        cw = self.scr("cw", [128, 4, 44], F32)
        P.op("sync", lambda e: e.dma_start(out=cw[:], in_=cw_d), writes=["cw"], slot="cw")
        wu_keys = self.precast(Wu, 8, 2 * DFF, self.wu_bf, "wubf", True)
        wd_keys = self.precast(Wd, 22, D, self.wd_bf, "wdbf", False)
        TC = 1024
        NK = T // TC
        a = self.scr("a", [128, 22, TC], BF16)
        Us = [[self.scr("U", [128, TC + 2], F32) for _ in range(2)] for _ in range(2)]
        T1s = [[self.scr("T1", [128, TC], F32) for _ in range(2)] for _ in range(2)]
        gsc = lambda k: self.gcol[:, k:k + 1]
        up_units = [[(f0 * 128, 256), (DFF + f0 * 128, 256)] for f0 in range(0, 22, 2)]
        dn_units = [(kc, n) for n in range(2) for kc in (list(range(0, 8)), list(range(8, 16)), list(range(16, 22)))]
        pcount = 0
        for k in range(NK):
            cb = TC * k
            allr = [("actT", c, t) for c in range(8) for t in range(max(8 * k - 1, 0), min(8 * k + 9, NT))] + ["halo"]
            nxt = self.load_unit_bf(self.wu_bf, list(range(8)), up_units[0], wu_keys)
            for ui in range(11):
                cur = nxt
                if ui + 1 < 11:
                    nxt = self.load_unit_bf(self.wu_bf, list(range(8)), up_units[ui + 1], wu_keys)
                slot, wkeys = cur
                wb = self.wb[slot]
                for pi in range(2):
                    f = ui * 2 + pi
                    par = pcount % 2
                    pcount += 1
                    for which in range(2):
                        off = which * 256 + pi * 128
                        fc = f + 22 * which
                        bA, bB, bE = self.banks[4 * which], self.banks[4 * which + 1], self.banks[4 * which + 2]
                        bks = [("bank", 4 * which + i) for i in range(3)]
                        for c in range(8):
                            P.op("tensor", lambda e, c=c, wb=wb, off=off, bA=bA, cb=cb: e.matmul(bA[:, 0:512], lhsT=wb[:, c, off:off + 128], rhs=self.actT[:, c, cb + 1:cb + 513], start=(c == 0), stop=(c == 7)),
                                 reads=allr + wkeys, writes=[bks[0]])
                        for c in range(8):
                            P.op("tensor", lambda e, c=c, wb=wb, off=off, bB=bB, cb=cb: e.matmul(bB[:, 0:512], lhsT=wb[:, c, off:off + 128], rhs=self.actT[:, c, cb + 513:cb + 1025], start=(c == 0), stop=(c == 7)),
                                 reads=allr + wkeys, writes=[bks[1]])
                        for c in range(8):
                            P.op("tensor", lambda e, c=c, wb=wb, off=off, bE=bE, cb=cb: e.matmul(bE[:, 0:2], lhsT=wb[:, c, off:off + 128], rhs=self.actT[:, c, cb:cb + 1026:1025], start=(c == 0), stop=(c == 7)),
                                 reads=allr + wkeys, writes=[bks[2]])
                        U = Us[which][par]
                        uk = ("U", which, par)
                        t1 = T1s[which][par]
                        tk = ("T1", which, par)
                        P.op("scalar", lambda e, U=U, bA=bA: e.activation(out=U[:, 1:513], in_=bA[:, 0:512], func=AF.Copy), reads=[bks[0]], writes=[(uk, 0)])
                        P.op("scalar", lambda e, U=U, bB=bB: e.activation(out=U[:, 513:1025], in_=bB[:, 0:512], func=AF.Copy), reads=[bks[1]], writes=[(uk, 1)])
                        P.op("scalar", lambda e, U=U, bE=bE: e.activation(out=U[:, 0:1026:1025], in_=bE[:, 0:2], func=AF.Copy), reads=[bks[2]], writes=[(uk, 2)])
                        P.op("scalar", lambda e, t1=t1, bA=bA, fc=fc: e.activation(out=t1[:, 0:512], in_=bA[:, 0:512], func=AF.Identity, scale=cw[:, 1, fc:fc + 1], bias=cw[:, 3, fc:fc + 1]),
                             reads=[bks[0], "cw"], writes=[(tk, 0)])
                        P.op("scalar", lambda e, t1=t1, bB=bB, fc=fc: e.activation(out=t1[:, 512:1024], in_=bB[:, 0:512], func=AF.Identity, scale=cw[:, 1, fc:fc + 1], bias=cw[:, 3, fc:fc + 1]),
                             reads=[bks[1], "cw"], writes=[(tk, 1)])
                        P.op("vector", lambda e, t1=t1, U=U, fc=fc: e.scalar_tensor_tensor(out=t1[:], in0=U[:, 0:TC], scalar=cw[:, 0, fc:fc + 1], in1=t1[:], op0=ALU.mult, op1=ALU.add),
                             reads=[(uk, 0), (uk, 1), (uk, 2), (tk, 0), (tk, 1), "cw"], writes=[(tk, 0), (tk, 1)])
                        P.op("vector", lambda e, t1=t1, U=U, fc=fc: e.scalar_tensor_tensor(out=t1[:], in0=U[:, 2:TC + 2], scalar=cw[:, 2, fc:fc + 1], in1=t1[:], op0=ALU.mult, op1=ALU.add),
                             reads=[(uk, 0), (uk, 1), (uk, 2), (tk, 0), (tk, 1), "cw"], writes=[(tk, 0), (tk, 1)])
                    tg, tv = T1s[0][par], T1s[1][par]
                    kg, kv = ("T1", 0, par), ("T1", 1, par)
                    P.op("scalar", lambda e, tg=tg: e.activation(out=tg[:], in_=tg[:], func=AF.Silu), reads=[(kg, 0), (kg, 1)], writes=[(kg, 0), (kg, 1)])
                    P.op("vector", lambda e, f=f, tg=tg, tv=tv: e.tensor_tensor(out=a[:, f, :], in0=tg[:], in1=tv[:], op=ALU.mult),
                         reads=[(kg, 0), (kg, 1), (kv, 0), (kv, 1)], writes=[("a", f)])
            nxt = self.load_unit_bf(self.wd_bf, dn_units[0][0], [(dn_units[0][1] * 512, 512)], wd_keys)
            for di, (kc, n) in enumerate(dn_units):
                cur = nxt
                if di + 1 < len(dn_units):
                    nxt = self.load_unit_bf(self.wd_bf, dn_units[di + 1][0], [(dn_units[di + 1][1] * 512, 512)], wd_keys)
                slot, wkeys = cur
                wb = self.wb[slot]
                if kc[0] == 0:
                    pend = [self.x_load(8 * k + t, n) for t in range(2)]
                for t in range(8):
                    for ci, f in enumerate(kc):
                        P.op("tensor", lambda e, t=t, ci=ci, f=f, wb=wb: e.matmul(self.banks[t][:, 0:512], lhsT=a[:, f, t * 128:(t + 1) * 128], rhs=wb[:, ci, 0:512], start=(f == 0), stop=(f == 21)),
                             reads=[("a", f)] + wkeys, writes=[("bank", t)])
                if kc[-1] == 21:
                    for t in range(8):
                        tt = 8 * k + t
                        if t + 2 < 8:
                            pend.append(self.x_load(8 * k + t + 2, n))
                        xt, xk, xi = pend[t]
                        P.op("vector", lambda e, t=t, xt=xt: e.tensor_tensor(out=xt[:, 0:512], in0=self.banks[t][:, 0:512], in1=xt[:, 0:512], op=ALU.add),
                             reads=[("bank", t), xk], writes=[xk])
                        self.x_store(tt, n, xt, xk, xi)

    def finish(self):
        self.P.finalize()
        return self.nc


def rel_bucket_np(rel):
    nb = 16
    max_exact = 8
    n = np.abs(rel)
    nf = np.maximum(n, 1).astype(np.float32)
    large = max_exact + (np.log(nf / np.float32(max_exact)) / np.float32(math.log(1024 / max_exact)) * np.float32(nb - max_exact)).astype(np.int32)
    large = np.minimum(large, nb - 1)
    return np.where(rel > 0, nb, 0) + np.where(n < max_exact, n, large)


def bias_tables(rel_bias, kind):
    rb = np.asarray(rel_bias, np.float32)
    kp = np.arange(128)[:, None]
    qp = np.arange(128)[None, :]
    if kind == "A":
        out = np.empty((16, 128, 3 * 128), np.float32)
        for i, dl in enumerate((1, 0, -1)):
            rel = dl * 128 + kp - qp
            bk = rel_bucket_np(rel)
            ok = np.abs(rel) <= 128
            for h in range(16):
                out[h, :, i * 128:(i + 1) * 128] = np.where(ok, rb[bk, h], NEG)
        return out
    if kind == "B":
        out = np.empty((16, 128, 28 * 128), np.float32)
        for i in range(28):
            dl = 12 - i
            rel = dl * 128 + kp - qp
            bk = rel_bucket_np(rel)
            for m in range(16):
                out[m, :, i * 128:(i + 1) * 128] = rb[bk, m]
        return out
    ents = [(0, d_) for d_ in (1, 0, -1)] + [(1, d_) for d_ in (2, 1, 0, -1, -2)] + [(2, d_) for d_ in range(8, -9, -1)]
    dils = (1, 4, 16)
    out = np.empty((4, 128, 25 * 128), np.float32)
    for i, (g, dl) in enumerate(ents):
        rel = dl * 128 + kp - qp
        dil = dils[g]
        ok = (rel % dil == 0) & (np.abs(rel) <= 64 * dil)
        bk = rel_bucket_np(rel)
        for j in range(4):
            out[j, :, i * 128:(i + 1) * 128] = np.where(ok, rb[bk, g * 4 + j], NEG)
    return out


def gcols(g):
    return np.ascontiguousarray(np.asarray(g, np.float32).reshape(8, 128).T)


_PROG = None
DEBUG_LAYERS = 4


def get_prog():
    global _PROG
    if _PROG is None:
        b = Builder()
        for li in range(DEBUG_LAYERS):
            b.phase_qkv(li)
            b.phase_attn(li)
            b.phase_wo(li)
            b.phase_ffn(li)
        nc = b.finish()
        _PROG = (nc, list(b.din_names), list(b.dout_names))
    return _PROG


def kernel(x, rel_bias,
           l0_attn_norm, l0_w_qkv, l0_q_gain, l0_k_gain, l0_sink, l0_w_o,
           l0_ffn_norm, l0_w_up, l0_conv_w, l0_conv_b, l0_w_down,
           l1_attn_norm, l1_w_qkv, l1_q_gain, l1_k_gain, l1_lambda_q1, l1_lambda_k1,
           l1_lambda_q2, l1_lambda_k2, l1_sub_gain, l1_w_o,
           l1_ffn_norm, l1_w_up, l1_conv_w, l1_conv_b, l1_w_down,
           l2_attn_norm, l2_w_qkv, l2_q_gain, l2_k_gain, l2_w_o,
           l2_ffn_norm, l2_w_up, l2_conv_w, l2_conv_b, l2_w_down,
           l3_attn_norm, l3_w_qkv, l3_q_gain, l3_k_gain, l3_sink, l3_w_o,
           l3_ffn_norm, l3_w_up, l3_conv_w, l3_conv_b, l3_w_down):
    inp = {
        "x": x,
        "rel_bias": rel_bias,
        "l0_attn_norm": l0_attn_norm,
        "l0_w_qkv": l0_w_qkv,
        "l0_q_gain": l0_q_gain,
        "l0_k_gain": l0_k_gain,
        "l0_sink": l0_sink,
        "l0_w_o": l0_w_o,
        "l0_ffn_norm": l0_ffn_norm,
        "l0_w_up": l0_w_up,
        "l0_conv_w": l0_conv_w,
        "l0_conv_b": l0_conv_b,
        "l0_w_down": l0_w_down,
        "l1_attn_norm": l1_attn_norm,
        "l1_w_qkv": l1_w_qkv,
        "l1_q_gain": l1_q_gain,
        "l1_k_gain": l1_k_gain,
        "l1_lambda_q1": l1_lambda_q1,
        "l1_lambda_k1": l1_lambda_k1,
        "l1_lambda_q2": l1_lambda_q2,
        "l1_lambda_k2": l1_lambda_k2,
        "l1_sub_gain": l1_sub_gain,
        "l1_w_o": l1_w_o,
        "l1_ffn_norm": l1_ffn_norm,
        "l1_w_up": l1_w_up,
        "l1_conv_w": l1_conv_w,
        "l1_conv_b": l1_conv_b,
        "l1_w_down": l1_w_down,
        "l2_attn_norm": l2_attn_norm,
        "l2_w_qkv": l2_w_qkv,
        "l2_q_gain": l2_q_gain,
        "l2_k_gain": l2_k_gain,
        "l2_w_o": l2_w_o,
        "l2_ffn_norm": l2_ffn_norm,
        "l2_w_up": l2_w_up,
        "l2_conv_w": l2_conv_w,
        "l2_conv_b": l2_conv_b,
        "l2_w_down": l2_w_down,
        "l3_attn_norm": l3_attn_norm,
        "l3_w_qkv": l3_w_qkv,
        "l3_q_gain": l3_q_gain,
        "l3_k_gain": l3_k_gain,
        "l3_sink": l3_sink,
        "l3_w_o": l3_w_o,
        "l3_ffn_norm": l3_ffn_norm,
        "l3_w_up": l3_w_up,
        "l3_conv_w": l3_conv_w,
        "l3_conv_b": l3_conv_b,
        "l3_w_down": l3_w_down,
    }
    x = np.ascontiguousarray(np.asarray(x, np.float32))
    rel_bias = np.asarray(rel_bias, np.float32)
    shared = {"ident": np.eye(128, dtype=np.float32), "rel_bias": np.ascontiguousarray(rel_bias)}
    for kind in ("A", "B", "C"):
        shared["bt_" + kind] = bias_tables(rel_bias, kind)
    f32 = lambda a: np.ascontiguousarray(np.asarray(a, np.float32))
    for li in range(4):
        L = LAYERS[li]
        p = "l%d_" % li
        shared[p + "gcol_attn"] = gcols(inp[p + "attn_norm"])
        shared[p + "gcol_ffn"] = gcols(inp[p + "ffn_norm"])
        for w in ("w_qkv", "w_o", "w_up", "w_down"):
            shared[p + w] = f32(inp[p + w])
        shared[p + "qkg"] = np.ascontiguousarray(np.stack([f32(inp[p + "q_gain"]), f32(inp[p + "k_gain"])]))
        cwv = f32(inp[p + "conv_w"]).reshape(3, 44, 128)
        cbv = f32(inp[p + "conv_b"]).reshape(1, 44, 128)
        shared[p + "convp"] = np.ascontiguousarray(np.concatenate([cwv, cbv], 0).transpose(2, 0, 1))
        if L["kind"] == "A":
            shared[p + "sink"] = f32(inp[p + "sink"])
        if L["kind"] == "B":
            shared[p + "lam"] = np.ascontiguousarray(np.stack([f32(inp[p + k]) for k in ("lambda_q1", "lambda_k1", "lambda_q2", "lambda_k2")]))
            shared[p + "sgcol"] = f32(inp[p + "sub_gain"]).reshape(128, 1)
    import time as _t
    _t1 = _t.time()
    nc, dins, douts = get_prog()
    print("[kernel] host prep + build took %.1fs" % (_t.time() - _t1), flush=True)
    in_maps = []
    for c in range(NCORES):
        m = {}
        for n in dins:
            m[n] = x[c] if n == "x" else shared[n]
        in_maps.append(m)
    import time as _t
    _t0 = _t.time()
    res = run_bass_kernel_spmd(nc, in_maps, core_ids=list(range(NCORES)))
    print("[kernel] run_bass_kernel_spmd took %.1fs" % (_t.time() - _t0), flush=True)
    return np.stack([res.results[c]["x_out"] for c in range(NCORES)]).astype(np.float32)
```

```python
import math
from contextlib import ExitStack

import numpy as np
import ml_dtypes

import concourse.bass as bass
import concourse.mybir as mybir
from concourse.ap import AP
from concourse.bass_utils import run_bass_kernel_spmd

F32 = mybir.dt.float32
BF16 = mybir.dt.bfloat16
AF = mybir.ActivationFunctionType
ALU = mybir.AluOpType
AX = mybir.AxisListType
NPBF = ml_dtypes.bfloat16

NCORES = 4
T = 4096
NT = 32
D = 1024
DFF = 2816
EPS = 1e-6
NEG = -30000.0

LAYERS = [
    dict(kind="A", F=1536, nqb=8, nkb=2, FV=256, nfc=8),
    dict(kind="B", F=3072, nqb=8, nkb=8, FV=1024, nfc=8),
    dict(kind="C", F=2560, nqb=12, nkb=4, FV=512, nfc=4),
    dict(kind="A", F=1536, nqb=8, nkb=2, FV=256, nfc=8),
]
NI = {"A": 3, "B": 28, "C": 25}
NUNIT_BT = {"A": 16, "B": 16, "C": 4}


def lambda_init_fn(layer):
    return 0.8 - 0.6 * math.exp(-0.3 * layer)


class Op:
    __slots__ = ("eng", "fn", "deps", "needs_inc", "val", "semkey", "is_dma", "idx")

    def __init__(self, eng, fn, deps, semkey, is_dma):
        self.eng = eng
        self.fn = fn
        self.deps = deps
        self.needs_inc = False
        self.val = None
        self.semkey = semkey
        self.is_dma = is_dma


class Prog:
    ENGS = ("sync", "scalar", "vector", "gpsimd", "tensor")

    def __init__(self, nc):
        self.nc = nc
        self.ops = {e: [] for e in self.ENGS}
        self.lastw = {}
        self.readers = {}
        self.es = ExitStack()
        self.outs = []
        self.fence = []
        self.fence_pending = set()
        self.last_dma = {}
        self.epoch = 0

    def barrier(self):
        fence = []
        for e in self.ENGS:
            for o in reversed(self.ops[e]):
                if not o.is_dma:
                    fence.append(o)
                    break
        fence.extend(self.last_dma.values())
        self.fence = fence
        self.fence_pending = set(self.ENGS)
        self.epoch += 1

    def op(self, eng, fn, reads=(), writes=(), slot=None, out=False):
        deps = []
        if eng in self.fence_pending:
            deps.extend(self.fence)
            self.fence_pending.discard(eng)
        for b in reads:
            w = self.lastw.get(b)
            if w is not None:
                deps.append(w)
        for b in writes:
            w = self.lastw.get(b)
            if w is not None:
                deps.append(w)
            deps.extend(self.readers.get(b, ()))
        is_dma = slot is not None
        semkey = ("dma", slot) if is_dma else ("eng", eng, self.epoch % 3)
        o = Op(eng, fn, deps, semkey, is_dma)
        o.idx = len(self.ops[eng])
        self.ops[eng].append(o)
        if is_dma:
            self.last_dma[slot] = o
        for b in writes:
            self.lastw[b] = o
            self.readers[b] = []
        for b in reads:
            self.readers.setdefault(b, []).append(o)
        if out:
            self.outs.append(o)
        return o

    @staticmethod
    def _skip(d, o):
        return d is o or (d.eng == "tensor" and o.eng == "tensor" and not d.is_dma and not o.is_dma)

    def finalize(self):
        nc = self.nc
        final_waits = self.outs
        for e in self.ENGS:
            for o in self.ops[e]:
                best = {}
                for d in o.deps:
                    if self._skip(d, o):
                        continue
                    if d.is_dma:
                        d.needs_inc = True
                        continue
                    b = best.get(d.semkey)
                    if b is None or d.idx > b.idx:
                        best[d.semkey] = d
                for d in best.values():
                    d.needs_inc = True
        for d in final_waits:
            d.needs_inc = True
        for e in self.ENGS:
            for o in self.ops[e]:
                if o.is_dma:
                    o.needs_inc = True
        counters = {}
        for e in self.ENGS:
            for o in self.ops[e]:
                if o.needs_inc:
                    c = counters.get(o.semkey, 0) + (16 if o.is_dma else 1)
                    counters[o.semkey] = c
                    o.val = c
        sems = {}
        for i, k in enumerate(counters):
            sems[k] = self.es.enter_context(nc.semaphore("s%d" % i))
        self.nsem = len(sems)
        block = self.es.enter_context(nc.Block())

        def run(e, engine):
            known = {}
            for o in self.ops[e]:
                need = {}
                for d in o.deps:
                    if self._skip(d, o) or d.val is None:
                        continue
                    if need.get(d.semkey, 0) < d.val:
                        need[d.semkey] = d.val
                for k, v in need.items():
                    if known.get(k, 0) >= v:
                        continue
                    engine.wait_ge(sems[k], v)
                    known[k] = v
                ins = o.fn(engine)
                if o.needs_inc:
                    ins.then_inc(sems[o.semkey], 16 if o.is_dma else 1)
            if e == "sync":
                need = {}
                for d in final_waits:
                    if need.get(d.semkey, 0) < d.val:
                        need[d.semkey] = d.val
                for k, v in need.items():
                    engine.wait_ge(sems[k], v)

        @block.sync
        def _(eng):
            run("sync", eng)

        @block.scalar
        def _(eng):
            run("scalar", eng)

        @block.vector
        def _(eng):
            run("vector", eng)

        @block.gpsimd
        def _(eng):
            run("gpsimd", eng)

        @block.tensor
        def _(eng):
            run("tensor", eng)

        self.es.close()


SB_BASE = 16512
SCR_END = 229376


class Builder:
    def __init__(self):
        self.nc = bass.Bass("TRN2", target_bir_lowering=False)
        self.P = Prog(self.nc)
        self.din_names = []
        self.dout_names = []
        self.d = {}
        self.uid = 0
        self.perm_off = SB_BASE
        nc = self.nc
        self.actT = self.perm("actT", [128, 8, T + 2], BF16)
        self.ws = [self.perm("ws%d" % i, [128, 4, 512], F32) for i in range(2)]
        self.wb = [self.perm("wb%d" % i, [128, 8, 512], BF16) for i in range(2)]
        self.xts = [self.perm("xt%d" % i, [128, D], F32) for i in range(4)]
        self.hbs = [self.perm("hb%d" % i, [128, D], BF16) for i in range(2)]
        self.ident = self.perm("ident", [128, 128], BF16)
        self.identf = self.perm("identf", [128, 128], F32)
        self.sst = [self.perm("sst%d" % i, [128, 4], F32) for i in range(4)]
        self.epsb = self.perm("epsb", [128, 1], F32)
        self.gcol = self.perm("gcol", [128, 8], F32)
        self.PERM_END = (self.perm_off + 63) // 64 * 64
        self.scr_off = self.PERM_END
        self.nscr = 0
        self.xcnt = 0
        self.banks = [nc.alloc_psum_tensor("bank%d" % i, [128, 512], F32) for i in range(8)]
        self.wcount = 0
        self.xd = self.dout("x_out", [T, D], F32).ap()
        self.qT_d = nc.dram_tensor("qT_scr", [12, 128, T], BF16, kind="Internal").ap()
        self.kT_d = nc.dram_tensor("kT_scr", [8, 128, T], BF16, kind="Internal").ap()
        self.v_d = nc.dram_tensor("v_scr", [T, 1024], BF16, kind="Internal").ap()
        self.wu_bf = nc.dram_tensor("wu_bf", [D, 2 * DFF], BF16, kind="Internal").ap()
        self.wd_bf = nc.dram_tensor("wd_bf", [DFF, D], BF16, kind="Internal").ap()
        self.init_consts()

    def perm(self, name, shape, dt):
        n = int(np.prod(shape[1:])) * (4 if dt == F32 else 2)
        n = (n + 31) // 32 * 32
        t = self.nc.alloc_sbuf_tensor_at(name, shape, dt, offset=self.perm_off)
        self.perm_off += n
        return t

    def scr_reset(self):
        if self.nscr > 0:
            self.P.barrier()
        self.nscr += 1
        self.scr_off = self.PERM_END

    def scr(self, name, shape, dt):
        n = int(np.prod(shape[1:])) * (4 if dt == F32 else 2)
        n = (n + 31) // 32 * 32
        self.uid += 1
        t = self.nc.alloc_sbuf_tensor_at("%s_%d" % (name, self.uid), shape, dt, offset=self.scr_off)
        self.scr_off += n
        assert self.scr_off <= SCR_END, (name, self.scr_off)
        return t

    def din(self, name, shape, dt):
        if name not in self.d:
            self.d[name] = self.nc.dram_tensor(name, list(shape), dt, kind="ExternalInput")
            self.din_names.append(name)
        return self.d[name]

    def dout(self, name, shape, dt):
        if name not in self.d:
            self.d[name] = self.nc.dram_tensor(name, list(shape), dt, kind="ExternalOutput")
            self.dout_names.append(name)
        return self.d[name]

    def bank_bf(self, i):
        return self.banks[i][:].bitcast(BF16).rearrange("p (c t) -> p c t", t=128)

    def init_consts(self):
        P = self.P
        idd = self.din("ident", [128, 128], F32).ap()
        xin = self.din("x", [T, D], F32).ap()
        P.op("sync", lambda e: e.dma_start(out=self.identf[:], in_=idd), writes=["identf"], slot="c_id")
        P.op("vector", lambda e: e.tensor_copy(out=self.ident[:], in_=self.identf[:]), reads=["identf"], writes=["ident"])
        P.op("vector", lambda e: e.memset(self.epsb[:], EPS), writes=["eps"])
        P.op("vector", lambda e: e.memset(self.actT[:, :, 0:1], 0.0), writes=["halo"])
        P.op("vector", lambda e: e.memset(self.actT[:, :, T + 1:T + 2], 0.0), writes=["halo"])
        for t0 in range(0, NT, 4):
            P.op("sync", lambda e, t0=t0: e.dma_start(out=self.xd[t0 * 128:(t0 + 4) * 128, :], in_=xin[t0 * 128:(t0 + 4) * 128, :]),
                 writes=[("xd", t, n) for t in range(t0, t0 + 4) for n in range(2)], slot="xcp%d" % (t0 // 4), out=True)

    def x_load(self, t, n=None):
        P = self.P
        i = self.xcnt % 4
        self.xcnt += 1
        xt = self.xts[i]
        key = ("xt", i)
        if n is None:
            P.op("sync", lambda e, t=t, xt=xt: e.dma_start(out=xt[:], in_=self.xd[t * 128:(t + 1) * 128, :]),
                 reads=[("xd", t, 0), ("xd", t, 1)], writes=[key], slot="xl%d" % i)
        else:
            P.op("sync", lambda e, t=t, xt=xt, n=n: e.dma_start(out=xt[:, 0:512], in_=self.xd[t * 128:(t + 1) * 128, n * 512:(n + 1) * 512]),
                 reads=[("xd", t, n)], writes=[key], slot="xl%d" % i)
        return xt, key, i

    def x_store(self, t, n, xt, key, i):
        P = self.P
        P.op("gpsimd", lambda e, t=t, xt=xt, n=n: e.dma_start(out=self.xd[t * 128:(t + 1) * 128, n * 512:(n + 1) * 512], in_=xt[:, 0:512]),
             reads=[key], writes=[("xd", t, n)], slot="xs%d" % i, out=True)

    def load_unit(self, W, kchunks, segs, scale=None, scale_reads=()):
        P = self.P
        i = self.wcount
        self.wcount += 1
        slot = i % 2
        wb = self.wb[slot]
        key = ("wb", slot)
        Wa = W.ap()
        ncol = sum(s[1] for s in segs)
        halves = [kchunks[0:4], kchunks[4:8]]
        allkeys = []
        for hi, kc in enumerate(halves):
            if not kc:
                continue
            ws = self.ws[hi]
            co = 0
            for si, (c0, cn) in enumerate(segs):
                k0 = kc[0]
                src = Wa[k0 * 128:(k0 + len(kc)) * 128, c0:c0 + cn].rearrange("(c p) f -> p c f", p=128)
                P.op("sync", lambda e, ws=ws, src=src, co=co, cn=cn, n=len(kc): e.dma_start(out=ws[:, 0:n, co:co + cn], in_=src),
                     writes=[("ws", hi, si)], slot="ws%d_%d" % (hi, si))
                co += cn
            n = len(kc)
            if scale is None:
                P.op("gpsimd", lambda e, ws=ws, wb=wb, hi=hi, n=n, ncol=ncol: e.tensor_copy(out=wb[:, hi * 4:hi * 4 + n, 0:ncol], in_=ws[:, 0:n, 0:ncol]),
                     reads=[("ws", hi, si) for si in range(len(segs))], writes=[(key, hi, 0)])
                allkeys.append((key, hi, 0))
            else:
                for j, k in enumerate(kc):
                    sc = scale(k)
                    P.op("gpsimd", lambda e, ws=ws, wb=wb, hi=hi, j=j, ncol=ncol, sc=sc: e.tensor_scalar(out=wb[:, hi * 4 + j, 0:ncol], in0=ws[:, j, 0:ncol], scalar1=sc, scalar2=None, op0=ALU.mult),
                         reads=[("ws", hi, si) for si in range(len(segs))] + list(scale_reads), writes=[(key, hi, j)])
                    allkeys.append((key, hi, j))
        return slot, allkeys

    def precast(self, W, nchunks, ncols, Wbf, keyname, scaled):
        P = self.P
        Wa = W.ap()
        blocks = [(kg, min(4, nchunks - kg), c0) for kg in range(0, nchunks, 4) for c0 in range(0, ncols, 512)]
        keys = []
        for b, (kg, n, c0) in enumerate(blocks):
            ws = self.ws[b % 2]
            wsk = ("ws", b % 2, 0)
            ss_ = b % 4
            stg = self.wb[ss_ // 2][:, (ss_ % 2) * 4:(ss_ % 2) * 4 + n, :]
            stgk = (("wb", ss_ // 2), ss_ % 2, 0)
            src = Wa[kg * 128:(kg + n) * 128, c0:c0 + 512].rearrange("(c p) f -> p c f", p=128)
            P.op("sync", lambda e, ws=ws, src=src, n=n: e.dma_start(out=ws[:, 0:n, :], in_=src), writes=[wsk], slot="ws%d_0" % (b % 2))
            eng = "scalar" if b % 2 == 0 else "vector"
            if not scaled:
                if eng == "scalar":
                    P.op(eng, lambda e, ws=ws, stg=stg, n=n: e.activation(out=stg, in_=ws[:, 0:n, :], func=AF.Copy), reads=[wsk], writes=[stgk])
                else:
                    P.op(eng, lambda e, ws=ws, stg=stg, n=n: e.tensor_copy(out=stg, in_=ws[:, 0:n, :]), reads=[wsk], writes=[stgk])
            else:
                for j in range(n):
                    sc = self.gcol[:, kg + j:kg + j + 1]
                    if eng == "scalar":
                        P.op(eng, lambda e, ws=ws, stg=stg, j=j, sc=sc: e.activation(out=stg[:, j, :], in_=ws[:, j, :], func=AF.Copy, scale=sc), reads=[wsk, "gcol"], writes=[(stgk, j)] if False else [stgk])
                    else:
                        P.op(eng, lambda e, ws=ws, stg=stg, j=j, sc=sc: e.tensor_scalar(out=stg[:, j, :], in0=ws[:, j, :], scalar1=sc, scalar2=None, op0=ALU.mult), reads=[wsk, "gcol"], writes=[stgk])
            dst = Wbf[kg * 128:(kg + n) * 128, c0:c0 + 512].rearrange("(c p) f -> p c f", p=128)
            P.op("gpsimd", lambda e, stg=stg, dst=dst: e.dma_start(out=dst, in_=stg), reads=[stgk], writes=[(keyname, b)], slot="pc%d" % ss_)
            keys.append((keyname, b))
        return keys

    def load_unit_bf(self, Wbf, kchunks, segs, rkeys):
        P = self.P
        i = self.wcount
        self.wcount += 1
        slot = i % 2
        wb = self.wb[slot]
        keys = [(("wb", slot), 0, 0), (("wb", slot), 1, 0)]
        k0, n, co = kchunks[0], len(kchunks), 0
        for si, (c0, cn) in enumerate(segs):
            src = Wbf[k0 * 128:(k0 + n) * 128, c0:c0 + cn].rearrange("(c p) f -> p c f", p=128)
            P.op("sync", lambda e, wb=wb, src=src, n=n, co=co, cn=cn: e.dma_start(out=wb[:, 0:n, co:co + cn], in_=src),
                 reads=rkeys, writes=keys, slot="wbd%d_%d" % (slot, si))
            co += cn
        return slot, keys

    def bg_start(self, li, n_items, cast_eng):
        P = self.P
        pfx = "l%d_" % li
        Wu = self.din(pfx + "w_up", [D, 2 * DFF], F32)
        Wd = self.din(pfx + "w_down", [DFF, D], F32)
        self.load_gcol(pfx + "gcol_ffn")
        blocks = []
        for (W, nch, ncols, Wbf, kn, scaled) in ((Wu, 8, 2 * DFF, self.wu_bf, "wubf", True), (Wd, 22, D, self.wd_bf, "wdbf", False)):
            Wa = W.ap()
            bi = 0
            for kg in range(0, nch, 4):
                for c0 in range(0, ncols, 512):
                    blocks.append((Wa, Wbf, kn, bi, kg, min(4, nch - kg), c0, scaled))
                    bi += 1
        self.bg_keys = {"wubf": [], "wdbf": []}

        def stage_in(b):
            Wa, Wbf, kn, bi, kg, n, c0, scaled = blocks[b]
            ws = self.ws[b % 2]
            src = Wa[kg * 128:(kg + n) * 128, c0:c0 + 512].rearrange("(c p) f -> p c f", p=128)
            P.op("sync", lambda e, ws=ws, src=src, n=n: e.dma_start(out=ws[:, 0:n, :], in_=src), writes=[("ws", b % 2, 0)], slot="ws%d_0" % (b % 2))

        def stage_cast(b):
            Wa, Wbf, kn, bi, kg, n, c0, scaled = blocks[b]
            ws = self.ws[b % 2]
            wsk = ("ws", b % 2, 0)
            ss_ = b % 4
            stg = self.wb[ss_ // 2][:, (ss_ % 2) * 4:(ss_ % 2) * 4 + n, :]
            stgk = (("wb", ss_ // 2), ss_ % 2, 0)
            if not scaled:
                if cast_eng == "scalar":
                    P.op("scalar", lambda e, ws=ws, stg=stg, n=n: e.activation(out=stg, in_=ws[:, 0:n, :], func=AF.Copy), reads=[wsk], writes=[stgk])
                else:
                    P.op("vector", lambda e, ws=ws, stg=stg, n=n: e.tensor_copy(out=stg, in_=ws[:, 0:n, :]), reads=[wsk], writes=[stgk])
            else:
                for j in range(n):
                    sc = self.gcol[:, kg + j:kg + j + 1]
                    if cast_eng == "scalar":
                        P.op("scalar", lambda e, ws=ws, stg=stg, j=j, sc=sc: e.activation(out=stg[:, j, :], in_=ws[:, j, :], func=AF.Copy, scale=sc), reads=[wsk, "gcol"], writes=[stgk])
                    else:
                        P.op("vector", lambda e, ws=ws, stg=stg, j=j, sc=sc: e.tensor_scalar(out=stg[:, j, :], in0=ws[:, j, :], scalar1=sc, scalar2=None, op0=ALU.mult), reads=[wsk, "gcol"], writes=[stgk])
            dst = Wbf[kg * 128:(kg + n) * 128, c0:c0 + 512].rearrange("(c p) f -> p c f", p=128)
            P.op("gpsimd", lambda e, stg=stg, dst=dst: e.dma_start(out=dst, in_=stg), reads=[stgk], writes=[(kn, bi)], slot="pc%d" % ss_)
            self.bg_keys[kn].append((kn, bi))

        nb = len(blocks)
        q = []
        for k in range(nb + 1):
            def step(k=k):
                if k < nb:
                    stage_in(k)
                if k >= 1:
                    stage_cast(k - 1)
            q.append(step)
        self.bg_queue = q
        self.bg_every = max(1, n_items // (len(q) + 2))
        self.bg_count = 0
        self.bg_layer = li

    def bg_tick(self):
        q = getattr(self, "bg_queue", None)
        if not q:
            return
        self.bg_count += 1
        if self.bg_count % self.bg_every == 0:
            q.pop(0)()

    def bg_flush(self):
        q = getattr(self, "bg_queue", None)
        while q:
            q.pop(0)()

    def phase_norm(self):
        P = self.P
        junk = self.banks[7]
        hbs = self.hbs
        pend = {}

        def stA(t):
            xt, xk, _ = pend[t]
            st = self.sst[t % 4]
            sk = ("sst", t % 4)
            P.op("vector", lambda e, st=st: e.memset(st[:], 0.0), writes=[sk])
            P.op("scalar", lambda e, xt=xt, st=st: e.activation(out=junk[:, 0:512], in_=xt[:, 0:512], func=AF.Square, accum_out=st[:, 0:1]),
                 reads=[xk, sk], writes=[sk, ("bank", 7)])
            P.op("scalar", lambda e, xt=xt, st=st: e.activation(out=junk[:, 0:512], in_=xt[:, 512:1024], func=AF.Square, accum_out=st[:, 1:2]),
                 reads=[xk, sk], writes=[sk, ("bank", 7)])
            P.op("vector", lambda e, st=st: e.tensor_tensor(out=st[:, 2:3], in0=st[:, 0:1], in1=st[:, 1:2], op=ALU.add), reads=[sk], writes=[sk])

        def stB(t):
            xt, xk, _ = pend[t]
            st = self.sst[t % 4]
            sk = ("sst", t % 4)
            P.op("scalar", lambda e, st=st: e.activation(out=st[:, 2:3], in_=st[:, 2:3], func=AF.Ln, scale=1.0 / D, bias=self.epsb[:, 0:1]),
                 reads=[sk, "eps"], writes=[sk])
            P.op("scalar", lambda e, st=st: e.activation(out=st[:, 3:4], in_=st[:, 2:3], func=AF.Exp, scale=-0.5), reads=[sk], writes=[sk])
            hb = hbs[t % 2]
            hk = ("hb", t % 2)
            pk = ("bank", t % 2)
            pT = self.bank_bf(t % 2)
            P.op("vector", lambda e, xt=xt, hb=hb, st=st: e.tensor_scalar(out=hb[:], in0=xt[:], scalar1=st[:, 3:4], scalar2=None, op0=ALU.mult),
                 reads=[xk, sk], writes=[hk])
            for c in range(8):
                P.op("tensor", lambda e, c=c, hb=hb, pT=pT: e.transpose(out=pT[:, c, :], in_=hb[:, c * 128:(c + 1) * 128], identity=self.ident[:]),
                     reads=[hk, "ident"], writes=[pk])

        def stC(t):
            pk = ("bank", t % 2)
            pT = self.bank_bf(t % 2)
            P.op("scalar", lambda e, t=t, pT=pT: e.activation(out=self.actT[:, :, 1 + t * 128:1 + (t + 1) * 128], in_=pT, func=AF.Copy),
                 reads=[pk], writes=[("actT", c, t) for c in range(8)])

        pend[0] = self.x_load(0)
        pend[1] = self.x_load(1)
        for i in range(NT + 2):
            if i + 2 < NT:
                pend[i + 2] = self.x_load(i + 2)
            if i - 2 >= 0:
                stC(i - 2)
            if 0 <= i - 1 < NT:
                stB(i - 1)
            if i < NT:
                stA(i)

    def load_gcol(self, name):
        P = self.P
        g = self.din(name, [128, 8], F32).ap()
        P.op("sync", lambda e: e.dma_start(out=self.gcol[:], in_=g), writes=["gcol"], slot="gcol")

    def phase_qkv(self, li):
        P = self.P
        L = LAYERS[li]
        kind = L["kind"]
        pfx = "l%d_" % li
        self.load_gcol(pfx + "gcol_attn")
        self.phase_norm()
        self.scr_reset()
        W = self.din(pfx + "w_qkv", [D, L["F"]], F32)
        dh = 128 if kind == "C" else 64
        geff_d = self.din(pfx + "qkg", [2, dh], F32).ap()
        qT_d, kT_d, v_d = self.qT_d, self.kT_d, self.v_d
        gq = self.scr("gq", [128, dh], F32)
        gk = self.scr("gk", [128, dh], F32)
        P.op("sync", lambda e: e.dma_start(out=gq[:], in_=geff_d[0].partition_broadcast(128)), writes=["gq"], slot="gq")
        P.op("sync", lambda e: e.dma_start(out=gk[:], in_=geff_d[1].partition_broadcast(128)), writes=["gk"], slot="gk")
        P.op("vector", lambda e: e.scalar_tensor_tensor(out=gk[:], in0=gq[:], scalar=float(dh) ** -0.5, in1=gk[:], op0=ALU.mult, op1=ALU.mult),
             reads=["gq", "gk"], writes=["gk"])
        stages = [self.scr("stage", [128, 4, T], BF16) for _ in range(2)]
        ND = 4
        sqs = [self.scr("sq", [128, 512], F32) for _ in range(ND)]
        kfs = [self.scr("kf", [128, 512], F32) for _ in range(ND)]
        qns = [self.scr("qn", [128, 512], BF16) for _ in range(ND)]
        vsts = [self.scr("vst", [128, 512], BF16) for _ in range(ND)]
        ssq = [self.scr("ssq", [128, 8], F32) for _ in range(ND)]
        PYB = (2, 3, 4, 5)
        PTB = (0, 1, 6, 7)
        if kind == "A":
            chunks = [[("q", 0, 512, 0)], [("q", 0, 512, 4)], [("k", 0, 256, 0), ("v", 256, 256, 0)]]
        elif kind == "B":
            chunks = [[("q", 0, 512, 0)], [("q", 0, 512, 4)], [("k", 0, 512, 0)], [("k", 0, 512, 4)], [("v", 0, 512, 0)], [("v", 0, 512, 512)]]
        else:
            chunks = [[("q", 0, 512, 0)], [("q", 0, 512, 4)], [("q", 0, 512, 8)], [("k", 0, 512, 0)], [("v", 0, 512, 0)]]
        gsc = lambda k: self.gcol[:, k:k + 1]
        units = [None] * len(chunks)
        units[0] = self.load_unit(W, list(range(8)), [(0, 512)], scale=gsc, scale_reads=["gcol"])
        it = 0
        pend_rest = []

        def pipe_step(keep):
            n = len(pend_rest)
            for idx in range(n):
                ent = pend_rest[idx]
                lag = n - 1 - idx
                want = 3 if keep == 0 else min(3, lag)
                if keep == 0:
                    want = min(3, ent[1] + 1)
                while ent[1] < want:
                    ent[0](ent[1])
                    ent[1] += 1
            while pend_rest and pend_rest[0][1] >= 3:
                pend_rest.pop(0)

        for ci, segs in enumerate(chunks):
            if ci + 1 < len(chunks):
                units[ci + 1] = self.load_unit(W, list(range(8)), [((ci + 1) * 512, 512)], scale=gsc, scale_reads=["gcol"])
            slot, wkeys = units[ci]
            wb = self.wb[slot]
            stage = stages[ci % 2]
            stk = ("stage", ci % 2)
            for t in range(NT):
                bi = PYB[it % ND]
                py = self.banks[bi]
                pyk = ("bank", bi)
                for c in range(8):
                    P.op("tensor", lambda e, c=c, t=t, py=py, wb=wb: e.matmul(py[:, 0:512], lhsT=self.actT[:, c, 1 + t * 128:1 + (t + 1) * 128], rhs=wb[:, c, 0:512], start=(c == 0), stop=(c == 7)),
                         reads=[("actT", c, t)] + wkeys, writes=[pyk])
                def rest(stg_i, segs=segs, it=it, t=t, py=py, pyk=pyk, stage=stage, stk=stk):
                    for (ty, off, w, dst) in segs:
                        if ty == "v":
                            if stg_i != 0:
                                continue
                            vst = vsts[it % ND]
                            vk = ("vst", it % ND)
                            P.op("scalar", lambda e, py=py, vst=vst, off=off, w=w: e.activation(out=vst[:, 0:w], in_=py[:, off:off + w], func=AF.Copy),
                                 reads=[pyk], writes=[vk])
                            P.op("gpsimd", lambda e, vst=vst, t=t, dst=dst, w=w: e.dma_start(out=v_d[t * 128:(t + 1) * 128, dst:dst + w], in_=vst[:, 0:w]),
                                 reads=[vk], writes=[("vd", t, dst)], slot="vst%d" % (it % ND))
                            continue
                        nh = w // dh
                        sq = sqs[it % ND]
                        sqk = ("sq", it % ND)
                        s_ = ssq[it % ND]
                        sk = ("ssq", it % ND)
                        qn = qns[it % ND]
                        qk = ("qn", it % ND)
                        tb = PTB[it % ND]
                        pT = self.bank_bf(tb)
                        pk = ("bank", tb)
                        nb = w // 128
                        if stg_i == 0:
                            P.op("scalar", lambda e, py=py, sq=sq, off=off, w=w: e.activation(out=sq[:, 0:w], in_=py[:, off:off + w], func=AF.Square),
                                 reads=[pyk], writes=[sqk])
                            P.op("vector", lambda e, sq=sq, s_=s_, w=w, nh=nh: e.tensor_reduce(out=s_[:, 0:nh], in_=sq[:, 0:w].rearrange("p (h d) -> p h d", d=dh), axis=AX.X, op=ALU.add),
                                 reads=[sqk], writes=[sk])
                        elif stg_i == 1:
                            P.op("scalar", lambda e, s_=s_, nh=nh: e.activation(out=s_[:, 0:nh], in_=s_[:, 0:nh], func=AF.Ln, scale=1.0 / dh, bias=self.epsb[:, 0:1]),
                                 reads=[sk, "eps"], writes=[sk])
                            P.op("scalar", lambda e, s_=s_, nh=nh: e.activation(out=s_[:, 0:nh], in_=s_[:, 0:nh], func=AF.Exp, scale=-0.5), reads=[sk], writes=[sk])
                            rb = AP(s_, 0, [[8, 128], [1, nh], [0, dh]])
                            if ty == "q":
                                P.op("vector", lambda e, py=py, qn=qn, off=off, w=w, rb=rb: e.tensor_tensor(out=qn[:, 0:w].rearrange("p (h d) -> p h d", d=dh), in0=py[:, off:off + w].rearrange("p (h d) -> p h d", d=dh), in1=rb, op=ALU.mult),
                                     reads=[pyk, sk], writes=[qk])
                            else:
                                kf = kfs[it % ND]
                                kfk = ("kf", it % ND)
                                gb = AP(gk, 0, [[dh, 128], [0, nh], [1, dh]])
                                P.op("vector", lambda e, py=py, kf=kf, off=off, w=w, rb=rb: e.tensor_tensor(out=kf[:, 0:w].rearrange("p (h d) -> p h d", d=dh), in0=py[:, off:off + w].rearrange("p (h d) -> p h d", d=dh), in1=rb, op=ALU.mult),
                                     reads=[pyk, sk], writes=[kfk])
                                P.op("gpsimd", lambda e, kf=kf, qn=qn, w=w, gb=gb: e.tensor_tensor(out=qn[:, 0:w].rearrange("p (h d) -> p h d", d=dh), in0=kf[:, 0:w].rearrange("p (h d) -> p h d", d=dh), in1=gb, op=ALU.mult),
                                     reads=[kfk, "gk"], writes=[qk])
                            for j in range(nb):
                                P.op("tensor", lambda e, j=j, qn=qn, pT=pT: e.transpose(out=pT[:, j, :], in_=qn[:, j * 128:(j + 1) * 128], identity=self.ident[:]),
                                     reads=[qk, "ident"], writes=[pk])
                        else:
                            P.op("scalar", lambda e, pT=pT, stage=stage, nb=nb, t=t: e.activation(out=stage[:, 0:nb, t * 128:(t + 1) * 128], in_=pT[:, 0:nb, :], func=AF.Copy),
                                 reads=[pk], writes=[(stk, t)])
                pend_rest.append([rest, 0])
                pipe_step(3)
                it += 1
            while pend_rest:
                pipe_step(0)
            for (ty, off, w, dst) in segs:
                if ty == "v":
                    continue
                dd = qT_d if ty == "q" else kT_d
                dk = "qTd" if ty == "q" else "kTd"
                for j in range(w // 128):
                    P.op("gpsimd", lambda e, dd=dd, j=j, dst=dst, stage=stage: e.dma_start(out=dd[dst + j], in_=stage[:, j, :]),
                         reads=[(stk, t) for t in range(NT)], writes=[(dk, dst + j)], slot="stg%d_%d" % (ci % 2, j))

    ST_BANKS = (0, 1, 6)

    def attn_item(self, it, mms, tb_ap, ncols, pvs, tbkey, extra_reads, post=None, back_fn=None, const_bias=None):
        P = self.P
        self.bg_tick()
        bi = self.ST_BANKS[it % 3]
        st = self.banks[bi]
        stk = ("bank", bi)
        sc = self.a_sc[it % 3]
        sck = ("sc", it % 3)
        pt = self.a_pt[it % 3]
        ptk = ("pt", it % 3)
        for (lh, rh, c0, n) in mms:
            P.op("tensor", lambda e, lh=lh, rh=rh, c0=c0, n=n, st=st: e.matmul(st[:, c0:c0 + n], lhsT=lh, rhs=rh, start=True, stop=True),
                 reads=extra_reads, writes=[stk])
        if const_bias is not None:
            P.op("scalar", lambda e, st=st, pt=pt, ncols=ncols: e.activation(out=pt[:, 0:ncols], in_=st[:, 0:ncols], func=AF.Exp, bias=const_bias),
                 reads=[stk, "satb"], writes=[ptk])
        else:
            P.op("vector", lambda e, st=st, sc=sc, tb_ap=tb_ap, ncols=ncols: e.tensor_tensor(out=sc[:, 0:ncols], in0=st[:, 0:ncols], in1=tb_ap, op=ALU.add),
                 reads=[stk, tbkey], writes=[sck])
            P.op("scalar", lambda e, sc=sc, pt=pt, ncols=ncols: e.activation(out=pt[:, 0:ncols], in_=sc[:, 0:ncols], func=AF.Exp),
                 reads=[sck], writes=[ptk])
        q = self.a_queue
        q.append((pt, ptk, pvs, list(extra_reads), post, back_fn))
        while len(q) > 2:
            self._attn_back(q.pop(0))

    def _attn_back(self, pend):
        P = self.P
        pt, ptk, pvs, extra_reads, post, back_fn = pend
        if back_fn is not None:
            back_fn(pt, ptk)
            pvs = []
        for (c0, v_ap, ab, wdt, s0, s1) in pvs:
            acc = self.banks[ab]
            P.op("tensor", lambda e, c0=c0, v_ap=v_ap, acc=acc, wdt=wdt, s0=s0, s1=s1, pt=pt: e.matmul(acc[:, 0:wdt], lhsT=pt[:, c0:c0 + 128], rhs=v_ap, start=s0, stop=s1),
                 reads=[ptk] + extra_reads, writes=[("bank", ab)])
        if post is not None:
            post()

    def attn_flush(self):
        q = self.a_queue
        while q:
            self._attn_back(q.pop(0))

    def out_transpose(self, on, onk, chunk, qt, trk):
        P = self.P
        bi = 7
        pT = self.bank_bf(bi)
        P.op("tensor", lambda e, on=on, pT=pT: e.transpose(out=pT[:, 0, :], in_=on, identity=self.ident[:]),
             reads=[onk, "ident"], writes=[("bank", bi)])
        P.op("scalar", lambda e, pT=pT, chunk=chunk, qt=qt: e.activation(out=self.actT[:, chunk, 1 + qt * 128:1 + (qt + 1) * 128], in_=pT[:, 0, :], func=AF.Copy),
             reads=[("bank", bi)], writes=[("actT", chunk, qt)])

    def phase_attn(self, li):
        P = self.P
        L = LAYERS[li]
        kind = L["kind"]
        pfx = "l%d_" % li
        self.scr_reset()
        nkb, nqb, FV = L["nkb"], L["nqb"], L["FV"]
        qT_d, kT_d, v_d = self.qT_d, self.kT_d, self.v_d
        ni = NI[kind]
        bt_d = self.din("bt_" + kind, [NUNIT_BT[kind], 128, ni * 128], F32).ap()
        self.a_sc = [self.scr("sc", [128, 512], F32) for _ in range(3)]
        self.a_pt = [self.scr("pt", [128, 512], BF16) for _ in range(3)]
        self.a_queue = []
        rr = [self.scr("rr", [128, 4], F32) for _ in range(4)]
        ons = [self.scr("on", [128, 128], BF16) for _ in range(2)]
        v_r = v_d.rearrange("(kb p) f -> p kb f", p=128)
        self.bg_start(li, {"A": 16 * NT, "B": 16 * (NT // 4) * NT, "C": 4 * NT * 7}[kind], "vector" if kind == "B" else "scalar")
        qkeys = [("qTd", b) for b in range(nqb)]
        kkeys = [("kTd", b) for b in range(nkb)]
        vkeys = [("vd", t, c0) for t in range(NT) for c0 in range(0, FV, 512 if FV >= 512 else 256)]
        it = 0
        fin = 0
        if kind == "B":
            dv = 128
            KTs = [self.scr("KT", [128, T], BF16) for _ in range(2)]
            Vs = [self.scr("V", [128, NT, dv], BF16) for _ in range(2)]
            QTz = [self.scr("QT", [128, T], BF16) for _ in range(2)]
            TB = self.scr("TB", [128, ni * 128], F32)
            o1 = self.scr("o1", [128, T], F32)
            onesb = self.scr("onesb", [128, 128], BF16)
            onesf = self.scr("onesf", [128, 128], F32)
            halves = [self.xts[i][:, k * 512:(k + 1) * 512] for i in range(4) for k in range(2)]
            Rb, odb, sqb, rsb = halves[0:2], halves[2:4], halves[4], halves[5]
            P.op("gpsimd", lambda e: e.memset(onesb[:], 1.0), writes=["onesb"])
            P.op("gpsimd", lambda e: e.memset(onesf[:], 1.0), writes=["onesf"])
            P.op("gpsimd", lambda e: e.memset(QTz[0][64:128, :], 0.0), writes=[("QTz", 0)])
            P.op("gpsimd", lambda e: e.memset(QTz[1][0:64, :], 0.0), writes=[("QTz", 1)])
            lam = self.hbs[0][:].bitcast(F32)[:, 0:256].rearrange("p (a d) -> p a d", d=64)
            lamv = self.scr("lamv", [128, 4], F32)
            lam_d = self.din(pfx + "lam", [4, 64], F32).ap()
            P.op("sync", lambda e: e.dma_start(out=lam, in_=AP(lam_d.tensor, 0, [[0, 128], [64, 4], [1, 64]])), writes=["lam", ("hb", 0)], slot="lam")
            P.op("vector", lambda e: e.tensor_tensor(out=lam[:, 0, :], in0=lam[:, 0, :], in1=lam[:, 1, :], op=ALU.mult), reads=["lam"], writes=["lam"])
            P.op("vector", lambda e: e.tensor_tensor(out=lam[:, 2, :], in0=lam[:, 2, :], in1=lam[:, 3, :], op=ALU.mult), reads=["lam"], writes=["lam"])
            P.op("vector", lambda e: e.tensor_reduce(out=lamv[:, 0:1], in_=lam[:, 0, :], axis=AX.X, op=ALU.add), reads=["lam"], writes=["lamv"])
            P.op("vector", lambda e: e.tensor_reduce(out=lamv[:, 1:2], in_=lam[:, 2, :], axis=AX.X, op=ALU.add), reads=["lam"], writes=["lamv", ("hb", 0)])
            P.op("scalar", lambda e: e.activation(out=lamv[:, 0:2], in_=lamv[:, 0:2], func=AF.Exp), reads=["lamv"], writes=["lamv"])
            P.op("vector", lambda e: e.scalar_tensor_tensor(out=lamv[:, 2:3], in0=lamv[:, 1:2], scalar=-lambda_init_fn(li), in1=lamv[:, 0:1], op0=ALU.add, op1=ALU.subtract),
                 reads=["lamv"], writes=["lamv"])
            neglam = lamv[:, 2:3]
            satb = self.scr("satb", [128, 2, 16], F32)
            rb_d = self.din("rel_bias", [32, 16], F32).ap()
            P.op("sync", lambda e: e.dma_start(out=satb[:, 0, :], in_=rb_d[15].partition_broadcast(128)), writes=["satb"], slot="satb0")
            P.op("sync", lambda e: e.dma_start(out=satb[:, 1, :], in_=rb_d[31].partition_broadcast(128)), writes=["satb"], slot="satb1")

            def load_head(h):
                s = h % 2
                P.op("sync", lambda e, h=h, s=s: e.dma_start(out=KTs[s][:], in_=kT_d[h]), reads=kkeys, writes=[("KT", s)], slot="kt%da" % s)
                for half in range(4):
                    P.op("sync", lambda e, h=h, s=s, half=half: e.dma_start(out=Vs[s][:, half * 8:(half + 1) * 8, :], in_=v_r[:, half * 8:(half + 1) * 8, h * dv:(h + 1) * dv]),
                         reads=vkeys, writes=[("V", s, half)], slot="v%d_%d" % (s, half))

            def fin_B(g, qc, j, h):
                otb, dnb = (2, 3)[g % 2], (4, 5)[g % 2]
                OT, DN = self.banks[otb], self.banks[dnb]
                R, od = Rb[g % 2], odb[g % 2]
                Rk, odk = ("Rb", g % 2), ("odb", g % 2)
                cs = slice(qc * 512, (qc + 1) * 512)
                P.op("vector", lambda e, R=R, DN=DN: e.reciprocal(out=R, in_=DN[:, 0:512]), reads=[("bank", dnb)], writes=[Rk])
                if j == 0:
                    P.op("vector", lambda e, R=R, OT=OT, cs=cs: e.tensor_tensor(out=o1[:, cs], in0=OT[:, 0:512], in1=R, op=ALU.mult),
                         reads=[("bank", otb), Rk], writes=[("o1", qc)])
                    return
                P.op("vector", lambda e, R=R: e.tensor_scalar(out=R, in0=R, scalar1=neglam, scalar2=None, op0=ALU.mult), reads=[Rk, "lamv"], writes=[Rk])
                P.op("vector", lambda e, R=R, OT=OT, od=od: e.tensor_tensor(out=od, in0=OT[:, 0:512], in1=R, op=ALU.mult),
                     reads=[("bank", otb), Rk], writes=[odk])
                P.op("vector", lambda e, od=od, cs=cs: e.tensor_tensor(out=od, in0=od, in1=o1[:, cs], op=ALU.add), reads=[odk, ("o1", qc)], writes=[odk])
                P.op("scalar", lambda e, od=od: e.activation(out=sqb, in_=od, func=AF.Square), reads=[odk], writes=["sqb"])
                P.op("tensor", lambda e: e.matmul(self.banks[7][:, 0:512], lhsT=onesf[:], rhs=sqb, start=True, stop=True),
                     reads=["sqb", "onesf"], writes=[("bank", 7)])
                P.op("scalar", lambda e: e.activation(out=rsb, in_=self.banks[7][:, 0:512], func=AF.Ln, scale=1.0 / 128, bias=self.epsb[:, 0:1]),
                     reads=[("bank", 7), "eps"], writes=["rsb"])
                P.op("scalar", lambda e: e.activation(out=rsb, in_=rsb, func=AF.Exp, scale=-0.5), reads=["rsb"], writes=["rsb"])
                P.op("vector", lambda e, od=od, h=h, qc=qc: e.tensor_tensor(out=self.actT[:, h, 1 + qc * 512:1 + (qc + 1) * 512], in0=od, in1=rsb, op=ALU.mult),
                     reads=[odk, "rsb"], writes=[("actT", h, t) for t in range(qc * 4, qc * 4 + 4)])

            load_head(0)
            g = 0
            for h in range(8):
                s = h % 2
                self.attn_flush()
                if h + 1 < 8:
                    load_head(h + 1)
                P.op("sync", lambda e, h=h: e.dma_start(out=QTz[0][0:64, :], in_=qT_d[h, 0:64, :]), reads=qkeys + [("QTz", 0)], writes=["QT"], slot="qt")
                P.op("sync", lambda e, h=h: e.dma_start(out=QTz[1][64:128, :], in_=qT_d[h, 64:128, :]), reads=qkeys + [("QTz", 1)], writes=["QT"], slot="qtb")
                KT, V = KTs[s], Vs[s]
                rds = [("KT", s), ("V", s, 0), ("V", s, 1), ("V", s, 2), ("V", s, 3), "QT"]
                for j in range(2):
                    for half in range(2):
                        hw = ni * 64
                        P.op("sync", lambda e, h=h, j=j, half=half, hw=hw: e.dma_start(out=TB[:, half * hw:(half + 1) * hw], in_=bt_d[h * 2 + j, :, half * hw:(half + 1) * hw]),
                             writes=["TB"], slot="tb%d" % half)
                    for qc in range(NT // 4):
                        otb, dnb = (2, 3)[g % 2], (4, 5)[g % 2]
                        for kb in range(NT):
                            dp = min(max(kb - 4 * qc, -12), 12)
                            i0 = 12 - dp
                            mms = [(KT[:, kb * 128:(kb + 1) * 128], QTz[j][:, qc * 512:(qc + 1) * 512], 0, 512)]

                            def back(pt, ptk, kb=kb, V=V, s=s, otb=otb, dnb=dnb):
                                P.op("tensor", lambda e, pt=pt: e.matmul(self.banks[otb][:, 0:512], lhsT=V[:, kb, :], rhs=pt[:, 0:512], start=(kb == 0), stop=(kb == NT - 1)),
                                     reads=[ptk, ("V", s, kb // 8)], writes=[("bank", otb)])
                                P.op("tensor", lambda e, pt=pt: e.matmul(self.banks[dnb][:, 0:512], lhsT=onesb[:], rhs=pt[:, 0:512], start=(kb == 0), stop=(kb == NT - 1)),
                                     reads=[ptk, "onesb"], writes=[("bank", dnb)])
                            post = None
                            if kb == NT - 1:
                                post = (lambda g=g, qc=qc, j=j, h=h: fin_B(g, qc, j, h))
                            cbias = None
                            dfull = kb - 4 * qc
                            if dfull >= 12:
                                cbias = satb[:, 1, 2 * h + j:2 * h + j + 1]
                            elif dfull <= -9:
                                cbias = satb[:, 0, 2 * h + j:2 * h + j + 1]
                            self.attn_item(it, mms, TB[:, i0 * 128:i0 * 128 + 512], 512, [], "TB", rds, post, back, cbias)
                            it += 1
                        g += 1
            self.attn_flush()
            P.barrier()
        elif kind == "A":
            dv = 64
            NE = NT + 2
            KTs = [self.scr("KT", [64, NE * 128], BF16) for _ in range(2)]
            Vs = [self.scr("V", [128, NE, dv + 1], BF16) for _ in range(2)]
            QTs = [self.scr("QT", [64, T], BF16) for _ in range(2)]
            TBs = [self.scr("TB", [128, ni * 128], F32) for _ in range(2)]
            pairs = [self.scr("pair", [128, NT, 128], BF16) for _ in range(2)]
            esink = self.scr("esink", [128, 16], F32)
            sink_d = self.din(pfx + "sink", [16], F32).ap()
            P.op("sync", lambda e: e.dma_start(out=esink[:], in_=sink_d.partition_broadcast(128)), writes=["esink"], slot="esink")
            P.op("scalar", lambda e: e.activation(out=esink[:], in_=esink[:], func=AF.Exp), reads=["esink"], writes=["esink"])
            for s_i in range(2):
                vb, kb_ = Vs[s_i], KTs[s_i]
                P.op("gpsimd", lambda e, vb=vb: e.memset(vb[:], 0.0), writes=[("Vones", s_i), ("V", s_i)])
                P.op("gpsimd", lambda e, vb=vb: e.memset(vb[:, 1:NE - 1, dv:dv + 1], 1.0), writes=[("Vones", s_i), ("V", s_i)])
                P.op("gpsimd", lambda e, kb_=kb_: e.memset(kb_[:], 0.0), writes=[("KT", s_i)])

            def load_kv(kvh):
                s = kvh % 2
                blk, r0 = kvh // 2, (kvh % 2) * 64
                P.op("sync", lambda e: e.dma_start(out=KTs[s][:, 128:128 + T], in_=kT_d[blk, r0:r0 + 64, :]), reads=kkeys, writes=[("KT", s)], slot="kt%db" % s)
                for half in range(4):
                    P.op("sync", lambda e, half=half: e.dma_start(out=Vs[s][:, 1 + half * 8:1 + (half + 1) * 8, 0:dv], in_=v_r[:, half * 8:(half + 1) * 8, kvh * dv:(kvh + 1) * dv]),
                         reads=vkeys, writes=[("V", s)], slot="v%db%d" % (s, half))

            def load_q(h):
                s = h % 2
                P.op("sync", lambda e: e.dma_start(out=QTs[s][:], in_=qT_d[h // 2, (h % 2) * 64:(h % 2) * 64 + 64, :]), reads=qkeys, writes=[("QT", s)], slot="qt%d" % s)
                P.op("sync", lambda e: e.dma_start(out=TBs[s][:], in_=bt_d[h]), writes=[("TB", s)], slot="tb%d" % s)

            fin_box = [0]
            load_kv(0)
            load_q(0)
            for h in range(16):
                kvh = h // 4
                s = kvh % 2
                self.attn_flush()
                if h % 4 == 0 and kvh + 1 < 4:
                    load_kv(kvh + 1)
                if h + 1 < 16:
                    load_q(h + 1)
                KT, V, QT, TB = KTs[s], Vs[s], QTs[h % 2], TBs[h % 2]
                pair = pairs[(h // 2) % 2]
                rds = [("KT", s), ("V", s), ("QT", h % 2), ("Vones", s)]
                for qt in range(NT):
                    ab = 2 + (it % 4)
                    mms = [(KT[0:64, (qt + 2 - i) * 128:(qt + 3 - i) * 128], QT[0:64, qt * 128:(qt + 1) * 128], i * 128, 128) for i in range(3)]
                    pvs = [(i * 128, V[:, qt + 2 - i, 0:dv + 1], ab, dv + 1, i == 0, i == 2) for i in range(3)]
                    def post_A(acc=self.banks[ab], ak=("bank", ab), qt=qt, h=h, pair=pair):
                        fin_box[0] += 1
                        fin = fin_box[0]
                        r = rr[fin % 4]
                        rk = ("rr", fin % 4)
                        P.op("vector", lambda e, acc=acc, r=r, h=h: e.tensor_tensor(out=r[:, 0:1], in0=acc[:, dv:dv + 1], in1=esink[:, h:h + 1], op=ALU.add),
                             reads=[ak, "esink"], writes=[rk])
                        P.op("vector", lambda e, r=r: e.reciprocal(out=r[:, 1:2], in_=r[:, 0:1]), reads=[rk], writes=[rk])
                        P.op("vector", lambda e, acc=acc, r=r, pair=pair, qt=qt, h=h: e.tensor_scalar(out=pair[:, qt, (h % 2) * 64:(h % 2) * 64 + 64], in0=acc[:, 0:dv], scalar1=r[:, 1:2], scalar2=None, op0=ALU.mult),
                             reads=[ak, rk], writes=[("pair", (h // 2) % 2, qt, h % 2)])
                    self.attn_item(it, mms, TB[:, 0:384], 384, pvs, ("TB", h % 2), rds, post_A)
                    it += 1
                if h % 2 == 1:
                    self.attn_flush()
                    for qt in range(NT):
                        fin += 1
                        bi = 7
                        pT = self.bank_bf(bi)
                        P.op("tensor", lambda e, pair=pair, qt=qt, pT=pT: e.transpose(out=pT[:, 0, :], in_=pair[:, qt, :], identity=self.ident[:]),
                             reads=[("pair", (h // 2) % 2, qt, 0), ("pair", (h // 2) % 2, qt, 1), "ident"], writes=[("bank", bi)])
                        P.op("scalar", lambda e, pT=pT, h=h, qt=qt: e.activation(out=self.actT[:, h // 2, 1 + qt * 128:1 + (qt + 1) * 128], in_=pT[:, 0, :], func=AF.Copy),
                             reads=[("bank", bi)], writes=[("actT", h // 2, qt)])
        else:
            dv = 128
            NE = NT + 16
            KT = self.scr("KT", [128, NE * 128], BF16)
            V = self.scr("V", [128, NE, dv + 1], BF16)
            QT = self.scr("QT", [128, 3, T], BF16)
            TB = self.scr("TB", [128, ni * 128], F32)
            P.op("gpsimd", lambda e: e.memset(V[:], 0.0), writes=["Vones", "V"])
            P.op("gpsimd", lambda e: e.memset(V[:, 8:NE - 8, dv:dv + 1], 1.0), writes=["Vones", "V"])
            P.op("gpsimd", lambda e: e.memset(KT[:], 0.0), writes=["KT"])
            ents = [(0, d_) for d_ in (1, 0, -1)] + [(1, d_) for d_ in (2, 1, 0, -1, -2)] + [(2, d_) for d_ in range(8, -9, -1)]
            fin_box = [0]
            for j in range(4):
                self.attn_flush()
                P.op("sync", lambda e, j=j: e.dma_start(out=KT[:, 1024:1024 + T], in_=kT_d[j]), reads=kkeys, writes=["KT"], slot="ktb")
                for half in range(4):
                    P.op("sync", lambda e, j=j, half=half: e.dma_start(out=V[:, 8 + half * 8:8 + (half + 1) * 8, 0:dv], in_=v_r[:, half * 8:(half + 1) * 8, j * dv:(j + 1) * dv]),
                         reads=vkeys, writes=["V"], slot="vb%d" % half)
                for g in range(3):
                    P.op("sync", lambda e, g=g, j=j: e.dma_start(out=QT[:, g, :], in_=qT_d[g * 4 + j]), reads=qkeys, writes=["QT"], slot="qt%d" % g)
                P.op("sync", lambda e, j=j: e.dma_start(out=TB[:], in_=bt_d[j]), writes=["TB"], slot="tb")
                rds = ["KT", "V", "QT", "Vones"]
                for qt in range(NT):
                    ab = 2 + (qt % 4)
                    for i0 in range(0, 25, 4):
                        grp = list(range(i0, min(i0 + 4, 25)))
                        mms = []
                        pvs = []
                        for n_, idx in enumerate(grp):
                            g, dl = ents[idx]
                            eb = qt + 8 + dl
                            mms.append((KT[:, eb * 128:(eb + 1) * 128], QT[:, g, qt * 128:(qt + 1) * 128], n_ * 128, 128))
                            pvs.append((n_ * 128, V[:, eb, 0:dv + 1], ab, dv + 1, idx == 0, idx == 24))
                        post = None
                        if grp[-1] == 24:
                            def post(acc=self.banks[ab], ak=("bank", ab), qt=qt, j=j):
                                fin_box[0] += 1
                                fin = fin_box[0]
                                r = rr[fin % 4]
                                rk = ("rr", fin % 4)
                                on = ons[fin % 2]
                                onk = ("on", fin % 2)
                                P.op("vector", lambda e, acc=acc, r=r: e.reciprocal(out=r[:, 0:1], in_=acc[:, dv:dv + 1]), reads=[ak], writes=[rk])
                                P.op("vector", lambda e, acc=acc, r=r, on=on: e.tensor_scalar(out=on[:], in0=acc[:, 0:dv], scalar1=r[:, 0:1], scalar2=None, op0=ALU.mult),
                                     reads=[ak, rk], writes=[onk])
                                self.out_transpose(on[:], onk, j, qt, fin)
                        self.attn_item(it, mms, TB[:, i0 * 128:(i0 + len(grp)) * 128], len(grp) * 128, pvs, "TB", rds, post)
                        it += 1
            self.attn_flush()

    def phase_wo(self, li):
        P = self.P
        self.bg_flush()
        L = LAYERS[li]
        pfx = "l%d_" % li
        nfc = L["nfc"]
        W = self.din(pfx + "w_o", [nfc * 128, D], F32)
        scale = None
        srd = ()
        if L["kind"] == "B":
            sg = self.din(pfx + "sgcol", [128, 1], F32).ap()
            sgt = self.scr("sgt", [128, 1], F32)
            P.op("sync", lambda e: e.dma_start(out=sgt[:], in_=sg), writes=["sgt"], slot="sgt")
            P.op("vector", lambda e: e.tensor_scalar(out=sgt[:], in0=sgt[:], scalar1=1.0 - lambda_init_fn(li), scalar2=None, op0=ALU.mult), reads=["sgt"], writes=["sgt"])
            scale = lambda k: sgt[:, 0:1]
            srd = ["sgt"]
        kch = list(range(nfc))
        units = [self.load_unit(W, kch, [(n * 512, 512)], scale=scale, scale_reads=srd) for n in range(2)]
        pend = [self.x_load(0, 0), self.x_load(0, 1)]
        for t in range(NT):
            for n in range(2):
                nx = t * 2 + n + 2
                if nx < NT * 2:
                    pend.append(self.x_load(nx // 2, nx % 2))
                slot, wkeys = units[n]
                wb = self.wb[slot]
                bi = 2 + n
                py = self.banks[bi]
                for c in range(nfc):
                    P.op("tensor", lambda e, c=c, t=t, py=py, wb=wb: e.matmul(py[:, 0:512], lhsT=self.actT[:, c, 1 + t * 128:1 + (t + 1) * 128], rhs=wb[:, c, 0:512], start=(c == 0), stop=(c == nfc - 1)),
                         reads=[("actT", c, t)] + wkeys, writes=[("bank", bi)])
                xt, xk, xi = pend[t * 2 + n]
                P.op("vector", lambda e, xt=xt, py=py: e.tensor_tensor(out=xt[:, 0:512], in0=py[:, 0:512], in1=xt[:, 0:512], op=ALU.add),
                     reads=[("bank", bi), xk], writes=[xk])
                self.x_store(t, n, xt, xk, xi)

    def phase_ffn(self, li):
        P = self.P
        pfx = "l%d_" % li
        have_bg = getattr(self, "bg_layer", None) == li
        if not have_bg:
            self.load_gcol(pfx + "gcol_ffn")
        self.phase_norm()
        self.scr_reset()
        Wu = self.din(pfx + "w_up", [D, 2 * DFF], F32)
        Wd = self.din(pfx + "w_down", [DFF, D], F32)
        cw_d = self.din(pfx + "convp", [128, 4, 44], F32).ap()
        cw = self.scr("cw", [128, 4, 44], F32)
        P.op("sync", lambda e: e.dma_start(out=cw[:], in_=cw_d), writes=["cw"], slot="cw")
        if have_bg:
            wu_keys, wd_keys = self.bg_keys["wubf"], self.bg_keys["wdbf"]
        else:
            wu_keys = self.precast(Wu, 8, 2 * DFF, self.wu_bf, "wubf", True)
            wd_keys = self.precast(Wd, 22, D, self.wd_bf, "wdbf", False)
        TC = 1024
        NK = T // TC
        a = self.scr("a", [128, 22, TC], BF16)
        Us = [[self.scr("U", [128, TC + 2], F32) for _ in range(2)] for _ in range(2)]
        T1s = [[self.scr("T1", [128, TC], F32) for _ in range(2)] for _ in range(2)]
        gsc = lambda k: self.gcol[:, k:k + 1]
        up_units = [[(f0 * 128, 256), (DFF + f0 * 128, 256)] for f0 in range(0, 22, 2)]
        dn_units = [(kc, n) for n in range(2) for kc in (list(range(0, 8)), list(range(8, 16)), list(range(16, 22)))]
        pcount = 0
        for k in range(NK):
            cb = TC * k
            allr = [("actT", c, t) for c in range(8) for t in range(max(8 * k - 1, 0), min(8 * k + 9, NT))] + ["halo"]
            nxt = self.load_unit_bf(self.wu_bf, list(range(8)), up_units[0], wu_keys)
            for ui in range(11):
                cur = nxt
                if ui + 1 < 11:
                    nxt = self.load_unit_bf(self.wu_bf, list(range(8)), up_units[ui + 1], wu_keys)
                slot, wkeys = cur
                wb = self.wb[slot]
                for pi in range(2):
                    f = ui * 2 + pi
                    par = pcount % 2
                    pcount += 1
                    for which in range(2):
                        off = which * 256 + pi * 128
                        fc = f + 22 * which
                        bA, bB, bE = self.banks[4 * which], self.banks[4 * which + 1], self.banks[4 * which + 2]
                        bks = [("bank", 4 * which + i) for i in range(3)]
                        for c in range(8):
                            P.op("tensor", lambda e, c=c, wb=wb, off=off, bA=bA, cb=cb: e.matmul(bA[:, 0:512], lhsT=wb[:, c, off:off + 128], rhs=self.actT[:, c, cb + 1:cb + 513], start=(c == 0), stop=(c == 7)),
                                 reads=allr + wkeys, writes=[bks[0]])
                        for c in range(8):
                            P.op("tensor", lambda e, c=c, wb=wb, off=off, bB=bB, cb=cb: e.matmul(bB[:, 0:512], lhsT=wb[:, c, off:off + 128], rhs=self.actT[:, c, cb + 513:cb + 1025], start=(c == 0), stop=(c == 7)),
                                 reads=allr + wkeys, writes=[bks[1]])
                        for c in range(8):
                            P.op("tensor", lambda e, c=c, wb=wb, off=off, bE=bE, cb=cb: e.matmul(bE[:, 0:2], lhsT=wb[:, c, off:off + 128], rhs=self.actT[:, c, cb:cb + 1026:1025], start=(c == 0), stop=(c == 7)),
                                 reads=allr + wkeys, writes=[bks[2]])
                        U = Us[which][par]
                        uk = ("U", which, par)
                        t1 = T1s[which][par]
                        tk = ("T1", which, par)
                        P.op("scalar", lambda e, U=U, bA=bA: e.activation(out=U[:, 1:513], in_=bA[:, 0:512], func=AF.Copy), reads=[bks[0]], writes=[(uk, 0)])
                        P.op("scalar", lambda e, U=U, bB=bB: e.activation(out=U[:, 513:1025], in_=bB[:, 0:512], func=AF.Copy), reads=[bks[1]], writes=[(uk, 1)])
                        P.op("scalar", lambda e, U=U, bE=bE: e.activation(out=U[:, 0:1026:1025], in_=bE[:, 0:2], func=AF.Copy), reads=[bks[2]], writes=[(uk, 2)])
                        P.op("scalar", lambda e, t1=t1, bA=bA, fc=fc: e.activation(out=t1[:, 0:512], in_=bA[:, 0:512], func=AF.Identity, scale=cw[:, 1, fc:fc + 1], bias=cw[:, 3, fc:fc + 1]),
                             reads=[bks[0], "cw"], writes=[(tk, 0)])
                        P.op("scalar", lambda e, t1=t1, bB=bB, fc=fc: e.activation(out=t1[:, 512:1024], in_=bB[:, 0:512], func=AF.Identity, scale=cw[:, 1, fc:fc + 1], bias=cw[:, 3, fc:fc + 1]),
                             reads=[bks[1], "cw"], writes=[(tk, 1)])
                        P.op("vector", lambda e, t1=t1, U=U, fc=fc: e.scalar_tensor_tensor(out=t1[:], in0=U[:, 0:TC], scalar=cw[:, 0, fc:fc + 1], in1=t1[:], op0=ALU.mult, op1=ALU.add),
                             reads=[(uk, 0), (uk, 1), (uk, 2), (tk, 0), (tk, 1), "cw"], writes=[(tk, 0), (tk, 1)])
                        P.op("vector", lambda e, t1=t1, U=U, fc=fc: e.scalar_tensor_tensor(out=t1[:], in0=U[:, 2:TC + 2], scalar=cw[:, 2, fc:fc + 1], in1=t1[:], op0=ALU.mult, op1=ALU.add),
                             reads=[(uk, 0), (uk, 1), (uk, 2), (tk, 0), (tk, 1), "cw"], writes=[(tk, 0), (tk, 1)])
                    tg, tv = T1s[0][par], T1s[1][par]
                    kg, kv = ("T1", 0, par), ("T1", 1, par)
                    P.op("scalar", lambda e, tg=tg: e.activation(out=tg[:], in_=tg[:], func=AF.Silu), reads=[(kg, 0), (kg, 1)], writes=[(kg, 0), (kg, 1)])
                    P.op("vector", lambda e, f=f, tg=tg, tv=tv: e.tensor_tensor(out=a[:, f, :], in0=tg[:], in1=tv[:], op=ALU.mult),
                         reads=[(kg, 0), (kg, 1), (kv, 0), (kv, 1)], writes=[("a", f)])
            nxt = self.load_unit_bf(self.wd_bf, dn_units[0][0], [(dn_units[0][1] * 512, 512)], wd_keys)
            for di, (kc, n) in enumerate(dn_units):
                cur = nxt
                if di + 1 < len(dn_units):
                    nxt = self.load_unit_bf(self.wd_bf, dn_units[di + 1][0], [(dn_units[di + 1][1] * 512, 512)], wd_keys)
                slot, wkeys = cur
                wb = self.wb[slot]
                if kc[0] == 0:
                    pend = [self.x_load(8 * k + t, n) for t in range(2)]
                for t in range(8):
                    for ci, f in enumerate(kc):
                        P.op("tensor", lambda e, t=t, ci=ci, f=f, wb=wb: e.matmul(self.banks[t][:, 0:512], lhsT=a[:, f, t * 128:(t + 1) * 128], rhs=wb[:, ci, 0:512], start=(f == 0), stop=(f == 21)),
                             reads=[("a", f)] + wkeys, writes=[("bank", t)])
                if kc[-1] == 21:
                    for t in range(8):
                        tt = 8 * k + t
                        if t + 2 < 8:
                            pend.append(self.x_load(8 * k + t + 2, n))
                        xt, xk, xi = pend[t]
                        P.op("vector", lambda e, t=t, xt=xt: e.tensor_tensor(out=xt[:, 0:512], in0=self.banks[t][:, 0:512], in1=xt[:, 0:512], op=ALU.add),
                             reads=[("bank", t), xk], writes=[xk])
                        self.x_store(tt, n, xt, xk, xi)

    def finish(self):
        self.P.finalize()
        return self.nc


def rel_bucket_np(rel):
    nb = 16
    max_exact = 8
    n = np.abs(rel)
    nf = np.maximum(n, 1).astype(np.float32)
    large = max_exact + (np.log(nf / np.float32(max_exact)) / np.float32(math.log(1024 / max_exact)) * np.float32(nb - max_exact)).astype(np.int32)
    large = np.minimum(large, nb - 1)
    return np.where(rel > 0, nb, 0) + np.where(n < max_exact, n, large)


def bias_tables(rel_bias, kind):
    rb = np.asarray(rel_bias, np.float32)
    kp = np.arange(128)[:, None]
    qp = np.arange(128)[None, :]
    if kind == "A":
        out = np.empty((16, 128, 3 * 128), np.float32)
        for i, dl in enumerate((1, 0, -1)):
            rel = dl * 128 + kp - qp
            bk = rel_bucket_np(rel)
            ok = np.abs(rel) <= 128
            for h in range(16):
                out[h, :, i * 128:(i + 1) * 128] = np.where(ok, rb[bk, h], NEG)
        return out
    if kind == "B":
        out = np.empty((16, 128, 28 * 128), np.float32)
        for i in range(28):
            dl = 12 - i
            rel = dl * 128 + kp - qp
            bk = rel_bucket_np(rel)
            for m in range(16):
                out[m, :, i * 128:(i + 1) * 128] = rb[bk, m]
        return out
    ents = [(0, d_) for d_ in (1, 0, -1)] + [(1, d_) for d_ in (2, 1, 0, -1, -2)] + [(2, d_) for d_ in range(8, -9, -1)]
    dils = (1, 4, 16)
    out = np.empty((4, 128, 25 * 128), np.float32)
    for i, (g, dl) in enumerate(ents):
        rel = dl * 128 + kp - qp
        dil = dils[g]
        ok = (rel % dil == 0) & (np.abs(rel) <= 64 * dil)
        bk = rel_bucket_np(rel)
        for j in range(4):
            out[j, :, i * 128:(i + 1) * 128] = np.where(ok, rb[bk, g * 4 + j], NEG)
    return out


def gcols(g):
    return np.ascontiguousarray(np.asarray(g, np.float32).reshape(8, 128).T)


_PROG = None
DEBUG_LAYERS = 4


def get_prog():
    global _PROG
    if _PROG is None:
        b = Builder()
        for li in range(DEBUG_LAYERS):
            b.phase_qkv(li)
            b.phase_attn(li)
            b.phase_wo(li)
            b.phase_ffn(li)
        nc = b.finish()
        _PROG = (nc, list(b.din_names), list(b.dout_names))
    return _PROG


def kernel(x, rel_bias,
           l0_attn_norm, l0_w_qkv, l0_q_gain, l0_k_gain, l0_sink, l0_w_o,
           l0_ffn_norm, l0_w_up, l0_conv_w, l0_conv_b, l0_w_down,
           l1_attn_norm, l1_w_qkv, l1_q_gain, l1_k_gain, l1_lambda_q1, l1_lambda_k1,
           l1_lambda_q2, l1_lambda_k2, l1_sub_gain, l1_w_o,
           l1_ffn_norm, l1_w_up, l1_conv_w, l1_conv_b, l1_w_down,
           l2_attn_norm, l2_w_qkv, l2_q_gain, l2_k_gain, l2_w_o,
           l2_ffn_norm, l2_w_up, l2_conv_w, l2_conv_b, l2_w_down,
           l3_attn_norm, l3_w_qkv, l3_q_gain, l3_k_gain, l3_sink, l3_w_o,
           l3_ffn_norm, l3_w_up, l3_conv_w, l3_conv_b, l3_w_down):
    inp = {
        "x": x,
        "rel_bias": rel_bias,
        "l0_attn_norm": l0_attn_norm,
        "l0_w_qkv": l0_w_qkv,
        "l0_q_gain": l0_q_gain,
        "l0_k_gain": l0_k_gain,
        "l0_sink": l0_sink,
        "l0_w_o": l0_w_o,
        "l0_ffn_norm": l0_ffn_norm,
        "l0_w_up": l0_w_up,
        "l0_conv_w": l0_conv_w,
        "l0_conv_b": l0_conv_b,
        "l0_w_down": l0_w_down,
        "l1_attn_norm": l1_attn_norm,
        "l1_w_qkv": l1_w_qkv,
        "l1_q_gain": l1_q_gain,
        "l1_k_gain": l1_k_gain,
        "l1_lambda_q1": l1_lambda_q1,
        "l1_lambda_k1": l1_lambda_k1,
        "l1_lambda_q2": l1_lambda_q2,
        "l1_lambda_k2": l1_lambda_k2,
        "l1_sub_gain": l1_sub_gain,
        "l1_w_o": l1_w_o,
        "l1_ffn_norm": l1_ffn_norm,
        "l1_w_up": l1_w_up,
        "l1_conv_w": l1_conv_w,
        "l1_conv_b": l1_conv_b,
        "l1_w_down": l1_w_down,
        "l2_attn_norm": l2_attn_norm,
        "l2_w_qkv": l2_w_qkv,
        "l2_q_gain": l2_q_gain,
        "l2_k_gain": l2_k_gain,
        "l2_w_o": l2_w_o,
        "l2_ffn_norm": l2_ffn_norm,
        "l2_w_up": l2_w_up,
        "l2_conv_w": l2_conv_w,
        "l2_conv_b": l2_conv_b,
        "l2_w_down": l2_w_down,
        "l3_attn_norm": l3_attn_norm,
        "l3_w_qkv": l3_w_qkv,
        "l3_q_gain": l3_q_gain,
        "l3_k_gain": l3_k_gain,
        "l3_sink": l3_sink,
        "l3_w_o": l3_w_o,
        "l3_ffn_norm": l3_ffn_norm,
        "l3_w_up": l3_w_up,
        "l3_conv_w": l3_conv_w,
        "l3_conv_b": l3_conv_b,
        "l3_w_down": l3_w_down,
    }
    x = np.ascontiguousarray(np.asarray(x, np.float32))
    rel_bias = np.asarray(rel_bias, np.float32)
    shared = {"ident": np.eye(128, dtype=np.float32), "rel_bias": np.ascontiguousarray(rel_bias)}
    for kind in ("A", "B", "C"):
        shared["bt_" + kind] = bias_tables(rel_bias, kind)
    f32 = lambda a: np.ascontiguousarray(np.asarray(a, np.float32))
    for li in range(4):
        L = LAYERS[li]
        p = "l%d_" % li
        shared[p + "gcol_attn"] = gcols(inp[p + "attn_norm"])
        shared[p + "gcol_ffn"] = gcols(inp[p + "ffn_norm"])
        for w in ("w_qkv", "w_o", "w_up", "w_down"):
            shared[p + w] = f32(inp[p + w])
        shared[p + "qkg"] = np.ascontiguousarray(np.stack([f32(inp[p + "q_gain"]), f32(inp[p + "k_gain"])]))
        cwv = f32(inp[p + "conv_w"]).reshape(3, 44, 128)
        cbv = f32(inp[p + "conv_b"]).reshape(1, 44, 128)
        shared[p + "convp"] = np.ascontiguousarray(np.concatenate([cwv, cbv], 0).transpose(2, 0, 1))
        if L["kind"] == "A":
            shared[p + "sink"] = f32(inp[p + "sink"])
        if L["kind"] == "B":
            shared[p + "lam"] = np.ascontiguousarray(np.stack([f32(inp[p + k]) for k in ("lambda_q1", "lambda_k1", "lambda_q2", "lambda_k2")]))
            shared[p + "sgcol"] = f32(inp[p + "sub_gain"]).reshape(128, 1)
    import time as _t
    _t1 = _t.time()
    nc, dins, douts = get_prog()
    print("[kernel] host prep + build took %.1fs" % (_t.time() - _t1), flush=True)
    in_maps = []
    for c in range(NCORES):
        m = {}
        for n in dins:
            m[n] = x[c] if n == "x" else shared[n]
        in_maps.append(m)
    import time as _t
    _t0 = _t.time()
    res = run_bass_kernel_spmd(nc, in_maps, core_ids=list(range(NCORES)))
    print("[kernel] run_bass_kernel_spmd took %.1fs" % (_t.time() - _t0), flush=True)
    return np.stack([res.results[c]["x_out"] for c in range(NCORES)]).astype(np.float32)
```

```python
import math
from contextlib import ExitStack

import numpy as np
import ml_dtypes

import concourse.bass as bass
import concourse.mybir as mybir
from concourse.ap import AP
from concourse.bass_utils import run_bass_kernel_spmd

F32 = mybir.dt.float32
BF16 = mybir.dt.bfloat16
AF = mybir.ActivationFunctionType
ALU = mybir.AluOpType
AX = mybir.AxisListType
NPBF = ml_dtypes.bfloat16

NCORES = 4
T = 4096
NT = 32
D = 1024
DFF = 2816
EPS = 1e-6
NEG = -30000.0

LAYERS = [
    dict(kind="A", F=1536, nqb=8, nkb=2, FV=256, nfc=8),
    dict(kind="B", F=3072, nqb=8, nkb=8, FV=1024, nfc=8),
    dict(kind="C", F=2560, nqb=12, nkb=4, FV=512, nfc=4),
    dict(kind="A", F=1536, nqb=8, nkb=2, FV=256, nfc=8),
]
NI = {"A": 3, "B": 28, "C": 25}
NUNIT_BT = {"A": 16, "B": 16, "C": 4}


def lambda_init_fn(layer):
    return 0.8 - 0.6 * math.exp(-0.3 * layer)


class Op:
    __slots__ = ("eng", "fn", "deps", "needs_inc", "val", "semkey", "is_dma", "idx")

    def __init__(self, eng, fn, deps, semkey, is_dma):
        self.eng = eng
        self.fn = fn
        self.deps = deps
        self.needs_inc = False
        self.val = None
        self.semkey = semkey
        self.is_dma = is_dma


class Prog:
    ENGS = ("sync", "scalar", "vector", "gpsimd", "tensor")

    def __init__(self, nc):
        self.nc = nc
        self.ops = {e: [] for e in self.ENGS}
        self.lastw = {}
        self.readers = {}
        self.es = ExitStack()
        self.outs = []
        self.fence = []
        self.fence_pending = set()
        self.last_dma = {}
        self.epoch = 0

    def barrier(self):
        fence = []
        for e in self.ENGS:
            for o in reversed(self.ops[e]):
                if not o.is_dma:
                    fence.append(o)
                    break
        fence.extend(self.last_dma.values())
        self.fence = fence
        self.fence_pending = set(self.ENGS)
        self.epoch += 1

    def op(self, eng, fn, reads=(), writes=(), slot=None, out=False):
        deps = []
        if eng in self.fence_pending:
            deps.extend(self.fence)
            self.fence_pending.discard(eng)
        for b in reads:
            w = self.lastw.get(b)
            if w is not None:
                deps.append(w)
        for b in writes:
            w = self.lastw.get(b)
            if w is not None:
                deps.append(w)
            deps.extend(self.readers.get(b, ()))
        is_dma = slot is not None
        semkey = ("dma", slot) if is_dma else ("eng", eng, self.epoch % 3)
        o = Op(eng, fn, deps, semkey, is_dma)
        o.idx = len(self.ops[eng])
        self.ops[eng].append(o)
        if is_dma:
            self.last_dma[slot] = o
        for b in writes:
            self.lastw[b] = o
            self.readers[b] = []
        for b in reads:
            self.readers.setdefault(b, []).append(o)
        if out:
            self.outs.append(o)
        return o

    @staticmethod
    def _skip(d, o):
        return d is o or (d.eng == "tensor" and o.eng == "tensor" and not d.is_dma and not o.is_dma)

    def finalize(self):
        nc = self.nc
        final_waits = self.outs
        for e in self.ENGS:
            for o in self.ops[e]:
                best = {}
                for d in o.deps:
                    if self._skip(d, o):
                        continue
                    if d.is_dma:
                        d.needs_inc = True
                        continue
                    b = best.get(d.semkey)
                    if b is None or d.idx > b.idx:
                        best[d.semkey] = d
                for d in best.values():
                    d.needs_inc = True
        for d in final_waits:
            d.needs_inc = True
        for e in self.ENGS:
            for o in self.ops[e]:
                if o.is_dma:
                    o.needs_inc = True
        counters = {}
        for e in self.ENGS:
            for o in self.ops[e]:
                if o.needs_inc:
                    c = counters.get(o.semkey, 0) + (16 if o.is_dma else 1)
                    counters[o.semkey] = c
                    o.val = c
        sems = {}
        for i, k in enumerate(counters):
            sems[k] = self.es.enter_context(nc.semaphore("s%d" % i))
        self.nsem = len(sems)
        block = self.es.enter_context(nc.Block())

        def run(e, engine):
            known = {}
            for o in self.ops[e]:
                need = {}
                for d in o.deps:
                    if self._skip(d, o) or d.val is None:
                        continue
                    if need.get(d.semkey, 0) < d.val:
                        need[d.semkey] = d.val
                for k, v in need.items():
                    if known.get(k, 0) >= v:
                        continue
                    engine.wait_ge(sems[k], v)
                    known[k] = v
                ins = o.fn(engine)
                if o.needs_inc:
                    ins.then_inc(sems[o.semkey], 16 if o.is_dma else 1)
            if e == "sync":
                need = {}
                for d in final_waits:
                    if need.get(d.semkey, 0) < d.val:
                        need[d.semkey] = d.val
                for k, v in need.items():
                    engine.wait_ge(sems[k], v)

        @block.sync
        def _(eng):
            run("sync", eng)

        @block.scalar
        def _(eng):
            run("scalar", eng)

        @block.vector
        def _(eng):
            run("vector", eng)

        @block.gpsimd
        def _(eng):
            run("gpsimd", eng)

        @block.tensor
        def _(eng):
            run("tensor", eng)

        self.es.close()


SB_BASE = 16512
SCR_END = 229376


class Builder:
    def __init__(self):
        self.nc = bass.Bass("TRN2", target_bir_lowering=False)
        self.P = Prog(self.nc)
        self.din_names = []
        self.dout_names = []
        self.d = {}
        self.uid = 0
        self.perm_off = SB_BASE
        nc = self.nc
        self.actT = self.perm("actT", [128, 8, T + 2], BF16)
        self.ws = [self.perm("ws%d" % i, [128, 4, 512], F32) for i in range(2)]
        self.wb = [self.perm("wb%d" % i, [128, 8, 512], BF16) for i in range(2)]
        self.xts = [self.perm("xt%d" % i, [128, D], F32) for i in range(4)]
        self.hbs = [self.perm("hb%d" % i, [128, D], BF16) for i in range(2)]
        self.ident = self.perm("ident", [128, 128], BF16)
        self.identf = self.perm("identf", [128, 128], F32)
        self.sst = [self.perm("sst%d" % i, [128, 4], F32) for i in range(4)]
        self.epsb = self.perm("epsb", [128, 1], F32)
        self.gcol = self.perm("gcol", [128, 8], F32)
        self.PERM_END = (self.perm_off + 63) // 64 * 64
        self.scr_off = self.PERM_END
        self.nscr = 0
        self.xcnt = 0
        self.banks = [nc.alloc_psum_tensor("bank%d" % i, [128, 512], F32) for i in range(8)]
        self.wcount = 0
        self.xd = self.dout("x_out", [T, D], F32).ap()
        self.qT_d = nc.dram_tensor("qT_scr", [12, 128, T], BF16, kind="Internal").ap()
        self.kT_d = nc.dram_tensor("kT_scr", [8, 128, T], BF16, kind="Internal").ap()
        self.v_d = nc.dram_tensor("v_scr", [T, 1024], BF16, kind="Internal").ap()
        self.wu_bf = nc.dram_tensor("wu_bf", [D, 2 * DFF], BF16, kind="Internal").ap()
        self.wd_bf = nc.dram_tensor("wd_bf", [DFF, D], BF16, kind="Internal").ap()
        self.init_consts()

    def perm(self, name, shape, dt):
        n = int(np.prod(shape[1:])) * (4 if dt == F32 else 2)
        n = (n + 31) // 32 * 32
        t = self.nc.alloc_sbuf_tensor_at(name, shape, dt, offset=self.perm_off)
        self.perm_off += n
        return t

    def scr_reset(self):
        if self.nscr > 0:
            self.P.barrier()
        self.nscr += 1
        self.scr_off = self.PERM_END

    def scr(self, name, shape, dt):
        n = int(np.prod(shape[1:])) * (4 if dt == F32 else 2)
        n = (n + 31) // 32 * 32
        self.uid += 1
        t = self.nc.alloc_sbuf_tensor_at("%s_%d" % (name, self.uid), shape, dt, offset=self.scr_off)
        self.scr_off += n
        assert self.scr_off <= SCR_END, (name, self.scr_off)
        return t

    def din(self, name, shape, dt):
        if name not in self.d:
            self.d[name] = self.nc.dram_tensor(name, list(shape), dt, kind="ExternalInput")
            self.din_names.append(name)
        return self.d[name]

    def dout(self, name, shape, dt):
        if name not in self.d:
            self.d[name] = self.nc.dram_tensor(name, list(shape), dt, kind="ExternalOutput")
            self.dout_names.append(name)
        return self.d[name]

    def bank_bf(self, i):
        return self.banks[i][:].bitcast(BF16).rearrange("p (c t) -> p c t", t=128)

    def init_consts(self):
        P = self.P
        idd = self.din("ident", [128, 128], F32).ap()
        xin = self.din("x", [T, D], F32).ap()
        P.op("sync", lambda e: e.dma_start(out=self.identf[:], in_=idd), writes=["identf"], slot="c_id")
        P.op("vector", lambda e: e.tensor_copy(out=self.ident[:], in_=self.identf[:]), reads=["identf"], writes=["ident"])
        P.op("vector", lambda e: e.memset(self.epsb[:], EPS), writes=["eps"])
        P.op("vector", lambda e: e.memset(self.actT[:, :, 0:1], 0.0), writes=["halo"])
        P.op("vector", lambda e: e.memset(self.actT[:, :, T + 1:T + 2], 0.0), writes=["halo"])
        for t0 in range(0, NT, 4):
            P.op("sync", lambda e, t0=t0: e.dma_start(out=self.xd[t0 * 128:(t0 + 4) * 128, :], in_=xin[t0 * 128:(t0 + 4) * 128, :]),
                 writes=[("xd", t, n) for t in range(t0, t0 + 4) for n in range(2)], slot="xcp%d" % (t0 // 4), out=True)

    def x_load(self, t, n=None):
        P = self.P
        i = self.xcnt % 4
        self.xcnt += 1
        xt = self.xts[i]
        key = ("xt", i)
        if n is None:
            P.op("sync", lambda e, t=t, xt=xt: e.dma_start(out=xt[:], in_=self.xd[t * 128:(t + 1) * 128, :]),
                 reads=[("xd", t, 0), ("xd", t, 1)], writes=[key], slot="xl%d" % i)
        else:
            P.op("sync", lambda e, t=t, xt=xt, n=n: e.dma_start(out=xt[:, 0:512], in_=self.xd[t * 128:(t + 1) * 128, n * 512:(n + 1) * 512]),
                 reads=[("xd", t, n)], writes=[key], slot="xl%d" % i)
        return xt, key, i

    def x_store(self, t, n, xt, key, i):
        P = self.P
        P.op("gpsimd", lambda e, t=t, xt=xt, n=n: e.dma_start(out=self.xd[t * 128:(t + 1) * 128, n * 512:(n + 1) * 512], in_=xt[:, 0:512]),
             reads=[key], writes=[("xd", t, n)], slot="xs%d" % i, out=True)

    def load_unit(self, W, kchunks, segs, scale=None, scale_reads=(), defer_act=False):
        P = self.P
        i = self.wcount
        self.wcount += 1
        slot = i % 2
        wb = self.wb[slot]
        key = ("wb", slot)
        Wa = W.ap()
        ncol = sum(s[1] for s in segs)
        halves = [kchunks[0:4], kchunks[4:8]]
        allkeys = []
        deferred = []
        for hi, kc in enumerate(halves):
            if not kc:
                continue
            ws = self.ws[hi]
            co = 0
            for si, (c0, cn) in enumerate(segs):
                k0 = kc[0]
                src = Wa[k0 * 128:(k0 + len(kc)) * 128, c0:c0 + cn].rearrange("(c p) f -> p c f", p=128)
                P.op("sync", lambda e, ws=ws, src=src, co=co, cn=cn, n=len(kc): e.dma_start(out=ws[:, 0:n, co:co + cn], in_=src),
                     writes=[("ws", hi, si)], slot="ws%d_%d" % (hi, si))
                co += cn
            n = len(kc)
            if defer_act:
                for j, k in enumerate(kc):
                    sc = scale(k)
                    deferred.append((ws, hi, j, sc, [("ws", hi, si) for si in range(len(segs))] + list(scale_reads), (key, hi, j)))
                    allkeys.append((key, hi, j))
                continue
            if scale is None:
                P.op("gpsimd", lambda e, ws=ws, wb=wb, hi=hi, n=n, ncol=ncol: e.tensor_copy(out=wb[:, hi * 4:hi * 4 + n, 0:ncol], in_=ws[:, 0:n, 0:ncol]),
                     reads=[("ws", hi, si) for si in range(len(segs))], writes=[(key, hi, 0)])
                allkeys.append((key, hi, 0))
            else:
                for j, k in enumerate(kc):
                    sc = scale(k)
                    P.op("gpsimd", lambda e, ws=ws, wb=wb, hi=hi, j=j, ncol=ncol, sc=sc: e.tensor_scalar(out=wb[:, hi * 4 + j, 0:ncol], in0=ws[:, j, 0:ncol], scalar1=sc, scalar2=None, op0=ALU.mult),
                         reads=[("ws", hi, si) for si in range(len(segs))] + list(scale_reads), writes=[(key, hi, j)])
                    allkeys.append((key, hi, j))
        if defer_act:
            def cast_fn():
                for (ws, hi, j, sc, rds, wk) in deferred:
                    P.op("scalar", lambda e, ws=ws, hi=hi, j=j, sc=sc: e.activation(out=wb[:, hi * 4 + j, 0:ncol], in_=ws[:, j, 0:ncol], func=AF.Copy, scale=sc),
                         reads=rds, writes=[wk])
            return slot, allkeys, cast_fn
        return slot, allkeys

    def precast(self, W, nchunks, ncols, Wbf, keyname, scaled):
        P = self.P
        Wa = W.ap()
        blocks = [(kg, min(4, nchunks - kg), c0) for kg in range(0, nchunks, 4) for c0 in range(0, ncols, 512)]
        keys = []
        for b, (kg, n, c0) in enumerate(blocks):
            ws = self.ws[b % 2]
            wsk = ("ws", b % 2, 0)
            ss_ = b % 4
            stg = self.wb[ss_ // 2][:, (ss_ % 2) * 4:(ss_ % 2) * 4 + n, :]
            stgk = (("wb", ss_ // 2), ss_ % 2, 0)
            src = Wa[kg * 128:(kg + n) * 128, c0:c0 + 512].rearrange("(c p) f -> p c f", p=128)
            P.op("sync", lambda e, ws=ws, src=src, n=n: e.dma_start(out=ws[:, 0:n, :], in_=src), writes=[wsk], slot="ws%d_0" % (b % 2))
            eng = "scalar" if b % 2 == 0 else "vector"
            if not scaled:
                if eng == "scalar":
                    P.op(eng, lambda e, ws=ws, stg=stg, n=n: e.activation(out=stg, in_=ws[:, 0:n, :], func=AF.Copy), reads=[wsk], writes=[stgk])
                else:
                    P.op(eng, lambda e, ws=ws, stg=stg, n=n: e.tensor_copy(out=stg, in_=ws[:, 0:n, :]), reads=[wsk], writes=[stgk])
            else:
                for j in range(n):
                    sc = self.gcol[:, kg + j:kg + j + 1]
                    if eng == "scalar":
                        P.op(eng, lambda e, ws=ws, stg=stg, j=j, sc=sc: e.activation(out=stg[:, j, :], in_=ws[:, j, :], func=AF.Copy, scale=sc), reads=[wsk, "gcol"], writes=[(stgk, j)] if False else [stgk])
                    else:
                        P.op(eng, lambda e, ws=ws, stg=stg, j=j, sc=sc: e.tensor_scalar(out=stg[:, j, :], in0=ws[:, j, :], scalar1=sc, scalar2=None, op0=ALU.mult), reads=[wsk, "gcol"], writes=[stgk])
            dst = Wbf[kg * 128:(kg + n) * 128, c0:c0 + 512].rearrange("(c p) f -> p c f", p=128)
            P.op("gpsimd", lambda e, stg=stg, dst=dst: e.dma_start(out=dst, in_=stg), reads=[stgk], writes=[(keyname, b)], slot="pc%d" % ss_)
            keys.append((keyname, b))
        return keys

    def load_unit_bf(self, Wbf, kchunks, segs, rkeys):
        P = self.P
        i = self.wcount
        self.wcount += 1
        slot = i % 2
        wb = self.wb[slot]
        keys = [(("wb", slot), 0, 0), (("wb", slot), 1, 0)]
        k0, n, co = kchunks[0], len(kchunks), 0
        for si, (c0, cn) in enumerate(segs):
            src = Wbf[k0 * 128:(k0 + n) * 128, c0:c0 + cn].rearrange("(c p) f -> p c f", p=128)
            P.op("sync", lambda e, wb=wb, src=src, n=n, co=co, cn=cn: e.dma_start(out=wb[:, 0:n, co:co + cn], in_=src),
                 reads=rkeys, writes=keys, slot="wbd%d_%d" % (slot, si))
            co += cn
        return slot, keys

    def bg_start(self, li, n_items, cast_eng):
        P = self.P
        pfx = "l%d_" % li
        Wu = self.din(pfx + "w_up", [D, 2 * DFF], F32)
        Wd = self.din(pfx + "w_down", [DFF, D], F32)
        self.load_gcol(pfx + "gcol_ffn")
        blocks = []
        for (W, nch, ncols, Wbf, kn, scaled) in ((Wu, 8, 2 * DFF, self.wu_bf, "wubf", True), (Wd, 22, D, self.wd_bf, "wdbf", False)):
            Wa = W.ap()
            bi = 0
            for kg in range(0, nch, 4):
                for c0 in range(0, ncols, 512):
                    blocks.append((Wa, Wbf, kn, bi, kg, min(4, nch - kg), c0, scaled))
                    bi += 1
        self.bg_keys = {"wubf": [], "wdbf": []}

        def stage_in(b):
            Wa, Wbf, kn, bi, kg, n, c0, scaled = blocks[b]
            ws = self.ws[b % 2]
            src = Wa[kg * 128:(kg + n) * 128, c0:c0 + 512].rearrange("(c p) f -> p c f", p=128)
            P.op("sync", lambda e, ws=ws, src=src, n=n: e.dma_start(out=ws[:, 0:n, :], in_=src), writes=[("ws", b % 2, 0)], slot="ws%d_0" % (b % 2))

        def stage_cast(b):
            Wa, Wbf, kn, bi, kg, n, c0, scaled = blocks[b]
            ws = self.ws[b % 2]
            wsk = ("ws", b % 2, 0)
            ss_ = b % 4
            stg = self.wb[ss_ // 2][:, (ss_ % 2) * 4:(ss_ % 2) * 4 + n, :]
            stgk = (("wb", ss_ // 2), ss_ % 2, 0)
            if not scaled:
                if cast_eng == "scalar":
                    P.op("scalar", lambda e, ws=ws, stg=stg, n=n: e.activation(out=stg, in_=ws[:, 0:n, :], func=AF.Copy), reads=[wsk], writes=[stgk])
                else:
                    P.op("vector", lambda e, ws=ws, stg=stg, n=n: e.tensor_copy(out=stg, in_=ws[:, 0:n, :]), reads=[wsk], writes=[stgk])
            else:
                for j in range(n):
                    sc = self.gcol[:, kg + j:kg + j + 1]
                    if cast_eng == "scalar":
                        P.op("scalar", lambda e, ws=ws, stg=stg, j=j, sc=sc: e.activation(out=stg[:, j, :], in_=ws[:, j, :], func=AF.Copy, scale=sc), reads=[wsk, "gcol"], writes=[stgk])
                    else:
                        P.op("vector", lambda e, ws=ws, stg=stg, j=j, sc=sc: e.tensor_scalar(out=stg[:, j, :], in0=ws[:, j, :], scalar1=sc, scalar2=None, op0=ALU.mult), reads=[wsk, "gcol"], writes=[stgk])
            dst = Wbf[kg * 128:(kg + n) * 128, c0:c0 + 512].rearrange("(c p) f -> p c f", p=128)
            P.op("gpsimd", lambda e, stg=stg, dst=dst: e.dma_start(out=dst, in_=stg), reads=[stgk], writes=[(kn, bi)], slot="pc%d" % ss_)
            self.bg_keys[kn].append((kn, bi))

        nb = len(blocks)
        q = []
        for k in range(nb + 1):
            def step(k=k):
                if k < nb:
                    stage_in(k)
                if k >= 1:
                    stage_cast(k - 1)
            q.append(step)
        self.bg_queue = q
        self.bg_every = max(1, n_items // (len(q) + 2))
        self.bg_count = 0
        self.bg_layer = li

    def bg_tick(self):
        q = getattr(self, "bg_queue", None)
        if not q:
            return
        self.bg_count += 1
        if self.bg_count % self.bg_every == 0:
            q.pop(0)()

    def bg_flush(self):
        q = getattr(self, "bg_queue", None)
        while q:
            q.pop(0)()

    def phase_norm(self):
        P = self.P
        junk = self.banks[7]
        hbs = self.hbs
        pend = {}

        def stA(t):
            xt, xk, _ = pend[t]
            st = self.sst[t % 4]
            sk = ("sst", t % 4)
            P.op("vector", lambda e, st=st: e.memset(st[:], 0.0), writes=[sk])
            P.op("scalar", lambda e, xt=xt, st=st: e.activation(out=junk[:, 0:512], in_=xt[:, 0:512], func=AF.Square, accum_out=st[:, 0:1]),
                 reads=[xk, sk], writes=[sk, ("bank", 7)])
            P.op("scalar", lambda e, xt=xt, st=st: e.activation(out=junk[:, 0:512], in_=xt[:, 512:1024], func=AF.Square, accum_out=st[:, 1:2]),
                 reads=[xk, sk], writes=[sk, ("bank", 7)])
            P.op("vector", lambda e, st=st: e.tensor_tensor(out=st[:, 2:3], in0=st[:, 0:1], in1=st[:, 1:2], op=ALU.add), reads=[sk], writes=[sk])

        def stB(t):
            xt, xk, _ = pend[t]
            st = self.sst[t % 4]
            sk = ("sst", t % 4)
            P.op("scalar", lambda e, st=st: e.activation(out=st[:, 2:3], in_=st[:, 2:3], func=AF.Ln, scale=1.0 / D, bias=self.epsb[:, 0:1]),
                 reads=[sk, "eps"], writes=[sk])
            P.op("scalar", lambda e, st=st: e.activation(out=st[:, 3:4], in_=st[:, 2:3], func=AF.Exp, scale=-0.5), reads=[sk], writes=[sk])
            hb = hbs[t % 2]
            hk = ("hb", t % 2)
            pk = ("bank", t % 2)
            pT = self.bank_bf(t % 2)
            P.op("vector", lambda e, xt=xt, hb=hb, st=st: e.tensor_scalar(out=hb[:], in0=xt[:], scalar1=st[:, 3:4], scalar2=None, op0=ALU.mult),
                 reads=[xk, sk], writes=[hk])
            for c in range(8):
                P.op("tensor", lambda e, c=c, hb=hb, pT=pT: e.transpose(out=pT[:, c, :], in_=hb[:, c * 128:(c + 1) * 128], identity=self.ident[:]),
                     reads=[hk, "ident"], writes=[pk])

        def stC(t):
            pk = ("bank", t % 2)
            pT = self.bank_bf(t % 2)
            P.op("scalar", lambda e, t=t, pT=pT: e.activation(out=self.actT[:, :, 1 + t * 128:1 + (t + 1) * 128], in_=pT, func=AF.Copy),
                 reads=[pk], writes=[("actT", c, t) for c in range(8)])

        pend[0] = self.x_load(0)
        pend[1] = self.x_load(1)
        for i in range(NT + 2):
            if i + 2 < NT:
                pend[i + 2] = self.x_load(i + 2)
            if i - 2 >= 0:
                stC(i - 2)
            if 0 <= i - 1 < NT:
                stB(i - 1)
            if i < NT:
                stA(i)

    def load_gcol(self, name):
        P = self.P
        g = self.din(name, [128, 8], F32).ap()
        P.op("sync", lambda e: e.dma_start(out=self.gcol[:], in_=g), writes=["gcol"], slot="gcol")

    def phase_qkv(self, li):
        P = self.P
        L = LAYERS[li]
        kind = L["kind"]
        pfx = "l%d_" % li
        self.load_gcol(pfx + "gcol_attn")
        self.phase_norm()
        self.scr_reset()
        W = self.din(pfx + "w_qkv", [D, L["F"]], F32)
        dh = 128 if kind == "C" else 64
        geff_d = self.din(pfx + "qkg", [2, dh], F32).ap()
        qT_d, kT_d, v_d = self.qT_d, self.kT_d, self.v_d
        gq = self.scr("gq", [128, dh], F32)
        gk = self.scr("gk", [128, dh], F32)
        P.op("sync", lambda e: e.dma_start(out=gq[:], in_=geff_d[0].partition_broadcast(128)), writes=["gq"], slot="gq")
        P.op("sync", lambda e: e.dma_start(out=gk[:], in_=geff_d[1].partition_broadcast(128)), writes=["gk"], slot="gk")
        P.op("vector", lambda e: e.scalar_tensor_tensor(out=gk[:], in0=gq[:], scalar=float(dh) ** -0.5, in1=gk[:], op0=ALU.mult, op1=ALU.mult),
             reads=["gq", "gk"], writes=["gk"])
        stages = [self.scr("stage", [128, 4, T], BF16) for _ in range(2)]
        ND = 4
        sqs = [self.scr("sq", [128, 512], F32) for _ in range(ND)]
        kfs = [self.scr("kf", [128, 512], F32) for _ in range(ND)]
        qns = [self.scr("qn", [128, 512], BF16) for _ in range(ND)]
        vsts = [self.scr("vst", [128, 512], BF16) for _ in range(ND)]
        ssq = [self.scr("ssq", [128, 8], F32) for _ in range(ND)]
        PYB = (2, 3, 4, 5)
        PTB = (0, 1, 6, 7)
        if kind == "A":
            chunks = [[("q", 0, 512, 0)], [("q", 0, 512, 4)], [("k", 0, 256, 0), ("v", 256, 256, 0)]]
        elif kind == "B":
            chunks = [[("q", 0, 512, 0)], [("q", 0, 512, 4)], [("k", 0, 512, 0)], [("k", 0, 512, 4)], [("v", 0, 512, 0)], [("v", 0, 512, 512)]]
        else:
            chunks = [[("q", 0, 512, 0)], [("q", 0, 512, 4)], [("q", 0, 512, 8)], [("k", 0, 512, 0)], [("v", 0, 512, 0)]]
        gsc = lambda k: self.gcol[:, k:k + 1]
        units = [None] * len(chunks)
        u0 = self.load_unit(W, list(range(8)), [(0, 512)], scale=gsc, scale_reads=["gcol"], defer_act=True)
        u0[2]()
        units[0] = u0[:2]
        it = 0
        pend_rest = []

        def pipe_step(keep):
            n = len(pend_rest)
            for idx in range(n):
                ent = pend_rest[idx]
                lag = n - 1 - idx
                want = 3 if keep == 0 else min(3, lag)
                if keep == 0:
                    want = min(3, ent[1] + 1)
                while ent[1] < want:
                    ent[0](ent[1])
                    ent[1] += 1
            while pend_rest and pend_rest[0][1] >= 3:
                pend_rest.pop(0)

        for ci, segs in enumerate(chunks):
            next_cast = None
            if ci + 1 < len(chunks):
                un = self.load_unit(W, list(range(8)), [((ci + 1) * 512, 512)], scale=gsc, scale_reads=["gcol"], defer_act=True)
                units[ci + 1] = un[:2]
                next_cast = un[2]
            slot, wkeys = units[ci]
            wb = self.wb[slot]
            stage = stages[ci % 2]
            stk = ("stage", ci % 2)
            for t in range(NT):
                if t == 8 and next_cast is not None:
                    next_cast()
                bi = PYB[it % ND]
                py = self.banks[bi]
                pyk = ("bank", bi)
                for c in range(8):
                    P.op("tensor", lambda e, c=c, t=t, py=py, wb=wb: e.matmul(py[:, 0:512], lhsT=self.actT[:, c, 1 + t * 128:1 + (t + 1) * 128], rhs=wb[:, c, 0:512], start=(c == 0), stop=(c == 7)),
                         reads=[("actT", c, t)] + wkeys, writes=[pyk])
                def rest(stg_i, segs=segs, it=it, t=t, py=py, pyk=pyk, stage=stage, stk=stk):
                    for (ty, off, w, dst) in segs:
                        if ty == "v":
                            if stg_i != 0:
                                continue
                            vst = vsts[it % ND]
                            vk = ("vst", it % ND)
                            P.op("scalar", lambda e, py=py, vst=vst, off=off, w=w: e.activation(out=vst[:, 0:w], in_=py[:, off:off + w], func=AF.Copy),
                                 reads=[pyk], writes=[vk])
                            P.op("gpsimd", lambda e, vst=vst, t=t, dst=dst, w=w: e.dma_start(out=v_d[t * 128:(t + 1) * 128, dst:dst + w], in_=vst[:, 0:w]),
                                 reads=[vk], writes=[("vd", t, dst)], slot="vst%d" % (it % ND))
                            continue
                        nh = w // dh
                        sq = sqs[it % ND]
                        sqk = ("sq", it % ND)
                        s_ = ssq[it % ND]
                        sk = ("ssq", it % ND)
                        qn = qns[it % ND]
                        qk = ("qn", it % ND)
                        tb = PTB[it % ND]
                        pT = self.bank_bf(tb)
                        pk = ("bank", tb)
                        nb = w // 128
                        if stg_i == 0:
                            P.op("scalar", lambda e, py=py, sq=sq, off=off, w=w: e.activation(out=sq[:, 0:w], in_=py[:, off:off + w], func=AF.Square),
                                 reads=[pyk], writes=[sqk])
                            P.op("vector", lambda e, sq=sq, s_=s_, w=w, nh=nh: e.tensor_reduce(out=s_[:, 0:nh], in_=sq[:, 0:w].rearrange("p (h d) -> p h d", d=dh), axis=AX.X, op=ALU.add),
                                 reads=[sqk], writes=[sk])
                        elif stg_i == 1:
                            P.op("scalar", lambda e, s_=s_, nh=nh: e.activation(out=s_[:, 0:nh], in_=s_[:, 0:nh], func=AF.Ln, scale=1.0 / dh, bias=self.epsb[:, 0:1]),
                                 reads=[sk, "eps"], writes=[sk])
                            P.op("scalar", lambda e, s_=s_, nh=nh: e.activation(out=s_[:, 0:nh], in_=s_[:, 0:nh], func=AF.Exp, scale=-0.5), reads=[sk], writes=[sk])
                            rb = AP(s_, 0, [[8, 128], [1, nh], [0, dh]])
                            if ty == "q":
                                P.op("vector", lambda e, py=py, qn=qn, off=off, w=w, rb=rb: e.tensor_tensor(out=qn[:, 0:w].rearrange("p (h d) -> p h d", d=dh), in0=py[:, off:off + w].rearrange("p (h d) -> p h d", d=dh), in1=rb, op=ALU.mult),
                                     reads=[pyk, sk], writes=[qk])
                            else:
                                kf = kfs[it % ND]
                                kfk = ("kf", it % ND)
                                gb = AP(gk, 0, [[dh, 128], [0, nh], [1, dh]])
                                P.op("vector", lambda e, py=py, kf=kf, off=off, w=w, rb=rb: e.tensor_tensor(out=kf[:, 0:w].rearrange("p (h d) -> p h d", d=dh), in0=py[:, off:off + w].rearrange("p (h d) -> p h d", d=dh), in1=rb, op=ALU.mult),
                                     reads=[pyk, sk], writes=[kfk])
                                P.op("gpsimd", lambda e, kf=kf, qn=qn, w=w, gb=gb: e.tensor_tensor(out=qn[:, 0:w].rearrange("p (h d) -> p h d", d=dh), in0=kf[:, 0:w].rearrange("p (h d) -> p h d", d=dh), in1=gb, op=ALU.mult),
                                     reads=[kfk, "gk"], writes=[qk])
                            for j in range(nb):
                                P.op("tensor", lambda e, j=j, qn=qn, pT=pT: e.transpose(out=pT[:, j, :], in_=qn[:, j * 128:(j + 1) * 128], identity=self.ident[:]),
                                     reads=[qk, "ident"], writes=[pk])
                        else:
                            P.op("scalar", lambda e, pT=pT, stage=stage, nb=nb, t=t: e.activation(out=stage[:, 0:nb, t * 128:(t + 1) * 128], in_=pT[:, 0:nb, :], func=AF.Copy),
                                 reads=[pk], writes=[(stk, t)])
                pend_rest.append([rest, 0])
                pipe_step(3)
                it += 1
            while pend_rest:
                pipe_step(0)
            for (ty, off, w, dst) in segs:
                if ty == "v":
                    continue
                dd = qT_d if ty == "q" else kT_d
                dk = "qTd" if ty == "q" else "kTd"
                for j in range(w // 128):
                    P.op("gpsimd", lambda e, dd=dd, j=j, dst=dst, stage=stage: e.dma_start(out=dd[dst + j], in_=stage[:, j, :]),
                         reads=[(stk, t) for t in range(NT)], writes=[(dk, dst + j)], slot="stg%d_%d" % (ci % 2, j))

    ST_BANKS = (0, 1, 6)
    A_DEPTH = 2

    def attn_item(self, it, mms, tb_ap, ncols, pvs, tbkey, extra_reads, post=None, back_fn=None, const_bias=None):
        P = self.P
        self.bg_tick()
        nbuf = len(self.ST_BANKS)
        bi = self.ST_BANKS[it % nbuf]
        st = self.banks[bi]
        stk = ("bank", bi)
        sc = self.a_sc[it % nbuf]
        sck = ("sc", it % nbuf)
        pt = self.a_pt[it % nbuf]
        ptk = ("pt", it % nbuf)
        for (lh, rh, c0, n) in mms:
            P.op("tensor", lambda e, lh=lh, rh=rh, c0=c0, n=n, st=st: e.matmul(st[:, c0:c0 + n], lhsT=lh, rhs=rh, start=True, stop=True),
                 reads=extra_reads, writes=[stk])
        if const_bias is not None:
            P.op("scalar", lambda e, st=st, pt=pt, ncols=ncols: e.activation(out=pt[:, 0:ncols], in_=st[:, 0:ncols], func=AF.Exp, bias=const_bias),
                 reads=[stk, "satb"], writes=[ptk])
        else:
            P.op("vector", lambda e, st=st, sc=sc, tb_ap=tb_ap, ncols=ncols: e.tensor_tensor(out=sc[:, 0:ncols], in0=st[:, 0:ncols], in1=tb_ap, op=ALU.add),
                 reads=[stk, tbkey], writes=[sck])
            P.op("scalar", lambda e, sc=sc, pt=pt, ncols=ncols: e.activation(out=pt[:, 0:ncols], in_=sc[:, 0:ncols], func=AF.Exp),
                 reads=[sck], writes=[ptk])
        q = self.a_queue
        q.append((pt, ptk, pvs, list(extra_reads), post, back_fn))
        while len(q) > self.A_DEPTH:
            self._attn_back(q.pop(0))

    def _attn_back(self, pend):
        P = self.P
        pt, ptk, pvs, extra_reads, post, back_fn = pend
        if back_fn is not None:
            back_fn(pt, ptk)
            pvs = []
        for (c0, v_ap, ab, wdt, s0, s1) in pvs:
            acc = self.banks[ab]
            P.op("tensor", lambda e, c0=c0, v_ap=v_ap, acc=acc, wdt=wdt, s0=s0, s1=s1, pt=pt: e.matmul(acc[:, 0:wdt], lhsT=pt[:, c0:c0 + 128], rhs=v_ap, start=s0, stop=s1),
                 reads=[ptk] + extra_reads, writes=[("bank", ab)])
        if post is not None:
            post()

    def attn_flush(self):
        q = self.a_queue
        while q:
            self._attn_back(q.pop(0))

    def out_transpose(self, on, onk, chunk, qt, trk):
        P = self.P
        bi = 7
        pT = self.bank_bf(bi)
        P.op("tensor", lambda e, on=on, pT=pT: e.transpose(out=pT[:, 0, :], in_=on, identity=self.ident[:]),
             reads=[onk, "ident"], writes=[("bank", bi)])
        P.op("scalar", lambda e, pT=pT, chunk=chunk, qt=qt: e.activation(out=self.actT[:, chunk, 1 + qt * 128:1 + (qt + 1) * 128], in_=pT[:, 0, :], func=AF.Copy),
             reads=[("bank", bi)], writes=[("actT", chunk, qt)])

    def phase_attn(self, li):
        P = self.P
        L = LAYERS[li]
        kind = L["kind"]
        pfx = "l%d_" % li
        self.scr_reset()
        nkb, nqb, FV = L["nkb"], L["nqb"], L["FV"]
        qT_d, kT_d, v_d = self.qT_d, self.kT_d, self.v_d
        ni = NI[kind]
        bt_d = self.din("bt_" + kind, [NUNIT_BT[kind], 128, ni * 128], F32).ap()
        nb_ = 4 if kind == "B" else 3
        self.ST_BANKS = (0, 1, 6, 7) if kind == "B" else (0, 1, 6)
        self.A_DEPTH = 3 if kind == "B" else 2
        self.a_sc = [self.scr("sc", [128, 512], F32) for _ in range(nb_)]
        self.a_pt = [self.scr("pt", [128, 512], BF16) for _ in range(nb_)]
        self.a_queue = []
        rr = [self.scr("rr", [128, 4], F32) for _ in range(4)]
        ons = [self.scr("on", [128, 128], BF16) for _ in range(2)]
        v_r = v_d.rearrange("(kb p) f -> p kb f", p=128)
        self.bg_start(li, {"A": 16 * NT, "B": 16 * (NT // 4) * NT, "C": 4 * NT * 7}[kind], "vector" if kind == "B" else "scalar")
        qkeys = [("qTd", b) for b in range(nqb)]
        kkeys = [("kTd", b) for b in range(nkb)]
        vkeys = [("vd", t, c0) for t in range(NT) for c0 in range(0, FV, 512 if FV >= 512 else 256)]
        it = 0
        fin = 0
        if kind == "B":
            dv = 128
            KTs = [self.scr("KT", [128, T], BF16) for _ in range(2)]
            Vs = [self.scr("V", [128, NT, dv], BF16) for _ in range(2)]
            QTz = [self.scr("QT", [128, T], BF16) for _ in range(2)]
            TB = self.scr("TB", [128, ni * 128], F32)
            o1 = self.scr("o1", [128, T], F32)
            onesb = self.hbs[1][:, 0:128]
            onesf = self.hbs[1][:, 256:512].bitcast(F32)
            halves = [self.xts[i][:, k * 512:(k + 1) * 512] for i in range(4) for k in range(2)]
            Rb, odb, sqb, rsb = halves[0:2], halves[2:4], halves[4], halves[5]
            P.op("gpsimd", lambda e: e.memset(onesb, 1.0), writes=["onesb", ("hb", 1)])
            P.op("gpsimd", lambda e: e.memset(onesf, 1.0), writes=["onesf", ("hb", 1)])
            P.op("gpsimd", lambda e: e.memset(QTz[0][64:128, :], 0.0), writes=[("QTz", 0)])
            P.op("gpsimd", lambda e: e.memset(QTz[1][0:64, :], 0.0), writes=[("QTz", 1)])
            lam = self.hbs[0][:].bitcast(F32)[:, 0:256].rearrange("p (a d) -> p a d", d=64)
            lamv = self.scr("lamv", [128, 4], F32)
            lam_d = self.din(pfx + "lam", [4, 64], F32).ap()
            P.op("sync", lambda e: e.dma_start(out=lam, in_=AP(lam_d.tensor, 0, [[0, 128], [64, 4], [1, 64]])), writes=["lam", ("hb", 0)], slot="lam")
            P.op("vector", lambda e: e.tensor_tensor(out=lam[:, 0, :], in0=lam[:, 0, :], in1=lam[:, 1, :], op=ALU.mult), reads=["lam"], writes=["lam"])
            P.op("vector", lambda e: e.tensor_tensor(out=lam[:, 2, :], in0=lam[:, 2, :], in1=lam[:, 3, :], op=ALU.mult), reads=["lam"], writes=["lam"])
            P.op("vector", lambda e: e.tensor_reduce(out=lamv[:, 0:1], in_=lam[:, 0, :], axis=AX.X, op=ALU.add), reads=["lam"], writes=["lamv"])
            P.op("vector", lambda e: e.tensor_reduce(out=lamv[:, 1:2], in_=lam[:, 2, :], axis=AX.X, op=ALU.add), reads=["lam"], writes=["lamv", ("hb", 0)])
            P.op("scalar", lambda e: e.activation(out=lamv[:, 0:2], in_=lamv[:, 0:2], func=AF.Exp), reads=["lamv"], writes=["lamv"])
            P.op("vector", lambda e: e.scalar_tensor_tensor(out=lamv[:, 2:3], in0=lamv[:, 1:2], scalar=-lambda_init_fn(li), in1=lamv[:, 0:1], op0=ALU.add, op1=ALU.subtract),
                 reads=["lamv"], writes=["lamv"])
            neglam = lamv[:, 2:3]
            satb = self.scr("satb", [128, 2, 16], F32)
            rb_d = self.din("rel_bias", [32, 16], F32).ap()
            P.op("sync", lambda e: e.dma_start(out=satb[:, 0, :], in_=rb_d[15].partition_broadcast(128)), writes=["satb"], slot="satb0")
            P.op("sync", lambda e: e.dma_start(out=satb[:, 1, :], in_=rb_d[31].partition_broadcast(128)), writes=["satb"], slot="satb1")

            def load_head(h):
                s = h % 2
                P.op("sync", lambda e, h=h, s=s: e.dma_start(out=KTs[s][:], in_=kT_d[h]), reads=kkeys, writes=[("KT", s)], slot="kt%da" % s)
                for half in range(4):
                    P.op("sync", lambda e, h=h, s=s, half=half: e.dma_start(out=Vs[s][:, half * 8:(half + 1) * 8, :], in_=v_r[:, half * 8:(half + 1) * 8, h * dv:(h + 1) * dv]),
                         reads=vkeys, writes=[("V", s, half)], slot="v%d_%d" % (s, half))

            def fin_B(g, qc, j, h):
                otb, dnb = (2, 3)[g % 2], (4, 5)[g % 2]
                OT, DN = self.banks[otb], self.banks[dnb]
                R, od = Rb[g % 2], odb[g % 2]
                Rk, odk = ("Rb", g % 2), ("odb", g % 2)
                cs = slice(qc * 512, (qc + 1) * 512)
                P.op("vector", lambda e, R=R, DN=DN: e.reciprocal(out=R, in_=DN[:, 0:512]), reads=[("bank", dnb)], writes=[Rk])
                if j == 0:
                    P.op("vector", lambda e, R=R, OT=OT, cs=cs: e.tensor_tensor(out=o1[:, cs], in0=OT[:, 0:512], in1=R, op=ALU.mult),
                         reads=[("bank", otb), Rk], writes=[("o1", qc)])
                    return
                P.op("vector", lambda e, R=R: e.tensor_scalar(out=R, in0=R, scalar1=neglam, scalar2=None, op0=ALU.mult), reads=[Rk, "lamv"], writes=[Rk])
                P.op("vector", lambda e, R=R, OT=OT, od=od: e.tensor_tensor(out=od, in0=OT[:, 0:512], in1=R, op=ALU.mult),
                     reads=[("bank", otb), Rk], writes=[odk])
                P.op("vector", lambda e, od=od, cs=cs: e.tensor_tensor(out=od, in0=od, in1=o1[:, cs], op=ALU.add), reads=[odk, ("o1", qc)], writes=[odk])
                P.op("scalar", lambda e, od=od: e.activation(out=sqb, in_=od, func=AF.Square), reads=[odk], writes=["sqb"])
                P.op("tensor", lambda e, DN=DN: e.matmul(DN[:, 0:512], lhsT=onesf, rhs=sqb, start=True, stop=True),
                     reads=["sqb", "onesf", Rk], writes=[("bank", dnb)])
                P.op("scalar", lambda e, DN=DN: e.activation(out=rsb, in_=DN[:, 0:512], func=AF.Ln, scale=1.0 / 128, bias=self.epsb[:, 0:1]),
                     reads=[("bank", dnb), "eps"], writes=["rsb"])
                P.op("scalar", lambda e: e.activation(out=rsb, in_=rsb, func=AF.Exp, scale=-0.5), reads=["rsb"], writes=["rsb"])
                P.op("vector", lambda e, od=od, h=h, qc=qc: e.tensor_tensor(out=self.actT[:, h, 1 + qc * 512:1 + (qc + 1) * 512], in0=od, in1=rsb, op=ALU.mult),
                     reads=[odk, "rsb"], writes=[("actT", h, t) for t in range(qc * 4, qc * 4 + 4)])

            load_head(0)
            g = 0
            for h in range(8):
                s = h % 2
                self.attn_flush()
                if h + 1 < 8:
                    load_head(h + 1)
                P.op("sync", lambda e, h=h: e.dma_start(out=QTz[0][0:64, :], in_=qT_d[h, 0:64, :]), reads=qkeys + [("QTz", 0)], writes=["QT"], slot="qt")
                P.op("sync", lambda e, h=h: e.dma_start(out=QTz[1][64:128, :], in_=qT_d[h, 64:128, :]), reads=qkeys + [("QTz", 1)], writes=["QT"], slot="qtb")
                KT, V = KTs[s], Vs[s]
                rds = [("KT", s), ("V", s, 0), ("V", s, 1), ("V", s, 2), ("V", s, 3), "QT"]
                for j in range(2):
                    for half in range(2):
                        hw = ni * 64
                        P.op("sync", lambda e, h=h, j=j, half=half, hw=hw: e.dma_start(out=TB[:, half * hw:(half + 1) * hw], in_=bt_d[h * 2 + j, :, half * hw:(half + 1) * hw]),
                             writes=["TB"], slot="tb%d" % half)
                    for qc in range(NT // 4):
                        otb, dnb = (2, 3)[g % 2], (4, 5)[g % 2]
                        for kb in range(NT):
                            dp = min(max(kb - 4 * qc, -12), 12)
                            i0 = 12 - dp
                            mms = [(KT[:, kb * 128:(kb + 1) * 128], QTz[j][:, qc * 512:(qc + 1) * 512], 0, 512)]

                            def back(pt, ptk, kb=kb, V=V, s=s, otb=otb, dnb=dnb):
                                P.op("tensor", lambda e, pt=pt: e.matmul(self.banks[otb][:, 0:512], lhsT=V[:, kb, :], rhs=pt[:, 0:512], start=(kb == 0), stop=(kb == NT - 1)),
                                     reads=[ptk, ("V", s, kb // 8)], writes=[("bank", otb)])
                                P.op("tensor", lambda e, pt=pt: e.matmul(self.banks[dnb][:, 0:512], lhsT=onesb, rhs=pt[:, 0:512], start=(kb == 0), stop=(kb == NT - 1)),
                                     reads=[ptk, "onesb"], writes=[("bank", dnb)])
                            post = None
                            if kb == NT - 1:
                                post = (lambda g=g, qc=qc, j=j, h=h: fin_B(g, qc, j, h))
                            cbias = None
                            dfull = kb - 4 * qc
                            if dfull >= 12:
                                cbias = satb[:, 1, 2 * h + j:2 * h + j + 1]
                            elif dfull <= -9:
                                cbias = satb[:, 0, 2 * h + j:2 * h + j + 1]
                            self.attn_item(it, mms, TB[:, i0 * 128:i0 * 128 + 512], 512, [], "TB", rds, post, back, cbias)
                            it += 1
                        g += 1
            self.attn_flush()
            P.barrier()
        elif kind == "A":
            dv = 64
            NE = NT + 2
            KTs = [self.scr("KT", [64, NE * 128], BF16) for _ in range(2)]
            Vs = [self.scr("V", [128, NE, dv + 1], BF16) for _ in range(2)]
            QTs = [self.scr("QT", [64, T], BF16) for _ in range(2)]
            TBs = [self.scr("TB", [128, ni * 128], F32) for _ in range(2)]
            pairs = [self.scr("pair", [128, NT, 128], BF16) for _ in range(2)]
            esink = self.scr("esink", [128, 16], F32)
            sink_d = self.din(pfx + "sink", [16], F32).ap()
            P.op("sync", lambda e: e.dma_start(out=esink[:], in_=sink_d.partition_broadcast(128)), writes=["esink"], slot="esink")
            P.op("scalar", lambda e: e.activation(out=esink[:], in_=esink[:], func=AF.Exp), reads=["esink"], writes=["esink"])
            for s_i in range(2):
                vb, kb_ = Vs[s_i], KTs[s_i]
                P.op("gpsimd", lambda e, vb=vb: e.memset(vb[:], 0.0), writes=[("Vones", s_i), ("V", s_i)])
                P.op("gpsimd", lambda e, vb=vb: e.memset(vb[:, 1:NE - 1, dv:dv + 1], 1.0), writes=[("Vones", s_i), ("V", s_i)])
                P.op("gpsimd", lambda e, kb_=kb_: e.memset(kb_[:], 0.0), writes=[("KT", s_i)])

            def load_kv(kvh):
                s = kvh % 2
                blk, r0 = kvh // 2, (kvh % 2) * 64
                P.op("sync", lambda e: e.dma_start(out=KTs[s][:, 128:128 + T], in_=kT_d[blk, r0:r0 + 64, :]), reads=kkeys, writes=[("KT", s)], slot="kt%db" % s)
                for half in range(4):
                    P.op("sync", lambda e, half=half: e.dma_start(out=Vs[s][:, 1 + half * 8:1 + (half + 1) * 8, 0:dv], in_=v_r[:, half * 8:(half + 1) * 8, kvh * dv:(kvh + 1) * dv]),
                         reads=vkeys, writes=[("V", s)], slot="v%db%d" % (s, half))

            def load_q(h):
                s = h % 2
                P.op("sync", lambda e: e.dma_start(out=QTs[s][:], in_=qT_d[h // 2, (h % 2) * 64:(h % 2) * 64 + 64, :]), reads=qkeys, writes=[("QT", s)], slot="qt%d" % s)
                P.op("sync", lambda e: e.dma_start(out=TBs[s][:], in_=bt_d[h]), writes=[("TB", s)], slot="tb%d" % s)

            fin_box = [0]
            load_kv(0)
            load_q(0)
            for h in range(16):
                kvh = h // 4
                s = kvh % 2
                self.attn_flush()
                if h % 4 == 0 and kvh + 1 < 4:
                    load_kv(kvh + 1)
                if h + 1 < 16:
                    load_q(h + 1)
                KT, V, QT, TB = KTs[s], Vs[s], QTs[h % 2], TBs[h % 2]
                pair = pairs[(h // 2) % 2]
                rds = [("KT", s), ("V", s), ("QT", h % 2), ("Vones", s)]
                for qt in range(NT):
                    ab = 2 + (it % 4)
                    mms = [(KT[0:64, (qt + 2 - i) * 128:(qt + 3 - i) * 128], QT[0:64, qt * 128:(qt + 1) * 128], i * 128, 128) for i in range(3)]
                    pvs = [(i * 128, V[:, qt + 2 - i, 0:dv + 1], ab, dv + 1, i == 0, i == 2) for i in range(3)]
                    def post_A(acc=self.banks[ab], ak=("bank", ab), qt=qt, h=h, pair=pair):
                        fin_box[0] += 1
                        fin = fin_box[0]
                        r = rr[fin % 4]
                        rk = ("rr", fin % 4)
                        P.op("vector", lambda e, acc=acc, r=r, h=h: e.tensor_tensor(out=r[:, 0:1], in0=acc[:, dv:dv + 1], in1=esink[:, h:h + 1], op=ALU.add),
                             reads=[ak, "esink"], writes=[rk])
                        P.op("vector", lambda e, r=r: e.reciprocal(out=r[:, 1:2], in_=r[:, 0:1]), reads=[rk], writes=[rk])
                        P.op("vector", lambda e, acc=acc, r=r, pair=pair, qt=qt, h=h: e.tensor_scalar(out=pair[:, qt, (h % 2) * 64:(h % 2) * 64 + 64], in0=acc[:, 0:dv], scalar1=r[:, 1:2], scalar2=None, op0=ALU.mult),
                             reads=[ak, rk], writes=[("pair", (h // 2) % 2, qt, h % 2)])
                    self.attn_item(it, mms, TB[:, 0:384], 384, pvs, ("TB", h % 2), rds, post_A)
                    it += 1
                if h % 2 == 1:
                    self.attn_flush()
                    for qt in range(NT):
                        fin += 1
                        bi = 7
                        pT = self.bank_bf(bi)
                        P.op("tensor", lambda e, pair=pair, qt=qt, pT=pT: e.transpose(out=pT[:, 0, :], in_=pair[:, qt, :], identity=self.ident[:]),
                             reads=[("pair", (h // 2) % 2, qt, 0), ("pair", (h // 2) % 2, qt, 1), "ident"], writes=[("bank", bi)])
                        P.op("scalar", lambda e, pT=pT, h=h, qt=qt: e.activation(out=self.actT[:, h // 2, 1 + qt * 128:1 + (qt + 1) * 128], in_=pT[:, 0, :], func=AF.Copy),
                             reads=[("bank", bi)], writes=[("actT", h // 2, qt)])
        else:
            dv = 128
            NE = NT + 16
            KT = self.scr("KT", [128, NE * 128], BF16)
            V = self.scr("V", [128, NE, dv + 1], BF16)
            QT = self.scr("QT", [128, 3, T], BF16)
            TB = self.scr("TB", [128, ni * 128], F32)
            P.op("gpsimd", lambda e: e.memset(V[:], 0.0), writes=["Vones", "V"])
            P.op("gpsimd", lambda e: e.memset(V[:, 8:NE - 8, dv:dv + 1], 1.0), writes=["Vones", "V"])
            P.op("gpsimd", lambda e: e.memset(KT[:], 0.0), writes=["KT"])
            ents = [(0, d_) for d_ in (1, 0, -1)] + [(1, d_) for d_ in (2, 1, 0, -1, -2)] + [(2, d_) for d_ in range(8, -9, -1)]
            fin_box = [0]
            for j in range(4):
                self.attn_flush()
                P.op("sync", lambda e, j=j: e.dma_start(out=KT[:, 1024:1024 + T], in_=kT_d[j]), reads=kkeys, writes=["KT"], slot="ktb")
                for half in range(4):
                    P.op("sync", lambda e, j=j, half=half: e.dma_start(out=V[:, 8 + half * 8:8 + (half + 1) * 8, 0:dv], in_=v_r[:, half * 8:(half + 1) * 8, j * dv:(j + 1) * dv]),
                         reads=vkeys, writes=["V"], slot="vb%d" % half)
                for g in range(3):
                    P.op("sync", lambda e, g=g, j=j: e.dma_start(out=QT[:, g, :], in_=qT_d[g * 4 + j]), reads=qkeys, writes=["QT"], slot="qt%d" % g)
                P.op("sync", lambda e, j=j: e.dma_start(out=TB[:], in_=bt_d[j]), writes=["TB"], slot="tb")
                rds = ["KT", "V", "QT", "Vones"]
                for qt in range(NT):
                    ab = 2 + (qt % 4)
                    for i0 in range(0, 25, 4):
                        grp = list(range(i0, min(i0 + 4, 25)))
                        mms = []
                        pvs = []
                        for n_, idx in enumerate(grp):
                            g, dl = ents[idx]
                            eb = qt + 8 + dl
                            mms.append((KT[:, eb * 128:(eb + 1) * 128], QT[:, g, qt * 128:(qt + 1) * 128], n_ * 128, 128))
                            pvs.append((n_ * 128, V[:, eb, 0:dv + 1], ab, dv + 1, idx == 0, idx == 24))
                        post = None
                        if grp[-1] == 24:
                            def post(acc=self.banks[ab], ak=("bank", ab), qt=qt, j=j):
                                fin_box[0] += 1
                                fin = fin_box[0]
                                r = rr[fin % 4]
                                rk = ("rr", fin % 4)
                                on = ons[fin % 2]
                                onk = ("on", fin % 2)
                                P.op("vector", lambda e, acc=acc, r=r: e.reciprocal(out=r[:, 0:1], in_=acc[:, dv:dv + 1]), reads=[ak], writes=[rk])
                                P.op("vector", lambda e, acc=acc, r=r, on=on: e.tensor_scalar(out=on[:], in0=acc[:, 0:dv], scalar1=r[:, 0:1], scalar2=None, op0=ALU.mult),
                                     reads=[ak, rk], writes=[onk])
                                self.out_transpose(on[:], onk, j, qt, fin)
                        self.attn_item(it, mms, TB[:, i0 * 128:(i0 + len(grp)) * 128], len(grp) * 128, pvs, "TB", rds, post)
                        it += 1
            self.attn_flush()

    def phase_wo(self, li):
        P = self.P
        self.bg_flush()
        L = LAYERS[li]
        pfx = "l%d_" % li
        nfc = L["nfc"]
        W = self.din(pfx + "w_o", [nfc * 128, D], F32)
        scale = None
        srd = ()
        if L["kind"] == "B":
            sg = self.din(pfx + "sgcol", [128, 1], F32).ap()
            sgt = self.scr("sgt", [128, 1], F32)
            P.op("sync", lambda e: e.dma_start(out=sgt[:], in_=sg), writes=["sgt"], slot="sgt")
            P.op("vector", lambda e: e.tensor_scalar(out=sgt[:], in0=sgt[:], scalar1=1.0 - lambda_init_fn(li), scalar2=None, op0=ALU.mult), reads=["sgt"], writes=["sgt"])
            scale = lambda k: sgt[:, 0:1]
            srd = ["sgt"]
        kch = list(range(nfc))
        units = [self.load_unit(W, kch, [(n * 512, 512)], scale=scale, scale_reads=srd) for n in range(2)]
        pend = [self.x_load(0, 0), self.x_load(0, 1)]
        for t in range(NT):
            for n in range(2):
                nx = t * 2 + n + 2
                if nx < NT * 2:
                    pend.append(self.x_load(nx // 2, nx % 2))
                slot, wkeys = units[n]
                wb = self.wb[slot]
                bi = 2 + n
                py = self.banks[bi]
                for c in range(nfc):
                    P.op("tensor", lambda e, c=c, t=t, py=py, wb=wb: e.matmul(py[:, 0:512], lhsT=self.actT[:, c, 1 + t * 128:1 + (t + 1) * 128], rhs=wb[:, c, 0:512], start=(c == 0), stop=(c == nfc - 1)),
                         reads=[("actT", c, t)] + wkeys, writes=[("bank", bi)])
                xt, xk, xi = pend[t * 2 + n]
                P.op("vector", lambda e, xt=xt, py=py: e.tensor_tensor(out=xt[:, 0:512], in0=py[:, 0:512], in1=xt[:, 0:512], op=ALU.add),
                     reads=[("bank", bi), xk], writes=[xk])
                self.x_store(t, n, xt, xk, xi)

    def phase_ffn(self, li):
        P = self.P
        pfx = "l%d_" % li
        have_bg = getattr(self, "bg_layer", None) == li
        if not have_bg:
            self.load_gcol(pfx + "gcol_ffn")
        self.phase_norm()
        self.scr_reset()
        Wu = self.din(pfx + "w_up", [D, 2 * DFF], F32)
        Wd = self.din(pfx + "w_down", [DFF, D], F32)
        cw_d = self.din(pfx + "convp", [128, 4, 44], F32).ap()
        cw = self.scr("cw", [128, 4, 44], F32)
        P.op("sync", lambda e: e.dma_start(out=cw[:], in_=cw_d), writes=["cw"], slot="cw")
        if have_bg:
            wu_keys, wd_keys = self.bg_keys["wubf"], self.bg_keys["wdbf"]
        else:
            wu_keys = self.precast(Wu, 8, 2 * DFF, self.wu_bf, "wubf", True)
            wd_keys = self.precast(Wd, 22, D, self.wd_bf, "wdbf", False)
        TC = 1024
        NK = T // TC
        a = self.scr("a", [128, 22, TC], BF16)
        Us = [[self.scr("U", [128, TC + 2], F32) for _ in range(2)] for _ in range(2)]
        T1s = [[self.scr("T1", [128, TC], F32) for _ in range(2)] for _ in range(2)]
        gsc = lambda k: self.gcol[:, k:k + 1]
        up_units = [[(f0 * 128, 256), (DFF + f0 * 128, 256)] for f0 in range(0, 22, 2)]
        dn_units = [(kc, n) for n in range(2) for kc in (list(range(0, 8)), list(range(8, 16)), list(range(16, 22)))]
        pcount = 0
        for k in range(NK):
            cb = TC * k
            allr = [("actT", c, t) for c in range(8) for t in range(max(8 * k - 1, 0), min(8 * k + 9, NT))] + ["halo"]
            nxt = self.load_unit_bf(self.wu_bf, list(range(8)), up_units[0], wu_keys)
            for ui in range(11):
                cur = nxt
                if ui + 1 < 11:
                    nxt = self.load_unit_bf(self.wu_bf, list(range(8)), up_units[ui + 1], wu_keys)
                slot, wkeys = cur
                wb = self.wb[slot]
                for pi in range(2):
                    f = ui * 2 + pi
                    par = pcount % 2
                    pcount += 1
                    for which in range(2):
                        off = which * 256 + pi * 128
                        fc = f + 22 * which
                        bA, bB, bE = self.banks[4 * which], self.banks[4 * which + 1], self.banks[4 * which + 2]
                        bks = [("bank", 4 * which + i) for i in range(3)]
                        for c in range(8):
                            P.op("tensor", lambda e, c=c, wb=wb, off=off, bA=bA, cb=cb: e.matmul(bA[:, 0:512], lhsT=wb[:, c, off:off + 128], rhs=self.actT[:, c, cb + 1:cb + 513], start=(c == 0), stop=(c == 7)),
                                 reads=allr + wkeys, writes=[bks[0]])
                        for c in range(8):
                            P.op("tensor", lambda e, c=c, wb=wb, off=off, bB=bB, cb=cb: e.matmul(bB[:, 0:512], lhsT=wb[:, c, off:off + 128], rhs=self.actT[:, c, cb + 513:cb + 1025], start=(c == 0), stop=(c == 7)),
                                 reads=allr + wkeys, writes=[bks[1]])
                        for c in range(8):
                            P.op("tensor", lambda e, c=c, wb=wb, off=off, bE=bE, cb=cb: e.matmul(bE[:, 0:2], lhsT=wb[:, c, off:off + 128], rhs=self.actT[:, c, cb:cb + 1026:1025], start=(c == 0), stop=(c == 7)),
                                 reads=allr + wkeys, writes=[bks[2]])
                        U = Us[which][par]
                        uk = ("U", which, par)
                        t1 = T1s[which][par]
                        tk = ("T1", which, par)
                        P.op("scalar", lambda e, U=U, bA=bA: e.activation(out=U[:, 1:513], in_=bA[:, 0:512], func=AF.Copy), reads=[bks[0]], writes=[(uk, 0)])
                        P.op("scalar", lambda e, U=U, bB=bB: e.activation(out=U[:, 513:1025], in_=bB[:, 0:512], func=AF.Copy), reads=[bks[1]], writes=[(uk, 1)])
                        P.op("scalar", lambda e, U=U, bE=bE: e.activation(out=U[:, 0:1026:1025], in_=bE[:, 0:2], func=AF.Copy), reads=[bks[2]], writes=[(uk, 2)])
                        P.op("scalar", lambda e, t1=t1, bA=bA, fc=fc: e.activation(out=t1[:, 0:512], in_=bA[:, 0:512], func=AF.Identity, scale=cw[:, 1, fc:fc + 1], bias=cw[:, 3, fc:fc + 1]),
                             reads=[bks[0], "cw"], writes=[(tk, 0)])
                        P.op("scalar", lambda e, t1=t1, bB=bB, fc=fc: e.activation(out=t1[:, 512:1024], in_=bB[:, 0:512], func=AF.Identity, scale=cw[:, 1, fc:fc + 1], bias=cw[:, 3, fc:fc + 1]),
                             reads=[bks[1], "cw"], writes=[(tk, 1)])
                        P.op("vector", lambda e, t1=t1, U=U, fc=fc: e.scalar_tensor_tensor(out=t1[:], in0=U[:, 0:TC], scalar=cw[:, 0, fc:fc + 1], in1=t1[:], op0=ALU.mult, op1=ALU.add),
                             reads=[(uk, 0), (uk, 1), (uk, 2), (tk, 0), (tk, 1), "cw"], writes=[(tk, 0), (tk, 1)])
                        P.op("vector", lambda e, t1=t1, U=U, fc=fc: e.scalar_tensor_tensor(out=t1[:], in0=U[:, 2:TC + 2], scalar=cw[:, 2, fc:fc + 1], in1=t1[:], op0=ALU.mult, op1=ALU.add),
                             reads=[(uk, 0), (uk, 1), (uk, 2), (tk, 0), (tk, 1), "cw"], writes=[(tk, 0), (tk, 1)])
                    tg, tv = T1s[0][par], T1s[1][par]
                    kg, kv = ("T1", 0, par), ("T1", 1, par)
                    P.op("scalar", lambda e, tg=tg: e.activation(out=tg[:], in_=tg[:], func=AF.Silu), reads=[(kg, 0), (kg, 1)], writes=[(kg, 0), (kg, 1)])
                    P.op("vector", lambda e, f=f, tg=tg, tv=tv: e.tensor_tensor(out=a[:, f, :], in0=tg[:], in1=tv[:], op=ALU.mult),
                         reads=[(kg, 0), (kg, 1), (kv, 0), (kv, 1)], writes=[("a", f)])
            nxt = self.load_unit_bf(self.wd_bf, dn_units[0][0], [(dn_units[0][1] * 512, 512)], wd_keys)
            for di, (kc, n) in enumerate(dn_units):
                cur = nxt
                if di + 1 < len(dn_units):
                    nxt = self.load_unit_bf(self.wd_bf, dn_units[di + 1][0], [(dn_units[di + 1][1] * 512, 512)], wd_keys)
                slot, wkeys = cur
                wb = self.wb[slot]
                if kc[0] == 0:
                    pend = [self.x_load(8 * k + t, n) for t in range(2)]
                for t in range(8):
                    for ci, f in enumerate(kc):
                        P.op("tensor", lambda e, t=t, ci=ci, f=f, wb=wb: e.matmul(self.banks[t][:, 0:512], lhsT=a[:, f, t * 128:(t + 1) * 128], rhs=wb[:, ci, 0:512], start=(f == 0), stop=(f == 21)),
                             reads=[("a", f)] + wkeys, writes=[("bank", t)])
                if kc[-1] == 21:
                    for t in range(8):
                        tt = 8 * k + t
                        if t + 2 < 8:
                            pend.append(self.x_load(8 * k + t + 2, n))
                        xt, xk, xi = pend[t]
                        P.op("vector", lambda e, t=t, xt=xt: e.tensor_tensor(out=xt[:, 0:512], in0=self.banks[t][:, 0:512], in1=xt[:, 0:512], op=ALU.add),
                             reads=[("bank", t), xk], writes=[xk])
                        self.x_store(tt, n, xt, xk, xi)

    def finish(self):
        self.P.finalize()
        return self.nc


def rel_bucket_np(rel):
    nb = 16
    max_exact = 8
    n = np.abs(rel)
    nf = np.maximum(n, 1).astype(np.float32)
    large = max_exact + (np.log(nf / np.float32(max_exact)) / np.float32(math.log(1024 / max_exact)) * np.float32(nb - max_exact)).astype(np.int32)
    large = np.minimum(large, nb - 1)
    return np.where(rel > 0, nb, 0) + np.where(n < max_exact, n, large)


def bias_tables(rel_bias, kind):
    rb = np.asarray(rel_bias, np.float32)
    kp = np.arange(128)[:, None]
    qp = np.arange(128)[None, :]
    if kind == "A":
        out = np.empty((16, 128, 3 * 128), np.float32)
        for i, dl in enumerate((1, 0, -1)):
            rel = dl * 128 + kp - qp
            bk = rel_bucket_np(rel)
            ok = np.abs(rel) <= 128
            for h in range(16):
                out[h, :, i * 128:(i + 1) * 128] = np.where(ok, rb[bk, h], NEG)
        return out
    if kind == "B":
        out = np.empty((16, 128, 28 * 128), np.float32)
        for i in range(28):
            dl = 12 - i
            rel = dl * 128 + kp - qp
            bk = rel_bucket_np(rel)
            for m in range(16):
                out[m, :, i * 128:(i + 1) * 128] = rb[bk, m]
        return out
    ents = [(0, d_) for d_ in (1, 0, -1)] + [(1, d_) for d_ in (2, 1, 0, -1, -2)] + [(2, d_) for d_ in range(8, -9, -1)]
    dils = (1, 4, 16)
    out = np.empty((4, 128, 25 * 128), np.float32)
    for i, (g, dl) in enumerate(ents):
        rel = dl * 128 + kp - qp
        dil = dils[g]
        ok = (rel % dil == 0) & (np.abs(rel) <= 64 * dil)
        bk = rel_bucket_np(rel)
        for j in range(4):
            out[j, :, i * 128:(i + 1) * 128] = np.where(ok, rb[bk, g * 4 + j], NEG)
    return out


def gcols(g):
    return np.ascontiguousarray(np.asarray(g, np.float32).reshape(8, 128).T)


_PROG = None
DEBUG_LAYERS = 4


def get_prog():
    global _PROG
    if _PROG is None:
        b = Builder()
        for li in range(DEBUG_LAYERS):
            b.phase_qkv(li)
            b.phase_attn(li)
            b.phase_wo(li)
            b.phase_ffn(li)
        nc = b.finish()
        _PROG = (nc, list(b.din_names), list(b.dout_names))
    return _PROG


def kernel(x, rel_bias,
           l0_attn_norm, l0_w_qkv, l0_q_gain, l0_k_gain, l0_sink, l0_w_o,
           l0_ffn_norm, l0_w_up, l0_conv_w, l0_conv_b, l0_w_down,
           l1_attn_norm, l1_w_qkv, l1_q_gain, l1_k_gain, l1_lambda_q1, l1_lambda_k1,
           l1_lambda_q2, l1_lambda_k2, l1_sub_gain, l1_w_o,
           l1_ffn_norm, l1_w_up, l1_conv_w, l1_conv_b, l1_w_down,
           l2_attn_norm, l2_w_qkv, l2_q_gain, l2_k_gain, l2_w_o,
           l2_ffn_norm, l2_w_up, l2_conv_w, l2_conv_b, l2_w_down,
           l3_attn_norm, l3_w_qkv, l3_q_gain, l3_k_gain, l3_sink, l3_w_o,
           l3_ffn_norm, l3_w_up, l3_conv_w, l3_conv_b, l3_w_down):
    inp = {
        "x": x,
        "rel_bias": rel_bias,
        "l0_attn_norm": l0_attn_norm,
        "l0_w_qkv": l0_w_qkv,
        "l0_q_gain": l0_q_gain,
        "l0_k_gain": l0_k_gain,
        "l0_sink": l0_sink,
        "l0_w_o": l0_w_o,
        "l0_ffn_norm": l0_ffn_norm,
        "l0_w_up": l0_w_up,
        "l0_conv_w": l0_conv_w,
        "l0_conv_b": l0_conv_b,
        "l0_w_down": l0_w_down,
        "l1_attn_norm": l1_attn_norm,
        "l1_w_qkv": l1_w_qkv,
        "l1_q_gain": l1_q_gain,
        "l1_k_gain": l1_k_gain,
        "l1_lambda_q1": l1_lambda_q1,
        "l1_lambda_k1": l1_lambda_k1,
        "l1_lambda_q2": l1_lambda_q2,
        "l1_lambda_k2": l1_lambda_k2,
        "l1_sub_gain": l1_sub_gain,
        "l1_w_o": l1_w_o,
        "l1_ffn_norm": l1_ffn_norm,
        "l1_w_up": l1_w_up,
        "l1_conv_w": l1_conv_w,
        "l1_conv_b": l1_conv_b,
        "l1_w_down": l1_w_down,
        "l2_attn_norm": l2_attn_norm,
        "l2_w_qkv": l2_w_qkv,
        "l2_q_gain": l2_q_gain,
        "l2_k_gain": l2_k_gain,
        "l2_w_o": l2_w_o,
        "l2_ffn_norm": l2_ffn_norm,
        "l2_w_up": l2_w_up,
        "l2_conv_w": l2_conv_w,
        "l2_conv_b": l2_conv_b,
        "l2_w_down": l2_w_down,
        "l3_attn_norm": l3_attn_norm,
        "l3_w_qkv": l3_w_qkv,
        "l3_q_gain": l3_q_gain,
        "l3_k_gain": l3_k_gain,
        "l3_sink": l3_sink,
        "l3_w_o": l3_w_o,
        "l3_ffn_norm": l3_ffn_norm,
        "l3_w_up": l3_w_up,
        "l3_conv_w": l3_conv_w,
        "l3_conv_b": l3_conv_b,
        "l3_w_down": l3_w_down,
    }
    x = np.ascontiguousarray(np.asarray(x, np.float32))
    rel_bias = np.asarray(rel_bias, np.float32)
    shared = {"ident": np.eye(128, dtype=np.float32), "rel_bias": np.ascontiguousarray(rel_bias)}
    for kind in ("A", "B", "C"):
        shared["bt_" + kind] = bias_tables(rel_bias, kind)
    f32 = lambda a: np.ascontiguousarray(np.asarray(a, np.float32))
    for li in range(4):
        L = LAYERS[li]
        p = "l%d_" % li
        shared[p + "gcol_attn"] = gcols(inp[p + "attn_norm"])
        shared[p + "gcol_ffn"] = gcols(inp[p + "ffn_norm"])
        for w in ("w_qkv", "w_o", "w_up", "w_down"):
            shared[p + w] = f32(inp[p + w])
        shared[p + "qkg"] = np.ascontiguousarray(np.stack([f32(inp[p + "q_gain"]), f32(inp[p + "k_gain"])]))
        cwv = f32(inp[p + "conv_w"]).reshape(3, 44, 128)
        cbv = f32(inp[p + "conv_b"]).reshape(1, 44, 128)
        shared[p + "convp"] = np.ascontiguousarray(np.concatenate([cwv, cbv], 0).transpose(2, 0, 1))
        if L["kind"] == "A":
            shared[p + "sink"] = f32(inp[p + "sink"])
        if L["kind"] == "B":
            shared[p + "lam"] = np.ascontiguousarray(np.stack([f32(inp[p + k]) for k in ("lambda_q1", "lambda_k1", "lambda_q2", "lambda_k2")]))
            shared[p + "sgcol"] = f32(inp[p + "sub_gain"]).reshape(128, 1)
    import time as _t
    _t1 = _t.time()
    nc, dins, douts = get_prog()
    print("[kernel] host prep + build took %.1fs" % (_t.time() - _t1), flush=True)
    in_maps = []
    for c in range(NCORES):
        m = {}
        for n in dins:
            m[n] = x[c] if n == "x" else shared[n]
        in_maps.append(m)
    import time as _t
    _t0 = _t.time()
    res = run_bass_kernel_spmd(nc, in_maps, core_ids=list(range(NCORES)))
    print("[kernel] run_bass_kernel_spmd took %.1fs" % (_t.time() - _t0), flush=True)
    return np.stack([res.results[c]["x_out"] for c in range(NCORES)]).astype(np.float32)
```
